# Optimizing a Trainium2 kernel written in Bass

```python
import jax, jax.numpy as jnp
from jax import lax
import numpy as np

D_MODEL = 1024
BATCH = 32
SEQ = 256
DEPTH = 2
DEC_BATCH = 8
DEC_SEQ = 1024
PAST_LEN = 512

GRID_W = 64
HEAD_DIM = 128
A_HEADS = 4
A_DK = 128
A_DV = 128
SHORT_CONV = 5
DELTA_CHUNK = 64
B_Q_HEADS = 4
B_KV_HEADS = 2
WINDOW = 128
ATTN_BLOCK = 128
C_HEADS = 4
C_DK = 128
C_DV = 256
GATE_RANK = 16
GATE_NORM = 16.0
GLA_CHUNK = 16
D_FF = 2816
ROPE_THETA = 10000.0
EPS = 1e-6
N_MOD = 9
N_EVEN = (DEPTH + 1) // 2
N_ODD = DEPTH // 2
EVEN_SIZES = (2 * A_HEADS * A_DK + A_HEADS * A_DV, A_HEADS * A_DV, 2 * A_HEADS, 2 * A_HEADS,
              B_Q_HEADS * HEAD_DIM, B_KV_HEADS * HEAD_DIM, B_KV_HEADS * HEAD_DIM)
EVEN_IN = sum(EVEN_SIZES)
EVEN_MIX = A_HEADS * A_DV + B_Q_HEADS * HEAD_DIM
ODD_SIZES = (C_HEADS * C_DK, C_HEADS * C_DK, C_HEADS * C_DV, C_HEADS * C_DV, 2 * GATE_RANK)
ODD_IN = sum(ODD_SIZES)
ODD_MIX = C_HEADS * C_DV
F32 = jnp.float32

kernel_name = 'hybrid_diffusion_deltanet_swa_gla_step'


def _split(x, sizes):
    idx, acc = [], 0
    for s in sizes[:-1]:
        acc += s
        idx.append(acc)
    return jnp.split(x, idx, axis=-1)


def _rmsnorm(x, g):
    xf = x.astype(F32)
    y = xf * lax.rsqrt(jnp.mean(xf * xf, axis=-1, keepdims=True) + EPS)
    return (y * g.astype(F32)).astype(x.dtype)


def _l2norm(x):
    return x * lax.rsqrt(jnp.sum(x * x, axis=-1, keepdims=True) + EPS)


def _modnorm(x, g, shift, scale):
    return _rmsnorm(x, g) * (1.0 + scale) + shift


def _swiglu(h, w_gu, w_down):
    gt, up = jnp.split(h @ w_gu, 2, axis=-1)
    return (jax.nn.silu(gt) * up) @ w_down


def _ffn_half(x, mod, i, g, w_gu, w_down):
    h = _modnorm(x, g, mod[:, i], mod[:, i + 1])
    return x + 0.5 * mod[:, i + 2] * _swiglu(h, w_gu, w_down)


def _ada(cond, w, b):
    m = jax.nn.silu(cond) @ w + b
    return m.reshape(cond.shape[0], N_MOD, 1, D_MODEL)


def _depthwise_conv(x, w):
    pad = (SHORT_CONV - 1) // 2
    return lax.conv_general_dilated(x, w[:, None, :].astype(x.dtype), window_strides=(1,),
                                    padding=[(pad, pad)], dimension_numbers=('NWC', 'WIO', 'NWC'),
                                    feature_group_count=x.shape[-1])


def _to_chunks(x, c):
    b, t, h = x.shape[:3]
    x = x.reshape(b, t // c, c, h, *x.shape[3:])
    return x.transpose(0, 3, 1, 2, *range(4, x.ndim))


def _from_chunks(o):
    n, b, h, c, d = o.shape
    return o.transpose(1, 0, 3, 2, 4).reshape(b, n * c, h, d)


def _chunk_first(x):
    return jnp.moveaxis(x, 2, 0)


def _delta_rule(q, k, v, g, beta, s0):
    dt = v.dtype
    b, t, h, dk = q.shape
    dv = v.shape[-1]
    c = DELTA_CHUNK
    q = _l2norm(q.astype(F32)) * (dk ** -0.5)
    k = _l2norm(k.astype(F32))
    qc, kc, vc = _to_chunks(q, c), _to_chunks(k, c), _to_chunks(v.astype(F32), c)
    gc = jnp.cumsum(_to_chunks(g.astype(F32), c), axis=-1)
    bc = _to_chunks(beta.astype(F32), c)[..., None]
    tril = jnp.tril(jnp.ones((c, c), bool))
    strict = jnp.tril(jnp.ones((c, c), bool), -1)
    decay = jnp.exp(jnp.where(tril, gc[..., :, None] - gc[..., None, :], -jnp.inf))
    kk = jnp.einsum('bhnid,bhnjd->bhnij', kc * bc, kc) * decay
    lower = jnp.where(strict, kk, 0.0) + jnp.eye(c, dtype=F32)
    rhs = jnp.concatenate([vc * bc, kc * bc * jnp.exp(gc)[..., None]], axis=-1)
    sol = jax.lax.linalg.triangular_solve(lower, rhs, left_side=True, lower=True, unit_diagonal=True)
    u, w = sol[..., :dv], sol[..., dv:]
    qk = jnp.einsum('bhnid,bhnjd->bhnij', qc, kc) * decay
    q_dec = qc * jnp.exp(gc)[..., None]
    k_dec = kc * jnp.exp(gc[..., -1:] - gc)[..., None]
    g_last = jnp.exp(gc[..., -1])

    def step(s, xs):
        qd, kd, u_c, w_c, a_c, gl = xs
        v_new = u_c - jnp.einsum('bhck,bhkv->bhcv', w_c, s)
        o = jnp.einsum('bhck,bhkv->bhcv', qd, s) + jnp.einsum('bhij,bhjv->bhiv', a_c, v_new)
        s = s * gl[..., None, None] + jnp.einsum('bhck,bhcv->bhkv', kd, v_new)
        return s, o

    s_fin, o = lax.scan(step, s0.astype(F32), (_chunk_first(q_dec), _chunk_first(k_dec), _chunk_first(u),
                                              _chunk_first(w), _chunk_first(qk), _chunk_first(g_last)))
    return _from_chunks(o).astype(dt), s_fin


def _gla(q, k, v, gk, s0):
    dt = v.dtype
    b, t, h, dk = q.shape
    c = GLA_CHUNK
    qc = _to_chunks(q.astype(F32) * (dk ** -0.5), c)
    kc = _to_chunks(k.astype(F32), c)
    vc = _to_chunks(v.astype(F32), c)
    gcum = jnp.cumsum(_to_chunks(gk.astype(F32), c), axis=-2)
    tril = jnp.tril(jnp.ones((c, c), bool))
    rel = jnp.exp(jnp.where(tril[..., None], gcum[..., :, None, :] - gcum[..., None, :, :], -jnp.inf))
    a = jnp.einsum('bhnid,bhnjd,bhnijd->bhnij', qc, kc, rel)
    q_dec = qc * jnp.exp(gcum)
    k_dec = kc * jnp.exp(gcum[..., -1:, :] - gcum)
    g_last = jnp.exp(gcum[..., -1, :])

    def step(s, xs):
        qd, kd, v_c, a_c, gl = xs
        o = jnp.einsum('bhck,bhkv->bhcv', qd, s) + jnp.einsum('bhij,bhjv->bhiv', a_c, v_c)
        s = s * gl[..., None] + jnp.einsum('bhck,bhcv->bhkv', kd, v_c)
        return s, o

    s_fin, o = lax.scan(step, s0.astype(F32), (_chunk_first(q_dec), _chunk_first(k_dec), _chunk_first(vc),
                                              _chunk_first(a), _chunk_first(g_last)))
    return _from_chunks(o).astype(dt), s_fin


def _flip(x):
    return jnp.flip(x, axis=1)


def _delta_bidir(q, k, v, g, beta, s0):
    o_f, s_f = _delta_rule(q, k, v, g[:, :, 0], beta[:, :, 0], s0[:, 0])
    o_b, s_b = _delta_rule(_flip(q), _flip(k), _flip(v), _flip(g[:, :, 1]), _flip(beta[:, :, 1]), s0[:, 1])
    return o_f + _flip(o_b), jnp.stack([s_f, s_b], axis=1)


def _gla_bidir(q, k, v, gk, s0):
    o_f, s_f = _gla(q, k, v, gk[:, :, 0], s0[:, 0])
    o_b, s_b = _gla(_flip(q), _flip(k), _flip(v), _flip(gk[:, :, 1]), s0[:, 1])
    return o_f + _flip(o_b), jnp.stack([s_f, s_b], axis=1)


def _axial_rope(x):
    t, dh = x.shape[1], x.shape[-1]
    rows = t // GRID_W
    row = jnp.repeat(jnp.arange(rows), GRID_W).astype(F32)
    col = (jnp.arange(t) % GRID_W).astype(F32)
    half = dh // 2
    quarter = half // 2
    inv = ROPE_THETA ** (-jnp.arange(quarter, dtype=F32) / quarter)

    def rot(xh, pos):
        ang = pos[:, None] * inv[None, :]
        cos = jnp.cos(ang)[None, :, None, :]
        sin = jnp.sin(ang)[None, :, None, :]
        x1, x2 = xh[..., :quarter], xh[..., quarter:]
        return jnp.concatenate([x1 * cos - x2 * sin, x2 * cos + x1 * sin], axis=-1)

    xf = x.astype(F32)
    return jnp.concatenate([rot(xf[..., :half], row), rot(xf[..., half:], col)], axis=-1).astype(x.dtype)


def _sink_probs(s, sink):
    sk = sink.astype(F32)[None, :, :, None, None]
    m = jnp.maximum(jnp.max(s, axis=-1, keepdims=True), sk)
    p = jnp.exp(s - m)
    return p / (jnp.sum(p, axis=-1, keepdims=True) + jnp.exp(sk - m))


def _context_attention(q, k, v, sink):
    b, s, hq, dh = q.shape
    grp = hq // B_KV_HEADS
    qg = (q.astype(F32) * dh ** -0.5).reshape(b, s, B_KV_HEADS, grp, dh)
    kf, vf = k.astype(F32), v.astype(F32)
    sink = sink.reshape(B_KV_HEADS, grp)

    def block(start):
        qb = lax.dynamic_slice_in_dim(qg, start, ATTN_BLOCK, axis=1)
        p = _sink_probs(jnp.einsum('bqhgd,bkhd->bhgqk', qb, kf), sink)
        return jnp.einsum('bhgqk,bkhd->bqhgd', p, vf)

    o = lax.map(block, jnp.arange(s // ATTN_BLOCK) * ATTN_BLOCK)
    return o.transpose(1, 0, 2, 3, 4, 5).reshape(b, s, hq * dh).astype(v.dtype)


def _latent_attention(q_rot, q_plain, k, v, k_ctx, v_ctx, sink):
    b, t, hq, dh = q_rot.shape
    grp = hq // B_KV_HEADS
    span = ATTN_BLOCK + 2 * WINDOW
    scale = dh ** -0.5
    qr = (q_rot.astype(F32) * scale).reshape(b, t, B_KV_HEADS, grp, dh)
    qp = (q_plain.astype(F32) * scale).reshape(b, t, B_KV_HEADS, grp, dh)
    pad = ((0, 0), (WINDOW, WINDOW), (0, 0), (0, 0))
    kp = jnp.pad(k.astype(F32), pad)
    vp = jnp.pad(v.astype(F32), pad)
    kc, vc = k_ctx.astype(F32), v_ctx.astype(F32)
    sink = sink.reshape(B_KV_HEADS, grp)

    def block(start):
        q_loc = lax.dynamic_slice_in_dim(qr, start, ATTN_BLOCK, axis=1)
        q_cb = lax.dynamic_slice_in_dim(qp, start, ATTN_BLOCK, axis=1)
        k_loc = lax.dynamic_slice_in_dim(kp, start, span, axis=1)
        v_loc = lax.dynamic_slice_in_dim(vp, start, span, axis=1)
        qpos = start + jnp.arange(ATTN_BLOCK)
        kpos = start - WINDOW + jnp.arange(span)
        valid = (jnp.abs(qpos[:, None] - kpos[None, :]) <= WINDOW) & (kpos >= 0) & (kpos < t)
        s_loc = jnp.where(valid, jnp.einsum('bqhgd,bkhd->bhgqk', q_loc, k_loc), -jnp.inf)
        s_ctx = jnp.einsum('bqhgd,bkhd->bhgqk', q_cb, kc)
        p = _sink_probs(jnp.concatenate([s_loc, s_ctx], axis=-1), sink)
        return (jnp.einsum('bhgqk,bkhd->bqhgd', p[..., :span], v_loc)
                + jnp.einsum('bhgqk,bkhd->bqhgd', p[..., span:], vc))

    o = lax.map(block, jnp.arange(t // ATTN_BLOCK) * ATTN_BLOCK)
    return o.transpose(1, 0, 2, 3, 4, 5).reshape(b, t, hq * dh).astype(v.dtype)


def _even_inputs(h, w_in, conv_w, a_log, dt_bias):
    b, t, _ = h.shape
    qkv, gate, b_raw, a_raw, qb, kb, vb = _split(h @ w_in, EVEN_SIZES)
    qkv = jax.nn.silu(_depthwise_conv(qkv, conv_w))
    qa, ka, va = _split(qkv, (A_HEADS * A_DK, A_HEADS * A_DK, A_HEADS * A_DV))
    beta = jax.nn.sigmoid(b_raw.astype(F32)).reshape(b, t, 2, A_HEADS)
    g = -jnp.exp(a_log.astype(F32)) * jax.nn.softplus(a_raw.astype(F32).reshape(b, t, 2, A_HEADS)
                                                       + dt_bias.astype(F32))
    return (qa.reshape(b, t, A_HEADS, A_DK), ka.reshape(b, t, A_HEADS, A_DK), va.reshape(b, t, A_HEADS, A_DV),
            g, beta, gate.reshape(b, t, A_HEADS, A_DV), qb.reshape(b, t, B_Q_HEADS, HEAD_DIM),
            kb.reshape(b, t, B_KV_HEADS, HEAD_DIM), vb.reshape(b, t, B_KV_HEADS, HEAD_DIM))


def _even_merge(o_delta, gate, o_attn, onorm, w_out):
    b, t = gate.shape[:2]
    o_a = _rmsnorm(o_delta, onorm) * jax.nn.silu(gate)
    mix = jnp.concatenate([o_a.reshape(b, t, A_HEADS * A_DV).astype(o_attn.dtype), o_attn], axis=-1)
    return mix @ w_out


def _even_context(h, w_in, conv_w, a_log, dt_bias, onorm, sink, w_out):
    qa, ka, va, g, beta, gate, qb, kb, vb = _even_inputs(h, w_in, conv_w, a_log, dt_bias)
    s0 = jnp.zeros((h.shape[0], 2, A_HEADS, A_DK, A_DV), F32)
    o_d, state = _delta_bidir(qa, ka, va, g, beta, s0)
    o_att = _context_attention(qb, kb, vb, sink)
    return _even_merge(o_d, gate, o_att, onorm, w_out), state, kb, vb


def _even_latent(h, s_ctx, k_ctx, v_ctx, w_in, conv_w, a_log, dt_bias, onorm, sink, w_out):
    qa, ka, va, g, beta, gate, qb, kb, vb = _even_inputs(h, w_in, conv_w, a_log, dt_bias)
    o_d, _ = _delta_bidir(qa, ka, va, g, beta, s_ctx)
    o_att = _latent_attention(_axial_rope(qb), qb, _axial_rope(kb), vb, k_ctx, v_ctx, sink)
    return _even_merge(o_d, gate, o_att, onorm, w_out)


def _odd_inputs(h, w_in, w_gate, gate_bias):
    b, t, _ = h.shape
    q, k, v, g_out, lr = _split(h @ w_in, ODD_SIZES)
    lr = lr.astype(F32).reshape(b, t, 2, GATE_RANK)
    gk = jax.nn.log_sigmoid(jnp.einsum('btzr,zrk->btzk', lr, w_gate.astype(F32))
                            + gate_bias.astype(F32)) / GATE_NORM
    return (q.reshape(b, t, C_HEADS, C_DK), k.reshape(b, t, C_HEADS, C_DK), v.reshape(b, t, C_HEADS, C_DV),
            gk.reshape(b, t, 2, C_HEADS, C_DK), g_out.reshape(b, t, C_HEADS, C_DV))


def _odd_merge(o, g_out, onorm, w_out):
    b, t = g_out.shape[:2]
    o = _rmsnorm(o, onorm) * jax.nn.silu(g_out)
    return o.reshape(b, t, ODD_MIX).astype(g_out.dtype) @ w_out


def _odd_context(h, w_in, w_gate, gate_bias, onorm, w_out):
    q, k, v, gk, g_out = _odd_inputs(h, w_in, w_gate, gate_bias)
    s0 = jnp.zeros((h.shape[0], 2, C_HEADS, C_DK, C_DV), F32)
    o, state = _gla_bidir(q, k, v, gk, s0)
    return _odd_merge(o, g_out, onorm, w_out), state


def _odd_latent(h, s_ctx, w_in, w_gate, gate_bias, onorm, w_out):
    q, k, v, gk, g_out = _odd_inputs(h, w_in, w_gate, gate_bias)
    o, _ = _gla_bidir(q, k, v, gk, s_ctx)
    return _odd_merge(o, g_out, onorm, w_out)


def setup_inputs(seed: int = 0) -> dict:
    key = jax.random.key(seed)
    ks = jax.random.split(key, 26)

    def nrm(i, shape, s=1.0):
        return jax.random.normal(ks[i], shape, F32) * s

    a_log = jnp.log(jax.random.uniform(ks[15], (N_EVEN, 2, A_HEADS), F32, 1.0, 16.0))
    dt = jnp.exp(jax.random.uniform(ks[16], (N_EVEN, 2, A_HEADS), F32,
                                    float(np.log(1e-3)), float(np.log(1e-1))))
    return {
        'x_prompt': nrm(0, (BATCH, SEQ, D_MODEL)),
        'x_sample': nrm(1, (DEC_BATCH, DEC_SEQ, D_MODEL)),
        'state_delta': nrm(2, (DEC_BATCH, N_EVEN, 2, A_HEADS, A_DK, A_DV), A_DK ** -0.5),
        'cache_k': nrm(3, (DEC_BATCH, N_EVEN, PAST_LEN, B_KV_HEADS, HEAD_DIM)),
        'cache_v': nrm(4, (DEC_BATCH, N_EVEN, PAST_LEN, B_KV_HEADS, HEAD_DIM)),
        'state_gla': nrm(5, (DEC_BATCH, N_ODD, 2, C_HEADS, C_DK, C_DV), 0.5),
        'c': nrm(6, (DEC_BATCH, D_MODEL)),
        'c_ctx': nrm(7, (D_MODEL,)),
        'norm_g': 1.0 + nrm(8, (DEPTH, 3, D_MODEL), 0.02),
        'ada_w': nrm(9, (DEPTH, D_MODEL, N_MOD * D_MODEL), 0.5 * D_MODEL ** -0.5),
        'ada_b': nrm(10, (DEPTH, N_MOD * D_MODEL), 0.01),
        'ffn_w_gu': nrm(11, (DEPTH, 2, D_MODEL, 2 * D_FF), D_MODEL ** -0.5),
        'ffn_w_down': nrm(12, (DEPTH, 2, D_FF, D_MODEL), D_FF ** -0.5),
        'even_w_in': nrm(13, (N_EVEN, D_MODEL, EVEN_IN), D_MODEL ** -0.5),
        'even_conv': nrm(14, (N_EVEN, SHORT_CONV, EVEN_SIZES[0]), SHORT_CONV ** -0.5),
        'even_a_log': a_log,
        'even_dt_bias': dt + jnp.log(-jnp.expm1(-dt)),
        'even_onorm': 1.0 + nrm(17, (N_EVEN, A_DV), 0.02),
        'even_sink': nrm(18, (N_EVEN, B_Q_HEADS), 0.5),
        'even_w_out': nrm(19, (N_EVEN, EVEN_MIX, D_MODEL), EVEN_MIX ** -0.5),
        'odd_w_in': nrm(20, (N_ODD, D_MODEL, ODD_IN), D_MODEL ** -0.5),
        'odd_w_gate': nrm(21, (N_ODD, 2, GATE_RANK, C_HEADS * C_DK), GATE_RANK ** -0.5),
        'odd_gate_bias': nrm(22, (N_ODD, 2, C_HEADS * C_DK), 0.1),
        'odd_onorm': 1.0 + nrm(23, (N_ODD, C_DV), 0.02),
        'odd_w_out': nrm(24, (N_ODD, ODD_MIX, D_MODEL), ODD_MIX ** -0.5),
        'final_g': 1.0 + nrm(25, (D_MODEL,), 0.02),
    }


def reference(x_prompt, x_sample, state_delta, cache_k, cache_v, state_gla, c, c_ctx,
              norm_g, ada_w, ada_b, ffn_w_gu, ffn_w_down,
              even_w_in, even_conv, even_a_log, even_dt_bias, even_onorm, even_sink, even_w_out,
              odd_w_in, odd_w_gate, odd_gate_bias, odd_onorm, odd_w_out, final_g):
    xp, xs = x_prompt, x_sample
    new_delta, new_k, new_v, new_gla = [], [], [], []
    for l in range(DEPTH):
        j = l // 2
        mp = _ada(c_ctx[None, :], ada_w[l], ada_b[l])
        ms = _ada(c, ada_w[l], ada_b[l])
        xp = _ffn_half(xp, mp, 0, norm_g[l, 0], ffn_w_gu[l, 0], ffn_w_down[l, 0])
        xs = _ffn_half(xs, ms, 0, norm_g[l, 0], ffn_w_gu[l, 0], ffn_w_down[l, 0])
        hp = _modnorm(xp, norm_g[l, 1], mp[:, 3], mp[:, 4])
        hs = _modnorm(xs, norm_g[l, 1], ms[:, 3], ms[:, 4])
        if l % 2 == 0:
            even = (even_w_in[j], even_conv[j], even_a_log[j], even_dt_bias[j], even_onorm[j],
                    even_sink[j], even_w_out[j])
            op, st, kc, vc = _even_context(hp, *even)
            os_ = _even_latent(hs, state_delta[:, j], cache_k[:, j], cache_v[:, j], *even)
            new_delta.append(st)
            new_k.append(kc)
            new_v.append(vc)
        else:
            odd = (odd_w_in[j], odd_w_gate[j], odd_gate_bias[j], odd_onorm[j], odd_w_out[j])
            op, st = _odd_context(hp, *odd)
            os_ = _odd_latent(hs, state_gla[:, j], *odd)
            new_gla.append(st)
        xp = xp + mp[:, 5] * op
        xs = xs + ms[:, 5] * os_
        xp = _ffn_half(xp, mp, 6, norm_g[l, 2], ffn_w_gu[l, 1], ffn_w_down[l, 1])
        xs = _ffn_half(xs, ms, 6, norm_g[l, 2], ffn_w_gu[l, 1], ffn_w_down[l, 1])
    y_prompt = _rmsnorm(xp, final_g)
    y_sample = _rmsnorm(xs, final_g)
    new_state_delta = jnp.stack(new_delta, axis=1)
    new_cache_k = jnp.stack(new_k, axis=1)
    new_cache_v = jnp.stack(new_v, axis=1)
    new_state_gla = jnp.stack(new_gla, axis=1)
    return (y_prompt, y_sample, new_state_delta, new_cache_k, new_cache_v, new_state_gla)
```

```python
import numpy as np
import concourse.bass as bass
import concourse.mybir as mybir

F32 = mybir.dt.float32
BF16 = mybir.dt.bfloat16
AF = mybir.ActivationFunctionType
ALU = mybir.AluOpType
AX = mybir.AxisListType

_DTSZ = {F32: 4, BF16: 2}


def _region(ap):
    sp = str(ap.space)
    if "DRAM" in sp.upper() or "HBM" in sp.upper():
        return None
    sz = _DTSZ[ap.dtype]
    pat = ap.ap
    pstep, pcnt = pat[0]
    off = int(ap.offset)
    if pstep == 0:
        p0, f0 = 0, off
        pstep = 1 << 40
    else:
        p0, f0 = off // pstep, off % pstep
    ext = 1
    for st, cnt in pat[1:]:
        ext += (cnt - 1) * abs(st)
    b0, b1 = f0 * sz, (f0 + ext) * sz
    if "PSUM" in sp.upper():
        b0 = (b0 // 2048) * 2048
        b1 = ((b1 + 2047) // 2048) * 2048
        return (ap.tensor.name, 0, 128, b0, b1)
    return (ap.tensor.name, p0, p0 + pcnt, b0, b1)


def _ovl(a, b):
    return a[0] == b[0] and a[1] < b[2] and b[1] < a[2] and a[3] < b[4] and b[3] < a[4]


def _covers(a, b):
    return a[0] == b[0] and a[1] <= b[1] and a[2] >= b[2] and a[3] <= b[3] and a[4] >= b[4]


class Op:
    __slots__ = ("eng", "fn", "seq", "inc", "waits", "ctr", "is_dma", "val")

    def __init__(self, eng, fn, is_dma=False):
        self.eng = eng
        self.fn = fn
        self.inc = False
        self.waits = []
        self.is_dma = is_dma
        self.ctr = None
        self.seq = 0
        self.val = 0


ENGS = ("pe", "act", "dve", "pool", "sp")
NDMA = 24


class Prog:
    def __init__(self, nc):
        self.nc = nc
        self.streams = {e: [] for e in ENGS}
        self.seqc = {}
        self.known = {e: {} for e in ENGS}
        self.recs = {}
        self.dma_rr = {"h": 0, "s": 0}
        self.dma_last = {}
        self.nops = 0
        self.out_dmas = []

    def _need(self, op, dep):
        if dep is op:
            return
        if dep.ctr == op.ctr and not dep.is_dma:
            pass
        k = self.known[op.eng]
        if k.get(dep.ctr, -1) >= dep.seq:
            return
        k[dep.ctr] = dep.seq
        dep.inc = True
        op.waits.append(dep)

    BK = 2048

    def _buckets(self, r):
        return range(r[3] // self.BK, (r[4] - 1) // self.BK + 1)

    def _track(self, op, reads, writes):
        BK = self.BK
        for ap in reads:
            r = _region(ap)
            if r is None:
                continue
            for b in self._buckets(r):
                lst = self.recs.setdefault((r[0], b), [])
                is_ps = (r[0] == "ps")
                for rec in lst:
                    if _ovl(rec[0], r) and (rec[1] == "w" or (is_ps and rec[2].ctr != op.ctr)):
                        d = rec[2]
                        if d.ctr == op.ctr and op.eng == "pe":
                            continue
                        self._need(op, d)
                for i, rec in enumerate(lst):
                    if rec[1] == "r" and rec[2].ctr == op.ctr and rec[0] == r:
                        lst.pop(i)
                        break
                lst.append([r, "r", op])
        for ap in writes:
            r = _region(ap)
            if r is None:
                continue
            for b in self._buckets(r):
                lst = self.recs.setdefault((r[0], b), [])
                keep = []
                lo, hi = b * BK, (b + 1) * BK
                for rec in lst:
                    if rec[2] is op:
                        keep.append(rec)
                        continue
                    rr = rec[0]
                    if _ovl(rr, r):
                        d = rec[2]
                        if not (d.ctr == op.ctr and op.eng == "pe"):
                            self._need(op, d)
                        if (r[1] <= rr[1] and r[2] >= rr[2]
                                and r[3] <= max(rr[3], lo) and r[4] >= min(rr[4], hi)):
                            continue
                    keep.append(rec)
                keep.append([r, "w", op])
                self.recs[(r[0], b)] = keep

    def _add(self, eng, fn, reads, writes):
        op = Op(eng, fn)
        op.ctr = eng
        op.seq = self.seqc.get(eng, 0)
        self.seqc[eng] = op.seq + 1
        self._track(op, reads, writes)
        self.streams[eng].append(op)
        self.nops += 1
        return op

    def dma(self, out, in_, q="sp", is_output=False):
        op = Op(q, None, is_dma=True)
        kind = "s" if q == "pool" else "h"
        k = self.dma_rr[kind]
        self.dma_rr[kind] = (k + 1) % (NDMA // 2)
        op.ctr = "dma%s%d" % (kind, k)
        op.seq = self.seqc.get(op.ctr, 0)
        self.seqc[op.ctr] = op.seq + 1
        prev = self.dma_last.get(op.ctr)
        if prev is not None:
            self._need(op, prev)
        self.dma_last[op.ctr] = op
        op.inc = True
        self._track(op, [in_], [out])
        op.fn = lambda e, out=out, in_=in_: e.dma_start(out=out, in_=in_)
        self.streams[q].append(op)
        if is_output:
            self.out_dmas.append(op)
        return op

    def mm(self, out, lhsT, rhs, start=True, stop=True):
        return self._add("pe", lambda e: e.matmul(out, lhsT, rhs, start=start, stop=stop),
                         [lhsT, rhs], [out])

    def tr(self, out, in_, ident):
        return self._add("pe", lambda e: e.transpose(out, in_, ident), [in_, ident], [out])

    def act(self, out, in_, func, bias=None, scale=1.0, accum=None):
        rd = [in_]
        if bias is not None and not isinstance(bias, (int, float)):
            rd.append(bias)
        if not isinstance(scale, (int, float)):
            rd.append(scale)
        wr = [out] + ([accum] if accum is not None else [])
        kw = {}
        if bias is not None:
            kw["bias"] = bias
        if accum is not None:
            kw["accum_out"] = accum
        return self._add("act", lambda e: e.activation(out=out, in_=in_, func=func, scale=scale, **kw),
                         rd, wr)

    def tt(self, out, in0, in1, op, eng="dve"):
        return self._add(eng, lambda e: e.tensor_tensor(out=out, in0=in0, in1=in1, op=op),
                         [in0, in1], [out])

    def ts(self, out, in0, s1, op0, s2=None, op1=None, eng="dve", accum=None):
        rd = [in0] + [s for s in (s1, s2) if s is not None and not isinstance(s, (int, float))]
        kw = {}
        if op1 is not None:
            kw["op1"] = op1
        if accum is not None:
            kw["accum_out"] = accum
        wr = [out] + ([accum] if accum is not None else [])
        return self._add(eng, lambda e: e.tensor_scalar(out=out, in0=in0, scalar1=s1, scalar2=s2, op0=op0, **kw),
                         rd, wr)

    def stt(self, out, in0, scalar, in1, op0, op1, eng="dve"):
        rd = [in0, in1] + ([scalar] if not isinstance(scalar, (int, float)) else [])
        return self._add(eng, lambda e: e.scalar_tensor_tensor(out=out, in0=in0, scalar=scalar, in1=in1,
                                                               op0=op0, op1=op1), rd, [out])

    def copy(self, out, in_, eng="dve"):
        if eng == "act":
            return self.act(out, in_, AF.Copy)
        return self._add(eng, lambda e: e.tensor_copy(out=out, in_=in_), [in_], [out])

    def reduce(self, out, in_, op, eng="dve", axis=None):
        axis = axis or AX.X
        return self._add(eng, lambda e: e.tensor_reduce(out=out, in_=in_, axis=axis, op=op), [in_], [out])

    def recip(self, out, in_):
        return self._add("dve", lambda e: e.reciprocal(out=out, in_=in_), [in_], [out])

    def memset(self, ap, val, eng="dve"):
        return self._add(eng, lambda e: e.memset(ap, val), [], [ap])

    def emit(self):
        nc = self.nc
        sems = {}
        import contextlib
        stack = contextlib.ExitStack()
        allops = []
        for e in ENGS:
            allops.extend(self.streams[e])
        ctrs = sorted(set(o.ctr for o in allops))
        for c in ctrs:
            sems[c] = stack.enter_context(nc.semaphore("s_" + c))
        cnt = {c: 0 for c in ctrs}
        byctr = {c: [] for c in ctrs}
        for o in allops:
            byctr[o.ctr].append(o)
        for c in ctrs:
            ops = sorted(byctr[c], key=lambda o: o.seq)
            v = 0
            for o in ops:
                if o.inc:
                    v += 16 if o.is_dma else 1
                o.val = v
        fin = stack.enter_context(nc.semaphore("s_fin"))
        block = stack.enter_context(nc.Block())
        prog = self

        def run_stream(eng_name, e):
            for o in prog.streams[eng_name]:
                for d in o.waits:
                    e.wait_ge(sems[d.ctr], d.val)
                ins = o.fn(e)
                if o.inc:
                    ins.then_inc(sems[o.ctr], 16 if o.is_dma else 1)

        @block.tensor
        def _(e):
            run_stream("pe", e)

        @block.scalar
        def _(e):
            run_stream("act", e)

        @block.vector
        def _(e):
            run_stream("dve", e)

        @block.gpsimd
        def _(e):
            run_stream("pool", e)

        @block.sync
        def _(e):
            run_stream("sp", e)
            for o in prog.out_dmas:
                e.wait_ge(sems[o.ctr], o.val)

        stack.close()
from concourse.bass_utils import run_bass_kernel_spmd

D = 1024
NTOK = 2048
KC = 8
TT = 512
NTT = 4
DFF = 2816
NHC = 22
EPS = 1e-6
EVEN_IN = 3088
ODD_IN = 3104


class Arena:
    def __init__(self, nc, name, nbytes, stack):
        self.t = stack.enter_context(nc.sbuf_tensor(name, [128, nbytes // 4], F32))
        self.nbytes = nbytes

    def view(self, off, shape, dtype):
        sz = 4 if dtype == F32 else 2
        n = 1
        for s in shape[1:]:
            n *= s
        assert off % 4 == 0 and (n * sz) % 4 == 0 and off + n * sz <= self.nbytes, (off, shape, self.nbytes)
        ap = self.t[0:shape[0], off // 4: off // 4 + (n * sz) // 4]
        if dtype != F32:
            ap = ap.bitcast(dtype)
        if len(shape) == 3:
            ap = ap.rearrange("p (a b) -> p a b", a=shape[1], b=shape[2])
        elif len(shape) == 4:
            ap = ap.rearrange("p (a b c) -> p a b c", a=shape[1], b=shape[2], c=shape[3])
        return ap


def build_program(stage="all", debug_names=()):
    import contextlib
    nc = bass.Bass("TRN2", target_bir_lowering=False)
    P = Prog(nc)
    stack = contextlib.ExitStack()

    def din(name, shape, dt=F32):
        return nc.dram_tensor(name, list(shape), dt, kind="ExternalInput").ap()

    def dout(name, shape, dt=F32):
        return nc.dram_tensor(name, list(shape), dt, kind="ExternalOutput").ap()

    xin = din("xin", [NTOK, D])
    condT = din("condT", [128, KC, 2])
    ptab = din("ptab", [128, PT_COLS])
    cst = din("cst", [128, CST_COLS])
    ada_w = din("ada_w", [2, D, 9 * D])
    if stage in ("all", "ffn0"):
        w_gu = din("ffn_w_gu", [2, 2, D, 2 * DFF])
        w_dn = din("ffn_w_down", [2, 2, DFF, D])
    yout = dout("yout", [NTOK, D])
    even_w_in = din("even_w_in", [1, D, EVEN_IN])
    even_w_out = din("even_w_out", [1, D, D])
    rope = din("rope", [128, 2, 1024])
    bmask = din("bmask", [128, 9 * 128])
    odd_w_in = din("odd_w_in", [1, D, ODD_IN])
    odd_w_out = din("odd_w_out", [1, D, D])
    wgpad = din("wgpad", [33, 2, 512])
    state_gla = din("state_gla", [2, 4, 128, 256])
    nsg = dout("nsg", [4, 2, 4, 128, 256])
    cache_k = din("cache_k", [512, 2, 128])
    cache_v = din("cache_v", [512, 2, 128])
    state_delta = din("state_delta", [2, 4, 128, 128])
    nck = dout("nck", [1024, 256])
    ncv = dout("ncv", [1024, 256])
    nsd = dout("nsd", [4, 2, 4, 128, 128])

    dbg = {}
    xT = stack.enter_context(nc.sbuf_tensor("xT", [128, KC, NTOK], F32))
    ptab_sb = stack.enter_context(nc.sbuf_tensor("ptab_sb", [128, PT_COLS], F32))
    cst_sb = stack.enter_context(nc.sbuf_tensor("cst_sb", [128, CST_COLS], F32))
    cstb = stack.enter_context(nc.sbuf_tensor("cstb", [128, CSTB_COLS], BF16))
    modT = stack.enter_context(nc.sbuf_tensor("modT", [128, 2, 72, 2], F32))
    modA = stack.enter_context(nc.sbuf_tensor("modA", [128, 2, 3, KC, 2], F32))
    modG = stack.enter_context(nc.sbuf_tensor("modG", [128, 2, 3, KC, 2], F32))
    scT = stack.enter_context(nc.sbuf_tensor("scT", [128, KC, 2], BF16))
    condsb = stack.enter_context(nc.sbuf_tensor("condsb", [128, KC, 2], F32))
    gates = stack.enter_context(nc.sbuf_tensor("gates", [128, 8, 8, 16], F32))
    epsT = stack.enter_context(nc.sbuf_tensor("epsT", [128, 2], F32))
    rstd = stack.enter_context(nc.sbuf_tensor("rstd", [128, 2, TT], F32))
    ar = Arena(nc, "arena", 126976, stack)
    ps = stack.enter_context(nc.psum_tensor("ps", [128, 8, 512], F32))

    def bank(b):
        return ps[:, b, :]

    identF = cst_sb[:, C_IDENT:C_IDENT + 128]
    identB = cstb[:, 0:128]
    onesP = cstb[:, 128:256]

    P.dma(ptab_sb[:, :], ptab[:, :], q="sp")
    P.dma(cst_sb[:, :], cst[:, :], q="sp")
    P.dma(condsb[:, :, :], condT[:, :, :], q="sp")
    P.copy(identB, identF, eng="dve")
    P.memset(onesP, 1.0, eng="dve")
    P.memset(epsT[:, :], EPS, eng="dve")
    P.memset(gates[:, :, :, :], 0.0, eng="pool")

    STG = 32768
    for i in range(16):
        stg = ar.view(STG + (i % 2) * 4096, [128, D], F32)
        P.dma(stg, xin[i * 128:(i + 1) * 128, :], q="sp" if i % 2 == 0 else "act")
        for half in range(2):
            b = (i * 2 + half) % 4
            for kk in range(4):
                kc = half * 4 + kk
                P.tr(ps[:, b, kk * 128:(kk + 1) * 128], stg[:, kc * 128:(kc + 1) * 128], identF)
            src = ps[:, b, :].rearrange("p (a t) -> p a t", a=4, t=128)
            dst = xT[:, half * 4:half * 4 + 4, i * 128:(i + 1) * 128]
            if half == 0:
                P.copy(dst, src, eng="dve")
            else:
                P.copy(dst, src, eng="act")

    P.act(scT[:, :, :], condsb[:, :, :], AF.Silu)
    ADAW = 40960

    def ada_finish(l, s):
        ng = ptab_sb[:, PT_NORMG + (l * 3 + s) * 8: PT_NORMG + (l * 3 + s + 1) * 8]
        sc = modT[:, l, (3 * s + 1) * 8:(3 * s + 2) * 8, :]
        P.stt(modA[:, l, s, :, :], sc, 1.0, ng.unsqueeze(2).broadcast_to([128, KC, 2]), ALU.add, ALU.mult)
        gt = modT[:, l, (3 * s + 2) * 8:(3 * s + 3) * 8, :]
        P.ts(modG[:, l, s, :, :], gt, 0.5 if s != 1 else 1.0, ALU.mult)

    for i in range(3):
        wb = ar.view(ADAW + (i % 3) * 16384, [128, KC, 1024], BF16)
        src = ada_w[0].rearrange("(kc p) n -> p kc n", p=128)[:, :, i * 1024:(i + 1) * 1024]
        P.dma(wb, src, q="pool")
        for n in range(8):
            j = i * 8 + n
            for kc in range(KC):
                P.mm(ps[:, 4, 2 * j:2 * j + 2], wb[:, kc, n * 128:(n + 1) * 128], scT[:, kc, :],
                     start=(kc == 0), stop=(kc == KC - 1))
    P.tt(modT[:, 0, 0:24, :], ps[:, 4, 0:48].rearrange("p (j c) -> p j c", j=24, c=2),
         ptab_sb[:, PT_ADAB:PT_ADAB + 24].unsqueeze(2).broadcast_to([128, 24, 2]), ALU.add)
    ada_finish(0, 0)
    ada_tasks = [(0, i, q4) for i in range(3, 9) for q4 in range(4)] + \
                [(1, i, q4) for i in range(9) for q4 in range(4)]
    ada_cnt = [0, 0, 0]

    ada_pending = []

    def ada_load():
        l, i, q4 = ada_tasks.pop(0)
        wb = ar.view(TMP + (ada_cnt[0] % 2) * 4096, [128, KC, 256], BF16)
        ada_cnt[0] += 1
        c0 = i * 1024 + q4 * 256
        P.dma(wb, ada_w[l].rearrange("(kc p) n -> p kc n", p=128)[:, :, c0:c0 + 256], q="pool")
        ada_pending.append((l, i, q4, wb))

    def ada_compute():
        l, i, q4, wb = ada_pending.pop(0)
        bank_b = 6 + (ada_cnt[1] % 2)
        ada_cnt[1] += 1
        for nn in range(2):
            for kc in range(KC):
                P.mm(ps[:, bank_b, 2 * nn:2 * nn + 2], wb[:, kc, nn * 128:(nn + 1) * 128], scT[:, kc, :],
                     start=(kc == 0), stop=(kc == KC - 1))
        j0 = i * 8 + q4 * 2
        P.tt(modT[:, l, j0:j0 + 2, :], ps[:, bank_b, 0:4].rearrange("p (j c) -> p j c", j=2, c=2),
             ptab_sb[:, PT_ADAB + l * 72 + j0:PT_ADAB + l * 72 + j0 + 2].unsqueeze(2).broadcast_to([128, 2, 2]),
             ALU.add)
        if q4 == 3 and i % 3 == 2:
            ada_finish(l, i // 3)

    def ada_tick():
        ada_cnt[2] += 1
        if ada_cnt[2] % 3 != 0:
            return
        if len(ada_pending) == 2 or (ada_pending and not ada_tasks):
            ada_compute()
        if ada_tasks and len(ada_pending) < 2:
            ada_load()

    def ada_flush():
        while ada_tasks or ada_pending:
            if ada_tasks and len(ada_pending) < 2:
                ada_load()
            else:
                ada_compute()

    HT = 0
    FW = 32768
    ACTB = FW + 49152
    SG = ACTB + 32768
    TMP = SG + 4096
    SQ = TMP + 4096
    assert SQ + 4096 <= ar.nbytes, SQ + 4096
    hT = ar.view(HT, [128, KC, NTOK], BF16)

    def rms_tile(tt, bank0=6):
        b = bank0 + (tt % 2)
        for kc in range(KC):
            sq = ar.view(SQ + ((tt * KC + kc) % 4) * 1024, [128, TT], BF16)
            P.act(sq, xT[:, kc, tt * TT:(tt + 1) * TT], AF.Square)
            P.mm(bank(b), onesP, sq, start=(kc == 0), stop=(kc == KC - 1))
        rs = rstd[:, tt % 2, :]
        P.act(rs, bank(b), AF.Ln, bias=epsT[:, 0:1], scale=1.0 / 1024.0)
        P.act(rs, rs, AF.Exp, scale=-0.5)
        return rs

    def modnorm(l, s):
        for tt in range(NTT):
            rs = rms_tile(tt)
            c = 0 if tt < 2 else 1
            for kc in range(KC):
                tmp = ar.view(TMP + ((tt * KC + kc) % 2) * 2048, [128, TT], F32)
                P.stt(tmp, xT[:, kc, tt * TT:(tt + 1) * TT], modA[:, l, s, kc, c:c + 1],
                      rs, ALU.mult, ALU.mult)
                P.act(hT[:, kc, tt * TT:(tt + 1) * TT], tmp, AF.Identity,
                      bias=modT[:, l, 3 * s * 8 + kc, c:c + 1])

    GROUPS = [(0, 4), (4, 8), (8, 12), (12, 16), (16, 19), (19, 22)]

    def ffn(l, i):
        s = 0 if i == 0 else 2
        modnorm(l, s)
        wgu = w_gu[l, i].rearrange("(kc p) n -> p kc n", p=128)
        wdn = w_dn[l, i].rearrange("(g p) n -> p g n", p=128)

        def load(g):
            j0, j1 = GROUPS[g]
            G = j1 - j0
            base = FW + (g % 2) * 24576
            wg = ar.view(base, [128, KC, 512], BF16)
            wu = ar.view(base + 8192, [128, KC, 512], BF16)
            wd = ar.view(base + 16384, [128, 4, 1024], BF16)
            P.dma(wg[:, :, 0:G * 128], wgu[:, :, j0 * 128:j1 * 128], q="pool")
            P.dma(wu[:, :, 0:G * 128], wgu[:, :, DFF + j0 * 128:DFF + j1 * 128], q="pool")
            P.dma(wd[:, 0:G, :], wdn[:, j0:j1, :], q="pool")
            return wg, wu, wd

        pair = [0]

        def gu(g, W):
            j0, j1 = GROUPS[g]
            wg, wu, _ = W
            ab = ar.view(ACTB + (g % 2) * 16384, [128, 4, NTOK], BF16)
            for jj in range(j1 - j0):
                for tt in range(NTT):
                    pb = (pair[0] % 3) * 2
                    pair[0] += 1
                    rhs = None
                    for kc in range(KC):
                        P.mm(bank(pb), wg[:, kc, jj * 128:(jj + 1) * 128], hT[:, kc, tt * TT:(tt + 1) * TT],
                             start=(kc == 0), stop=(kc == KC - 1))
                    for kc in range(KC):
                        P.mm(bank(pb + 1), wu[:, kc, jj * 128:(jj + 1) * 128], hT[:, kc, tt * TT:(tt + 1) * TT],
                             start=(kc == 0), stop=(kc == KC - 1))
                    sg = ar.view(SG + (pair[0] % 2) * 2048, [128, TT], F32)
                    P.act(sg, bank(pb), AF.Silu)
                    P.tt(ab[:, jj, tt * TT:(tt + 1) * TT], sg, bank(pb + 1), ALU.mult)
                    ada_tick()

        ycnt = [0]

        def down(g, W):
            j0, j1 = GROUPS[g]
            _, _, wd = W
            ab = ar.view(ACTB + (g % 2) * 16384, [128, 4, NTOK], BF16)
            for n in range(KC):
                for tt in range(NTT):
                    c = 0 if tt < 2 else 1
                    yb = 6 + (ycnt[0] % 2)
                    ycnt[0] += 1
                    for jj in range(j1 - j0):
                        P.mm(bank(yb), wd[:, jj, n * 128:(n + 1) * 128], ab[:, jj, tt * TT:(tt + 1) * TT],
                             start=(jj == 0), stop=(jj == j1 - j0 - 1))
                    xs = xT[:, n, tt * TT:(tt + 1) * TT]
                    P.stt(xs, bank(yb), modG[:, l, s, n, c:c + 1], xs, ALU.mult, ALU.add)

        W = {}
        W[0] = load(0)
        W[1] = load(1)
        gu(0, W[0])
        for g in range(len(GROUPS)):
            if g + 1 < len(GROUPS):
                gu(g + 1, W[g + 1])
            down(g, W[g])
            if g + 2 < len(GROUPS):
                W[g + 2] = load(g + 2)
        while ada_pending:
            ada_compute()
        if (l, i) == (0, 1):
            ada_flush()

    def final_out():
        fg = ptab_sb[:, PT_FINALG:PT_FINALG + 8]
        YS = HT
        for i in range(16):
            if i % 4 == 0:
                rs = rms_tile(i // 4)
            yt = ar.view(YS + (i % 2) * 4096, [128, KC, 128], F32)
            for kc in range(KC):
                P.stt(yt[:, kc, :], xT[:, kc, i * 128:(i + 1) * 128], fg[:, kc:kc + 1],
                      rs[:, (i % 4) * 128:(i % 4 + 1) * 128], ALU.mult, ALU.mult)
            st = ar.view(YS + 8192 + (i % 2) * 4096, [128, D], F32)
            for half in range(2):
                b = (i * 2 + half) % 4
                for kk in range(4):
                    kc = half * 4 + kk
                    P.tr(ps[:, b, kk * 128:(kk + 1) * 128], yt[:, kc, :], identF)
                if half == 0:
                    P.copy(st[:, 0:512], bank(b), eng="dve")
                else:
                    P.copy(st[:, 512:1024], bank(b), eng="act")
            P.dma(yout[i * 128:(i + 1) * 128, :], st, q="sp", is_output=True)

    NEG = -30000.0
    Uf = cst_sb[:, C_UF:C_UF + 128]
    Ub = cst_sb[:, C_UB:C_UB + 128]
    sel127 = cst_sb[:, C_SEL127:C_SEL127 + 128]
    sel0 = cst_sb[:, C_SEL0:C_SEL0 + 128]
    onesF = cst_sb[:, C_ONES:C_ONES + 128]
    offdiag = cst_sb[:, C_OFFD:C_OFFD + 128]
    MLf = cst_sb[:, C_ML:C_ML + 128]
    MUf = cst_sb[:, C_MU:C_MU + 128]
    Rm = cst_sb[:, C_RM:C_RM + 128]
    MLb = cstb[:, 256:384]
    MUb = cstb[:, 384:512]
    P.dma(cstb[:, 512:512 + 9 * 128], bmask[:, :], q="pool")
    P.copy(MLb, MLf, eng="dve")
    P.copy(MUb, MUf, eng="dve")

    def bc_h(m):
        return m.unsqueeze(1).broadcast_to([128, 4, 128])

    def bc_i(v):
        return v.unsqueeze(2).broadcast_to([128, 4, 128])

    def b4(b):
        return ps[:, b, :].rearrange("p (h t) -> p h t", h=4, t=128)

    def bbf(b):
        return ps[:, b, :].bitcast(BF16)

    nbc = [0]

    def nb():
        b = nbc[0] % 8
        nbc[0] += 1
        return b

    def nbk(k):
        c = (nbc[0] + k - 1) // k * k
        nbc[0] = c + k
        return c % 8

    def dump(name, ap, shape, dt):
        d = dout(name, shape, dt)
        P.dma(d, ap, q="sp", is_output=True)
        dbg[name] = (shape, dt)

    QN, KN, VS, GS = 32768, 40960, 49152, 57344
    QPL, QRO, KFM, VTM, KCT, VC = 65536, 73728, 81920, 86016, 90112, 92160
    WP = 94208
    SCR = 110592
    OACC = 65536
    TST = 81920
    SCN = 98304
    SHR = 114688
    STF = 120832
    STB = 124928

    def mixer_even():
        l = 0
        modnorm(l, 1)
        w_in = even_w_in[0].rearrange("(kc p) n -> p kc n", p=128)
        w_out = even_w_out[0].rearrange("(kc p) n -> p kc n", p=128)
        cw = ptab_sb[:, PT_CONV:PT_CONV + 60].rearrange("p (c j) -> p c j", c=12, j=5)
        sink_bc = ptab_sb[:, PT_SINK:PT_SINK + 4]
        wpc = [0]

        def load_piece(c0, c1):
            wp = ar.view(WP + (wpc[0] % 2) * 8192, [128, KC, 512], BF16)
            wpc[0] += 1
            P.dma(wp[:, :, 0:c1 - c0], w_in[:, :, c0:c1], q="pool")
            return wp

        for grp in range(2):
            T0 = grp * 1024
            nseq, L = (4, 256) if grp == 0 else (1, 1024)
            qn = ar.view(QN, [128, 4, 1024], BF16)
            kn = ar.view(KN, [128, 4, 1024], BF16)
            vS = ar.view(VS, [128, 4, 1024], BF16)
            gS = ar.view(GS, [128, 4, 1024], BF16)
            qpl = ar.view(QPL, [128, 4, 1024], BF16)
            qro = ar.view(QRO, [128, 4, 1024], BF16)
            kfm = ar.view(KFM, [128, 2, 1024], BF16)
            vtm = ar.view(VTM, [128, 8, 256], BF16)
            kcT = ar.view(KCT, [128, 2, 512], BF16)
            vc = ar.view(VC, [128, 4, 256], BF16)
            mixT = hT[:, :, T0:T0 + 1024]

            def proj_fm(wp, cc):
                b = nbk(2)
                for tt in range(2):
                    for kc in range(KC):
                        P.mm(bank(b + tt), wp[:, kc, cc * 128:(cc + 1) * 128],
                             hT[:, kc, T0 + tt * TT:T0 + (tt + 1) * TT], start=(kc == 0), stop=(kc == KC - 1))
                return ps[:, b:b + 2, :].rearrange("p a t -> p (a t)")

            SCALE = 128.0 ** -0.5
            if grp == 1:
                ropeT = ar.view(SCR, [128, 2, 1024], F32)
                P.dma(ropeT, rope[:, :, :], q="sp")
                kst = ar.view(SCR + 8192, [128, 4, 256], BF16)
                P.dma(kst, cache_k.rearrange("(kt p) g d -> p kt (g d)", p=128), q="pool")
                P.dma(vc, cache_v.rearrange("(kt p) g d -> p kt (g d)", p=128), q="pool")
                for g in range(2):
                    b = nb()
                    for kt in range(4):
                        P.tr(bbf(b)[:, kt * 128:(kt + 1) * 128], kst[:, kt, g * 128:(g + 1) * 128], identB)
                    P.copy(kcT[:, g, :], bbf(b)[:, 0:512], eng="dve")

            def rope_apply(dst_bf, xf):
                t1 = ar.view(SCR + 12288, [128, 1024], F32)
                for tt in range(2):
                    b = nb()
                    P.mm(bank(b), Rm, xf[:, tt * TT:(tt + 1) * TT])
                    P.tt(t1[:, tt * TT:(tt + 1) * TT], bank(b), ropeT[:, 1, tt * TT:(tt + 1) * TT], ALU.mult)
                P.tt(xf, xf, ropeT[:, 0, :], ALU.mult, eng="pool")
                P.tt(dst_bf, t1, xf, ALU.add)

            wp = load_piece(2064, 2576)
            for h in range(4):
                pp = proj_fm(wp, h)
                P.act(qpl[:, h, :], pp, AF.Copy, scale=SCALE)
                if grp == 1:
                    xf = ar.view(SCR + 8192, [128, 1024], F32)
                    P.ts(xf, pp, SCALE, ALU.mult)
                    rope_apply(qro[:, h, :], xf)
            if "stop_ip1" in debug_names:
                return
            wp = load_piece(2576, 3088)
            for g in range(2):
                pp = proj_fm(wp, g)
                if grp == 0:
                    P.act(kfm[:, g, :], pp, AF.Copy)
                else:
                    xf = ar.view(SCR + 8192, [128, 1024], F32)
                    P.act(xf, pp, AF.Copy)
                    rope_apply(kfm[:, g, :], xf)
            if "stop_ip1b" in debug_names:
                return
            for i in range(8):
                b = nb()
                for kc in range(KC):
                    P.mm(bank(b), hT[:, kc, T0 + i * 128:T0 + (i + 1) * 128], wp[:, kc, 0:512],
                         start=(kc == 0), stop=(kc == KC - 1))
                if grp == 1:
                    P.act(vtm[:, i, :], ps[:, b, 256:512], AF.Copy)
                else:
                    st = ar.view(SCR + (i % 2) * 2048, [128, 512], F32)
                    P.copy(st, bank(b), eng="dve")
                    P.act(vtm[:, i, :], st[:, 256:512], AF.Copy)
                    P.dma(nck[i * 128:(i + 1) * 128, :], st[:, 0:256], q="sp", is_output=True)
                    P.dma(ncv[i * 128:(i + 1) * 128, :], st[:, 256:512], q="act", is_output=True)
            if "stop_ip2" in debug_names:
                return
            wpab = load_piece(2048, 2064)
            abT = gates[:, 0, :, :]
            for i in range(8):
                b = nb()
                for kc in range(KC):
                    P.mm(ps[:, b, 0:16], hT[:, kc, T0 + i * 128:T0 + (i + 1) * 128], wpab[:, kc, 0:16],
                         start=(kc == 0), stop=(kc == KC - 1))
                P.copy(abT[:, i, :], ps[:, b, 0:16], eng="dve")

            if "stop_ip3" in debug_names:
                return
            Lp = L + 4
            xpad = [ar.view(SCR, [128, nseq, Lp], F32) for k in range(2)]
            acc = ar.view(SCR + 4160, [128, nseq, L], F32)
            qs = ar.view(SCR + 8256, [128, 1024], F32)
            P.memset(ar.view(SCR, [128, 1040], F32), 0.0, eng="pool")
            for pc in range(3):
                wp = load_piece(pc * 512, (pc + 1) * 512)
                for hh in range(4):
                    c = pc * 4 + hh
                    pp = proj_fm(wp, hh)
                    xp = xpad[c % 2]
                    P.act(xp[:, :, 2:2 + L], pp.rearrange("p (s t) -> p s t", s=nseq, t=L), AF.Copy)
                    P.ts(acc, xp[:, :, 0:L], cw[:, c, 0:1], ALU.mult)
                    for j in range(1, 5):
                        P.stt(acc, xp[:, :, j:j + L], cw[:, c, j:j + 1], acc, ALU.mult, ALU.add)
                    accf = acc.rearrange("p s t -> p (s t)")
                    if pc == 2:
                        P.act(vS[:, hh, :], accf, AF.Silu)
                    else:
                        P.act(qs, accf, AF.Silu)
                        dst = (qn if pc == 0 else kn)[:, hh, :]
                        for tt in range(2):
                            sq = ar.view(SCR + 12352 + (tt % 2) * 1024, [128, TT], BF16)
                            P.act(sq, qs[:, tt * TT:(tt + 1) * TT], AF.Square)
                            b = nb()
                            P.mm(bank(b), onesP, sq)
                            rs = rstd[:, tt % 2, :]
                            P.act(rs, bank(b), AF.Ln, bias=epsT[:, 0:1])
                            P.act(rs, rs, AF.Exp, scale=-0.5)
                            P.stt(dst[:, tt * TT:(tt + 1) * TT], qs[:, tt * TT:(tt + 1) * TT],
                                  SCALE if pc == 0 else 1.0, rs, ALU.mult, ALU.mult)
            wp = load_piece(1536, 2048)
            for hh in range(4):
                pp = proj_fm(wp, hh)
                P.act(gS[:, hh, :], pp, AF.Silu)
            if "inproj" in debug_names and grp == DBG_GRP:
                dump("d_qn", qn, [128, 4, 1024], BF16)
                dump("d_kn", kn, [128, 4, 1024], BF16)
                dump("d_vS", vS, [128, 4, 1024], BF16)
                dump("d_gS", gS, [128, 4, 1024], BF16)
                dump("d_qpl", qpl, [128, 4, 1024], BF16)
                dump("d_qro", qro, [128, 4, 1024], BF16)
                dump("d_kfm", kfm, [128, 2, 1024], BF16)
                dump("d_vtm", vtm, [128, 8, 256], BF16)
                dump("d_ab", abT, [128, 8, 16], F32)

            if "stop_inproj" in debug_names:
                return
            ATT = WP
            if grp == 0:
                for s_ in range(4):
                    for qb in range(2):
                        tq = s_ * 256 + qb * 128
                        Pb = ar.view(ATT + ((s_ * 2 + qb) % 2) * 2048, [128, 4, 256], BF16)
                        PTs = ar.view(ATT + 4096 + ((s_ * 2 + qb) % 2) * 2048, [128, 8, 128], BF16)
                        stt_ = ar.view(ATT + 8192 + ((s_ * 2 + qb) % 2) * 256, [128, 16], F32)
                        on = ar.view(ATT + 8704 + ((s_ * 2 + qb) % 2) * 1024, [128, 4, 128], BF16)
                        bS = nbk(2)
                        P.memset(stt_, 0.0, eng="pool")
                        for h in range(4):
                            P.mm(ps[:, bS + h // 2, (h % 2) * 256:(h % 2 + 1) * 256],
                                 qpl[:, h, tq:tq + 128], kfm[:, h // 2, s_ * 256:(s_ + 1) * 256])
                        S4 = ps[:, bS:bS + 2, :].rearrange("p a (h k) -> p (a h) k", h=2, k=256)
                        mx = stt_[:, 0:4]
                        negm = stt_[:, 4:8]
                        rsum = stt_[:, 8:12]
                        es = stt_[:, 12:16]
                        P.reduce(mx, S4, ALU.max)
                        P.tt(mx, mx, sink_bc, ALU.max)
                        P.ts(negm, mx, -1.0, ALU.mult)
                        for h in range(4):
                            P.act(Pb[:, h, :], S4[:, h, :], AF.Exp, bias=negm[:, h:h + 1], accum=rsum[:, h:h + 1])
                        P.tt(es, sink_bc, negm, ALU.add)
                        P.act(es, es, AF.Exp)
                        P.tt(rsum, rsum, es, ALU.add)
                        P.recip(rsum, rsum)
                        bT = nb()
                        for h in range(4):
                            for kt in range(2):
                                P.tr(bbf(bT)[:, (h * 2 + kt) * 128:(h * 2 + kt + 1) * 128],
                                     Pb[:, h, kt * 128:(kt + 1) * 128], identB)
                        P.copy(PTs.rearrange("p a t -> p (a t)"), bbf(bT)[:, 0:1024], eng="act")
                        bO = nb()
                        for h in range(4):
                            for kt in range(2):
                                P.mm(ps[:, bO, h * 128:(h + 1) * 128], PTs[:, h * 2 + kt, :],
                                     vtm[:, s_ * 2 + kt, (h // 2) * 128:(h // 2 + 1) * 128],
                                     start=(kt == 0), stop=(kt == 1))
                        P.tt(on, b4(bO), bc_i(rsum), ALU.mult)
                        bT2 = nb()
                        for h in range(4):
                            P.tr(bbf(bT2)[:, h * 128:(h + 1) * 128], on[:, h, :], identB)
                        P.copy(mixT[:, 4:8, tq:tq + 128],
                               bbf(bT2)[:, 0:512].rearrange("p (h t) -> p h t", h=4, t=128), eng="dve")
            else:
                for qb in range(8):
                    tq = qb * 128
                    blks = [k for k in (qb - 1, qb, qb + 1) if 0 <= k < 8]
                    nl = len(blks)
                    W = 512 + nl * 128
                    on = ar.view(ATT + 16384 + (qb % 2) * 1024, [128, 4, 128], BF16)
                    for hp in range(2):
                        it = qb * 2 + hp
                        Pb = ar.view(ATT + (it % 2) * 4096, [128, 2, 1024], BF16)
                        PTs = ar.view(ATT + 8192 + (it % 2) * 4096, [128, 2, 1024], BF16)
                        stt_ = ar.view(ATT + 18432 + (it % 2) * 256, [128, 16], F32)
                        mx = stt_[:, 0:2]
                        negm = stt_[:, 2:4]
                        rsum = stt_[:, 4:6]
                        es = stt_[:, 6:8]
                        bS = nbk(4)
                        P.memset(stt_, 0.0, eng="pool")
                        for hh in range(2):
                            h = hp * 2 + hh
                            g = hp
                            P.mm(bank(bS + 2 * hh), qpl[:, h, tq:tq + 128], kcT[:, g, :])
                            k0 = blks[0] * 128
                            has_mask = (blks[0] == qb - 1) or (blks[-1] == qb + 1)
                            P.mm(ps[:, bS + 2 * hh + 1, 0:nl * 128], qro[:, h, tq:tq + 128],
                                 kfm[:, g, k0:k0 + nl * 128], start=True, stop=not has_mask)
                            nm = (1 if blks[0] == qb - 1 else 0) + (1 if blks[-1] == qb + 1 else 0)
                            cnt = 0
                            for bi, k in enumerate(blks):
                                if k == qb - 1 or k == qb + 1:
                                    cnt += 1
                                    P.mm(ps[:, bS + 2 * hh + 1, bi * 128:(bi + 1) * 128], identB,
                                         MUb if k == qb - 1 else MLb, start=False, stop=(cnt == nm))
                        S2 = ps[:, bS:bS + 4, :].rearrange("p (h a) t -> p h (a t)", h=2, a=2)[:, :, 0:W]
                        P.reduce(mx, S2, ALU.max)
                        P.tt(mx, mx, sink_bc[:, hp * 2:hp * 2 + 2], ALU.max)
                        P.ts(negm, mx, -1.0, ALU.mult)
                        for hh in range(2):
                            P.act(Pb[:, hh, 0:W], S2[:, hh, :], AF.Exp, bias=negm[:, hh:hh + 1],
                                  accum=rsum[:, hh:hh + 1])
                        P.tt(es, sink_bc[:, hp * 2:hp * 2 + 2], negm, ALU.add)
                        P.act(es, es, AF.Exp)
                        P.tt(rsum, rsum, es, ALU.add)
                        P.recip(rsum, rsum)
                        nblk = 4 + nl
                        for hh in range(2):
                            bT = nb()
                            for bi in range(nblk):
                                P.tr(bbf(bT)[:, bi * 128:(bi + 1) * 128], Pb[:, hh, bi * 128:(bi + 1) * 128], identB)
                            P.copy(PTs[:, hh, 0:W], bbf(bT)[:, 0:W], eng="act" if hh == 0 else "dve")
                        bO = nb()
                        for hh in range(2):
                            g = hp
                            for bi in range(nblk):
                                if bi < 4:
                                    rhs = vc[:, bi, g * 128:(g + 1) * 128]
                                else:
                                    rhs = vtm[:, blks[bi - 4], g * 128:(g + 1) * 128]
                                P.mm(ps[:, bO, hh * 128:(hh + 1) * 128], PTs[:, hh, bi * 128:(bi + 1) * 128], rhs,
                                     start=(bi == 0), stop=(bi == nblk - 1))
                        P.tt(on[:, hp * 2:hp * 2 + 2, :],
                             ps[:, bO, 0:256].rearrange("p (h t) -> p h t", h=2, t=128),
                             rsum.unsqueeze(2).broadcast_to([128, 2, 128]), ALU.mult)
                    bT2 = nb()
                    for h in range(4):
                        P.tr(bbf(bT2)[:, h * 128:(h + 1) * 128], on[:, h, :], identB)
                    P.copy(mixT[:, 4:8, tq:tq + 128],
                           bbf(bT2)[:, 0:512].rearrange("p (h t) -> p h t", h=4, t=128), eng="dve")
            if "attn" in debug_names and grp == DBG_GRP:
                dump("d_oatt", mixT[:, 4:8, :], [128, 4, 1024], BF16)

            if "stop_attn" in debug_names:
                return
            delta_net(grp, T0, nseq, L, qn, kn, vS, gS, mixT)
            if "delta" in debug_names and grp == DBG_GRP:
                dump("d_oa", mixT[:, 0:4, :], [128, 4, 1024], BF16)

            if "stop_delta" in debug_names:
                return
            wo = ar.view(WP, [128, KC, 1024], BF16)
            P.dma(wo, w_out[:, :, :], q="pool")
            for n in range(KC):
                for tt in range(2):
                    b = nb()
                    for kc in range(KC):
                        P.mm(bank(b), wo[:, kc, n * 128:(n + 1) * 128], mixT[:, kc, tt * TT:(tt + 1) * TT],
                             start=(kc == 0), stop=(kc == KC - 1))
                    xs = xT[:, n, T0 + tt * TT:T0 + (tt + 1) * TT]
                    P.stt(xs, bank(b), modG[:, l, 1, n, grp:grp + 1], xs, ALU.mult, ALU.add)

    def delta_net(grp, T0, nseq, L, qn, kn, vS, gS, mixT):
        abT = gates[:, 0, :, :]
        gT = gates[:, 1, :, 0:8]
        beta = gates[:, 2, :, 0:8]
        gc = gates[:, 3, :, 0:8]
        glb = gates[:, 4, :, 0:8]
        egc = gates[:, 5, :, 0:8]
        kdf = gates[:, 6, :, 0:8]
        glast = gates[:, 7, :, 0:8]
        bege = gates[:, 1, :, 8:16]
        tmpA = gates[:, 2, :, 8:16]
        tmpB = gates[:, 3, :, 8:16]
        alog_bc = ptab_sb[:, PT_ALOG:PT_ALOG + 8]
        dtb_bc = ptab_sb[:, PT_DTB:PT_DTB + 8]
        onorm = ptab_sb[:, PT_ONORME:PT_ONORME + 1]
        bc8 = lambda v: v.unsqueeze(1).broadcast_to([128, 8, 8])
        P.act(beta, abT[:, :, 0:8], AF.Exp, scale=-1.0)
        P.ts(beta, beta, 1.0, ALU.add)
        P.recip(beta, beta)
        P.tt(tmpA, abT[:, :, 8:16], bc8(dtb_bc), ALU.add)
        P.act(tmpB, tmpA, AF.Abs)
        P.act(tmpB, tmpB, AF.Exp, scale=-1.0)
        P.act(tmpB, tmpB, AF.Ln, bias=onesF[:, 0:1])
        P.ts(tmpA, tmpA, 0.0, ALU.max)
        P.tt(tmpA, tmpA, tmpB, ALU.add)
        P.act(tmpB[:, 0, :], alog_bc, AF.Exp)
        P.stt(gT, tmpA, -1.0, bc8(tmpB[:, 0, :]), ALU.mult, ALU.mult)
        g64 = gates[:, 1, :, :].rearrange("p a b -> p (a b)")
        b = nb()
        P.mm(ps[:, b, 0:128], Uf, g64)
        P.mm(ps[:, b, 128:256], Ub, g64)
        pv = ps[:, b, 0:256].rearrange("p (d a c) -> p d a c", d=2, a=8, c=16)
        P.copy(gc[:, :, 0:4], pv[:, 0, :, 0:4], eng="dve")
        P.copy(gc[:, :, 4:8], pv[:, 1, :, 4:8], eng="dve")
        gc64 = gates[:, 3, :, :].rearrange("p a b -> p (a b)")
        b = nb()
        P.mm(ps[:, b, 0:128], sel127, gc64)
        P.mm(ps[:, b, 128:256], sel0, gc64)
        pv = ps[:, b, 0:256].rearrange("p (d a c) -> p d a c", d=2, a=8, c=16)
        P.copy(glb[:, :, 0:4], pv[:, 0, :, 0:4], eng="dve")
        P.copy(glb[:, :, 4:8], pv[:, 1, :, 4:8], eng="dve")
        P.act(egc, gc, AF.Exp)
        P.tt(kdf, glb, gc, ALU.subtract)
        P.act(kdf, kdf, AF.Exp)
        P.act(glast, glb, AF.Exp)
        P.tt(bege, beta, egc, ALU.mult)

        if "gates" in debug_names and grp == DBG_GRP:
            dump("d_g", gT, [128, 8, 8], F32)
            dump("d_beta", beta, [128, 8, 8], F32)
            dump("d_gc", gc, [128, 8, 8], F32)
            dump("d_glb", glb, [128, 8, 8], F32)
        DB = 65536
        TSTS = [DB, DB + 12288]
        SCN2 = DB + 24576
        SHR2 = SCN2 + 16384
        STF2 = SHR2 + 8192
        STB2 = STF2 + 4096
        OACCA = STB2 + 2048
        assert OACCA + 4096 <= ar.nbytes
        Sf = [ar.view(STF2 + d * 2048, [128, 4, 128], F32) for d in range(2)]
        Sb = [ar.view(STB2 + d * 1024, [128, 4, 128], BF16) for d in range(2)]
        nch = L // 128
        if grp == 0:
            oaccA = ar.view(OACCA, [128, 4, 256], F32)
        else:
            oaccB = ar.t[:, 0:8192].rearrange("p (k t) -> p k t", k=8, t=1024)[:, :, 0:512].rearrange(
                "p (h a) t -> p h a t", h=4, a=2)

        def oslice(c):
            if grp == 0:
                return oaccA[:, :, c * 128:(c + 1) * 128]
            t0_ = c * 128
            return oaccB[:, :, t0_ // 512, t0_ % 512:t0_ % 512 + 128]

        def tv(off, dt):
            return ar.view(off, [128, 4, 128], dt)

        def mask_b(k_):
            return cstb[:, 512 + k_ * 128: 512 + (k_ + 1) * 128]

        def mm4(lh, rh):
            bb = nb()
            for h in range(4):
                P.mm(ps[:, bb, h * 128:(h + 1) * 128], lh[:, h, :], rh[:, h, :])
            return bb

        def step(s_, c, d, first_touch):
            T_ = TSTS[d]
            dec, decT = tv(T_, F32), tv(T_ + 2048, F32)
            G1, G2 = dec, decT
            Lb, LTb = tv(T_ + 4096, BF16), tv(T_ + 5120, BF16)
            Ab, Bb = tv(T_ + 6144, BF16), tv(T_ + 7168, BF16)
            Cb, Db, Eb, Tb = [tv(T_ + 8192 + 1024 * k_, BF16) for k_ in range(4)]
            B2, B3 = Cb, Db
            S_ = SCN2 + d * 8192
            Xb, qkT, QdT, Vb_, Kbe, kdec, negwT, vnew = [tv(S_ + 1024 * k_, BF16) for k_ in range(8)]
            H_ = SHR2 + d * 4096
            Ktm, Vtm, KKs, KQs = [tv(H_ + 1024 * k_, BF16) for k_ in range(4)]
            ti = s_ * nch + c
            tsl = slice(ti * 128, (ti + 1) * 128)
            dsl = slice(d * 4, d * 4 + 4)
            bK = nb()
            for h in range(4):
                P.tr(bbf(bK)[:, h * 128:(h + 1) * 128], kn[:, h, tsl], identB)
            P.copy(Ktm.rearrange("p h t -> p (h t)"), bbf(bK)[:, 0:512], eng="act")
            bV = nb()
            for h in range(4):
                P.tr(bbf(bV)[:, h * 128:(h + 1) * 128], vS[:, h, tsl], identB)
            P.copy(Vtm.rearrange("p h t -> p (h t)"), bbf(bV)[:, 0:512], eng="act")
            yield
            bKK = nb()
            for h in range(4):
                P.mm(ps[:, bKK, h * 128:(h + 1) * 128], kn[:, h, tsl], kn[:, h, tsl])
            P.tt(KKs, b4(bKK), bc_h(offdiag), ALU.mult)
            bKQ = nb()
            for h in range(4):
                P.mm(ps[:, bKQ, h * 128:(h + 1) * 128], kn[:, h, tsl], qn[:, h, tsl])
            P.copy(KQs.rearrange("p h t -> p (h t)"), bank(bKQ), eng="act")
            yield
            U_ = Uf if d == 0 else Ub
            Mdec, MdecT = (MLf, MUf) if d == 0 else (MUf, MLf)
            P.copy(G1, bc_i(gT[:, ti, dsl]), eng="pool")
            P.stt(G2, G1, -1.0, bc_h(U_), ALU.mult, ALU.mult)
            bD = nb()
            P.mm(bank(bD), U_, G1.rearrange("p h t -> p (h t)"), start=True, stop=False)
            P.mm(bank(bD), onesF, G2.rearrange("p h t -> p (h t)"), start=False, stop=True)
            yield
            P.tt(dec, b4(bD), bc_h(Mdec), ALU.add)
            P.stt(decT, b4(bD), -1.0, bc_h(MdecT), ALU.mult, ALU.add)
            P.act(dec, dec, AF.Exp)
            P.act(decT, decT, AF.Exp)
            P.tt(B2, bc_h(identB), bc_i(beta[:, ti, dsl]), ALU.mult, eng="pool")
            P.tt(B3, bc_h(identB), bc_i(egc[:, ti, dsl]), ALU.mult, eng="pool")
            bR = nb()
            P.mm(bank(bR), onesP, B2.rearrange("p h t -> p (h t)"))
            bE = nb()
            P.mm(bank(bE), onesP, B3.rearrange("p h t -> p (h t)"))
            yield
            P.tt(Lb, KKs, dec, ALU.mult)
            P.tt(Lb, Lb, bc_i(beta[:, ti, dsl]), ALU.mult)
            P.tt(LTb, KKs, decT, ALU.mult, eng="pool")
            P.tt(LTb, LTb, b4(bR), ALU.mult)
            P.tt(qkT, KQs, decT, ALU.mult, eng="pool")
            P.tt(QdT, qn[:, :, tsl], b4(bE), ALU.mult)
            P.tt(Vb_, Vtm, bc_i(beta[:, ti, dsl]), ALU.mult, eng="pool")
            P.tt(Kbe, Ktm, bc_i(bege[:, ti, dsl]), ALU.mult, eng="pool")
            P.tt(kdec, Ktm, bc_i(kdf[:, ti, dsl]), ALU.mult, eng="pool")
            yield
            mi = (lambda k: k) if d == 0 else (lambda k: (k + 4) if 1 <= k <= 4 else (k - 4 if k >= 5 else k))
            P.tt(Ab, Lb, bc_h(mask_b(0)), ALU.mult)
            P.tt(Bb, LTb, bc_h(mask_b(0)), ALU.mult)
            P.stt(Tb, Ab, -1.0, bc_h(identF), ALU.mult, ALU.add)
            P.stt(Xb, Bb, -1.0, bc_h(identF), ALU.mult, ALU.add)
            yield
            b1 = mm4(Bb, Ab)
            b2 = mm4(Ab, Bb)
            P.copy(Cb, b4(b1), eng="act")
            P.copy(Db, b4(b2), eng="act")
            yield
            bx = mm4(Cb, Xb)
            bt = mm4(Xb, Cb)
            b3 = mm4(Db, Cb)
            P.tt(Xb, Xb, b4(bx), ALU.add)
            P.tt(Tb, Tb, b4(bt), ALU.add)
            P.copy(Eb, b4(b3), eng="act")
            yield
            bx = mm4(Eb, Xb)
            bt = mm4(Xb, Eb)
            P.tt(Xb, Xb, b4(bx), ALU.add)
            P.tt(Tb, Tb, b4(bt), ALU.add)
            yield
            for lv in range(1, 5):
                last = (lv == 4)
                P.tt(Ab, Lb, bc_h(mask_b(mi(lv))), ALU.mult, eng="pool")
                if not last:
                    P.tt(Bb, LTb, bc_h(mask_b(mi(lv + 4))), ALU.mult)
                b1 = mm4(Ab, Xb)
                if not last:
                    b2 = mm4(Bb, Tb)
                P.copy(Cb, b4(b1), eng="act")
                if not last:
                    P.copy(Db, b4(b2), eng="act")
                yield
                bx = mm4(Tb, Cb)
                if not last:
                    bt = mm4(Xb, Db)
                P.tt(Xb, Xb, b4(bx), ALU.subtract)
                if not last:
                    P.tt(Tb, Tb, b4(bt), ALU.subtract)
                yield
            if "step0" in debug_names and grp == DBG_GRP and s_ == 0 and (c, d) == DBG_STEP:
                dump("d_dec", dec, [128, 4, 128], F32)
                dump("d_decT", decT, [128, 4, 128], F32)
                dump("d_X", Xb, [128, 4, 128], BF16)
                dump("d_qkT", qkT, [128, 4, 128], BF16)
                dump("d_QdT", QdT, [128, 4, 128], BF16)
                dump("d_KKs", KKs, [128, 4, 128], BF16)
            bW = mm4(Kbe, Xb)
            P.act(negwT.rearrange("p h t -> p (h t)"), bank(bW), AF.Copy, scale=-1.0)
            yield
            bVn = nb()
            for h in range(4):
                P.mm(ps[:, bVn, h * 128:(h + 1) * 128], Xb[:, h, :], Vb_[:, h, :], start=True, stop=False)
                P.mm(ps[:, bVn, h * 128:(h + 1) * 128], negwT[:, h, :], Sb[d][:, h, :], start=False, stop=True)
            P.copy(vnew.rearrange("p h t -> p (h t)"), bank(bVn), eng="act")
            yield
            bO = nb()
            for h in range(4):
                P.mm(ps[:, bO, h * 128:(h + 1) * 128], Sb[d][:, h, :], QdT[:, h, :], start=True, stop=False)
                P.mm(ps[:, bO, h * 128:(h + 1) * 128], vnew[:, h, :], qkT[:, h, :], start=False, stop=True)
            bS_ = mm4(kdec, vnew)
            oa = oslice(c if grp == 1 else c)
            if first_touch[c]:
                P.copy(oa, b4(bO), eng="dve")
                first_touch[c] = False
            else:
                P.tt(oa, oa, b4(bO), ALU.add)
            P.tt(Sf[d], Sf[d], bc_i(glast[:, ti, dsl]), ALU.mult)
            P.tt(Sf[d], Sf[d], b4(bS_), ALU.add)
            P.copy(Sb[d], Sf[d], eng="act")
            yield

        for s_ in range(nseq):
            for d in range(2):
                if grp == 0:
                    P.memset(Sf[d], 0.0, eng="pool")
                else:
                    P.dma(Sf[d], state_delta[d].rearrange("h k v -> k h v"), q="sp")
                P.copy(Sb[d], Sf[d], eng="pool")
            first_touch = [True] * nch
            for k in range(nch):
                gens = [step(s_, k, 0, first_touch), step(s_, nch - 1 - k, 1, first_touch)]
                while gens:
                    for g_ in list(gens):
                        try:
                            next(g_)
                        except StopIteration:
                            gens.remove(g_)
            if grp == 0:
                for d in range(2):
                    P.dma(nsd[s_, d].rearrange("h k v -> k h v"), Sf[d], q="sp", is_output=True)
            for h in range(4):
                for t_ in range(0, L, 512):
                    w = min(512, L - t_)
                    sl = slice(s_ * L + t_, s_ * L + t_ + w)
                    src = oaccA[:, h, 0:w] if grp == 0 else oaccB[:, h, t_ // 512, 0:512]
                    sq = ar.view(TSTS[0], [128, 512], BF16)
                    P.act(sq[:, 0:w], src, AF.Square)
                    b = nb()
                    P.mm(ps[:, b, 0:w], onesP, sq[:, 0:w])
                    rs = rstd[:, 0, 0:w]
                    P.act(rs, ps[:, b, 0:w], AF.Ln, bias=epsT[:, 0:1], scale=1.0 / 128.0)
                    P.act(rs, rs, AF.Exp, scale=-0.5)
                    t_f = ar.view(TSTS[0] + 2048, [128, 512], F32)
                    P.stt(t_f[:, 0:w], src, onorm[:, 0:1], rs, ALU.mult, ALU.mult)
                    P.tt(mixT[:, h, sl], t_f[:, 0:w], gS[:, h, sl], ALU.mult)

    def mixer_odd():
        l = 1
        modnorm(l, 1)
        w_in = odd_w_in[0].rearrange("(kc p) n -> p kc n", p=128)
        w_out = odd_w_out[0].rearrange("(kc p) n -> p kc n", p=128)
        onorm2 = ptab_sb[:, PT_ONORMO:PT_ONORMO + 2]
        QT_, KT_, VT_, GS_, LRT_, OACC2 = 32768, 40960, 49152, 65536, 81920, 86016
        WPO = 102400
        ZB_, ZE_ = 102400, 104448
        EG_, ENG_ = ZE_, ZB_
        QTL_, KTL_, QTT_, KTT_, ATB_ = 110592, 111616, 112640, 113664, 114688
        SF_, SB_, WG_ = 115712, 119808, 121856
        SC = 128.0 ** -0.5
        wgp = ar.view(WG_, [128, 2, 512], F32)
        P.dma(wgp[0:33, :, :], wgpad[:, :, :], q="sp")
        wpc = [0]

        def load_piece(c0, c1):
            wp = ar.view(WPO + (wpc[0] % 2) * 8192, [128, KC, 512], BF16)
            wpc[0] += 1
            P.dma(wp[:, :, 0:c1 - c0], w_in[:, :, c0:c1], q="pool")
            return wp

        qT = ar.view(QT_, [128, 8, 512], BF16)
        kT = ar.view(KT_, [128, 8, 512], BF16)
        vT = ar.view(VT_, [128, 8, 1024], BF16)
        gS = ar.view(GS_, [128, 8, 1024], BF16)
        lrT = ar.view(LRT_, [128, 1024], F32)
        glt = gates[:, 0, 0, 0:4]
        for grp in range(2):
            T0 = grp * 1024
            nseq, L = (4, 256) if grp == 0 else (1, 1024)
            nch = L // 128
            mixT = hT[:, :, T0:T0 + 1024]
            wp = load_piece(3072, 3104)
            for tt in range(2):
                b = nb()
                for kc in range(KC):
                    P.mm(ps[0:32, b, :], wp[:, kc, 0:32], hT[:, kc, T0 + tt * TT:T0 + (tt + 1) * TT],
                         start=(kc == 0), stop=(kc == KC - 1))
                P.copy(lrT[0:32, tt * TT:(tt + 1) * TT], ps[0:32, b, :], eng="dve")
            P.memset(lrT[32:33, :], 1.0, eng="dve")
            for (c0, dst, col0, scl) in ((0, qT, 0, SC), (512, kT, 0, 1.0), (1024, vT, 0, 1.0), (1536, vT, 512, 1.0)):
                wp = load_piece(c0, c0 + 512)
                for i in range(8):
                    b = nb()
                    for kc in range(KC):
                        P.mm(bank(b), hT[:, kc, T0 + i * 128:T0 + (i + 1) * 128], wp[:, kc, 0:512],
                             start=(kc == 0), stop=(kc == KC - 1))
                    if i % 2 == 0:
                        P.act(dst[:, i, col0:col0 + 512], bank(b), AF.Copy, scale=scl)
                    else:
                        P.ts(dst[:, i, col0:col0 + 512], bank(b), scl, ALU.mult)
            for pc in range(2):
                wp = load_piece(2048 + pc * 512, 2560 + pc * 512)
                for cc in range(4):
                    b = nbk(2)
                    for tt in range(2):
                        for kc in range(KC):
                            P.mm(bank(b + tt), wp[:, kc, cc * 128:(cc + 1) * 128],
                                 hT[:, kc, T0 + tt * TT:T0 + (tt + 1) * TT], start=(kc == 0), stop=(kc == KC - 1))
                    P.act(gS[:, pc * 4 + cc, :], ps[:, b:b + 2, :].rearrange("p a t -> p (a t)"), AF.Silu)
            if "oinproj" in debug_names and grp == DBG_GRP:
                dump("d_qT", qT, [128, 8, 512], BF16)
                dump("d_kT", kT, [128, 8, 512], BF16)
                dump("d_vT", vT, [128, 8, 1024], BF16)
                dump("d_gS", gS, [128, 8, 1024], BF16)
                dump("d_lrT", lrT[0:33, :], [33, 1024], F32)

            zb = ar.view(ZB_, [128, 512], F32)
            ze = ar.view(ZE_, [128, 512], F32)
            eg = ar.view(EG_, [128, 512], F32)
            eng_ = ar.view(ENG_, [128, 512], F32)
            qtl = ar.view(QTL_, [128, 512], BF16)
            ktl = ar.view(KTL_, [128, 512], BF16)
            qtt = ar.view(QTT_, [128, 4, 128], BF16)
            ktt = ar.view(KTT_, [128, 4, 128], BF16)
            atb = ar.view(ATB_, [128, 4, 128], BF16)
            Sf = ar.view(SF_, [128, 4, 256], F32)
            Sb = ar.view(SB_, [128, 4, 256], BF16)
            if grp == 1:
                oacc_lo = ar.t[:, 0:8192].rearrange("p (k t) -> p k t", k=8, t=1024)[:, :, 0:512]
                oacc_hi = ar.view(OACC2, [128, 8, 512], F32)
            else:
                oaccA = ar.view(OACC2, [128, 8, 256], F32)

            def oslice(c, fc0, fc1):
                if grp == 0:
                    return oaccA[:, fc0:fc1, c * 128:(c + 1) * 128]
                if c < 4:
                    return oacc_lo[:, fc0:fc1, c * 128:(c + 1) * 128]
                return oacc_hi[:, fc0:fc1, (c - 4) * 128:(c - 3) * 128]

            bufsets = []
            for k_ in range(2):
                if k_ == 0:
                    offs = (QTL_, KTL_, QTT_, KTT_, ATB_)
                else:
                    offs = (106496, 107520, 108544, 109568, 125952)
                bufsets.append((ar.view(offs[0], [128, 512], BF16), ar.view(offs[1], [128, 512], BF16),
                                ar.view(offs[2], [128, 4, 128], BF16), ar.view(offs[3], [128, 4, 128], BF16),
                                ar.view(offs[4], [128, 4, 128], BF16), gates[:, 0, k_, 0:4]))

            def prefix(s_, d, c, bs_):
                qtl, ktl, qtt, ktt, atb, glt = bs_
                U_ = Uf if d == 0 else Ub
                elast = identF[:, 127:128] if d == 0 else identF[:, 0:1]
                ti = s_ * nch + c
                tsl = slice(ti * 128, (ti + 1) * 128)
                bz = nb()
                P.mm(bank(bz), lrT[0:33, tsl], wgp[0:33, d, :])
                yield
                P.ts(ze, bank(bz), -80.0, ALU.max)
                P.act(ze, ze, AF.Exp, scale=-1.0)
                yield
                P.act(zb, ze, AF.Ln, bias=onesF[:, 0:1])
                yield
                bg = nb()
                P.mm(bank(bg), U_, zb)
                yield
                P.act(eg, bank(bg), AF.Exp, scale=-1.0 / 16.0)
                P.act(eng_, bank(bg), AF.Exp, scale=1.0 / 16.0)
                P.tt(qtl, qT[:, ti, :], eg, ALU.mult)
                P.tt(ktl, kT[:, ti, :], eng_, ALU.mult)
                yield
                bl = nb()
                for h in range(4):
                    P.mm(ps[:, bl, h:h + 1], eg[:, h * 128:(h + 1) * 128], elast)
                P.copy(glt, ps[:, bl, 0:4], eng="dve")
                bq = nb()
                for h in range(4):
                    P.tr(bbf(bq)[:, h * 128:(h + 1) * 128], qtl[:, h * 128:(h + 1) * 128], identB)
                P.copy(qtt.rearrange("p h t -> p (h t)"), bbf(bq)[:, 0:512], eng="act")
                bk = nb()
                for h in range(4):
                    P.tr(bbf(bk)[:, h * 128:(h + 1) * 128], ktl[:, h * 128:(h + 1) * 128], identB)
                P.copy(ktt.rearrange("p h t -> p (h t)"), bbf(bk)[:, 0:512], eng="dve")
                yield
                ba = nb()
                for h in range(4):
                    P.mm(ps[:, ba, h * 128:(h + 1) * 128], ktt[:, h, :], qtt[:, h, :])
                P.tt(atb, b4(ba), bc_h(U_), ALU.mult)
                yield

            def suffix(s_, d, c, bs_, first_touch):
                qtl, ktl, qtt, ktt, atb, glt = bs_
                ti = s_ * nch + c
                bo = nbk(2)
                for h in range(4):
                    for half in range(2):
                        fc = h * 2 + half
                        dstp = ps[:, bo + fc // 4, (fc % 4) * 128:(fc % 4 + 1) * 128]
                        P.mm(dstp, vT[:, ti, h * 256 + half * 128:h * 256 + (half + 1) * 128], atb[:, h, :],
                             start=True, stop=False)
                        P.mm(dstp, Sb[:, h, half * 128:(half + 1) * 128], qtt[:, h, :], start=False, stop=True)
                for k2 in range(2):
                    oa = oslice(c, k2 * 4, k2 * 4 + 4)
                    if first_touch[c]:
                        P.copy(oa, b4(bo + k2), eng="dve" if k2 == 0 else "act")
                    else:
                        P.tt(oa, oa, b4(bo + k2), ALU.add)
                first_touch[c] = False
                yield
                bs2 = nbk(2)
                for h in range(4):
                    P.mm(ps[:, bs2 + h // 2, (h % 2) * 256:(h % 2 + 1) * 256], ktl[:, h * 128:(h + 1) * 128],
                         vT[:, ti, h * 256:(h + 1) * 256])
                P.tt(Sf, Sf, ps[:, bs2:bs2 + 2, :].rearrange("p a (h v) -> p (a h) v", h=2, v=256), ALU.add)
                P.tt(Sf, Sf, glt.unsqueeze(2).broadcast_to([128, 4, 256]), ALU.mult)
                P.copy(Sb, Sf, eng="act")
                yield

            for s_ in range(nseq):
                first_touch = [True] * nch
                steps = [(d, (cidx if d == 0 else nch - 1 - cidx)) for d in range(2) for cidx in range(nch)]
                def drive(gens):
                    gens = [g_ for g_ in gens if g_ is not None]
                    while gens:
                        for g_ in list(gens):
                            try:
                                next(g_)
                            except StopIteration:
                                gens.remove(g_)

                drive([prefix(s_, steps[0][0], steps[0][1], bufsets[0])])
                for k_, (d, c) in enumerate(steps):
                    if k_ % nch == 0:
                        if grp == 0:
                            P.memset(Sf, 0.0, eng="pool")
                        else:
                            P.dma(Sf, state_gla[d].rearrange("h k v -> k h v"), q="sp")
                        P.copy(Sb, Sf, eng="pool")
                    nxt = None
                    if k_ + 1 < len(steps):
                        nxt = prefix(s_, steps[k_ + 1][0], steps[k_ + 1][1], bufsets[(k_ + 1) % 2])
                    drive([suffix(s_, d, c, bufsets[k_ % 2], first_touch), nxt])
                    if k_ % nch == nch - 1 and grp == 0:
                        P.dma(nsg[s_, d].rearrange("h k v -> k h v"), Sf, q="sp", is_output=True)
                for h in range(4):
                    for t_ in range(0, L, 512):
                        w = min(512, L - t_)
                        b = nb()
                        srcs = []
                        for half in range(2):
                            fc = h * 2 + half
                            if grp == 0:
                                src = oaccA[:, fc, t_:t_ + w]
                            else:
                                src = (oacc_lo if t_ == 0 else oacc_hi)[:, fc, 0:512]
                            srcs.append(src)
                            sq = ar.view(ZB_ + half * 1024, [128, 512], BF16)
                            P.act(sq[:, 0:w], src, AF.Square)
                            P.mm(ps[:, b, 0:w], onesP, sq[:, 0:w], start=(half == 0), stop=(half == 1))
                        rs = rstd[:, 0, 0:w]
                        P.act(rs, ps[:, b, 0:w], AF.Ln, bias=epsT[:, 0:1], scale=1.0 / 256.0)
                        P.act(rs, rs, AF.Exp, scale=-0.5)
                        for half in range(2):
                            fc = h * 2 + half
                            t_f = ar.view(EG_, [128, 512], F32)
                            P.stt(t_f[:, 0:w], srcs[half], onorm2[:, half:half + 1], rs, ALU.mult, ALU.mult)
                            sl = slice(s_ * L + t_, s_ * L + t_ + w)
                            P.tt(mixT[:, fc, sl], t_f[:, 0:w], gS[:, fc, sl], ALU.mult)
            if "gla" in debug_names and grp == DBG_GRP:
                dump("d_mix", mixT, [128, 8, 1024], BF16)
            wo = ar.view(WPO, [128, KC, 1024], BF16)
            P.dma(wo, w_out[:, :, :], q="pool")
            for n in range(KC):
                for tt in range(2):
                    b = nb()
                    for kc in range(KC):
                        P.mm(bank(b), wo[:, kc, n * 128:(n + 1) * 128], mixT[:, kc, tt * TT:(tt + 1) * TT],
                             start=(kc == 0), stop=(kc == KC - 1))
                    xs = xT[:, n, T0 + tt * TT:T0 + (tt + 1) * TT]
                    P.stt(xs, bank(b), modG[:, l, 1, n, grp:grp + 1], xs, ALU.mult, ALU.add)


    if stage != "all":
        ada_flush()
    if stage == "ffn0":
        ffn(0, 0)
        final_out()
    elif stage == "mix0":
        mixer_even()
        final_out()
    elif stage == "mix1":
        mixer_odd()
        final_out()
    else:
        ffn(0, 0)
        mixer_even()
        ffn(0, 1)
        ffn(1, 0)
        mixer_odd()
        ffn(1, 1)
        final_out()

    P.emit()
    stack.close()
    return nc, dbg


PT_ADAB = 0
PT_NORMG = PT_ADAB + 144
PT_FINALG = PT_NORMG + 48
PT_CONV = PT_FINALG + 8
PT_SINK = PT_CONV + 60
PT_ALOG = PT_SINK + 4
PT_DTB = PT_ALOG + 8
PT_ONORME = PT_DTB + 8
PT_ONORMO = PT_ONORME + 1
PT_COLS = PT_ONORMO + 2
DBG_GRP = 0
DBG_STEP = (0, 0)

C_IDENT = 0
C_UF, C_UB, C_SEL127, C_SEL0, C_ONES, C_OFFD, C_ML, C_MU, C_RM = [128 * i for i in range(1, 10)]
CST_COLS = 128 * 10
CSTB_COLS = 512 + 9 * 128


def _fm(v):
    v = np.asarray(v, np.float32)
    return np.ascontiguousarray(v.reshape(-1, 128).T)


def make_tables(inputs):
    pt = np.zeros((128, PT_COLS), np.float32)
    for l in range(2):
        pt[:, PT_ADAB + l * 72: PT_ADAB + (l + 1) * 72] = _fm(inputs["ada_b"][l])
        for s in range(3):
            o = PT_NORMG + (l * 3 + s) * 8
            pt[:, o:o + 8] = _fm(inputs["norm_g"][l, s])
    pt[:, PT_FINALG:PT_FINALG + 8] = _fm(inputs["final_g"])
    cv = np.asarray(inputs["even_conv"], np.float32)[0]
    pt[:, PT_CONV:PT_CONV + 60] = cv.T.reshape(12, 128, 5).transpose(1, 0, 2).reshape(128, 60)
    pt[:, PT_SINK:PT_SINK + 4] = np.broadcast_to(np.asarray(inputs["even_sink"], np.float32)[0][None, :], (128, 4))
    pt[:, PT_ALOG:PT_ALOG + 8] = np.broadcast_to(np.asarray(inputs["even_a_log"], np.float32)[0].reshape(1, 8), (128, 8))
    pt[:, PT_DTB:PT_DTB + 8] = np.broadcast_to(np.asarray(inputs["even_dt_bias"], np.float32)[0].reshape(1, 8), (128, 8))
    pt[:, PT_ONORME:PT_ONORME + 1] = np.asarray(inputs["even_onorm"], np.float32)[0].reshape(128, 1)
    pt[:, PT_ONORMO:PT_ONORMO + 2] = _fm(inputs["odd_onorm"][0])
    cst = np.zeros((128, CST_COLS), np.float32)
    cst[:, C_IDENT:C_IDENT + 128] = np.eye(128, dtype=np.float32)
    kk, ii = np.meshgrid(np.arange(128), np.arange(128), indexing="ij")
    NEG = -30000.0
    cst[:, C_UF:C_UF + 128] = (kk <= ii)
    cst[:, C_UB:C_UB + 128] = (kk >= ii)
    cst[127, C_SEL127:C_SEL127 + 128] = 1.0
    cst[0, C_SEL0:C_SEL0 + 128] = 1.0
    cst[:, C_ONES:C_ONES + 128] = 1.0
    cst[:, C_OFFD:C_OFFD + 128] = (kk != ii)
    cst[:, C_ML:C_ML + 128] = np.where(ii <= kk, 0.0, NEG)
    cst[:, C_MU:C_MU + 128] = np.where(ii >= kk, 0.0, NEG)
    rm = np.zeros((128, 128), np.float32)
    for dp in range(128):
        if (dp % 64) < 32:
            rm[dp + 32, dp] = -1.0
        else:
            rm[dp - 32, dp] = 1.0
    cst[:, C_RM:C_RM + 128] = rm
    return pt, cst


def make_bmask():
    i, j = np.meshgrid(np.arange(128), np.arange(128), indexing="ij")
    ms = [(i // 8 == j // 8)]
    for b in (8, 16, 32, 64):
        ms.append((i // (2 * b) == j // (2 * b)) & ((i // b) % 2 == 1) & ((j // b) % 2 == 0))
    for b in (8, 16, 32, 64):
        ms.append((i // (2 * b) == j // (2 * b)) & ((i // b) % 2 == 0) & ((j // b) % 2 == 1))
    return np.ascontiguousarray(np.concatenate([m.astype(np.float32) for m in ms], axis=1))


def make_rope():
    t = np.arange(1024)
    row = (t // 64).astype(np.float64)
    col = (t % 64).astype(np.float64)
    inv = 10000.0 ** (-np.arange(32, dtype=np.float64) / 32.0)
    ang = np.zeros((128, 1024))
    for d in range(128):
        pos = row if d < 64 else col
        ang[d] = pos * np.float32(inv[d % 32])
    ang32 = np.zeros((128, 1024), np.float32)
    inv32 = (np.float32(10000.0) ** (-np.arange(32, dtype=np.float32) / np.float32(32))).astype(np.float32)
    for d in range(128):
        pos = (row if d < 64 else col).astype(np.float32)
        ang32[d] = pos * inv32[d % 32]
    return np.ascontiguousarray(np.stack([np.cos(ang32), np.sin(ang32)], axis=1).astype(np.float32))


def make_in_maps(inputs, stage="all"):
    pt, cst = make_tables(inputs)
    rope_t = make_rope()
    bmask_t = make_bmask()
    wg = np.asarray(inputs["odd_w_gate"], np.float32)[0]
    wgpad_t = np.zeros((33, 2, 512), np.float32)
    wgpad_t[0:16, 0, :] = wg[0]
    wgpad_t[16:32, 1, :] = wg[1]
    wgpad_t[32, :, :] = np.asarray(inputs["odd_gate_bias"], np.float32)[0]
    maps = []
    xp = np.asarray(inputs["x_prompt"], np.float32)
    xs = np.asarray(inputs["x_sample"], np.float32)
    for c in range(8):
        xin = np.concatenate([xp[4 * c:4 * c + 4].reshape(1024, D), xs[c]], axis=0)
        cond = np.stack([np.asarray(inputs["c_ctx"], np.float32), np.asarray(inputs["c"], np.float32)[c]], axis=-1)
        condT = np.ascontiguousarray(cond.reshape(KC, 128, 2).transpose(1, 0, 2))
        m = {"xin": np.ascontiguousarray(xin), "condT": condT, "ptab": pt, "cst": cst,
             "ada_w": np.asarray(inputs["ada_w"], np.float32),
             "even_w_in": np.asarray(inputs["even_w_in"], np.float32),
             "even_w_out": np.asarray(inputs["even_w_out"], np.float32),
             "rope": rope_t, "bmask": bmask_t, "wgpad": wgpad_t,
             "odd_w_in": np.asarray(inputs["odd_w_in"], np.float32),
             "odd_w_out": np.asarray(inputs["odd_w_out"], np.float32),
             "state_gla": np.ascontiguousarray(np.asarray(inputs["state_gla"], np.float32)[c, 0]),
             "cache_k": np.ascontiguousarray(np.asarray(inputs["cache_k"], np.float32)[c, 0]),
             "cache_v": np.ascontiguousarray(np.asarray(inputs["cache_v"], np.float32)[c, 0]),
             "state_delta": np.ascontiguousarray(np.asarray(inputs["state_delta"], np.float32)[c, 0])}
        if stage in ("all", "ffn0"):
            m["ffn_w_gu"] = np.asarray(inputs["ffn_w_gu"], np.float32)
            m["ffn_w_down"] = np.asarray(inputs["ffn_w_down"], np.float32)
        maps.append(m)
    return maps


_CACHE = {}


def kernel(**inputs):
    if "nc" not in _CACHE:
        _CACHE["nc"] = build_program("all")[0]
    nc = _CACHE["nc"]
    maps = make_in_maps(inputs)
    res = run_bass_kernel_spmd(nc, maps, core_ids=list(range(8)))
    rs = res.results
    ys = [np.asarray(r["yout"], np.float32) for r in rs]
    y_prompt = np.concatenate([y[:1024].reshape(4, 256, D) for y in ys], axis=0)
    y_sample = np.stack([y[1024:] for y in ys], axis=0)
    nsd = np.concatenate([np.asarray(r["nsd"], np.float32) for r in rs], axis=0)[:, None]
    nck = np.concatenate([np.asarray(r["nck"], np.float32).reshape(4, 256, 2, 128) for r in rs], axis=0)[:, None]
    ncv = np.concatenate([np.asarray(r["ncv"], np.float32).reshape(4, 256, 2, 128) for r in rs], axis=0)[:, None]
    nsg = np.concatenate([np.asarray(r["nsg"], np.float32) for r in rs], axis=0)[:, None]
    return (y_prompt, y_sample, np.ascontiguousarray(nsd), np.ascontiguousarray(nck),
            np.ascontiguousarray(ncv), np.ascontiguousarray(nsg))
```

```python
import numpy as np
import concourse.bass as bass
import concourse.mybir as mybir

F32 = mybir.dt.float32
BF16 = mybir.dt.bfloat16
AF = mybir.ActivationFunctionType
ALU = mybir.AluOpType
AX = mybir.AxisListType

_DTSZ = {F32: 4, BF16: 2}


def _region(ap):
    sp = str(ap.space)
    if "DRAM" in sp.upper() or "HBM" in sp.upper():
        return None
    sz = _DTSZ[ap.dtype]
    pat = ap.ap
    pstep, pcnt = pat[0]
    off = int(ap.offset)
    if pstep == 0:
        p0, f0 = 0, off
        pstep = 1 << 40
    else:
        p0, f0 = off // pstep, off % pstep
    ext = 1
    for st, cnt in pat[1:]:
        ext += (cnt - 1) * abs(st)
    b0, b1 = f0 * sz, (f0 + ext) * sz
    if "PSUM" in sp.upper():
        b0 = (b0 // 2048) * 2048
        b1 = ((b1 + 2047) // 2048) * 2048
        return (ap.tensor.name, 0, 128, b0, b1)
    return (ap.tensor.name, p0, p0 + pcnt, b0, b1)


def _ovl(a, b):
    return a[0] == b[0] and a[1] < b[2] and b[1] < a[2] and a[3] < b[4] and b[3] < a[4]


def _covers(a, b):
    return a[0] == b[0] and a[1] <= b[1] and a[2] >= b[2] and a[3] <= b[3] and a[4] >= b[4]


class Op:
    __slots__ = ("eng", "fn", "seq", "inc", "waits", "ctr", "is_dma", "val")

    def __init__(self, eng, fn, is_dma=False):
        self.eng = eng
        self.fn = fn
        self.inc = False
        self.waits = []
        self.is_dma = is_dma
        self.ctr = None
        self.seq = 0
        self.val = 0


ENGS = ("pe", "act", "dve", "pool", "sp")
NDMA = 24


class Prog:
    def __init__(self, nc):
        self.nc = nc
        self.streams = {e: [] for e in ENGS}
        self.seqc = {}
        self.known = {e: {} for e in ENGS}
        self.recs = {}
        self.dma_rr = {"h": 0, "s": 0}
        self.dma_last = {}
        self.nops = 0
        self.out_dmas = []

    def _need(self, op, dep):
        if dep is op:
            return
        if dep.ctr == op.ctr and not dep.is_dma:
            pass
        k = self.known[op.eng]
        if k.get(dep.ctr, -1) >= dep.seq:
            return
        k[dep.ctr] = dep.seq
        dep.inc = True
        op.waits.append(dep)

    BK = 2048

    def _buckets(self, r):
        return range(r[3] // self.BK, (r[4] - 1) // self.BK + 1)

    def _track(self, op, reads, writes):
        BK = self.BK
        for ap in reads:
            r = _region(ap)
            if r is None:
                continue
            for b in self._buckets(r):
                lst = self.recs.setdefault((r[0], b), [])
                is_ps = (r[0] == "ps")
                for rec in lst:
                    if _ovl(rec[0], r) and (rec[1] == "w" or (is_ps and rec[2].ctr != op.ctr)):
                        d = rec[2]
                        if d.ctr == op.ctr and op.eng == "pe":
                            continue
                        self._need(op, d)
                for i, rec in enumerate(lst):
                    if rec[1] == "r" and rec[2].ctr == op.ctr and rec[0] == r:
                        lst.pop(i)
                        break
                lst.append([r, "r", op])
        for ap in writes:
            r = _region(ap)
            if r is None:
                continue
            for b in self._buckets(r):
                lst = self.recs.setdefault((r[0], b), [])
                keep = []
                lo, hi = b * BK, (b + 1) * BK
                for rec in lst:
                    if rec[2] is op:
                        keep.append(rec)
                        continue
                    rr = rec[0]
                    if _ovl(rr, r):
                        d = rec[2]
                        if not (d.ctr == op.ctr and op.eng == "pe"):
                            self._need(op, d)
                        if (r[1] <= rr[1] and r[2] >= rr[2]
                                and r[3] <= max(rr[3], lo) and r[4] >= min(rr[4], hi)):
                            continue
                    keep.append(rec)
                keep.append([r, "w", op])
                self.recs[(r[0], b)] = keep

    def _add(self, eng, fn, reads, writes):
        op = Op(eng, fn)
        op.ctr = eng
        op.seq = self.seqc.get(eng, 0)
        self.seqc[eng] = op.seq + 1
        self._track(op, reads, writes)
        self.streams[eng].append(op)
        self.nops += 1
        return op

    def dma(self, out, in_, q="sp", is_output=False):
        op = Op(q, None, is_dma=True)
        kind = "s" if q == "pool" else "h"
        k = self.dma_rr[kind]
        self.dma_rr[kind] = (k + 1) % (NDMA // 2)
        op.ctr = "dma%s%d" % (kind, k)
        op.seq = self.seqc.get(op.ctr, 0)
        self.seqc[op.ctr] = op.seq + 1
        prev = self.dma_last.get(op.ctr)
        if prev is not None:
            self._need(op, prev)
        self.dma_last[op.ctr] = op
        op.inc = True
        self._track(op, [in_], [out])
        op.fn = lambda e, out=out, in_=in_: e.dma_start(out=out, in_=in_)
        self.streams[q].append(op)
        if is_output:
            self.out_dmas.append(op)
        return op

    def mm(self, out, lhsT, rhs, start=True, stop=True):
        return self._add("pe", lambda e: e.matmul(out, lhsT, rhs, start=start, stop=stop),
                         [lhsT, rhs], [out])

    def tr(self, out, in_, ident):
        return self._add("pe", lambda e: e.transpose(out, in_, ident), [in_, ident], [out])

    def act(self, out, in_, func, bias=None, scale=1.0, accum=None):
        rd = [in_]
        if bias is not None and not isinstance(bias, (int, float)):
            rd.append(bias)
        if not isinstance(scale, (int, float)):
            rd.append(scale)
        wr = [out] + ([accum] if accum is not None else [])
        kw = {}
        if bias is not None:
            kw["bias"] = bias
        if accum is not None:
            kw["accum_out"] = accum
        return self._add("act", lambda e: e.activation(out=out, in_=in_, func=func, scale=scale, **kw),
                         rd, wr)

    def tt(self, out, in0, in1, op, eng="dve"):
        return self._add(eng, lambda e: e.tensor_tensor(out=out, in0=in0, in1=in1, op=op),
                         [in0, in1], [out])

    def ts(self, out, in0, s1, op0, s2=None, op1=None, eng="dve", accum=None):
        rd = [in0] + [s for s in (s1, s2) if s is not None and not isinstance(s, (int, float))]
        kw = {}
        if op1 is not None:
            kw["op1"] = op1
        if accum is not None:
            kw["accum_out"] = accum
        wr = [out] + ([accum] if accum is not None else [])
        return self._add(eng, lambda e: e.tensor_scalar(out=out, in0=in0, scalar1=s1, scalar2=s2, op0=op0, **kw),
                         rd, wr)

    def stt(self, out, in0, scalar, in1, op0, op1, eng="dve"):
        rd = [in0, in1] + ([scalar] if not isinstance(scalar, (int, float)) else [])
        return self._add(eng, lambda e: e.scalar_tensor_tensor(out=out, in0=in0, scalar=scalar, in1=in1,
                                                               op0=op0, op1=op1), rd, [out])

    def copy(self, out, in_, eng="dve"):
        if eng == "act":
            return self.act(out, in_, AF.Copy)
        return self._add(eng, lambda e: e.tensor_copy(out=out, in_=in_), [in_], [out])

    def reduce(self, out, in_, op, eng="dve", axis=None):
        axis = axis or AX.X
        return self._add(eng, lambda e: e.tensor_reduce(out=out, in_=in_, axis=axis, op=op), [in_], [out])

    def recip(self, out, in_):
        return self._add("dve", lambda e: e.reciprocal(out=out, in_=in_), [in_], [out])

    def memset(self, ap, val, eng="dve"):
        return self._add(eng, lambda e: e.memset(ap, val), [], [ap])

    def emit(self):
        nc = self.nc
        sems = {}
        import contextlib
        stack = contextlib.ExitStack()
        allops = []
        for e in ENGS:
            allops.extend(self.streams[e])
        ctrs = sorted(set(o.ctr for o in allops))
        for c in ctrs:
            sems[c] = stack.enter_context(nc.semaphore("s_" + c))
        cnt = {c: 0 for c in ctrs}
        byctr = {c: [] for c in ctrs}
        for o in allops:
            byctr[o.ctr].append(o)
        for c in ctrs:
            ops = sorted(byctr[c], key=lambda o: o.seq)
            v = 0
            for o in ops:
                if o.inc:
                    v += 16 if o.is_dma else 1
                o.val = v
        fin = stack.enter_context(nc.semaphore("s_fin"))
        block = stack.enter_context(nc.Block())
        prog = self

        def run_stream(eng_name, e):
            for o in prog.streams[eng_name]:
                for d in o.waits:
                    e.wait_ge(sems[d.ctr], d.val)
                ins = o.fn(e)
                if o.inc:
                    ins.then_inc(sems[o.ctr], 16 if o.is_dma else 1)

        @block.tensor
        def _(e):
            run_stream("pe", e)

        @block.scalar
        def _(e):
            run_stream("act", e)

        @block.vector
        def _(e):
            run_stream("dve", e)

        @block.gpsimd
        def _(e):
            run_stream("pool", e)

        @block.sync
        def _(e):
            run_stream("sp", e)
            for o in prog.out_dmas:
                e.wait_ge(sems[o.ctr], o.val)

        stack.close()
from concourse.bass_utils import run_bass_kernel_spmd

D = 1024
NTOK = 2048
KC = 8
TT = 512
NTT = 4
DFF = 2816
NHC = 22
EPS = 1e-6
EVEN_IN = 3088
ODD_IN = 3104


class Arena:
    def __init__(self, nc, name, nbytes, stack):
        self.t = stack.enter_context(nc.sbuf_tensor(name, [128, nbytes // 4], F32))
        self.nbytes = nbytes

    def view(self, off, shape, dtype):
        sz = 4 if dtype == F32 else 2
        n = 1
        for s in shape[1:]:
            n *= s
        assert off % 4 == 0 and (n * sz) % 4 == 0 and off + n * sz <= self.nbytes, (off, shape, self.nbytes)
        ap = self.t[0:shape[0], off // 4: off // 4 + (n * sz) // 4]
        if dtype != F32:
            ap = ap.bitcast(dtype)
        if len(shape) == 3:
            ap = ap.rearrange("p (a b) -> p a b", a=shape[1], b=shape[2])
        elif len(shape) == 4:
            ap = ap.rearrange("p (a b c) -> p a b c", a=shape[1], b=shape[2], c=shape[3])
        return ap


def build_program(stage="all", debug_names=()):
    import contextlib
    nc = bass.Bass("TRN2", target_bir_lowering=False)
    P = Prog(nc)
    stack = contextlib.ExitStack()

    def din(name, shape, dt=F32):
        return nc.dram_tensor(name, list(shape), dt, kind="ExternalInput").ap()

    def dout(name, shape, dt=F32):
        return nc.dram_tensor(name, list(shape), dt, kind="ExternalOutput").ap()

    xin = din("xin", [NTOK, D])
    condT = din("condT", [128, KC, 2])
    ptab = din("ptab", [128, PT_COLS])
    cst = din("cst", [128, CST_COLS])
    ada_w = din("ada_w", [2, D, 9 * D])
    if stage in ("all", "ffn0"):
        w_gu = din("ffn_w_gu", [2, 2, D, 2 * DFF])
        w_dn = din("ffn_w_down", [2, 2, DFF, D])
    yout = dout("yout", [NTOK, D])
    even_w_in = din("even_w_in", [1, D, EVEN_IN])
    even_w_out = din("even_w_out", [1, D, D])
    rope = din("rope", [128, 2, 1024])
    bmask = din("bmask", [128, 9 * 128])
    odd_w_in = din("odd_w_in", [1, D, ODD_IN])
    odd_w_out = din("odd_w_out", [1, D, D])
    wgpad = din("wgpad", [33, 2, 512])
    state_gla = din("state_gla", [2, 4, 128, 256])
    nsg = dout("nsg", [4, 2, 4, 128, 256])
    cache_k = din("cache_k", [512, 2, 128])
    cache_v = din("cache_v", [512, 2, 128])
    state_delta = din("state_delta", [2, 4, 128, 128])
    nck = dout("nck", [1024, 256])
    ncv = dout("ncv", [1024, 256])
    nsd = dout("nsd", [4, 2, 4, 128, 128])

    dbg = {}
    xT = stack.enter_context(nc.sbuf_tensor("xT", [128, KC, NTOK], F32))
    ptab_sb = stack.enter_context(nc.sbuf_tensor("ptab_sb", [128, PT_COLS], F32))
    cst_sb = stack.enter_context(nc.sbuf_tensor("cst_sb", [128, CST_COLS], F32))
    cstb = stack.enter_context(nc.sbuf_tensor("cstb", [128, CSTB_COLS], BF16))
    modT = stack.enter_context(nc.sbuf_tensor("modT", [128, 2, 72, 2], F32))
    modA = stack.enter_context(nc.sbuf_tensor("modA", [128, 2, 3, KC, 2], F32))
    modG = stack.enter_context(nc.sbuf_tensor("modG", [128, 2, 3, KC, 2], F32))
    scT = stack.enter_context(nc.sbuf_tensor("scT", [128, KC, 2], BF16))
    condsb = stack.enter_context(nc.sbuf_tensor("condsb", [128, KC, 2], F32))
    gates = stack.enter_context(nc.sbuf_tensor("gates", [128, 8, 8, 16], F32))
    epsT = stack.enter_context(nc.sbuf_tensor("epsT", [128, 2], F32))
    rstd = stack.enter_context(nc.sbuf_tensor("rstd", [128, 2, TT], F32))
    ar = Arena(nc, "arena", 126976, stack)
    ps = stack.enter_context(nc.psum_tensor("ps", [128, 8, 512], F32))

    def bank(b):
        return ps[:, b, :]

    identF = cst_sb[:, C_IDENT:C_IDENT + 128]
    identB = cstb[:, 0:128]
    onesP = cstb[:, 128:256]

    P.dma(ptab_sb[:, :], ptab[:, :], q="sp")
    P.dma(cst_sb[:, :], cst[:, :], q="sp")
    P.dma(condsb[:, :, :], condT[:, :, :], q="sp")
    P.copy(identB, identF, eng="dve")
    P.memset(onesP, 1.0, eng="dve")
    P.memset(epsT[:, :], EPS, eng="dve")
    P.memset(gates[:, :, :, :], 0.0, eng="pool")

    STG = 32768
    for i in range(16):
        stg = ar.view(STG + (i % 2) * 4096, [128, D], F32)
        P.dma(stg, xin[i * 128:(i + 1) * 128, :], q="sp" if i % 2 == 0 else "act")
        for half in range(2):
            b = (i * 2 + half) % 4
            for kk in range(4):
                kc = half * 4 + kk
                P.tr(ps[:, b, kk * 128:(kk + 1) * 128], stg[:, kc * 128:(kc + 1) * 128], identF)
            src = ps[:, b, :].rearrange("p (a t) -> p a t", a=4, t=128)
            dst = xT[:, half * 4:half * 4 + 4, i * 128:(i + 1) * 128]
            if half == 0:
                P.copy(dst, src, eng="dve")
            else:
                P.copy(dst, src, eng="act")

    P.act(scT[:, :, :], condsb[:, :, :], AF.Silu)
    ADAW = 40960

    def ada_finish(l, s):
        ng = ptab_sb[:, PT_NORMG + (l * 3 + s) * 8: PT_NORMG + (l * 3 + s + 1) * 8]
        sc = modT[:, l, (3 * s + 1) * 8:(3 * s + 2) * 8, :]
        P.stt(modA[:, l, s, :, :], sc, 1.0, ng.unsqueeze(2).broadcast_to([128, KC, 2]), ALU.add, ALU.mult)
        gt = modT[:, l, (3 * s + 2) * 8:(3 * s + 3) * 8, :]
        P.ts(modG[:, l, s, :, :], gt, 0.5 if s != 1 else 1.0, ALU.mult)

    for i in range(3):
        wb = ar.view(ADAW + (i % 3) * 16384, [128, KC, 1024], BF16)
        src = ada_w[0].rearrange("(kc p) n -> p kc n", p=128)[:, :, i * 1024:(i + 1) * 1024]
        P.dma(wb, src, q="pool")
        for n in range(8):
            j = i * 8 + n
            for kc in range(KC):
                P.mm(ps[:, 4, 2 * j:2 * j + 2], wb[:, kc, n * 128:(n + 1) * 128], scT[:, kc, :],
                     start=(kc == 0), stop=(kc == KC - 1))
    P.tt(modT[:, 0, 0:24, :], ps[:, 4, 0:48].rearrange("p (j c) -> p j c", j=24, c=2),
         ptab_sb[:, PT_ADAB:PT_ADAB + 24].unsqueeze(2).broadcast_to([128, 24, 2]), ALU.add)
    ada_finish(0, 0)
    ada_tasks = [(0, i, q4) for i in range(3, 9) for q4 in range(4)] + \
                [(1, i, q4) for i in range(9) for q4 in range(4)]
    ada_cnt = [0, 0, 0]

    ada_pending = []

    def ada_load():
        l, i, q4 = ada_tasks.pop(0)
        wb = ar.view(TMP + (ada_cnt[0] % 2) * 4096, [128, KC, 256], BF16)
        ada_cnt[0] += 1
        c0 = i * 1024 + q4 * 256
        P.dma(wb, ada_w[l].rearrange("(kc p) n -> p kc n", p=128)[:, :, c0:c0 + 256], q="pool")
        ada_pending.append((l, i, q4, wb))

    def ada_compute():
        l, i, q4, wb = ada_pending.pop(0)
        bank_b = 6 + (ada_cnt[1] % 2)
        ada_cnt[1] += 1
        for nn in range(2):
            for kc in range(KC):
                P.mm(ps[:, bank_b, 2 * nn:2 * nn + 2], wb[:, kc, nn * 128:(nn + 1) * 128], scT[:, kc, :],
                     start=(kc == 0), stop=(kc == KC - 1))
        j0 = i * 8 + q4 * 2
        P.tt(modT[:, l, j0:j0 + 2, :], ps[:, bank_b, 0:4].rearrange("p (j c) -> p j c", j=2, c=2),
             ptab_sb[:, PT_ADAB + l * 72 + j0:PT_ADAB + l * 72 + j0 + 2].unsqueeze(2).broadcast_to([128, 2, 2]),
             ALU.add)
        if q4 == 3 and i % 3 == 2:
            ada_finish(l, i // 3)

    def ada_tick():
        ada_cnt[2] += 1
        if ada_cnt[2] % 3 != 0:
            return
        if len(ada_pending) == 2 or (ada_pending and not ada_tasks):
            ada_compute()
        if ada_tasks and len(ada_pending) < 2:
            ada_load()

    def ada_flush():
        while ada_tasks or ada_pending:
            if ada_tasks and len(ada_pending) < 2:
                ada_load()
            else:
                ada_compute()

    HT = 0
    FW = 32768
    ACTB = FW + 49152
    SG = ACTB + 32768
    TMP = SG + 4096
    SQ = TMP + 4096
    assert SQ + 4096 <= ar.nbytes, SQ + 4096
    hT = ar.view(HT, [128, KC, NTOK], BF16)

    def rms_tile(tt, bank0=6):
        b = bank0 + (tt % 2)
        for kc in range(KC):
            sq = ar.view(SQ + ((tt * KC + kc) % 4) * 1024, [128, TT], BF16)
            P.act(sq, xT[:, kc, tt * TT:(tt + 1) * TT], AF.Square)
            P.mm(bank(b), onesP, sq, start=(kc == 0), stop=(kc == KC - 1))
        rs = rstd[:, tt % 2, :]
        P.act(rs, bank(b), AF.Ln, bias=epsT[:, 0:1], scale=1.0 / 1024.0)
        P.act(rs, rs, AF.Exp, scale=-0.5)
        return rs

    def modnorm_tile(l, s, tt):
        rs = rms_tile(tt)
        c = 0 if tt < 2 else 1
        for kc in range(KC):
            tmp = ar.view(TMP + ((tt * KC + kc) % 2) * 2048, [128, TT], F32)
            P.stt(tmp, xT[:, kc, tt * TT:(tt + 1) * TT], modA[:, l, s, kc, c:c + 1],
                  rs, ALU.mult, ALU.mult)
            P.act(hT[:, kc, tt * TT:(tt + 1) * TT], tmp, AF.Identity,
                  bias=modT[:, l, 3 * s * 8 + kc, c:c + 1])

    pre_normed = [None]

    def modnorm(l, s):
        if pre_normed[0] == (l, s):
            pre_normed[0] = None
            return
        for tt in range(NTT):
            modnorm_tile(l, s, tt)

    GROUPS = [(0, 4), (4, 8), (8, 12), (12, 16), (16, 19), (19, 22)]

    def ffn(l, i, after_tile=None):
        s = 0 if i == 0 else 2
        modnorm(l, s)
        wgu = w_gu[l, i].rearrange("(kc p) n -> p kc n", p=128)
        wdn = w_dn[l, i].rearrange("(g p) n -> p g n", p=128)

        def load(g):
            j0, j1 = GROUPS[g]
            G = j1 - j0
            base = FW + (g % 2) * 24576
            wg = ar.view(base, [128, KC, 512], BF16)
            wu = ar.view(base + 8192, [128, KC, 512], BF16)
            wd = ar.view(base + 16384, [128, 4, 1024], BF16)
            P.dma(wg[:, :, 0:G * 128], wgu[:, :, j0 * 128:j1 * 128], q="pool")
            P.dma(wu[:, :, 0:G * 128], wgu[:, :, DFF + j0 * 128:DFF + j1 * 128], q="pool")
            P.dma(wd[:, 0:G, :], wdn[:, j0:j1, :], q="pool")
            return wg, wu, wd

        pair = [0]

        def gu(g, W):
            j0, j1 = GROUPS[g]
            wg, wu, _ = W
            ab = ar.view(ACTB + (g % 2) * 16384, [128, 4, NTOK], BF16)
            for jj in range(j1 - j0):
                for tt in range(NTT):
                    pb = (pair[0] % 3) * 2
                    pair[0] += 1
                    rhs = None
                    for kc in range(KC):
                        P.mm(bank(pb), wg[:, kc, jj * 128:(jj + 1) * 128], hT[:, kc, tt * TT:(tt + 1) * TT],
                             start=(kc == 0), stop=(kc == KC - 1))
                    for kc in range(KC):
                        P.mm(bank(pb + 1), wu[:, kc, jj * 128:(jj + 1) * 128], hT[:, kc, tt * TT:(tt + 1) * TT],
                             start=(kc == 0), stop=(kc == KC - 1))
                    sg = ar.view(SG + (pair[0] % 2) * 2048, [128, TT], F32)
                    P.act(sg, bank(pb), AF.Silu)
                    P.tt(ab[:, jj, tt * TT:(tt + 1) * TT], sg, bank(pb + 1), ALU.mult)
                    ada_tick()

        ycnt = [0]

        def down(g, W, tile_major=False):
            j0, j1 = GROUPS[g]
            _, _, wd = W
            ab = ar.view(ACTB + (g % 2) * 16384, [128, 4, NTOK], BF16)
            order = [(n, tt) for n in range(KC) for tt in range(NTT)]
            if tile_major:
                order = [(n, tt) for tt in range(NTT) for n in range(KC)]
            for (n, tt) in order:
                if True:
                    c = 0 if tt < 2 else 1
                    yb = 6 + (ycnt[0] % 2)
                    ycnt[0] += 1
                    for jj in range(j1 - j0):
                        P.mm(bank(yb), wd[:, jj, n * 128:(n + 1) * 128], ab[:, jj, tt * TT:(tt + 1) * TT],
                             start=(jj == 0), stop=(jj == j1 - j0 - 1))
                    xs = xT[:, n, tt * TT:(tt + 1) * TT]
                    P.stt(xs, bank(yb), modG[:, l, s, n, c:c + 1], xs, ALU.mult, ALU.add)
                    if tile_major and n == KC - 1 and after_tile is not None:
                        after_tile(tt)

        W = {}
        W[0] = load(0)
        W[1] = load(1)
        gu(0, W[0])
        for g in range(len(GROUPS)):
            if g + 1 < len(GROUPS):
                gu(g + 1, W[g + 1])
            last = (g == len(GROUPS) - 1)
            if last:
                while ada_pending:
                    ada_compute()
            down(g, W[g], tile_major=last)
            if g + 2 < len(GROUPS):
                W[g + 2] = load(g + 2)
        while ada_pending:
            ada_compute()
        if (l, i) == (0, 1):
            ada_flush()

    def final_tile(tt):
        fg = ptab_sb[:, PT_FINALG:PT_FINALG + 8]
        YS = ACTB
        rs = rms_tile(tt)
        for i in range(4 * tt, 4 * tt + 4):
            yt = ar.view(YS + (i % 2) * 4096, [128, KC, 128], F32)
            for kc in range(KC):
                P.stt(yt[:, kc, :], xT[:, kc, i * 128:(i + 1) * 128], fg[:, kc:kc + 1],
                      rs[:, (i % 4) * 128:(i % 4 + 1) * 128], ALU.mult, ALU.mult)
            st = ar.view(YS + 8192 + (i % 2) * 4096, [128, D], F32)
            for half in range(2):
                b = (i * 2 + half) % 4
                for kk in range(4):
                    kc = half * 4 + kk
                    P.tr(ps[:, b, kk * 128:(kk + 1) * 128], yt[:, kc, :], identF)
                if half == 0:
                    P.copy(st[:, 0:512], bank(b), eng="dve")
                else:
                    P.copy(st[:, 512:1024], bank(b), eng="act")
            P.dma(yout[i * 128:(i + 1) * 128, :], st, q="sp", is_output=True)

    def final_out():
        for tt in range(NTT):
            final_tile(tt)

    NEG = -30000.0
    Uf = cst_sb[:, C_UF:C_UF + 128]
    Ub = cst_sb[:, C_UB:C_UB + 128]
    sel127 = cst_sb[:, C_SEL127:C_SEL127 + 128]
    sel0 = cst_sb[:, C_SEL0:C_SEL0 + 128]
    onesF = cst_sb[:, C_ONES:C_ONES + 128]
    offdiag = cst_sb[:, C_OFFD:C_OFFD + 128]
    MLf = cst_sb[:, C_ML:C_ML + 128]
    MUf = cst_sb[:, C_MU:C_MU + 128]
    Rm = cst_sb[:, C_RM:C_RM + 128]
    MLb = cstb[:, 256:384]
    MUb = cstb[:, 384:512]
    P.dma(cstb[:, 512:512 + 9 * 128], bmask[:, :], q="pool")
    P.copy(MLb, MLf, eng="dve")
    P.copy(MUb, MUf, eng="dve")

    def bc_h(m):
        return m.unsqueeze(1).broadcast_to([128, 4, 128])

    def bc_i(v):
        return v.unsqueeze(2).broadcast_to([128, 4, 128])

    def b4(b):
        return ps[:, b, :].rearrange("p (h t) -> p h t", h=4, t=128)

    def bbf(b):
        return ps[:, b, :].bitcast(BF16)

    nbc = [0]

    def nb():
        b = nbc[0] % 8
        nbc[0] += 1
        return b

    def nbk(k):
        c = (nbc[0] + k - 1) // k * k
        nbc[0] = c + k
        return c % 8

    def dump(name, ap, shape, dt):
        d = dout(name, shape, dt)
        P.dma(d, ap, q="sp", is_output=True)
        dbg[name] = (shape, dt)

    QN, KN, VS, GS = 32768, 40960, 49152, 57344
    QPL, QRO, KFM, VTM, KCT, VC = 65536, 73728, 81920, 86016, 90112, 92160
    WP = 94208
    SCR = 110592
    OACC = 65536
    TST = 81920
    SCN = 98304
    SHR = 114688
    STF = 120832
    STB = 124928

    def mixer_even():
        l = 0
        modnorm(l, 1)
        w_in = even_w_in[0].rearrange("(kc p) n -> p kc n", p=128)
        w_out = even_w_out[0].rearrange("(kc p) n -> p kc n", p=128)
        cw = ptab_sb[:, PT_CONV:PT_CONV + 60].rearrange("p (c j) -> p c j", c=12, j=5)
        sink_bc = ptab_sb[:, PT_SINK:PT_SINK + 4]
        wpc = [0]

        def load_piece(c0, c1):
            wp = ar.view(WP + (wpc[0] % 2) * 8192, [128, KC, 512], BF16)
            wpc[0] += 1
            P.dma(wp[:, :, 0:c1 - c0], w_in[:, :, c0:c1], q="pool")
            return wp

        for grp in range(2):
            T0 = grp * 1024
            nseq, L = (4, 256) if grp == 0 else (1, 1024)
            qn = ar.view(QN, [128, 4, 1024], BF16)
            kn = ar.view(KN, [128, 4, 1024], BF16)
            vS = ar.view(VS, [128, 4, 1024], BF16)
            gS = ar.view(GS, [128, 4, 1024], BF16)
            qpl = ar.view(QPL, [128, 4, 1024], BF16)
            qro = ar.view(QRO, [128, 4, 1024], BF16)
            kfm = ar.view(KFM, [128, 2, 1024], BF16)
            vtm = ar.view(VTM, [128, 8, 256], BF16)
            kcT = ar.view(KCT, [128, 2, 512], BF16)
            vc = ar.view(VC, [128, 4, 256], BF16)
            mixT = hT[:, :, T0:T0 + 1024]

            def proj_fm(wp, cc):
                b = nbk(2)
                for tt in range(2):
                    for kc in range(KC):
                        P.mm(bank(b + tt), wp[:, kc, cc * 128:(cc + 1) * 128],
                             hT[:, kc, T0 + tt * TT:T0 + (tt + 1) * TT], start=(kc == 0), stop=(kc == KC - 1))
                return ps[:, b:b + 2, :].rearrange("p a t -> p (a t)")

            SCALE = 128.0 ** -0.5
            if grp == 1:
                ropeT = ar.view(SCR, [128, 2, 1024], F32)
                P.dma(ropeT, rope[:, :, :], q="sp")
                kst = ar.view(SCR + 8192, [128, 4, 256], BF16)
                P.dma(kst, cache_k.rearrange("(kt p) g d -> p kt (g d)", p=128), q="pool")
                P.dma(vc, cache_v.rearrange("(kt p) g d -> p kt (g d)", p=128), q="pool")
                for g in range(2):
                    b = nb()
                    for kt in range(4):
                        P.tr(bbf(b)[:, kt * 128:(kt + 1) * 128], kst[:, kt, g * 128:(g + 1) * 128], identB)
                    P.copy(kcT[:, g, :], bbf(b)[:, 0:512], eng="dve")

            def rope_apply(dst_bf, xf):
                t1 = ar.view(SCR + 12288, [128, 1024], F32)
                for tt in range(2):
                    b = nb()
                    P.mm(bank(b), Rm, xf[:, tt * TT:(tt + 1) * TT])
                    P.tt(t1[:, tt * TT:(tt + 1) * TT], bank(b), ropeT[:, 1, tt * TT:(tt + 1) * TT], ALU.mult)
                P.tt(xf, xf, ropeT[:, 0, :], ALU.mult, eng="pool")
                P.tt(dst_bf, t1, xf, ALU.add)

            wp = load_piece(2064, 2576)
            for h in range(4):
                pp = proj_fm(wp, h)
                P.act(qpl[:, h, :], pp, AF.Copy, scale=SCALE)
                if grp == 1:
                    xf = ar.view(SCR + 8192, [128, 1024], F32)
                    P.ts(xf, pp, SCALE, ALU.mult)
                    rope_apply(qro[:, h, :], xf)
            if "stop_ip1" in debug_names:
                return
            wp = load_piece(2576, 3088)
            for g in range(2):
                pp = proj_fm(wp, g)
                if grp == 0:
                    P.act(kfm[:, g, :], pp, AF.Copy)
                else:
                    xf = ar.view(SCR + 8192, [128, 1024], F32)
                    P.act(xf, pp, AF.Copy)
                    rope_apply(kfm[:, g, :], xf)
            if "stop_ip1b" in debug_names:
                return
            for i in range(8):
                b = nb()
                for kc in range(KC):
                    P.mm(bank(b), hT[:, kc, T0 + i * 128:T0 + (i + 1) * 128], wp[:, kc, 0:512],
                         start=(kc == 0), stop=(kc == KC - 1))
                if grp == 1:
                    P.act(vtm[:, i, :], ps[:, b, 256:512], AF.Copy)
                else:
                    st = ar.view(SCR + (i % 2) * 2048, [128, 512], F32)
                    P.copy(st, bank(b), eng="dve")
                    P.act(vtm[:, i, :], st[:, 256:512], AF.Copy)
                    P.dma(nck[i * 128:(i + 1) * 128, :], st[:, 0:256], q="sp", is_output=True)
                    P.dma(ncv[i * 128:(i + 1) * 128, :], st[:, 256:512], q="act", is_output=True)
            if "stop_ip2" in debug_names:
                return
            wpab = load_piece(2048, 2064)
            abT = gates[:, 0, :, :]
            for i in range(8):
                b = nb()
                for kc in range(KC):
                    P.mm(ps[:, b, 0:16], hT[:, kc, T0 + i * 128:T0 + (i + 1) * 128], wpab[:, kc, 0:16],
                         start=(kc == 0), stop=(kc == KC - 1))
                P.copy(abT[:, i, :], ps[:, b, 0:16], eng="dve")

            if "stop_ip3" in debug_names:
                return
            Lp = L + 4
            xpad = [ar.view(SCR, [128, nseq, Lp], F32) for k in range(2)]
            acc = ar.view(SCR + 4160, [128, nseq, L], F32)
            qs = ar.view(SCR + 8256, [128, 1024], F32)
            P.memset(ar.view(SCR, [128, 1040], F32), 0.0, eng="pool")
            for pc in range(3):
                wp = load_piece(pc * 512, (pc + 1) * 512)
                for hh in range(4):
                    c = pc * 4 + hh
                    pp = proj_fm(wp, hh)
                    xp = xpad[c % 2]
                    P.act(xp[:, :, 2:2 + L], pp.rearrange("p (s t) -> p s t", s=nseq, t=L), AF.Copy)
                    P.ts(acc, xp[:, :, 0:L], cw[:, c, 0:1], ALU.mult)
                    for j in range(1, 5):
                        P.stt(acc, xp[:, :, j:j + L], cw[:, c, j:j + 1], acc, ALU.mult, ALU.add)
                    accf = acc.rearrange("p s t -> p (s t)")
                    if pc == 2:
                        P.act(vS[:, hh, :], accf, AF.Silu)
                    else:
                        P.act(qs, accf, AF.Silu)
                        dst = (qn if pc == 0 else kn)[:, hh, :]
                        for tt in range(2):
                            sq = ar.view(SCR + 12352 + (tt % 2) * 1024, [128, TT], BF16)
                            P.act(sq, qs[:, tt * TT:(tt + 1) * TT], AF.Square)
                            b = nb()
                            P.mm(bank(b), onesP, sq)
                            rs = rstd[:, tt % 2, :]
                            P.act(rs, bank(b), AF.Ln, bias=epsT[:, 0:1])
                            P.act(rs, rs, AF.Exp, scale=-0.5)
                            P.stt(dst[:, tt * TT:(tt + 1) * TT], qs[:, tt * TT:(tt + 1) * TT],
                                  SCALE if pc == 0 else 1.0, rs, ALU.mult, ALU.mult)
            wp = load_piece(1536, 2048)
            for hh in range(4):
                pp = proj_fm(wp, hh)
                P.act(gS[:, hh, :], pp, AF.Silu)
            if "inproj" in debug_names and grp == DBG_GRP:
                dump("d_qn", qn, [128, 4, 1024], BF16)
                dump("d_kn", kn, [128, 4, 1024], BF16)
                dump("d_vS", vS, [128, 4, 1024], BF16)
                dump("d_gS", gS, [128, 4, 1024], BF16)
                dump("d_qpl", qpl, [128, 4, 1024], BF16)
                dump("d_qro", qro, [128, 4, 1024], BF16)
                dump("d_kfm", kfm, [128, 2, 1024], BF16)
                dump("d_vtm", vtm, [128, 8, 256], BF16)
                dump("d_ab", abT, [128, 8, 16], F32)

            if "stop_inproj" in debug_names:
                return
            ATT = WP
            if grp == 0:
                for s_ in range(4):
                    for qb in range(2):
                        tq = s_ * 256 + qb * 128
                        Pb = ar.view(ATT + ((s_ * 2 + qb) % 2) * 2048, [128, 4, 256], BF16)
                        PTs = ar.view(ATT + 4096 + ((s_ * 2 + qb) % 2) * 2048, [128, 8, 128], BF16)
                        stt_ = ar.view(ATT + 8192 + ((s_ * 2 + qb) % 2) * 256, [128, 16], F32)
                        on = ar.view(ATT + 8704 + ((s_ * 2 + qb) % 2) * 1024, [128, 4, 128], BF16)
                        bS = nbk(2)
                        P.memset(stt_, 0.0, eng="pool")
                        for h in range(4):
                            P.mm(ps[:, bS + h // 2, (h % 2) * 256:(h % 2 + 1) * 256],
                                 qpl[:, h, tq:tq + 128], kfm[:, h // 2, s_ * 256:(s_ + 1) * 256])
                        S4 = ps[:, bS:bS + 2, :].rearrange("p a (h k) -> p (a h) k", h=2, k=256)
                        mx = stt_[:, 0:4]
                        negm = stt_[:, 4:8]
                        rsum = stt_[:, 8:12]
                        es = stt_[:, 12:16]
                        P.reduce(mx, S4, ALU.max)
                        P.tt(mx, mx, sink_bc, ALU.max)
                        P.ts(negm, mx, -1.0, ALU.mult)
                        for h in range(4):
                            P.act(Pb[:, h, :], S4[:, h, :], AF.Exp, bias=negm[:, h:h + 1], accum=rsum[:, h:h + 1])
                        P.tt(es, sink_bc, negm, ALU.add)
                        P.act(es, es, AF.Exp)
                        P.tt(rsum, rsum, es, ALU.add)
                        P.recip(rsum, rsum)
                        bT = nb()
                        for h in range(4):
                            for kt in range(2):
                                P.tr(bbf(bT)[:, (h * 2 + kt) * 128:(h * 2 + kt + 1) * 128],
                                     Pb[:, h, kt * 128:(kt + 1) * 128], identB)
                        P.copy(PTs.rearrange("p a t -> p (a t)"), bbf(bT)[:, 0:1024], eng="act")
                        bO = nb()
                        for h in range(4):
                            for kt in range(2):
                                P.mm(ps[:, bO, h * 128:(h + 1) * 128], PTs[:, h * 2 + kt, :],
                                     vtm[:, s_ * 2 + kt, (h // 2) * 128:(h // 2 + 1) * 128],
                                     start=(kt == 0), stop=(kt == 1))
                        P.tt(on, b4(bO), bc_i(rsum), ALU.mult)
                        bT2 = nb()
                        for h in range(4):
                            P.tr(bbf(bT2)[:, h * 128:(h + 1) * 128], on[:, h, :], identB)
                        P.copy(mixT[:, 4:8, tq:tq + 128],
                               bbf(bT2)[:, 0:512].rearrange("p (h t) -> p h t", h=4, t=128), eng="dve")
            else:
                for qb in range(8):
                    tq = qb * 128
                    blks = [k for k in (qb - 1, qb, qb + 1) if 0 <= k < 8]
                    nl = len(blks)
                    W = 512 + nl * 128
                    on = ar.view(ATT + 16384 + (qb % 2) * 1024, [128, 4, 128], BF16)
                    for hp in range(2):
                        it = qb * 2 + hp
                        Pb = ar.view(ATT + (it % 2) * 4096, [128, 2, 1024], BF16)
                        PTs = ar.view(ATT + 8192 + (it % 2) * 4096, [128, 2, 1024], BF16)
                        stt_ = ar.view(ATT + 18432 + (it % 2) * 256, [128, 16], F32)
                        mx = stt_[:, 0:2]
                        negm = stt_[:, 2:4]
                        rsum = stt_[:, 4:6]
                        es = stt_[:, 6:8]
                        bS = nbk(4)
                        P.memset(stt_, 0.0, eng="pool")
                        for hh in range(2):
                            h = hp * 2 + hh
                            g = hp
                            P.mm(bank(bS + 2 * hh), qpl[:, h, tq:tq + 128], kcT[:, g, :])
                            k0 = blks[0] * 128
                            has_mask = (blks[0] == qb - 1) or (blks[-1] == qb + 1)
                            P.mm(ps[:, bS + 2 * hh + 1, 0:nl * 128], qro[:, h, tq:tq + 128],
                                 kfm[:, g, k0:k0 + nl * 128], start=True, stop=not has_mask)
                            nm = (1 if blks[0] == qb - 1 else 0) + (1 if blks[-1] == qb + 1 else 0)
                            cnt = 0
                            for bi, k in enumerate(blks):
                                if k == qb - 1 or k == qb + 1:
                                    cnt += 1
                                    P.mm(ps[:, bS + 2 * hh + 1, bi * 128:(bi + 1) * 128], identB,
                                         MUb if k == qb - 1 else MLb, start=False, stop=(cnt == nm))
                        S2 = ps[:, bS:bS + 4, :].rearrange("p (h a) t -> p h (a t)", h=2, a=2)[:, :, 0:W]
                        P.reduce(mx, S2, ALU.max)
                        P.tt(mx, mx, sink_bc[:, hp * 2:hp * 2 + 2], ALU.max)
                        P.ts(negm, mx, -1.0, ALU.mult)
                        for hh in range(2):
                            P.act(Pb[:, hh, 0:W], S2[:, hh, :], AF.Exp, bias=negm[:, hh:hh + 1],
                                  accum=rsum[:, hh:hh + 1])
                        P.tt(es, sink_bc[:, hp * 2:hp * 2 + 2], negm, ALU.add)
                        P.act(es, es, AF.Exp)
                        P.tt(rsum, rsum, es, ALU.add)
                        P.recip(rsum, rsum)
                        nblk = 4 + nl
                        for hh in range(2):
                            bT = nb()
                            for bi in range(nblk):
                                P.tr(bbf(bT)[:, bi * 128:(bi + 1) * 128], Pb[:, hh, bi * 128:(bi + 1) * 128], identB)
                            P.copy(PTs[:, hh, 0:W], bbf(bT)[:, 0:W], eng="act" if hh == 0 else "dve")
                        bO = nb()
                        for hh in range(2):
                            g = hp
                            for bi in range(nblk):
                                if bi < 4:
                                    rhs = vc[:, bi, g * 128:(g + 1) * 128]
                                else:
                                    rhs = vtm[:, blks[bi - 4], g * 128:(g + 1) * 128]
                                P.mm(ps[:, bO, hh * 128:(hh + 1) * 128], PTs[:, hh, bi * 128:(bi + 1) * 128], rhs,
                                     start=(bi == 0), stop=(bi == nblk - 1))
                        P.tt(on[:, hp * 2:hp * 2 + 2, :],
                             ps[:, bO, 0:256].rearrange("p (h t) -> p h t", h=2, t=128),
                             rsum.unsqueeze(2).broadcast_to([128, 2, 128]), ALU.mult)
                    bT2 = nb()
                    for h in range(4):
                        P.tr(bbf(bT2)[:, h * 128:(h + 1) * 128], on[:, h, :], identB)
                    P.copy(mixT[:, 4:8, tq:tq + 128],
                           bbf(bT2)[:, 0:512].rearrange("p (h t) -> p h t", h=4, t=128), eng="dve")
            if "attn" in debug_names and grp == DBG_GRP:
                dump("d_oatt", mixT[:, 4:8, :], [128, 4, 1024], BF16)

            if "stop_attn" in debug_names:
                return
            delta_net(grp, T0, nseq, L, qn, kn, vS, gS, mixT)
            if "delta" in debug_names and grp == DBG_GRP:
                dump("d_oa", mixT[:, 0:4, :], [128, 4, 1024], BF16)

            if "stop_delta" in debug_names:
                return
            wo = ar.view(WP, [128, KC, 1024], BF16)
            P.dma(wo, w_out[:, :, :], q="pool")
            for n in range(KC):
                for tt in range(2):
                    b = nb()
                    for kc in range(KC):
                        P.mm(bank(b), wo[:, kc, n * 128:(n + 1) * 128], mixT[:, kc, tt * TT:(tt + 1) * TT],
                             start=(kc == 0), stop=(kc == KC - 1))
                    xs = xT[:, n, T0 + tt * TT:T0 + (tt + 1) * TT]
                    P.stt(xs, bank(b), modG[:, l, 1, n, grp:grp + 1], xs, ALU.mult, ALU.add)

    def delta_net(grp, T0, nseq, L, qn, kn, vS, gS, mixT):
        abT = gates[:, 0, :, :]
        gT = gates[:, 1, :, 0:8]
        beta = gates[:, 2, :, 0:8]
        gc = gates[:, 3, :, 0:8]
        glb = gates[:, 4, :, 0:8]
        egc = gates[:, 5, :, 0:8]
        kdf = gates[:, 6, :, 0:8]
        glast = gates[:, 7, :, 0:8]
        bege = gates[:, 1, :, 8:16]
        tmpA = gates[:, 2, :, 8:16]
        tmpB = gates[:, 3, :, 8:16]
        alog_bc = ptab_sb[:, PT_ALOG:PT_ALOG + 8]
        dtb_bc = ptab_sb[:, PT_DTB:PT_DTB + 8]
        onorm = ptab_sb[:, PT_ONORME:PT_ONORME + 1]
        bc8 = lambda v: v.unsqueeze(1).broadcast_to([128, 8, 8])
        P.act(beta, abT[:, :, 0:8], AF.Exp, scale=-1.0)
        P.ts(beta, beta, 1.0, ALU.add)
        P.recip(beta, beta)
        P.tt(tmpA, abT[:, :, 8:16], bc8(dtb_bc), ALU.add)
        P.act(tmpB, tmpA, AF.Abs)
        P.act(tmpB, tmpB, AF.Exp, scale=-1.0)
        P.act(tmpB, tmpB, AF.Ln, bias=onesF[:, 0:1])
        P.ts(tmpA, tmpA, 0.0, ALU.max)
        P.tt(tmpA, tmpA, tmpB, ALU.add)
        P.act(tmpB[:, 0, :], alog_bc, AF.Exp)
        P.stt(gT, tmpA, -1.0, bc8(tmpB[:, 0, :]), ALU.mult, ALU.mult)
        g64 = gates[:, 1, :, :].rearrange("p a b -> p (a b)")
        b = nb()
        P.mm(ps[:, b, 0:128], Uf, g64)
        P.mm(ps[:, b, 128:256], Ub, g64)
        pv = ps[:, b, 0:256].rearrange("p (d a c) -> p d a c", d=2, a=8, c=16)
        P.copy(gc[:, :, 0:4], pv[:, 0, :, 0:4], eng="dve")
        P.copy(gc[:, :, 4:8], pv[:, 1, :, 4:8], eng="dve")
        gc64 = gates[:, 3, :, :].rearrange("p a b -> p (a b)")
        b = nb()
        P.mm(ps[:, b, 0:128], sel127, gc64)
        P.mm(ps[:, b, 128:256], sel0, gc64)
        pv = ps[:, b, 0:256].rearrange("p (d a c) -> p d a c", d=2, a=8, c=16)
        P.copy(glb[:, :, 0:4], pv[:, 0, :, 0:4], eng="dve")
        P.copy(glb[:, :, 4:8], pv[:, 1, :, 4:8], eng="dve")
        P.act(egc, gc, AF.Exp)
        P.tt(kdf, glb, gc, ALU.subtract)
        P.act(kdf, kdf, AF.Exp)
        P.act(glast, glb, AF.Exp)
        P.tt(bege, beta, egc, ALU.mult)

        if "gates" in debug_names and grp == DBG_GRP:
            dump("d_g", gT, [128, 8, 8], F32)
            dump("d_beta", beta, [128, 8, 8], F32)
            dump("d_gc", gc, [128, 8, 8], F32)
            dump("d_glb", glb, [128, 8, 8], F32)
        DB = 65536
        TSTS = [DB, DB + 12288]
        SCN2 = DB + 24576
        SHR2 = SCN2 + 16384
        STF2 = SHR2 + 8192
        STB2 = STF2 + 4096
        OACCA = STB2 + 2048
        assert OACCA + 4096 <= ar.nbytes
        Sf = [ar.view(STF2 + d * 2048, [128, 4, 128], F32) for d in range(2)]
        Sb = [ar.view(STB2 + d * 1024, [128, 4, 128], BF16) for d in range(2)]
        nch = L // 128
        if grp == 0:
            oaccA = ar.view(OACCA, [128, 4, 256], F32)
        else:
            oaccB = ar.t[:, 0:8192].rearrange("p (k t) -> p k t", k=8, t=1024)[:, :, 0:512].rearrange(
                "p (h a) t -> p h a t", h=4, a=2)

        def oslice(c):
            if grp == 0:
                return oaccA[:, :, c * 128:(c + 1) * 128]
            t0_ = c * 128
            return oaccB[:, :, t0_ // 512, t0_ % 512:t0_ % 512 + 128]

        def tv(off, dt):
            return ar.view(off, [128, 4, 128], dt)

        def mask_b(k_):
            return cstb[:, 512 + k_ * 128: 512 + (k_ + 1) * 128]

        def mm4(lh, rh):
            bb = nb()
            for h in range(4):
                P.mm(ps[:, bb, h * 128:(h + 1) * 128], lh[:, h, :], rh[:, h, :])
            return bb

        def step(s_, c, d, first_touch):
            T_ = TSTS[d]
            dec, decT = tv(T_, F32), tv(T_ + 2048, F32)
            G1, G2 = dec, decT
            Lb, LTb = tv(T_ + 4096, BF16), tv(T_ + 5120, BF16)
            Ab, Bb = tv(T_ + 6144, BF16), tv(T_ + 7168, BF16)
            Cb, Db, Eb, Tb = [tv(T_ + 8192 + 1024 * k_, BF16) for k_ in range(4)]
            B2, B3 = Cb, Db
            S_ = SCN2 + d * 8192
            Xb, qkT, QdT, Vb_, Kbe, kdec, negwT, vnew = [tv(S_ + 1024 * k_, BF16) for k_ in range(8)]
            H_ = SHR2 + d * 4096
            Ktm, Vtm, KKs, KQs = [tv(H_ + 1024 * k_, BF16) for k_ in range(4)]
            ti = s_ * nch + c
            tsl = slice(ti * 128, (ti + 1) * 128)
            dsl = slice(d * 4, d * 4 + 4)
            bK = nb()
            for h in range(4):
                P.tr(bbf(bK)[:, h * 128:(h + 1) * 128], kn[:, h, tsl], identB)
            P.copy(Ktm.rearrange("p h t -> p (h t)"), bbf(bK)[:, 0:512], eng="act")
            bV = nb()
            for h in range(4):
                P.tr(bbf(bV)[:, h * 128:(h + 1) * 128], vS[:, h, tsl], identB)
            P.copy(Vtm.rearrange("p h t -> p (h t)"), bbf(bV)[:, 0:512], eng="act")
            yield
            bKK = nb()
            for h in range(4):
                P.mm(ps[:, bKK, h * 128:(h + 1) * 128], kn[:, h, tsl], kn[:, h, tsl])
            P.tt(KKs, b4(bKK), bc_h(offdiag), ALU.mult)
            bKQ = nb()
            for h in range(4):
                P.mm(ps[:, bKQ, h * 128:(h + 1) * 128], kn[:, h, tsl], qn[:, h, tsl])
            P.copy(KQs.rearrange("p h t -> p (h t)"), bank(bKQ), eng="act")
            yield
            U_ = Uf if d == 0 else Ub
            Mdec, MdecT = (MLf, MUf) if d == 0 else (MUf, MLf)
            P.copy(G1, bc_i(gT[:, ti, dsl]), eng="pool")
            P.stt(G2, G1, -1.0, bc_h(U_), ALU.mult, ALU.mult)
            bD = nb()
            P.mm(bank(bD), U_, G1.rearrange("p h t -> p (h t)"), start=True, stop=False)
            P.mm(bank(bD), onesF, G2.rearrange("p h t -> p (h t)"), start=False, stop=True)
            yield
            P.tt(dec, b4(bD), bc_h(Mdec), ALU.add)
            P.stt(decT, b4(bD), -1.0, bc_h(MdecT), ALU.mult, ALU.add)
            P.act(dec, dec, AF.Exp)
            P.act(decT, decT, AF.Exp)
            P.tt(B2, bc_h(identB), bc_i(beta[:, ti, dsl]), ALU.mult, eng="pool")
            P.tt(B3, bc_h(identB), bc_i(egc[:, ti, dsl]), ALU.mult, eng="pool")
            bR = nb()
            P.mm(bank(bR), onesP, B2.rearrange("p h t -> p (h t)"))
            bE = nb()
            P.mm(bank(bE), onesP, B3.rearrange("p h t -> p (h t)"))
            yield
            P.tt(Lb, KKs, dec, ALU.mult)
            P.tt(Lb, Lb, bc_i(beta[:, ti, dsl]), ALU.mult)
            P.tt(LTb, KKs, decT, ALU.mult, eng="pool")
            P.tt(LTb, LTb, b4(bR), ALU.mult)
            P.tt(qkT, KQs, decT, ALU.mult, eng="pool")
            P.tt(QdT, qn[:, :, tsl], b4(bE), ALU.mult)
            P.tt(Vb_, Vtm, bc_i(beta[:, ti, dsl]), ALU.mult, eng="pool")
            P.tt(Kbe, Ktm, bc_i(bege[:, ti, dsl]), ALU.mult, eng="pool")
            P.tt(kdec, Ktm, bc_i(kdf[:, ti, dsl]), ALU.mult, eng="pool")
            yield
            mi = (lambda k: k) if d == 0 else (lambda k: (k + 4) if 1 <= k <= 4 else (k - 4 if k >= 5 else k))
            P.tt(Ab, Lb, bc_h(mask_b(0)), ALU.mult)
            P.tt(Bb, LTb, bc_h(mask_b(0)), ALU.mult)
            P.stt(Tb, Ab, -1.0, bc_h(identF), ALU.mult, ALU.add)
            P.stt(Xb, Bb, -1.0, bc_h(identF), ALU.mult, ALU.add)
            yield
            b1 = mm4(Bb, Ab)
            b2 = mm4(Ab, Bb)
            P.copy(Cb, b4(b1), eng="act")
            P.copy(Db, b4(b2), eng="act")
            yield
            bx = mm4(Cb, Xb)
            bt = mm4(Xb, Cb)
            b3 = mm4(Db, Cb)
            P.tt(Xb, Xb, b4(bx), ALU.add)
            P.tt(Tb, Tb, b4(bt), ALU.add)
            P.copy(Eb, b4(b3), eng="act")
            yield
            bx = mm4(Eb, Xb)
            bt = mm4(Xb, Eb)
            P.tt(Xb, Xb, b4(bx), ALU.add)
            P.tt(Tb, Tb, b4(bt), ALU.add)
            yield
            for lv in range(1, 5):
                last = (lv == 4)
                P.tt(Ab, Lb, bc_h(mask_b(mi(lv))), ALU.mult, eng="pool")
                if not last:
                    P.tt(Bb, LTb, bc_h(mask_b(mi(lv + 4))), ALU.mult)
                b1 = mm4(Ab, Xb)
                if not last:
                    b2 = mm4(Bb, Tb)
                P.copy(Cb, b4(b1), eng="act")
                if not last:
                    P.copy(Db, b4(b2), eng="act")
                yield
                bx = mm4(Tb, Cb)
                if not last:
                    bt = mm4(Xb, Db)
                P.tt(Xb, Xb, b4(bx), ALU.subtract)
                if not last:
                    P.tt(Tb, Tb, b4(bt), ALU.subtract)
                yield
            if "step0" in debug_names and grp == DBG_GRP and s_ == 0 and (c, d) == DBG_STEP:
                dump("d_dec", dec, [128, 4, 128], F32)
                dump("d_decT", decT, [128, 4, 128], F32)
                dump("d_X", Xb, [128, 4, 128], BF16)
                dump("d_qkT", qkT, [128, 4, 128], BF16)
                dump("d_QdT", QdT, [128, 4, 128], BF16)
                dump("d_KKs", KKs, [128, 4, 128], BF16)
            bW = mm4(Kbe, Xb)
            P.act(negwT.rearrange("p h t -> p (h t)"), bank(bW), AF.Copy, scale=-1.0)
            yield
            bVn = nb()
            for h in range(4):
                P.mm(ps[:, bVn, h * 128:(h + 1) * 128], Xb[:, h, :], Vb_[:, h, :], start=True, stop=False)
                P.mm(ps[:, bVn, h * 128:(h + 1) * 128], negwT[:, h, :], Sb[d][:, h, :], start=False, stop=True)
            P.copy(vnew.rearrange("p h t -> p (h t)"), bank(bVn), eng="act")
            yield
            bO = nb()
            for h in range(4):
                P.mm(ps[:, bO, h * 128:(h + 1) * 128], Sb[d][:, h, :], QdT[:, h, :], start=True, stop=False)
                P.mm(ps[:, bO, h * 128:(h + 1) * 128], vnew[:, h, :], qkT[:, h, :], start=False, stop=True)
            bS_ = mm4(kdec, vnew)
            oa = oslice(c if grp == 1 else c)
            if first_touch[c]:
                P.copy(oa, b4(bO), eng="dve")
                first_touch[c] = False
            else:
                P.tt(oa, oa, b4(bO), ALU.add)
            P.tt(Sf[d], Sf[d], bc_i(glast[:, ti, dsl]), ALU.mult)
            P.tt(Sf[d], Sf[d], b4(bS_), ALU.add)
            P.copy(Sb[d], Sf[d], eng="act")
            yield

        for s_ in range(nseq):
            for d in range(2):
                if grp == 0:
                    P.memset(Sf[d], 0.0, eng="pool")
                else:
                    P.dma(Sf[d], state_delta[d].rearrange("h k v -> k h v"), q="sp")
                P.copy(Sb[d], Sf[d], eng="pool")
            first_touch = [True] * nch
            for k in range(nch):
                gens = [step(s_, k, 0, first_touch), step(s_, nch - 1 - k, 1, first_touch)]
                while gens:
                    for g_ in list(gens):
                        try:
                            next(g_)
                        except StopIteration:
                            gens.remove(g_)
            if grp == 0:
                for d in range(2):
                    P.dma(nsd[s_, d].rearrange("h k v -> k h v"), Sf[d], q="sp", is_output=True)
            for h in range(4):
                for t_ in range(0, L, 512):
                    w = min(512, L - t_)
                    sl = slice(s_ * L + t_, s_ * L + t_ + w)
                    src = oaccA[:, h, 0:w] if grp == 0 else oaccB[:, h, t_ // 512, 0:512]
                    sq = ar.view(TSTS[0], [128, 512], BF16)
                    P.act(sq[:, 0:w], src, AF.Square)
                    b = nb()
                    P.mm(ps[:, b, 0:w], onesP, sq[:, 0:w])
                    rs = rstd[:, 0, 0:w]
                    P.act(rs, ps[:, b, 0:w], AF.Ln, bias=epsT[:, 0:1], scale=1.0 / 128.0)
                    P.act(rs, rs, AF.Exp, scale=-0.5)
                    t_f = ar.view(TSTS[0] + 2048, [128, 512], F32)
                    P.stt(t_f[:, 0:w], src, onorm[:, 0:1], rs, ALU.mult, ALU.mult)
                    P.tt(mixT[:, h, sl], t_f[:, 0:w], gS[:, h, sl], ALU.mult)

    def mixer_odd():
        l = 1
        modnorm(l, 1)
        w_in = odd_w_in[0].rearrange("(kc p) n -> p kc n", p=128)
        w_out = odd_w_out[0].rearrange("(kc p) n -> p kc n", p=128)
        onorm2 = ptab_sb[:, PT_ONORMO:PT_ONORMO + 2]
        QT_, KT_, VT_, GS_, LRT_, OACC2 = 32768, 40960, 49152, 65536, 81920, 86016
        WPO = 102400
        ZB_, ZE_ = 102400, 104448
        EG_, ENG_ = ZE_, ZB_
        QTL_, KTL_, QTT_, KTT_, ATB_ = 110592, 111616, 112640, 113664, 114688
        SF_, SB_, WG_ = 115712, 119808, 121856
        SC = 128.0 ** -0.5
        wgp = ar.view(WG_, [128, 2, 512], F32)
        P.dma(wgp[0:33, :, :], wgpad[:, :, :], q="sp")
        wpc = [0]

        def load_piece(c0, c1):
            wp = ar.view(WPO + (wpc[0] % 2) * 8192, [128, KC, 512], BF16)
            wpc[0] += 1
            P.dma(wp[:, :, 0:c1 - c0], w_in[:, :, c0:c1], q="pool")
            return wp

        qT = ar.view(QT_, [128, 8, 512], BF16)
        kT = ar.view(KT_, [128, 8, 512], BF16)
        vT = ar.view(VT_, [128, 8, 1024], BF16)
        gS = ar.view(GS_, [128, 8, 1024], BF16)
        lrT = ar.view(LRT_, [128, 1024], F32)
        glt = gates[:, 0, 0, 0:4]
        for grp in range(2):
            T0 = grp * 1024
            nseq, L = (4, 256) if grp == 0 else (1, 1024)
            nch = L // 128
            mixT = hT[:, :, T0:T0 + 1024]
            wp = load_piece(3072, 3104)
            for tt in range(2):
                b = nb()
                for kc in range(KC):
                    P.mm(ps[0:32, b, :], wp[:, kc, 0:32], hT[:, kc, T0 + tt * TT:T0 + (tt + 1) * TT],
                         start=(kc == 0), stop=(kc == KC - 1))
                P.copy(lrT[0:32, tt * TT:(tt + 1) * TT], ps[0:32, b, :], eng="dve")
            P.memset(lrT[32:33, :], 1.0, eng="dve")
            for (c0, dst, col0, scl) in ((0, qT, 0, SC), (512, kT, 0, 1.0), (1024, vT, 0, 1.0), (1536, vT, 512, 1.0)):
                wp = load_piece(c0, c0 + 512)
                for i in range(8):
                    b = nb()
                    for kc in range(KC):
                        P.mm(bank(b), hT[:, kc, T0 + i * 128:T0 + (i + 1) * 128], wp[:, kc, 0:512],
                             start=(kc == 0), stop=(kc == KC - 1))
                    if i % 2 == 0:
                        P.act(dst[:, i, col0:col0 + 512], bank(b), AF.Copy, scale=scl)
                    else:
                        P.ts(dst[:, i, col0:col0 + 512], bank(b), scl, ALU.mult)
            for pc in range(2):
                wp = load_piece(2048 + pc * 512, 2560 + pc * 512)
                for cc in range(4):
                    b = nbk(2)
                    for tt in range(2):
                        for kc in range(KC):
                            P.mm(bank(b + tt), wp[:, kc, cc * 128:(cc + 1) * 128],
                                 hT[:, kc, T0 + tt * TT:T0 + (tt + 1) * TT], start=(kc == 0), stop=(kc == KC - 1))
                    P.act(gS[:, pc * 4 + cc, :], ps[:, b:b + 2, :].rearrange("p a t -> p (a t)"), AF.Silu)
            if "oinproj" in debug_names and grp == DBG_GRP:
                dump("d_qT", qT, [128, 8, 512], BF16)
                dump("d_kT", kT, [128, 8, 512], BF16)
                dump("d_vT", vT, [128, 8, 1024], BF16)
                dump("d_gS", gS, [128, 8, 1024], BF16)
                dump("d_lrT", lrT[0:33, :], [33, 1024], F32)

            zb = ar.view(ZB_, [128, 512], F32)
            ze = ar.view(ZE_, [128, 512], F32)
            eg = ar.view(EG_, [128, 512], F32)
            eng_ = ar.view(ENG_, [128, 512], F32)
            qtl = ar.view(QTL_, [128, 512], BF16)
            ktl = ar.view(KTL_, [128, 512], BF16)
            qtt = ar.view(QTT_, [128, 4, 128], BF16)
            ktt = ar.view(KTT_, [128, 4, 128], BF16)
            atb = ar.view(ATB_, [128, 4, 128], BF16)
            Sf = ar.view(SF_, [128, 4, 256], F32)
            Sb = ar.view(SB_, [128, 4, 256], BF16)
            if grp == 1:
                oacc_lo = ar.t[:, 0:8192].rearrange("p (k t) -> p k t", k=8, t=1024)[:, :, 0:512]
                oacc_hi = ar.view(OACC2, [128, 8, 512], F32)
            else:
                oaccA = ar.view(OACC2, [128, 8, 256], F32)

            def oslice(c, fc0, fc1):
                if grp == 0:
                    return oaccA[:, fc0:fc1, c * 128:(c + 1) * 128]
                if c < 4:
                    return oacc_lo[:, fc0:fc1, c * 128:(c + 1) * 128]
                return oacc_hi[:, fc0:fc1, (c - 4) * 128:(c - 3) * 128]

            bufsets = []
            for k_ in range(2):
                if k_ == 0:
                    offs = (QTL_, KTL_, QTT_, KTT_, ATB_)
                else:
                    offs = (106496, 107520, 108544, 109568, 125952)
                bufsets.append((ar.view(offs[0], [128, 512], BF16), ar.view(offs[1], [128, 512], BF16),
                                ar.view(offs[2], [128, 4, 128], BF16), ar.view(offs[3], [128, 4, 128], BF16),
                                ar.view(offs[4], [128, 4, 128], BF16), gates[:, 0, k_, 0:4]))

            def prefix(s_, d, c, bs_):
                qtl, ktl, qtt, ktt, atb, glt = bs_
                U_ = Uf if d == 0 else Ub
                elast = identF[:, 127:128] if d == 0 else identF[:, 0:1]
                ti = s_ * nch + c
                tsl = slice(ti * 128, (ti + 1) * 128)
                bz = nb()
                P.mm(bank(bz), lrT[0:33, tsl], wgp[0:33, d, :])
                yield
                P.ts(ze, bank(bz), -80.0, ALU.max)
                P.act(ze, ze, AF.Exp, scale=-1.0)
                yield
                P.act(zb, ze, AF.Ln, bias=onesF[:, 0:1])
                yield
                bg = nb()
                P.mm(bank(bg), U_, zb)
                yield
                P.act(eg, bank(bg), AF.Exp, scale=-1.0 / 16.0)
                P.act(eng_, bank(bg), AF.Exp, scale=1.0 / 16.0)
                P.tt(qtl, qT[:, ti, :], eg, ALU.mult)
                P.tt(ktl, kT[:, ti, :], eng_, ALU.mult)
                yield
                bl = nb()
                for h in range(4):
                    P.mm(ps[:, bl, h:h + 1], eg[:, h * 128:(h + 1) * 128], elast)
                P.copy(glt, ps[:, bl, 0:4], eng="dve")
                bq = nb()
                for h in range(4):
                    P.tr(bbf(bq)[:, h * 128:(h + 1) * 128], qtl[:, h * 128:(h + 1) * 128], identB)
                P.copy(qtt.rearrange("p h t -> p (h t)"), bbf(bq)[:, 0:512], eng="act")
                bk = nb()
                for h in range(4):
                    P.tr(bbf(bk)[:, h * 128:(h + 1) * 128], ktl[:, h * 128:(h + 1) * 128], identB)
                P.copy(ktt.rearrange("p h t -> p (h t)"), bbf(bk)[:, 0:512], eng="dve")
                yield
                ba = nb()
                for h in range(4):
                    P.mm(ps[:, ba, h * 128:(h + 1) * 128], ktt[:, h, :], qtt[:, h, :])
                P.tt(atb, b4(ba), bc_h(U_), ALU.mult)
                yield

            def suffix(s_, d, c, bs_, first_touch):
                qtl, ktl, qtt, ktt, atb, glt = bs_
                ti = s_ * nch + c
                bo = nbk(2)
                for h in range(4):
                    for half in range(2):
                        fc = h * 2 + half
                        dstp = ps[:, bo + fc // 4, (fc % 4) * 128:(fc % 4 + 1) * 128]
                        P.mm(dstp, vT[:, ti, h * 256 + half * 128:h * 256 + (half + 1) * 128], atb[:, h, :],
                             start=True, stop=False)
                        P.mm(dstp, Sb[:, h, half * 128:(half + 1) * 128], qtt[:, h, :], start=False, stop=True)
                for k2 in range(2):
                    oa = oslice(c, k2 * 4, k2 * 4 + 4)
                    if first_touch[c]:
                        P.copy(oa, b4(bo + k2), eng="dve" if k2 == 0 else "act")
                    else:
                        P.tt(oa, oa, b4(bo + k2), ALU.add)
                first_touch[c] = False
                yield
                bs2 = nbk(2)
                for h in range(4):
                    P.mm(ps[:, bs2 + h // 2, (h % 2) * 256:(h % 2 + 1) * 256], ktl[:, h * 128:(h + 1) * 128],
                         vT[:, ti, h * 256:(h + 1) * 256])
                P.tt(Sf, Sf, ps[:, bs2:bs2 + 2, :].rearrange("p a (h v) -> p (a h) v", h=2, v=256), ALU.add)
                P.tt(Sf, Sf, glt.unsqueeze(2).broadcast_to([128, 4, 256]), ALU.mult)
                P.copy(Sb, Sf, eng="act")
                yield

            for s_ in range(nseq):
                first_touch = [True] * nch
                steps = [(d, (cidx if d == 0 else nch - 1 - cidx)) for d in range(2) for cidx in range(nch)]
                def drive(gens):
                    gens = [g_ for g_ in gens if g_ is not None]
                    while gens:
                        for g_ in list(gens):
                            try:
                                next(g_)
                            except StopIteration:
                                gens.remove(g_)

                drive([prefix(s_, steps[0][0], steps[0][1], bufsets[0])])
                for k_, (d, c) in enumerate(steps):
                    if k_ % nch == 0:
                        if grp == 0:
                            P.memset(Sf, 0.0, eng="pool")
                        else:
                            P.dma(Sf, state_gla[d].rearrange("h k v -> k h v"), q="sp")
                        P.copy(Sb, Sf, eng="pool")
                    nxt = None
                    if k_ + 1 < len(steps):
                        nxt = prefix(s_, steps[k_ + 1][0], steps[k_ + 1][1], bufsets[(k_ + 1) % 2])
                    drive([suffix(s_, d, c, bufsets[k_ % 2], first_touch), nxt])
                    if k_ % nch == nch - 1 and grp == 0:
                        P.dma(nsg[s_, d].rearrange("h k v -> k h v"), Sf, q="sp", is_output=True)
                for h in range(4):
                    for t_ in range(0, L, 512):
                        w = min(512, L - t_)
                        b = nb()
                        srcs = []
                        for half in range(2):
                            fc = h * 2 + half
                            if grp == 0:
                                src = oaccA[:, fc, t_:t_ + w]
                            else:
                                src = (oacc_lo if t_ == 0 else oacc_hi)[:, fc, 0:512]
                            srcs.append(src)
                            sq = ar.view(ZB_ + half * 1024, [128, 512], BF16)
                            P.act(sq[:, 0:w], src, AF.Square)
                            P.mm(ps[:, b, 0:w], onesP, sq[:, 0:w], start=(half == 0), stop=(half == 1))
                        rs = rstd[:, 0, 0:w]
                        P.act(rs, ps[:, b, 0:w], AF.Ln, bias=epsT[:, 0:1], scale=1.0 / 256.0)
                        P.act(rs, rs, AF.Exp, scale=-0.5)
                        for half in range(2):
                            fc = h * 2 + half
                            t_f = ar.view(EG_, [128, 512], F32)
                            P.stt(t_f[:, 0:w], srcs[half], onorm2[:, half:half + 1], rs, ALU.mult, ALU.mult)
                            sl = slice(s_ * L + t_, s_ * L + t_ + w)
                            P.tt(mixT[:, fc, sl], t_f[:, 0:w], gS[:, fc, sl], ALU.mult)
            if "gla" in debug_names and grp == DBG_GRP:
                dump("d_mix", mixT, [128, 8, 1024], BF16)
            wo = ar.view(WPO, [128, KC, 1024], BF16)
            P.dma(wo, w_out[:, :, :], q="pool")
            for n in range(KC):
                for tt in range(2):
                    b = nb()
                    for kc in range(KC):
                        P.mm(bank(b), wo[:, kc, n * 128:(n + 1) * 128], mixT[:, kc, tt * TT:(tt + 1) * TT],
                             start=(kc == 0), stop=(kc == KC - 1))
                    xs = xT[:, n, T0 + tt * TT:T0 + (tt + 1) * TT]
                    P.stt(xs, bank(b), modG[:, l, 1, n, grp:grp + 1], xs, ALU.mult, ALU.add)


    if stage != "all":
        ada_flush()
    if stage == "ffn0":
        ffn(0, 0)
        final_out()
    elif stage == "mix0":
        mixer_even()
        final_out()
    elif stage == "mix1":
        mixer_odd()
        final_out()
    else:
        def pre(l_, s_):
            def f(tt):
                modnorm_tile(l_, s_, tt)
                if tt == NTT - 1:
                    pre_normed[0] = (l_, s_)
            return f
        ffn(0, 0, after_tile=pre(0, 1))
        mixer_even()
        ffn(0, 1, after_tile=pre(1, 0))
        ffn(1, 0, after_tile=pre(1, 1))
        mixer_odd()
        ffn(1, 1, after_tile=final_tile)

    P.emit()
    stack.close()
    return nc, dbg


PT_ADAB = 0
PT_NORMG = PT_ADAB + 144
PT_FINALG = PT_NORMG + 48
PT_CONV = PT_FINALG + 8
PT_SINK = PT_CONV + 60
PT_ALOG = PT_SINK + 4
PT_DTB = PT_ALOG + 8
PT_ONORME = PT_DTB + 8
PT_ONORMO = PT_ONORME + 1
PT_COLS = PT_ONORMO + 2
DBG_GRP = 0
DBG_STEP = (0, 0)

C_IDENT = 0
C_UF, C_UB, C_SEL127, C_SEL0, C_ONES, C_OFFD, C_ML, C_MU, C_RM = [128 * i for i in range(1, 10)]
CST_COLS = 128 * 10
CSTB_COLS = 512 + 9 * 128


def _fm(v):
    v = np.asarray(v, np.float32)
    return np.ascontiguousarray(v.reshape(-1, 128).T)


def make_tables(inputs):
    pt = np.zeros((128, PT_COLS), np.float32)
    for l in range(2):
        pt[:, PT_ADAB + l * 72: PT_ADAB + (l + 1) * 72] = _fm(inputs["ada_b"][l])
        for s in range(3):
            o = PT_NORMG + (l * 3 + s) * 8
            pt[:, o:o + 8] = _fm(inputs["norm_g"][l, s])
    pt[:, PT_FINALG:PT_FINALG + 8] = _fm(inputs["final_g"])
    cv = np.asarray(inputs["even_conv"], np.float32)[0]
    pt[:, PT_CONV:PT_CONV + 60] = cv.T.reshape(12, 128, 5).transpose(1, 0, 2).reshape(128, 60)
    pt[:, PT_SINK:PT_SINK + 4] = np.broadcast_to(np.asarray(inputs["even_sink"], np.float32)[0][None, :], (128, 4))
    pt[:, PT_ALOG:PT_ALOG + 8] = np.broadcast_to(np.asarray(inputs["even_a_log"], np.float32)[0].reshape(1, 8), (128, 8))
    pt[:, PT_DTB:PT_DTB + 8] = np.broadcast_to(np.asarray(inputs["even_dt_bias"], np.float32)[0].reshape(1, 8), (128, 8))
    pt[:, PT_ONORME:PT_ONORME + 1] = np.asarray(inputs["even_onorm"], np.float32)[0].reshape(128, 1)
    pt[:, PT_ONORMO:PT_ONORMO + 2] = _fm(inputs["odd_onorm"][0])
    cst = np.zeros((128, CST_COLS), np.float32)
    cst[:, C_IDENT:C_IDENT + 128] = np.eye(128, dtype=np.float32)
    kk, ii = np.meshgrid(np.arange(128), np.arange(128), indexing="ij")
    NEG = -30000.0
    cst[:, C_UF:C_UF + 128] = (kk <= ii)
    cst[:, C_UB:C_UB + 128] = (kk >= ii)
    cst[127, C_SEL127:C_SEL127 + 128] = 1.0
    cst[0, C_SEL0:C_SEL0 + 128] = 1.0
    cst[:, C_ONES:C_ONES + 128] = 1.0
    cst[:, C_OFFD:C_OFFD + 128] = (kk != ii)
    cst[:, C_ML:C_ML + 128] = np.where(ii <= kk, 0.0, NEG)
    cst[:, C_MU:C_MU + 128] = np.where(ii >= kk, 0.0, NEG)
    rm = np.zeros((128, 128), np.float32)
    for dp in range(128):
        if (dp % 64) < 32:
            rm[dp + 32, dp] = -1.0
        else:
            rm[dp - 32, dp] = 1.0
    cst[:, C_RM:C_RM + 128] = rm
    return pt, cst


def make_bmask():
    i, j = np.meshgrid(np.arange(128), np.arange(128), indexing="ij")
    ms = [(i // 8 == j // 8)]
    for b in (8, 16, 32, 64):
        ms.append((i // (2 * b) == j // (2 * b)) & ((i // b) % 2 == 1) & ((j // b) % 2 == 0))
    for b in (8, 16, 32, 64):
        ms.append((i // (2 * b) == j // (2 * b)) & ((i // b) % 2 == 0) & ((j // b) % 2 == 1))
    return np.ascontiguousarray(np.concatenate([m.astype(np.float32) for m in ms], axis=1))


def make_rope():
    t = np.arange(1024)
    row = (t // 64).astype(np.float64)
    col = (t % 64).astype(np.float64)
    inv = 10000.0 ** (-np.arange(32, dtype=np.float64) / 32.0)
    ang = np.zeros((128, 1024))
    for d in range(128):
        pos = row if d < 64 else col
        ang[d] = pos * np.float32(inv[d % 32])
    ang32 = np.zeros((128, 1024), np.float32)
    inv32 = (np.float32(10000.0) ** (-np.arange(32, dtype=np.float32) / np.float32(32))).astype(np.float32)
    for d in range(128):
        pos = (row if d < 64 else col).astype(np.float32)
        ang32[d] = pos * inv32[d % 32]
    return np.ascontiguousarray(np.stack([np.cos(ang32), np.sin(ang32)], axis=1).astype(np.float32))


def make_in_maps(inputs, stage="all"):
    pt, cst = make_tables(inputs)
    rope_t = make_rope()
    bmask_t = make_bmask()
    wg = np.asarray(inputs["odd_w_gate"], np.float32)[0]
    wgpad_t = np.zeros((33, 2, 512), np.float32)
    wgpad_t[0:16, 0, :] = wg[0]
    wgpad_t[16:32, 1, :] = wg[1]
    wgpad_t[32, :, :] = np.asarray(inputs["odd_gate_bias"], np.float32)[0]
    maps = []
    xp = np.asarray(inputs["x_prompt"], np.float32)
    xs = np.asarray(inputs["x_sample"], np.float32)
    for c in range(8):
        xin = np.concatenate([xp[4 * c:4 * c + 4].reshape(1024, D), xs[c]], axis=0)
        cond = np.stack([np.asarray(inputs["c_ctx"], np.float32), np.asarray(inputs["c"], np.float32)[c]], axis=-1)
        condT = np.ascontiguousarray(cond.reshape(KC, 128, 2).transpose(1, 0, 2))
        m = {"xin": np.ascontiguousarray(xin), "condT": condT, "ptab": pt, "cst": cst,
             "ada_w": np.asarray(inputs["ada_w"], np.float32),
             "even_w_in": np.asarray(inputs["even_w_in"], np.float32),
             "even_w_out": np.asarray(inputs["even_w_out"], np.float32),
             "rope": rope_t, "bmask": bmask_t, "wgpad": wgpad_t,
             "odd_w_in": np.asarray(inputs["odd_w_in"], np.float32),
             "odd_w_out": np.asarray(inputs["odd_w_out"], np.float32),
             "state_gla": np.ascontiguousarray(np.asarray(inputs["state_gla"], np.float32)[c, 0]),
             "cache_k": np.ascontiguousarray(np.asarray(inputs["cache_k"], np.float32)[c, 0]),
             "cache_v": np.ascontiguousarray(np.asarray(inputs["cache_v"], np.float32)[c, 0]),
             "state_delta": np.ascontiguousarray(np.asarray(inputs["state_delta"], np.float32)[c, 0])}
        if stage in ("all", "ffn0"):
            m["ffn_w_gu"] = np.asarray(inputs["ffn_w_gu"], np.float32)
            m["ffn_w_down"] = np.asarray(inputs["ffn_w_down"], np.float32)
        maps.append(m)
    return maps


_CACHE = {}


def kernel(**inputs):
    if "nc" not in _CACHE:
        _CACHE["nc"] = build_program("all")[0]
    nc = _CACHE["nc"]
    maps = make_in_maps(inputs)
    res = run_bass_kernel_spmd(nc, maps, core_ids=list(range(8)))
    rs = res.results
    ys = [np.asarray(r["yout"], np.float32) for r in rs]
    y_prompt = np.concatenate([y[:1024].reshape(4, 256, D) for y in ys], axis=0)
    y_sample = np.stack([y[1024:] for y in ys], axis=0)
    nsd = np.concatenate([np.asarray(r["nsd"], np.float32) for r in rs], axis=0)[:, None]
    nck = np.concatenate([np.asarray(r["nck"], np.float32).reshape(4, 256, 2, 128) for r in rs], axis=0)[:, None]
    ncv = np.concatenate([np.asarray(r["ncv"], np.float32).reshape(4, 256, 2, 128) for r in rs], axis=0)[:, None]
    nsg = np.concatenate([np.asarray(r["nsg"], np.float32) for r in rs], axis=0)[:, None]
    return (y_prompt, y_sample, np.ascontiguousarray(nsd), np.ascontiguousarray(nck),
            np.ascontiguousarray(ncv), np.ascontiguousarray(nsg))
```

```python
import numpy as np
import concourse.bass as bass
import concourse.mybir as mybir

F32 = mybir.dt.float32
BF16 = mybir.dt.bfloat16
AF = mybir.ActivationFunctionType
ALU = mybir.AluOpType
AX = mybir.AxisListType

_DTSZ = {F32: 4, BF16: 2}


def _region(ap):
    sp = str(ap.space)
    if "DRAM" in sp.upper() or "HBM" in sp.upper():
        return None
    sz = _DTSZ[ap.dtype]
    pat = ap.ap
    pstep, pcnt = pat[0]
    off = int(ap.offset)
    if pstep == 0:
        p0, f0 = 0, off
        pstep = 1 << 40
    else:
        p0, f0 = off // pstep, off % pstep
    ext = 1
    for st, cnt in pat[1:]:
        ext += (cnt - 1) * abs(st)
    b0, b1 = f0 * sz, (f0 + ext) * sz
    if "PSUM" in sp.upper():
        b0 = (b0 // 2048) * 2048
        b1 = ((b1 + 2047) // 2048) * 2048
        return (ap.tensor.name, 0, 128, b0, b1)
    return (ap.tensor.name, p0, p0 + pcnt, b0, b1)


def _ovl(a, b):
    return a[0] == b[0] and a[1] < b[2] and b[1] < a[2] and a[3] < b[4] and b[3] < a[4]


def _covers(a, b):
    return a[0] == b[0] and a[1] <= b[1] and a[2] >= b[2] and a[3] <= b[3] and a[4] >= b[4]


class Op:
    __slots__ = ("eng", "fn", "seq", "inc", "waits", "ctr", "is_dma", "val")

    def __init__(self, eng, fn, is_dma=False):
        self.eng = eng
        self.fn = fn
        self.inc = False
        self.waits = []
        self.is_dma = is_dma
        self.ctr = None
        self.seq = 0
        self.val = 0


ENGS = ("pe", "act", "dve", "pool", "sp")
NDMA = 24


class Prog:
    def __init__(self, nc):
        self.nc = nc
        self.streams = {e: [] for e in ENGS}
        self.seqc = {}
        self.known = {e: {} for e in ENGS}
        self.recs = {}
        self.dma_rr = {"h": 0, "s": 0}
        self.dma_last = {}
        self.nops = 0
        self.out_dmas = []

    def _need(self, op, dep):
        if dep is op:
            return
        if dep.ctr == op.ctr and not dep.is_dma:
            pass
        k = self.known[op.eng]
        if k.get(dep.ctr, -1) >= dep.seq:
            return
        k[dep.ctr] = dep.seq
        dep.inc = True
        op.waits.append(dep)

    BK = 2048

    def _buckets(self, r):
        return range(r[3] // self.BK, (r[4] - 1) // self.BK + 1)

    def _track(self, op, reads, writes):
        BK = self.BK
        for ap in reads:
            r = _region(ap)
            if r is None:
                continue
            for b in self._buckets(r):
                lst = self.recs.setdefault((r[0], b), [])
                is_ps = (r[0] == "ps")
                for rec in lst:
                    if _ovl(rec[0], r) and (rec[1] == "w" or (is_ps and rec[2].ctr != op.ctr)):
                        d = rec[2]
                        if d.ctr == op.ctr and op.eng == "pe":
                            continue
                        self._need(op, d)
                for i, rec in enumerate(lst):
                    if rec[1] == "r" and rec[2].ctr == op.ctr and rec[0] == r:
                        lst.pop(i)
                        break
                lst.append([r, "r", op])
        for ap in writes:
            r = _region(ap)
            if r is None:
                continue
            for b in self._buckets(r):
                lst = self.recs.setdefault((r[0], b), [])
                keep = []
                lo, hi = b * BK, (b + 1) * BK
                for rec in lst:
                    if rec[2] is op:
                        keep.append(rec)
                        continue
                    rr = rec[0]
                    if _ovl(rr, r):
                        d = rec[2]
                        if not (d.ctr == op.ctr and op.eng == "pe"):
                            self._need(op, d)
                        if (r[1] <= rr[1] and r[2] >= rr[2]
                                and r[3] <= max(rr[3], lo) and r[4] >= min(rr[4], hi)):
                            continue
                    keep.append(rec)
                keep.append([r, "w", op])
                self.recs[(r[0], b)] = keep

    def _add(self, eng, fn, reads, writes):
        op = Op(eng, fn)
        op.ctr = eng
        op.seq = self.seqc.get(eng, 0)
        self.seqc[eng] = op.seq + 1
        self._track(op, reads, writes)
        self.streams[eng].append(op)
        self.nops += 1
        return op

    def dma(self, out, in_, q="sp", is_output=False):
        op = Op(q, None, is_dma=True)
        kind = "s" if q == "pool" else "h"
        k = self.dma_rr[kind]
        self.dma_rr[kind] = (k + 1) % (NDMA // 2)
        op.ctr = "dma%s%d" % (kind, k)
        op.seq = self.seqc.get(op.ctr, 0)
        self.seqc[op.ctr] = op.seq + 1
        prev = self.dma_last.get(op.ctr)
        if prev is not None:
            self._need(op, prev)
        self.dma_last[op.ctr] = op
        op.inc = True
        self._track(op, [in_], [out])
        op.fn = lambda e, out=out, in_=in_: e.dma_start(out=out, in_=in_)
        self.streams[q].append(op)
        if is_output:
            self.out_dmas.append(op)
        return op

    def mm(self, out, lhsT, rhs, start=True, stop=True):
        return self._add("pe", lambda e: e.matmul(out, lhsT, rhs, start=start, stop=stop),
                         [lhsT, rhs], [out])

    def tr(self, out, in_, ident):
        return self._add("pe", lambda e: e.transpose(out, in_, ident), [in_, ident], [out])

    def act(self, out, in_, func, bias=None, scale=1.0, accum=None):
        rd = [in_]
        if bias is not None and not isinstance(bias, (int, float)):
            rd.append(bias)
        if not isinstance(scale, (int, float)):
            rd.append(scale)
        wr = [out] + ([accum] if accum is not None else [])
        kw = {}
        if bias is not None:
            kw["bias"] = bias
        if accum is not None:
            kw["accum_out"] = accum
        return self._add("act", lambda e: e.activation(out=out, in_=in_, func=func, scale=scale, **kw),
                         rd, wr)

    def tt(self, out, in0, in1, op, eng="dve"):
        return self._add(eng, lambda e: e.tensor_tensor(out=out, in0=in0, in1=in1, op=op),
                         [in0, in1], [out])

    def ts(self, out, in0, s1, op0, s2=None, op1=None, eng="dve", accum=None):
        rd = [in0] + [s for s in (s1, s2) if s is not None and not isinstance(s, (int, float))]
        kw = {}
        if op1 is not None:
            kw["op1"] = op1
        if accum is not None:
            kw["accum_out"] = accum
        wr = [out] + ([accum] if accum is not None else [])
        return self._add(eng, lambda e: e.tensor_scalar(out=out, in0=in0, scalar1=s1, scalar2=s2, op0=op0, **kw),
                         rd, wr)

    def stt(self, out, in0, scalar, in1, op0, op1, eng="dve"):
        rd = [in0, in1] + ([scalar] if not isinstance(scalar, (int, float)) else [])
        return self._add(eng, lambda e: e.scalar_tensor_tensor(out=out, in0=in0, scalar=scalar, in1=in1,
                                                               op0=op0, op1=op1), rd, [out])

    def copy(self, out, in_, eng="dve"):
        if eng == "act":
            return self.act(out, in_, AF.Copy)
        return self._add(eng, lambda e: e.tensor_copy(out=out, in_=in_), [in_], [out])

    def reduce(self, out, in_, op, eng="dve", axis=None):
        axis = axis or AX.X
        return self._add(eng, lambda e: e.tensor_reduce(out=out, in_=in_, axis=axis, op=op), [in_], [out])

    def recip(self, out, in_):
        return self._add("dve", lambda e: e.reciprocal(out=out, in_=in_), [in_], [out])

    def memset(self, ap, val, eng="dve"):
        return self._add(eng, lambda e: e.memset(ap, val), [], [ap])

    def emit(self):
        nc = self.nc
        sems = {}
        import contextlib
        stack = contextlib.ExitStack()
        allops = []
        for e in ENGS:
            allops.extend(self.streams[e])
        ctrs = sorted(set(o.ctr for o in allops))
        for c in ctrs:
            sems[c] = stack.enter_context(nc.semaphore("s_" + c))
        cnt = {c: 0 for c in ctrs}
        byctr = {c: [] for c in ctrs}
        for o in allops:
            byctr[o.ctr].append(o)
        for c in ctrs:
            ops = sorted(byctr[c], key=lambda o: o.seq)
            v = 0
            for o in ops:
                if o.inc:
                    v += 16 if o.is_dma else 1
                o.val = v
        fin = stack.enter_context(nc.semaphore("s_fin"))
        block = stack.enter_context(nc.Block())
        prog = self

        def run_stream(eng_name, e):
            for o in prog.streams[eng_name]:
                for d in o.waits:
                    e.wait_ge(sems[d.ctr], d.val)
                ins = o.fn(e)
                if o.inc:
                    ins.then_inc(sems[o.ctr], 16 if o.is_dma else 1)

        @block.tensor
        def _(e):
            run_stream("pe", e)

        @block.scalar
        def _(e):
            run_stream("act", e)

        @block.vector
        def _(e):
            run_stream("dve", e)

        @block.gpsimd
        def _(e):
            run_stream("pool", e)

        @block.sync
        def _(e):
            run_stream("sp", e)
            for o in prog.out_dmas:
                e.wait_ge(sems[o.ctr], o.val)

        stack.close()
from concourse.bass_utils import run_bass_kernel_spmd

D = 1024
NTOK = 2048
KC = 8
TT = 512
NTT = 4
DFF = 2816
NHC = 22
EPS = 1e-6
EVEN_IN = 3088
ODD_IN = 3104


class Arena:
    def __init__(self, nc, name, nbytes, stack):
        self.t = stack.enter_context(nc.sbuf_tensor(name, [128, nbytes // 4], F32))
        self.nbytes = nbytes

    def view(self, off, shape, dtype):
        sz = 4 if dtype == F32 else 2
        n = 1
        for s in shape[1:]:
            n *= s
        assert off % 4 == 0 and (n * sz) % 4 == 0 and off + n * sz <= self.nbytes, (off, shape, self.nbytes)
        ap = self.t[0:shape[0], off // 4: off // 4 + (n * sz) // 4]
        if dtype != F32:
            ap = ap.bitcast(dtype)
        if len(shape) == 3:
            ap = ap.rearrange("p (a b) -> p a b", a=shape[1], b=shape[2])
        elif len(shape) == 4:
            ap = ap.rearrange("p (a b c) -> p a b c", a=shape[1], b=shape[2], c=shape[3])
        return ap


def build_program(stage="all", debug_names=()):
    import contextlib
    nc = bass.Bass("TRN2", target_bir_lowering=False)
    P = Prog(nc)
    stack = contextlib.ExitStack()

    def din(name, shape, dt=F32):
        return nc.dram_tensor(name, list(shape), dt, kind="ExternalInput").ap()

    def dout(name, shape, dt=F32):
        return nc.dram_tensor(name, list(shape), dt, kind="ExternalOutput").ap()

    xin = din("xin", [NTOK, D])
    condT = din("condT", [128, KC, 2])
    ptab = din("ptab", [128, PT_COLS])
    cst = din("cst", [128, CST_COLS])
    ada_w = din("ada_w", [2, D, 9 * D])
    if stage in ("all", "ffn0"):
        w_gu = din("ffn_w_gu", [2, 2, D, 2 * DFF])
        w_dn = din("ffn_w_down", [2, 2, DFF, D])
    yout = dout("yout", [NTOK, D])
    even_w_in = din("even_w_in", [1, D, EVEN_IN])
    even_w_out = din("even_w_out", [1, D, D])
    rope = din("rope", [128, 2, 1024])
    bmask = din("bmask", [128, 9 * 128])
    odd_w_in = din("odd_w_in", [1, D, ODD_IN])
    odd_w_out = din("odd_w_out", [1, D, D])
    wgpad = din("wgpad", [33, 2, 512])
    state_gla = din("state_gla", [2, 4, 128, 256])
    nsg = dout("nsg", [4, 2, 4, 128, 256])
    cache_k = din("cache_k", [512, 2, 128])
    cache_v = din("cache_v", [512, 2, 128])
    state_delta = din("state_delta", [2, 4, 128, 128])
    nck = dout("nck", [1024, 256])
    ncv = dout("ncv", [1024, 256])
    nsd = dout("nsd", [4, 2, 4, 128, 128])

    dbg = {}
    xT = stack.enter_context(nc.sbuf_tensor("xT", [128, KC, NTOK], F32))
    ptab_sb = stack.enter_context(nc.sbuf_tensor("ptab_sb", [128, PT_COLS], F32))
    cst_sb = stack.enter_context(nc.sbuf_tensor("cst_sb", [128, CST_COLS], F32))
    cstb = stack.enter_context(nc.sbuf_tensor("cstb", [128, CSTB_COLS], BF16))
    modT = stack.enter_context(nc.sbuf_tensor("modT", [128, 2, 72, 2], F32))
    modA = stack.enter_context(nc.sbuf_tensor("modA", [128, 2, 3, KC, 2], F32))
    modG = stack.enter_context(nc.sbuf_tensor("modG", [128, 2, 3, KC, 2], F32))
    scT = stack.enter_context(nc.sbuf_tensor("scT", [128, KC, 2], BF16))
    condsb = stack.enter_context(nc.sbuf_tensor("condsb", [128, KC, 2], F32))
    gates = stack.enter_context(nc.sbuf_tensor("gates", [128, 8, 8, 16], F32))
    epsT = stack.enter_context(nc.sbuf_tensor("epsT", [128, 2], F32))
    rstd = stack.enter_context(nc.sbuf_tensor("rstd", [128, 2, TT], F32))
    ar = Arena(nc, "arena", 126976, stack)
    ps = stack.enter_context(nc.psum_tensor("ps", [128, 8, 512], F32))

    def bank(b):
        return ps[:, b, :]

    identF = cst_sb[:, C_IDENT:C_IDENT + 128]
    identB = cstb[:, 0:128]
    onesP = cstb[:, 128:256]

    P.dma(ptab_sb[:, :], ptab[:, :], q="sp")
    P.dma(cst_sb[:, :], cst[:, :], q="sp")
    P.dma(condsb[:, :, :], condT[:, :, :], q="sp")
    P.copy(identB, identF, eng="dve")
    P.memset(onesP, 1.0, eng="dve")
    P.memset(epsT[:, :], EPS, eng="dve")
    P.memset(gates[:, :, :, :], 0.0, eng="pool")

    STG = 32768
    for i in range(16):
        stg = ar.view(STG + (i % 2) * 4096, [128, D], F32)
        P.dma(stg, xin[i * 128:(i + 1) * 128, :], q="sp" if i % 2 == 0 else "act")
        for half in range(2):
            b = (i * 2 + half) % 4
            for kk in range(4):
                kc = half * 4 + kk
                P.tr(ps[:, b, kk * 128:(kk + 1) * 128], stg[:, kc * 128:(kc + 1) * 128], identF)
            src = ps[:, b, :].rearrange("p (a t) -> p a t", a=4, t=128)
            dst = xT[:, half * 4:half * 4 + 4, i * 128:(i + 1) * 128]
            if half == 0:
                P.copy(dst, src, eng="dve")
            else:
                P.copy(dst, src, eng="act")

    P.act(scT[:, :, :], condsb[:, :, :], AF.Silu)
    ADAW = 40960

    def ada_finish(l, s):
        ng = ptab_sb[:, PT_NORMG + (l * 3 + s) * 8: PT_NORMG + (l * 3 + s + 1) * 8]
        sc = modT[:, l, (3 * s + 1) * 8:(3 * s + 2) * 8, :]
        P.stt(modA[:, l, s, :, :], sc, 1.0, ng.unsqueeze(2).broadcast_to([128, KC, 2]), ALU.add, ALU.mult)
        gt = modT[:, l, (3 * s + 2) * 8:(3 * s + 3) * 8, :]
        P.ts(modG[:, l, s, :, :], gt, 0.5 if s != 1 else 1.0, ALU.mult)

    for i in range(3):
        wb = ar.view(ADAW + (i % 3) * 16384, [128, KC, 1024], BF16)
        src = ada_w[0].rearrange("(kc p) n -> p kc n", p=128)[:, :, i * 1024:(i + 1) * 1024]
        P.dma(wb, src, q="pool")
        for n in range(8):
            j = i * 8 + n
            for kc in range(KC):
                P.mm(ps[:, 4, 2 * j:2 * j + 2], wb[:, kc, n * 128:(n + 1) * 128], scT[:, kc, :],
                     start=(kc == 0), stop=(kc == KC - 1))
    P.tt(modT[:, 0, 0:24, :], ps[:, 4, 0:48].rearrange("p (j c) -> p j c", j=24, c=2),
         ptab_sb[:, PT_ADAB:PT_ADAB + 24].unsqueeze(2).broadcast_to([128, 24, 2]), ALU.add)
    ada_finish(0, 0)
    ada_tasks = [(0, i, q4) for i in range(3, 9) for q4 in range(4)] + \
                [(1, i, q4) for i in range(9) for q4 in range(4)]
    ada_cnt = [0, 0, 0]

    ada_pending = []

    def ada_load():
        l, i, q4 = ada_tasks.pop(0)
        wb = ar.view(TMP + (ada_cnt[0] % 2) * 4096, [128, KC, 256], BF16)
        ada_cnt[0] += 1
        c0 = i * 1024 + q4 * 256
        P.dma(wb, ada_w[l].rearrange("(kc p) n -> p kc n", p=128)[:, :, c0:c0 + 256], q="pool")
        ada_pending.append((l, i, q4, wb))

    def ada_compute():
        l, i, q4, wb = ada_pending.pop(0)
        bank_b = 6 + (ada_cnt[1] % 2)
        ada_cnt[1] += 1
        for nn in range(2):
            for kc in range(KC):
                P.mm(ps[:, bank_b, 2 * nn:2 * nn + 2], wb[:, kc, nn * 128:(nn + 1) * 128], scT[:, kc, :],
                     start=(kc == 0), stop=(kc == KC - 1))
        j0 = i * 8 + q4 * 2
        P.tt(modT[:, l, j0:j0 + 2, :], ps[:, bank_b, 0:4].rearrange("p (j c) -> p j c", j=2, c=2),
             ptab_sb[:, PT_ADAB + l * 72 + j0:PT_ADAB + l * 72 + j0 + 2].unsqueeze(2).broadcast_to([128, 2, 2]),
             ALU.add)
        if q4 == 3 and i % 3 == 2:
            ada_finish(l, i // 3)

    def ada_tick():
        ada_cnt[2] += 1
        if ada_cnt[2] % 3 != 0:
            return
        if len(ada_pending) == 2 or (ada_pending and not ada_tasks):
            ada_compute()
        if ada_tasks and len(ada_pending) < 2:
            ada_load()

    def ada_flush():
        while ada_tasks or ada_pending:
            if ada_tasks and len(ada_pending) < 2:
                ada_load()
            else:
                ada_compute()

    HT = 0
    FW = 32768
    ACTB = FW + 49152
    SG = ACTB + 32768
    TMP = SG + 4096
    SQ = TMP + 4096
    assert SQ + 4096 <= ar.nbytes, SQ + 4096
    hT = ar.view(HT, [128, KC, NTOK], BF16)

    def rms_tile(tt, bank0=6):
        b = bank0 + (tt % 2)
        for kc in range(KC):
            sq = ar.view(SQ + ((tt * KC + kc) % 4) * 1024, [128, TT], BF16)
            P.act(sq, xT[:, kc, tt * TT:(tt + 1) * TT], AF.Square)
            P.mm(bank(b), onesP, sq, start=(kc == 0), stop=(kc == KC - 1))
        rs = rstd[:, tt % 2, :]
        P.act(rs, bank(b), AF.Ln, bias=epsT[:, 0:1], scale=1.0 / 1024.0)
        P.act(rs, rs, AF.Exp, scale=-0.5)
        return rs

    def modnorm_tile(l, s, tt):
        rs = rms_tile(tt)
        c = 0 if tt < 2 else 1
        for kc in range(KC):
            tmp = ar.view(TMP + ((tt * KC + kc) % 2) * 2048, [128, TT], F32)
            P.stt(tmp, xT[:, kc, tt * TT:(tt + 1) * TT], modA[:, l, s, kc, c:c + 1],
                  rs, ALU.mult, ALU.mult)
            P.act(hT[:, kc, tt * TT:(tt + 1) * TT], tmp, AF.Identity,
                  bias=modT[:, l, 3 * s * 8 + kc, c:c + 1])

    pre_normed = [None]

    def modnorm(l, s):
        if pre_normed[0] == (l, s):
            pre_normed[0] = None
            return
        for tt in range(NTT):
            modnorm_tile(l, s, tt)

    GROUPS = [(0, 4), (4, 8), (8, 12), (12, 16), (16, 19), (19, 22)]

    def ffn(l, i, after_tile=None):
        s = 0 if i == 0 else 2
        modnorm(l, s)
        wgu = w_gu[l, i].rearrange("(kc p) n -> p kc n", p=128)
        wdn = w_dn[l, i].rearrange("(g p) n -> p g n", p=128)

        def load(g):
            j0, j1 = GROUPS[g]
            G = j1 - j0
            base = FW + (g % 2) * 24576
            wg = ar.view(base, [128, KC, 512], BF16)
            wu = ar.view(base + 8192, [128, KC, 512], BF16)
            wd = ar.view(base + 16384, [128, 4, 1024], BF16)
            P.dma(wg[:, :, 0:G * 128], wgu[:, :, j0 * 128:j1 * 128], q="pool")
            P.dma(wu[:, :, 0:G * 128], wgu[:, :, DFF + j0 * 128:DFF + j1 * 128], q="pool")
            P.dma(wd[:, 0:G, :], wdn[:, j0:j1, :], q="pool")
            return wg, wu, wd

        pair = [0]

        def gu(g, W):
            j0, j1 = GROUPS[g]
            wg, wu, _ = W
            ab = ar.view(ACTB + (g % 2) * 16384, [128, 4, NTOK], BF16)
            for jj in range(j1 - j0):
                for tt in range(NTT):
                    pb = (pair[0] % 2) * 2
                    pair[0] += 1
                    rhs = None
                    for kc in range(KC):
                        P.mm(bank(pb), wg[:, kc, jj * 128:(jj + 1) * 128], hT[:, kc, tt * TT:(tt + 1) * TT],
                             start=(kc == 0), stop=(kc == KC - 1))
                    for kc in range(KC):
                        P.mm(bank(pb + 1), wu[:, kc, jj * 128:(jj + 1) * 128], hT[:, kc, tt * TT:(tt + 1) * TT],
                             start=(kc == 0), stop=(kc == KC - 1))
                    sg = ar.view(SG + (pair[0] % 2) * 2048, [128, TT], F32)
                    P.act(sg, bank(pb), AF.Silu)
                    P.tt(ab[:, jj, tt * TT:(tt + 1) * TT], sg, bank(pb + 1), ALU.mult)
                    ada_tick()

        ycnt = [0]

        def down(g, W, tile_major=False):
            j0, j1 = GROUPS[g]
            _, _, wd = W
            ab = ar.view(ACTB + (g % 2) * 16384, [128, 4, NTOK], BF16)
            order = [(n, tt) for n in range(KC) for tt in range(NTT)]
            if tile_major:
                order = [(n, tt) for tt in range(NTT) for n in range(KC)]
            for (n, tt) in order:
                if True:
                    c = 0 if tt < 2 else 1
                    yb = 4 + (ycnt[0] % 4)
                    ycnt[0] += 1
                    for jj in range(j1 - j0):
                        P.mm(bank(yb), wd[:, jj, n * 128:(n + 1) * 128], ab[:, jj, tt * TT:(tt + 1) * TT],
                             start=(jj == 0), stop=(jj == j1 - j0 - 1))
                    xs = xT[:, n, tt * TT:(tt + 1) * TT]
                    P.stt(xs, bank(yb), modG[:, l, s, n, c:c + 1], xs, ALU.mult, ALU.add)
                    if tile_major and n == KC - 1 and after_tile is not None:
                        after_tile(tt)

        W = {}
        W[0] = load(0)
        W[1] = load(1)
        gu(0, W[0])
        for g in range(len(GROUPS)):
            if g + 1 < len(GROUPS):
                gu(g + 1, W[g + 1])
            last = (g == len(GROUPS) - 1)
            if last:
                while ada_pending:
                    ada_compute()
            down(g, W[g], tile_major=last)
            if g + 2 < len(GROUPS):
                W[g + 2] = load(g + 2)
        while ada_pending:
            ada_compute()
        if (l, i) == (0, 1):
            ada_flush()

    def final_tile(tt):
        fg = ptab_sb[:, PT_FINALG:PT_FINALG + 8]
        YS = ACTB
        rs = rms_tile(tt)
        for i in range(4 * tt, 4 * tt + 4):
            yt = ar.view(YS + (i % 2) * 4096, [128, KC, 128], F32)
            for kc in range(KC):
                P.stt(yt[:, kc, :], xT[:, kc, i * 128:(i + 1) * 128], fg[:, kc:kc + 1],
                      rs[:, (i % 4) * 128:(i % 4 + 1) * 128], ALU.mult, ALU.mult)
            st = ar.view(YS + 8192 + (i % 2) * 4096, [128, D], F32)
            for half in range(2):
                b = (i * 2 + half) % 4
                for kk in range(4):
                    kc = half * 4 + kk
                    P.tr(ps[:, b, kk * 128:(kk + 1) * 128], yt[:, kc, :], identF)
                if half == 0:
                    P.copy(st[:, 0:512], bank(b), eng="dve")
                else:
                    P.copy(st[:, 512:1024], bank(b), eng="act")
            P.dma(yout[i * 128:(i + 1) * 128, :], st, q="sp", is_output=True)

    def final_out():
        for tt in range(NTT):
            final_tile(tt)

    NEG = -30000.0
    Uf = cst_sb[:, C_UF:C_UF + 128]
    Ub = cst_sb[:, C_UB:C_UB + 128]
    sel127 = cst_sb[:, C_SEL127:C_SEL127 + 128]
    sel0 = cst_sb[:, C_SEL0:C_SEL0 + 128]
    onesF = cst_sb[:, C_ONES:C_ONES + 128]
    offdiag = cst_sb[:, C_OFFD:C_OFFD + 128]
    MLf = cst_sb[:, C_ML:C_ML + 128]
    MUf = cst_sb[:, C_MU:C_MU + 128]
    Rm = cst_sb[:, C_RM:C_RM + 128]
    MLb = cstb[:, 256:384]
    MUb = cstb[:, 384:512]
    P.dma(cstb[:, 512:512 + 9 * 128], bmask[:, :], q="pool")
    P.copy(MLb, MLf, eng="dve")
    P.copy(MUb, MUf, eng="dve")

    def bc_h(m):
        return m.unsqueeze(1).broadcast_to([128, 4, 128])

    def bc_i(v):
        return v.unsqueeze(2).broadcast_to([128, 4, 128])

    def b4(b):
        return ps[:, b, :].rearrange("p (h t) -> p h t", h=4, t=128)

    def bbf(b):
        return ps[:, b, :].bitcast(BF16)

    nbc = [0]

    def nb():
        b = nbc[0] % 8
        nbc[0] += 1
        return b

    def nbk(k):
        c = (nbc[0] + k - 1) // k * k
        nbc[0] = c + k
        return c % 8

    def dump(name, ap, shape, dt):
        d = dout(name, shape, dt)
        P.dma(d, ap, q="sp", is_output=True)
        dbg[name] = (shape, dt)

    QN, KN, VS, GS = 32768, 40960, 49152, 57344
    QPL, QRO, KFM, VTM, KCT, VC = 65536, 73728, 81920, 86016, 90112, 92160
    WP = 94208
    SCR = 110592
    OACC = 65536
    TST = 81920
    SCN = 98304
    SHR = 114688
    STF = 120832
    STB = 124928

    def mixer_even():
        l = 0
        modnorm(l, 1)
        w_in = even_w_in[0].rearrange("(kc p) n -> p kc n", p=128)
        w_out = even_w_out[0].rearrange("(kc p) n -> p kc n", p=128)
        cw = ptab_sb[:, PT_CONV:PT_CONV + 60].rearrange("p (c j) -> p c j", c=12, j=5)
        sink_bc = ptab_sb[:, PT_SINK:PT_SINK + 4]
        wpc = [0]

        def load_piece(c0, c1):
            wp = ar.view(WP + (wpc[0] % 2) * 8192, [128, KC, 512], BF16)
            wpc[0] += 1
            P.dma(wp[:, :, 0:c1 - c0], w_in[:, :, c0:c1], q="pool")
            return wp

        for grp in range(2):
            T0 = grp * 1024
            nseq, L = (4, 256) if grp == 0 else (1, 1024)
            qn = ar.view(QN, [128, 4, 1024], BF16)
            kn = ar.view(KN, [128, 4, 1024], BF16)
            vS = ar.view(VS, [128, 4, 1024], BF16)
            gS = ar.view(GS, [128, 4, 1024], BF16)
            qpl = ar.view(QPL, [128, 4, 1024], BF16)
            qro = ar.view(QRO, [128, 4, 1024], BF16)
            kfm = ar.view(KFM, [128, 2, 1024], BF16)
            vtm = ar.view(VTM, [128, 8, 256], BF16)
            kcT = ar.view(KCT, [128, 2, 512], BF16)
            vc = ar.view(VC, [128, 4, 256], BF16)
            mixT = hT[:, :, T0:T0 + 1024]

            def proj_fm(wp, cc):
                b = nbk(2)
                for tt in range(2):
                    for kc in range(KC):
                        P.mm(bank(b + tt), wp[:, kc, cc * 128:(cc + 1) * 128],
                             hT[:, kc, T0 + tt * TT:T0 + (tt + 1) * TT], start=(kc == 0), stop=(kc == KC - 1))
                return ps[:, b:b + 2, :].rearrange("p a t -> p (a t)")

            SCALE = 128.0 ** -0.5
            if grp == 1:
                ropeT = ar.view(SCR, [128, 2, 1024], F32)
                P.dma(ropeT, rope[:, :, :], q="sp")
                kst = ar.view(SCR + 8192, [128, 4, 256], BF16)
                P.dma(kst, cache_k.rearrange("(kt p) g d -> p kt (g d)", p=128), q="pool")
                P.dma(vc, cache_v.rearrange("(kt p) g d -> p kt (g d)", p=128), q="pool")
                for g in range(2):
                    b = nb()
                    for kt in range(4):
                        P.tr(bbf(b)[:, kt * 128:(kt + 1) * 128], kst[:, kt, g * 128:(g + 1) * 128], identB)
                    P.copy(kcT[:, g, :], bbf(b)[:, 0:512], eng="dve")

            def rope_apply(dst_bf, xf):
                t1 = ar.view(SCR + 12288, [128, 1024], F32)
                for tt in range(2):
                    b = nb()
                    P.mm(bank(b), Rm, xf[:, tt * TT:(tt + 1) * TT])
                    P.tt(t1[:, tt * TT:(tt + 1) * TT], bank(b), ropeT[:, 1, tt * TT:(tt + 1) * TT], ALU.mult)
                P.tt(xf, xf, ropeT[:, 0, :], ALU.mult, eng="pool")
                P.tt(dst_bf, t1, xf, ALU.add)

            wp = load_piece(2064, 2576)
            for h in range(4):
                pp = proj_fm(wp, h)
                P.act(qpl[:, h, :], pp, AF.Copy, scale=SCALE)
                if grp == 1:
                    xf = ar.view(SCR + 8192, [128, 1024], F32)
                    P.ts(xf, pp, SCALE, ALU.mult)
                    rope_apply(qro[:, h, :], xf)
            if "stop_ip1" in debug_names:
                return
            wp = load_piece(2576, 3088)
            for g in range(2):
                pp = proj_fm(wp, g)
                if grp == 0:
                    P.act(kfm[:, g, :], pp, AF.Copy)
                else:
                    xf = ar.view(SCR + 8192, [128, 1024], F32)
                    P.act(xf, pp, AF.Copy)
                    rope_apply(kfm[:, g, :], xf)
            if "stop_ip1b" in debug_names:
                return
            for i in range(8):
                b = nb()
                for kc in range(KC):
                    P.mm(bank(b), hT[:, kc, T0 + i * 128:T0 + (i + 1) * 128], wp[:, kc, 0:512],
                         start=(kc == 0), stop=(kc == KC - 1))
                if grp == 1:
                    P.act(vtm[:, i, :], ps[:, b, 256:512], AF.Copy)
                else:
                    st = ar.view(SCR + (i % 2) * 2048, [128, 512], F32)
                    P.copy(st, bank(b), eng="dve")
                    P.act(vtm[:, i, :], st[:, 256:512], AF.Copy)
                    P.dma(nck[i * 128:(i + 1) * 128, :], st[:, 0:256], q="sp", is_output=True)
                    P.dma(ncv[i * 128:(i + 1) * 128, :], st[:, 256:512], q="act", is_output=True)
            if "stop_ip2" in debug_names:
                return
            wpab = load_piece(2048, 2064)
            abT = gates[:, 0, :, :]
            for i in range(8):
                b = nb()
                for kc in range(KC):
                    P.mm(ps[:, b, 0:16], hT[:, kc, T0 + i * 128:T0 + (i + 1) * 128], wpab[:, kc, 0:16],
                         start=(kc == 0), stop=(kc == KC - 1))
                P.copy(abT[:, i, :], ps[:, b, 0:16], eng="dve")

            if "stop_ip3" in debug_names:
                return
            Lp = L + 4
            xpad = [ar.view(SCR, [128, nseq, Lp], F32) for k in range(2)]
            acc = ar.view(SCR + 4160, [128, nseq, L], F32)
            qs = ar.view(SCR + 8256, [128, 1024], F32)
            P.memset(ar.view(SCR, [128, 1040], F32), 0.0, eng="pool")
            for pc in range(3):
                wp = load_piece(pc * 512, (pc + 1) * 512)
                for hh in range(4):
                    c = pc * 4 + hh
                    pp = proj_fm(wp, hh)
                    xp = xpad[c % 2]
                    P.act(xp[:, :, 2:2 + L], pp.rearrange("p (s t) -> p s t", s=nseq, t=L), AF.Copy)
                    P.ts(acc, xp[:, :, 0:L], cw[:, c, 0:1], ALU.mult)
                    for j in range(1, 5):
                        P.stt(acc, xp[:, :, j:j + L], cw[:, c, j:j + 1], acc, ALU.mult, ALU.add)
                    accf = acc.rearrange("p s t -> p (s t)")
                    if pc == 2:
                        P.act(vS[:, hh, :], accf, AF.Silu)
                    else:
                        P.act(qs, accf, AF.Silu)
                        dst = (qn if pc == 0 else kn)[:, hh, :]
                        for tt in range(2):
                            sq = ar.view(SCR + 12352 + (tt % 2) * 1024, [128, TT], BF16)
                            P.act(sq, qs[:, tt * TT:(tt + 1) * TT], AF.Square)
                            b = nb()
                            P.mm(bank(b), onesP, sq)
                            rs = rstd[:, tt % 2, :]
                            P.act(rs, bank(b), AF.Ln, bias=epsT[:, 0:1])
                            P.act(rs, rs, AF.Exp, scale=-0.5)
                            P.stt(dst[:, tt * TT:(tt + 1) * TT], qs[:, tt * TT:(tt + 1) * TT],
                                  SCALE if pc == 0 else 1.0, rs, ALU.mult, ALU.mult)
            wp = load_piece(1536, 2048)
            for hh in range(4):
                pp = proj_fm(wp, hh)
                P.act(gS[:, hh, :], pp, AF.Silu)
            if "inproj" in debug_names and grp == DBG_GRP:
                dump("d_qn", qn, [128, 4, 1024], BF16)
                dump("d_kn", kn, [128, 4, 1024], BF16)
                dump("d_vS", vS, [128, 4, 1024], BF16)
                dump("d_gS", gS, [128, 4, 1024], BF16)
                dump("d_qpl", qpl, [128, 4, 1024], BF16)
                dump("d_qro", qro, [128, 4, 1024], BF16)
                dump("d_kfm", kfm, [128, 2, 1024], BF16)
                dump("d_vtm", vtm, [128, 8, 256], BF16)
                dump("d_ab", abT, [128, 8, 16], F32)

            if "stop_inproj" in debug_names:
                return
            ATT = WP
            if grp == 0:
                for s_ in range(4):
                    for qb in range(2):
                        tq = s_ * 256 + qb * 128
                        Pb = ar.view(ATT + ((s_ * 2 + qb) % 2) * 2048, [128, 4, 256], BF16)
                        PTs = ar.view(ATT + 4096 + ((s_ * 2 + qb) % 2) * 2048, [128, 8, 128], BF16)
                        stt_ = ar.view(ATT + 8192 + ((s_ * 2 + qb) % 2) * 256, [128, 16], F32)
                        on = ar.view(ATT + 8704 + ((s_ * 2 + qb) % 2) * 1024, [128, 4, 128], BF16)
                        bS = nbk(2)
                        P.memset(stt_, 0.0, eng="pool")
                        for h in range(4):
                            P.mm(ps[:, bS + h // 2, (h % 2) * 256:(h % 2 + 1) * 256],
                                 qpl[:, h, tq:tq + 128], kfm[:, h // 2, s_ * 256:(s_ + 1) * 256])
                        S4 = ps[:, bS:bS + 2, :].rearrange("p a (h k) -> p (a h) k", h=2, k=256)
                        mx = stt_[:, 0:4]
                        negm = stt_[:, 4:8]
                        rsum = stt_[:, 8:12]
                        es = stt_[:, 12:16]
                        P.reduce(mx, S4, ALU.max)
                        P.tt(mx, mx, sink_bc, ALU.max)
                        P.ts(negm, mx, -1.0, ALU.mult)
                        for h in range(4):
                            P.act(Pb[:, h, :], S4[:, h, :], AF.Exp, bias=negm[:, h:h + 1], accum=rsum[:, h:h + 1])
                        P.tt(es, sink_bc, negm, ALU.add)
                        P.act(es, es, AF.Exp)
                        P.tt(rsum, rsum, es, ALU.add)
                        P.recip(rsum, rsum)
                        bT = nb()
                        for h in range(4):
                            for kt in range(2):
                                P.tr(bbf(bT)[:, (h * 2 + kt) * 128:(h * 2 + kt + 1) * 128],
                                     Pb[:, h, kt * 128:(kt + 1) * 128], identB)
                        P.copy(PTs.rearrange("p a t -> p (a t)"), bbf(bT)[:, 0:1024], eng="act")
                        bO = nb()
                        for h in range(4):
                            for kt in range(2):
                                P.mm(ps[:, bO, h * 128:(h + 1) * 128], PTs[:, h * 2 + kt, :],
                                     vtm[:, s_ * 2 + kt, (h // 2) * 128:(h // 2 + 1) * 128],
                                     start=(kt == 0), stop=(kt == 1))
                        P.tt(on, b4(bO), bc_i(rsum), ALU.mult)
                        bT2 = nb()
                        for h in range(4):
                            P.tr(bbf(bT2)[:, h * 128:(h + 1) * 128], on[:, h, :], identB)
                        P.copy(mixT[:, 4:8, tq:tq + 128],
                               bbf(bT2)[:, 0:512].rearrange("p (h t) -> p h t", h=4, t=128), eng="dve")
            else:
                for qb in range(8):
                    tq = qb * 128
                    blks = [k for k in (qb - 1, qb, qb + 1) if 0 <= k < 8]
                    nl = len(blks)
                    W = 512 + nl * 128
                    on = ar.view(ATT + 16384 + (qb % 2) * 1024, [128, 4, 128], BF16)
                    for hp in range(2):
                        it = qb * 2 + hp
                        Pb = ar.view(ATT + (it % 2) * 4096, [128, 2, 1024], BF16)
                        PTs = ar.view(ATT + 8192 + (it % 2) * 4096, [128, 2, 1024], BF16)
                        stt_ = ar.view(ATT + 18432 + (it % 2) * 256, [128, 16], F32)
                        mx = stt_[:, 0:2]
                        negm = stt_[:, 2:4]
                        rsum = stt_[:, 4:6]
                        es = stt_[:, 6:8]
                        bS = nbk(4)
                        P.memset(stt_, 0.0, eng="pool")
                        for hh in range(2):
                            h = hp * 2 + hh
                            g = hp
                            P.mm(bank(bS + 2 * hh), qpl[:, h, tq:tq + 128], kcT[:, g, :])
                            k0 = blks[0] * 128
                            has_mask = (blks[0] == qb - 1) or (blks[-1] == qb + 1)
                            P.mm(ps[:, bS + 2 * hh + 1, 0:nl * 128], qro[:, h, tq:tq + 128],
                                 kfm[:, g, k0:k0 + nl * 128], start=True, stop=not has_mask)
                            nm = (1 if blks[0] == qb - 1 else 0) + (1 if blks[-1] == qb + 1 else 0)
                            cnt = 0
                            for bi, k in enumerate(blks):
                                if k == qb - 1 or k == qb + 1:
                                    cnt += 1
                                    P.mm(ps[:, bS + 2 * hh + 1, bi * 128:(bi + 1) * 128], identB,
                                         MUb if k == qb - 1 else MLb, start=False, stop=(cnt == nm))
                        S2 = ps[:, bS:bS + 4, :].rearrange("p (h a) t -> p h (a t)", h=2, a=2)[:, :, 0:W]
                        P.reduce(mx, S2, ALU.max)
                        P.tt(mx, mx, sink_bc[:, hp * 2:hp * 2 + 2], ALU.max)
                        P.ts(negm, mx, -1.0, ALU.mult)
                        for hh in range(2):
                            P.act(Pb[:, hh, 0:W], S2[:, hh, :], AF.Exp, bias=negm[:, hh:hh + 1],
                                  accum=rsum[:, hh:hh + 1])
                        P.tt(es, sink_bc[:, hp * 2:hp * 2 + 2], negm, ALU.add)
                        P.act(es, es, AF.Exp)
                        P.tt(rsum, rsum, es, ALU.add)
                        P.recip(rsum, rsum)
                        nblk = 4 + nl
                        for hh in range(2):
                            bT = nb()
                            for bi in range(nblk):
                                P.tr(bbf(bT)[:, bi * 128:(bi + 1) * 128], Pb[:, hh, bi * 128:(bi + 1) * 128], identB)
                            P.copy(PTs[:, hh, 0:W], bbf(bT)[:, 0:W], eng="act" if hh == 0 else "dve")
                        bO = nb()
                        for hh in range(2):
                            g = hp
                            for bi in range(nblk):
                                if bi < 4:
                                    rhs = vc[:, bi, g * 128:(g + 1) * 128]
                                else:
                                    rhs = vtm[:, blks[bi - 4], g * 128:(g + 1) * 128]
                                P.mm(ps[:, bO, hh * 128:(hh + 1) * 128], PTs[:, hh, bi * 128:(bi + 1) * 128], rhs,
                                     start=(bi == 0), stop=(bi == nblk - 1))
                        P.tt(on[:, hp * 2:hp * 2 + 2, :],
                             ps[:, bO, 0:256].rearrange("p (h t) -> p h t", h=2, t=128),
                             rsum.unsqueeze(2).broadcast_to([128, 2, 128]), ALU.mult)
                    bT2 = nb()
                    for h in range(4):
                        P.tr(bbf(bT2)[:, h * 128:(h + 1) * 128], on[:, h, :], identB)
                    P.copy(mixT[:, 4:8, tq:tq + 128],
                           bbf(bT2)[:, 0:512].rearrange("p (h t) -> p h t", h=4, t=128), eng="dve")
            if "attn" in debug_names and grp == DBG_GRP:
                dump("d_oatt", mixT[:, 4:8, :], [128, 4, 1024], BF16)

            if "stop_attn" in debug_names:
                return
            delta_net(grp, T0, nseq, L, qn, kn, vS, gS, mixT)
            if "delta" in debug_names and grp == DBG_GRP:
                dump("d_oa", mixT[:, 0:4, :], [128, 4, 1024], BF16)

            if "stop_delta" in debug_names:
                return
            wo = ar.view(WP, [128, KC, 1024], BF16)
            P.dma(wo, w_out[:, :, :], q="pool")
            for n in range(KC):
                for tt in range(2):
                    b = nb()
                    for kc in range(KC):
                        P.mm(bank(b), wo[:, kc, n * 128:(n + 1) * 128], mixT[:, kc, tt * TT:(tt + 1) * TT],
                             start=(kc == 0), stop=(kc == KC - 1))
                    xs = xT[:, n, T0 + tt * TT:T0 + (tt + 1) * TT]
                    P.stt(xs, bank(b), modG[:, l, 1, n, grp:grp + 1], xs, ALU.mult, ALU.add)

    def delta_net(grp, T0, nseq, L, qn, kn, vS, gS, mixT):
        abT = gates[:, 0, :, :]
        gT = gates[:, 1, :, 0:8]
        beta = gates[:, 2, :, 0:8]
        gc = gates[:, 3, :, 0:8]
        glb = gates[:, 4, :, 0:8]
        egc = gates[:, 5, :, 0:8]
        kdf = gates[:, 6, :, 0:8]
        glast = gates[:, 7, :, 0:8]
        bege = gates[:, 1, :, 8:16]
        tmpA = gates[:, 2, :, 8:16]
        tmpB = gates[:, 3, :, 8:16]
        alog_bc = ptab_sb[:, PT_ALOG:PT_ALOG + 8]
        dtb_bc = ptab_sb[:, PT_DTB:PT_DTB + 8]
        onorm = ptab_sb[:, PT_ONORME:PT_ONORME + 1]
        bc8 = lambda v: v.unsqueeze(1).broadcast_to([128, 8, 8])
        P.act(beta, abT[:, :, 0:8], AF.Exp, scale=-1.0)
        P.ts(beta, beta, 1.0, ALU.add)
        P.recip(beta, beta)
        P.tt(tmpA, abT[:, :, 8:16], bc8(dtb_bc), ALU.add)
        P.act(tmpB, tmpA, AF.Abs)
        P.act(tmpB, tmpB, AF.Exp, scale=-1.0)
        P.act(tmpB, tmpB, AF.Ln, bias=onesF[:, 0:1])
        P.ts(tmpA, tmpA, 0.0, ALU.max)
        P.tt(tmpA, tmpA, tmpB, ALU.add)
        P.act(tmpB[:, 0, :], alog_bc, AF.Exp)
        P.stt(gT, tmpA, -1.0, bc8(tmpB[:, 0, :]), ALU.mult, ALU.mult)
        g64 = gates[:, 1, :, :].rearrange("p a b -> p (a b)")
        b = nb()
        P.mm(ps[:, b, 0:128], Uf, g64)
        P.mm(ps[:, b, 128:256], Ub, g64)
        pv = ps[:, b, 0:256].rearrange("p (d a c) -> p d a c", d=2, a=8, c=16)
        P.copy(gc[:, :, 0:4], pv[:, 0, :, 0:4], eng="dve")
        P.copy(gc[:, :, 4:8], pv[:, 1, :, 4:8], eng="dve")
        gc64 = gates[:, 3, :, :].rearrange("p a b -> p (a b)")
        b = nb()
        P.mm(ps[:, b, 0:128], sel127, gc64)
        P.mm(ps[:, b, 128:256], sel0, gc64)
        pv = ps[:, b, 0:256].rearrange("p (d a c) -> p d a c", d=2, a=8, c=16)
        P.copy(glb[:, :, 0:4], pv[:, 0, :, 0:4], eng="dve")
        P.copy(glb[:, :, 4:8], pv[:, 1, :, 4:8], eng="dve")
        P.act(egc, gc, AF.Exp)
        P.tt(kdf, glb, gc, ALU.subtract)
        P.act(kdf, kdf, AF.Exp)
        P.act(glast, glb, AF.Exp)
        P.tt(bege, beta, egc, ALU.mult)

        if "gates" in debug_names and grp == DBG_GRP:
            dump("d_g", gT, [128, 8, 8], F32)
            dump("d_beta", beta, [128, 8, 8], F32)
            dump("d_gc", gc, [128, 8, 8], F32)
            dump("d_glb", glb, [128, 8, 8], F32)
        DB = 65536
        TSTS = [DB, DB + 12288]
        SCN2 = DB + 24576
        SHR2 = SCN2 + 16384
        STF2 = SHR2 + 8192
        STB2 = STF2 + 4096
        OACCA = STB2 + 2048
        assert OACCA + 4096 <= ar.nbytes
        Sf = [ar.view(STF2 + d * 2048, [128, 4, 128], F32) for d in range(2)]
        Sb = [ar.view(STB2 + d * 1024, [128, 4, 128], BF16) for d in range(2)]
        nch = L // 128
        if grp == 0:
            oaccA = ar.view(OACCA, [128, 4, 256], F32)
        else:
            oaccB = ar.t[:, 0:8192].rearrange("p (k t) -> p k t", k=8, t=1024)[:, :, 0:512].rearrange(
                "p (h a) t -> p h a t", h=4, a=2)

        def oslice(c):
            if grp == 0:
                return oaccA[:, :, c * 128:(c + 1) * 128]
            t0_ = c * 128
            return oaccB[:, :, t0_ // 512, t0_ % 512:t0_ % 512 + 128]

        def tv(off, dt):
            return ar.view(off, [128, 4, 128], dt)

        def mask_b(k_):
            return cstb[:, 512 + k_ * 128: 512 + (k_ + 1) * 128]

        def mm4(lh, rh):
            bb = nb()
            for h in range(4):
                P.mm(ps[:, bb, h * 128:(h + 1) * 128], lh[:, h, :], rh[:, h, :])
            return bb

        def step(s_, c, d, first_touch):
            T_ = TSTS[d]
            dec, decT = tv(T_, F32), tv(T_ + 2048, F32)
            G1, G2 = dec, decT
            Lb, LTb = tv(T_ + 4096, BF16), tv(T_ + 5120, BF16)
            Ab, Bb = tv(T_ + 6144, BF16), tv(T_ + 7168, BF16)
            Cb, Db, Eb, Tb = [tv(T_ + 8192 + 1024 * k_, BF16) for k_ in range(4)]
            B2, B3 = Cb, Db
            S_ = SCN2 + d * 8192
            Xb, qkT, QdT, Vb_, Kbe, kdec, negwT, vnew = [tv(S_ + 1024 * k_, BF16) for k_ in range(8)]
            H_ = SHR2 + d * 4096
            Ktm, Vtm, KKs, KQs = [tv(H_ + 1024 * k_, BF16) for k_ in range(4)]
            ti = s_ * nch + c
            tsl = slice(ti * 128, (ti + 1) * 128)
            dsl = slice(d * 4, d * 4 + 4)
            bK = nb()
            for h in range(4):
                P.tr(bbf(bK)[:, h * 128:(h + 1) * 128], kn[:, h, tsl], identB)
            P.copy(Ktm.rearrange("p h t -> p (h t)"), bbf(bK)[:, 0:512], eng="act")
            bV = nb()
            for h in range(4):
                P.tr(bbf(bV)[:, h * 128:(h + 1) * 128], vS[:, h, tsl], identB)
            P.copy(Vtm.rearrange("p h t -> p (h t)"), bbf(bV)[:, 0:512], eng="act")
            yield
            bKK = nb()
            for h in range(4):
                P.mm(ps[:, bKK, h * 128:(h + 1) * 128], kn[:, h, tsl], kn[:, h, tsl])
            P.tt(KKs, b4(bKK), bc_h(offdiag), ALU.mult)
            bKQ = nb()
            for h in range(4):
                P.mm(ps[:, bKQ, h * 128:(h + 1) * 128], kn[:, h, tsl], qn[:, h, tsl])
            P.copy(KQs.rearrange("p h t -> p (h t)"), bank(bKQ), eng="act")
            yield
            U_ = Uf if d == 0 else Ub
            Mdec, MdecT = (MLf, MUf) if d == 0 else (MUf, MLf)
            P.copy(G1, bc_i(gT[:, ti, dsl]), eng="pool")
            P.stt(G2, G1, -1.0, bc_h(U_), ALU.mult, ALU.mult)
            bD = nb()
            P.mm(bank(bD), U_, G1.rearrange("p h t -> p (h t)"), start=True, stop=False)
            P.mm(bank(bD), onesF, G2.rearrange("p h t -> p (h t)"), start=False, stop=True)
            yield
            P.tt(dec, b4(bD), bc_h(Mdec), ALU.add)
            P.stt(decT, b4(bD), -1.0, bc_h(MdecT), ALU.mult, ALU.add)
            P.act(dec, dec, AF.Exp)
            P.act(decT, decT, AF.Exp)
            P.tt(B2, bc_h(identB), bc_i(beta[:, ti, dsl]), ALU.mult, eng="pool")
            P.tt(B3, bc_h(identB), bc_i(egc[:, ti, dsl]), ALU.mult, eng="pool")
            bR = nb()
            P.mm(bank(bR), onesP, B2.rearrange("p h t -> p (h t)"))
            bE = nb()
            P.mm(bank(bE), onesP, B3.rearrange("p h t -> p (h t)"))
            yield
            P.tt(Lb, KKs, dec, ALU.mult)
            P.tt(Lb, Lb, bc_i(beta[:, ti, dsl]), ALU.mult)
            P.tt(LTb, KKs, decT, ALU.mult, eng="pool")
            P.tt(LTb, LTb, b4(bR), ALU.mult)
            P.tt(qkT, KQs, decT, ALU.mult, eng="pool")
            P.tt(QdT, qn[:, :, tsl], b4(bE), ALU.mult)
            P.tt(Vb_, Vtm, bc_i(beta[:, ti, dsl]), ALU.mult, eng="pool")
            P.tt(Kbe, Ktm, bc_i(bege[:, ti, dsl]), ALU.mult, eng="pool")
            P.tt(kdec, Ktm, bc_i(kdf[:, ti, dsl]), ALU.mult, eng="pool")
            yield
            mi = (lambda k: k) if d == 0 else (lambda k: (k + 4) if 1 <= k <= 4 else (k - 4 if k >= 5 else k))
            P.tt(Ab, Lb, bc_h(mask_b(0)), ALU.mult)
            P.tt(Bb, LTb, bc_h(mask_b(0)), ALU.mult)
            P.stt(Tb, Ab, -1.0, bc_h(identF), ALU.mult, ALU.add)
            P.stt(Xb, Bb, -1.0, bc_h(identF), ALU.mult, ALU.add)
            yield
            b1 = mm4(Bb, Ab)
            b2 = mm4(Ab, Bb)
            P.copy(Cb, b4(b1), eng="act")
            P.copy(Db, b4(b2), eng="act")
            yield
            bx = mm4(Cb, Xb)
            bt = mm4(Xb, Cb)
            b3 = mm4(Db, Cb)
            P.tt(Xb, Xb, b4(bx), ALU.add)
            P.tt(Tb, Tb, b4(bt), ALU.add)
            P.copy(Eb, b4(b3), eng="act")
            yield
            bx = mm4(Eb, Xb)
            bt = mm4(Xb, Eb)
            P.tt(Xb, Xb, b4(bx), ALU.add)
            P.tt(Tb, Tb, b4(bt), ALU.add)
            yield
            for lv in range(1, 5):
                last = (lv == 4)
                P.tt(Ab, Lb, bc_h(mask_b(mi(lv))), ALU.mult, eng="pool")
                if not last:
                    P.tt(Bb, LTb, bc_h(mask_b(mi(lv + 4))), ALU.mult)
                b1 = mm4(Ab, Xb)
                if not last:
                    b2 = mm4(Bb, Tb)
                P.copy(Cb, b4(b1), eng="act")
                if not last:
                    P.copy(Db, b4(b2), eng="act")
                yield
                bx = mm4(Tb, Cb)
                if not last:
                    bt = mm4(Xb, Db)
                P.tt(Xb, Xb, b4(bx), ALU.subtract)
                if not last:
                    P.tt(Tb, Tb, b4(bt), ALU.subtract)
                yield
            if "step0" in debug_names and grp == DBG_GRP and s_ == 0 and (c, d) == DBG_STEP:
                dump("d_dec", dec, [128, 4, 128], F32)
                dump("d_decT", decT, [128, 4, 128], F32)
                dump("d_X", Xb, [128, 4, 128], BF16)
                dump("d_qkT", qkT, [128, 4, 128], BF16)
                dump("d_QdT", QdT, [128, 4, 128], BF16)
                dump("d_KKs", KKs, [128, 4, 128], BF16)
            bW = mm4(Kbe, Xb)
            P.act(negwT.rearrange("p h t -> p (h t)"), bank(bW), AF.Copy, scale=-1.0)
            yield
            bVn = nb()
            for h in range(4):
                P.mm(ps[:, bVn, h * 128:(h + 1) * 128], Xb[:, h, :], Vb_[:, h, :], start=True, stop=False)
                P.mm(ps[:, bVn, h * 128:(h + 1) * 128], negwT[:, h, :], Sb[d][:, h, :], start=False, stop=True)
            P.copy(vnew.rearrange("p h t -> p (h t)"), bank(bVn), eng="act")
            yield
            bO = nb()
            for h in range(4):
                P.mm(ps[:, bO, h * 128:(h + 1) * 128], Sb[d][:, h, :], QdT[:, h, :], start=True, stop=False)
                P.mm(ps[:, bO, h * 128:(h + 1) * 128], vnew[:, h, :], qkT[:, h, :], start=False, stop=True)
            bS_ = mm4(kdec, vnew)
            oa = oslice(c if grp == 1 else c)
            if first_touch[c]:
                P.copy(oa, b4(bO), eng="dve")
                first_touch[c] = False
            else:
                P.tt(oa, oa, b4(bO), ALU.add)
            P.tt(Sf[d], Sf[d], bc_i(glast[:, ti, dsl]), ALU.mult)
            P.tt(Sf[d], Sf[d], b4(bS_), ALU.add)
            P.copy(Sb[d], Sf[d], eng="act")
            yield

        for s_ in range(nseq):
            for d in range(2):
                if grp == 0:
                    P.memset(Sf[d], 0.0, eng="pool")
                else:
                    P.dma(Sf[d], state_delta[d].rearrange("h k v -> k h v"), q="sp")
                P.copy(Sb[d], Sf[d], eng="pool")
            first_touch = [True] * nch
            for k in range(nch):
                gens = [step(s_, k, 0, first_touch), step(s_, nch - 1 - k, 1, first_touch)]
                while gens:
                    for g_ in list(gens):
                        try:
                            next(g_)
                        except StopIteration:
                            gens.remove(g_)
            if grp == 0:
                for d in range(2):
                    P.dma(nsd[s_, d].rearrange("h k v -> k h v"), Sf[d], q="sp", is_output=True)
            for h in range(4):
                for t_ in range(0, L, 512):
                    w = min(512, L - t_)
                    sl = slice(s_ * L + t_, s_ * L + t_ + w)
                    src = oaccA[:, h, 0:w] if grp == 0 else oaccB[:, h, t_ // 512, 0:512]
                    sq = ar.view(TSTS[0], [128, 512], BF16)
                    P.act(sq[:, 0:w], src, AF.Square)
                    b = nb()
                    P.mm(ps[:, b, 0:w], onesP, sq[:, 0:w])
                    rs = rstd[:, 0, 0:w]
                    P.act(rs, ps[:, b, 0:w], AF.Ln, bias=epsT[:, 0:1], scale=1.0 / 128.0)
                    P.act(rs, rs, AF.Exp, scale=-0.5)
                    t_f = ar.view(TSTS[0] + 2048, [128, 512], F32)
                    P.stt(t_f[:, 0:w], src, onorm[:, 0:1], rs, ALU.mult, ALU.mult)
                    P.tt(mixT[:, h, sl], t_f[:, 0:w], gS[:, h, sl], ALU.mult)

    def mixer_odd():
        l = 1
        modnorm(l, 1)
        w_in = odd_w_in[0].rearrange("(kc p) n -> p kc n", p=128)
        w_out = odd_w_out[0].rearrange("(kc p) n -> p kc n", p=128)
        onorm2 = ptab_sb[:, PT_ONORMO:PT_ONORMO + 2]
        QT_, KT_, VT_, GS_, LRT_, OACC2 = 32768, 40960, 49152, 65536, 81920, 86016
        WPO = 102400
        ZB_, ZE_ = 102400, 104448
        EG_, ENG_ = ZE_, ZB_
        QTL_, KTL_, QTT_, KTT_, ATB_ = 110592, 111616, 112640, 113664, 114688
        SF_, SB_, WG_ = 115712, 119808, 121856
        SC = 128.0 ** -0.5
        wgp = ar.view(WG_, [128, 2, 512], F32)
        P.dma(wgp[0:33, :, :], wgpad[:, :, :], q="sp")
        wpc = [0]

        def load_piece(c0, c1):
            wp = ar.view(WPO + (wpc[0] % 2) * 8192, [128, KC, 512], BF16)
            wpc[0] += 1
            P.dma(wp[:, :, 0:c1 - c0], w_in[:, :, c0:c1], q="pool")
            return wp

        qT = ar.view(QT_, [128, 8, 512], BF16)
        kT = ar.view(KT_, [128, 8, 512], BF16)
        vT = ar.view(VT_, [128, 8, 1024], BF16)
        gS = ar.view(GS_, [128, 8, 1024], BF16)
        lrT = ar.view(LRT_, [128, 1024], F32)
        glt = gates[:, 0, 0, 0:4]
        for grp in range(2):
            T0 = grp * 1024
            nseq, L = (4, 256) if grp == 0 else (1, 1024)
            nch = L // 128
            mixT = hT[:, :, T0:T0 + 1024]
            wp = load_piece(3072, 3104)
            for tt in range(2):
                b = nb()
                for kc in range(KC):
                    P.mm(ps[0:32, b, :], wp[:, kc, 0:32], hT[:, kc, T0 + tt * TT:T0 + (tt + 1) * TT],
                         start=(kc == 0), stop=(kc == KC - 1))
                P.copy(lrT[0:32, tt * TT:(tt + 1) * TT], ps[0:32, b, :], eng="dve")
            P.memset(lrT[32:33, :], 1.0, eng="dve")
            for (c0, dst, col0, scl) in ((0, qT, 0, SC), (512, kT, 0, 1.0), (1024, vT, 0, 1.0), (1536, vT, 512, 1.0)):
                wp = load_piece(c0, c0 + 512)
                for i in range(8):
                    b = nb()
                    for kc in range(KC):
                        P.mm(bank(b), hT[:, kc, T0 + i * 128:T0 + (i + 1) * 128], wp[:, kc, 0:512],
                             start=(kc == 0), stop=(kc == KC - 1))
                    if i % 2 == 0:
                        P.act(dst[:, i, col0:col0 + 512], bank(b), AF.Copy, scale=scl)
                    else:
                        P.ts(dst[:, i, col0:col0 + 512], bank(b), scl, ALU.mult)
            for pc in range(2):
                wp = load_piece(2048 + pc * 512, 2560 + pc * 512)
                for cc in range(4):
                    b = nbk(2)
                    for tt in range(2):
                        for kc in range(KC):
                            P.mm(bank(b + tt), wp[:, kc, cc * 128:(cc + 1) * 128],
                                 hT[:, kc, T0 + tt * TT:T0 + (tt + 1) * TT], start=(kc == 0), stop=(kc == KC - 1))
                    P.act(gS[:, pc * 4 + cc, :], ps[:, b:b + 2, :].rearrange("p a t -> p (a t)"), AF.Silu)
            if "oinproj" in debug_names and grp == DBG_GRP:
                dump("d_qT", qT, [128, 8, 512], BF16)
                dump("d_kT", kT, [128, 8, 512], BF16)
                dump("d_vT", vT, [128, 8, 1024], BF16)
                dump("d_gS", gS, [128, 8, 1024], BF16)
                dump("d_lrT", lrT[0:33, :], [33, 1024], F32)

            zb = ar.view(ZB_, [128, 512], F32)
            ze = ar.view(ZE_, [128, 512], F32)
            eg = ar.view(EG_, [128, 512], F32)
            eng_ = ar.view(ENG_, [128, 512], F32)
            qtl = ar.view(QTL_, [128, 512], BF16)
            ktl = ar.view(KTL_, [128, 512], BF16)
            qtt = ar.view(QTT_, [128, 4, 128], BF16)
            ktt = ar.view(KTT_, [128, 4, 128], BF16)
            atb = ar.view(ATB_, [128, 4, 128], BF16)
            Sf = ar.view(SF_, [128, 4, 256], F32)
            Sb = ar.view(SB_, [128, 4, 256], BF16)
            if grp == 1:
                oacc_lo = ar.t[:, 0:8192].rearrange("p (k t) -> p k t", k=8, t=1024)[:, :, 0:512]
                oacc_hi = ar.view(OACC2, [128, 8, 512], F32)
            else:
                oaccA = ar.view(OACC2, [128, 8, 256], F32)

            def oslice(c, fc0, fc1):
                if grp == 0:
                    return oaccA[:, fc0:fc1, c * 128:(c + 1) * 128]
                if c < 4:
                    return oacc_lo[:, fc0:fc1, c * 128:(c + 1) * 128]
                return oacc_hi[:, fc0:fc1, (c - 4) * 128:(c - 3) * 128]

            bufsets = []
            for k_ in range(2):
                if k_ == 0:
                    offs = (QTL_, KTL_, QTT_, KTT_, ATB_)
                else:
                    offs = (106496, 107520, 108544, 109568, 125952)
                bufsets.append((ar.view(offs[0], [128, 512], BF16), ar.view(offs[1], [128, 512], BF16),
                                ar.view(offs[2], [128, 4, 128], BF16), ar.view(offs[3], [128, 4, 128], BF16),
                                ar.view(offs[4], [128, 4, 128], BF16), gates[:, 0, k_, 0:4]))

            def prefix(s_, d, c, bs_):
                qtl, ktl, qtt, ktt, atb, glt = bs_
                U_ = Uf if d == 0 else Ub
                elast = identF[:, 127:128] if d == 0 else identF[:, 0:1]
                ti = s_ * nch + c
                tsl = slice(ti * 128, (ti + 1) * 128)
                bz = nb()
                P.mm(bank(bz), lrT[0:33, tsl], wgp[0:33, d, :])
                yield
                P.ts(ze, bank(bz), -80.0, ALU.max)
                P.act(ze, ze, AF.Exp, scale=-1.0)
                yield
                P.act(zb, ze, AF.Ln, bias=onesF[:, 0:1])
                yield
                bg = nb()
                P.mm(bank(bg), U_, zb)
                yield
                P.act(eg, bank(bg), AF.Exp, scale=-1.0 / 16.0)
                P.act(eng_, bank(bg), AF.Exp, scale=1.0 / 16.0)
                P.tt(qtl, qT[:, ti, :], eg, ALU.mult)
                P.tt(ktl, kT[:, ti, :], eng_, ALU.mult)
                yield
                bl = nb()
                for h in range(4):
                    P.mm(ps[:, bl, h:h + 1], eg[:, h * 128:(h + 1) * 128], elast)
                P.copy(glt, ps[:, bl, 0:4], eng="dve")
                bq = nb()
                for h in range(4):
                    P.tr(bbf(bq)[:, h * 128:(h + 1) * 128], qtl[:, h * 128:(h + 1) * 128], identB)
                P.copy(qtt.rearrange("p h t -> p (h t)"), bbf(bq)[:, 0:512], eng="act")
                bk = nb()
                for h in range(4):
                    P.tr(bbf(bk)[:, h * 128:(h + 1) * 128], ktl[:, h * 128:(h + 1) * 128], identB)
                P.copy(ktt.rearrange("p h t -> p (h t)"), bbf(bk)[:, 0:512], eng="dve")
                yield
                ba = nb()
                for h in range(4):
                    P.mm(ps[:, ba, h * 128:(h + 1) * 128], ktt[:, h, :], qtt[:, h, :])
                P.tt(atb, b4(ba), bc_h(U_), ALU.mult)
                yield

            def suffix(s_, d, c, bs_, first_touch):
                qtl, ktl, qtt, ktt, atb, glt = bs_
                ti = s_ * nch + c
                bo = nbk(2)
                for h in range(4):
                    for half in range(2):
                        fc = h * 2 + half
                        dstp = ps[:, bo + fc // 4, (fc % 4) * 128:(fc % 4 + 1) * 128]
                        P.mm(dstp, vT[:, ti, h * 256 + half * 128:h * 256 + (half + 1) * 128], atb[:, h, :],
                             start=True, stop=False)
                        P.mm(dstp, Sb[:, h, half * 128:(half + 1) * 128], qtt[:, h, :], start=False, stop=True)
                for k2 in range(2):
                    oa = oslice(c, k2 * 4, k2 * 4 + 4)
                    if first_touch[c]:
                        P.copy(oa, b4(bo + k2), eng="dve" if k2 == 0 else "act")
                    else:
                        P.tt(oa, oa, b4(bo + k2), ALU.add)
                first_touch[c] = False
                yield
                bs2 = nbk(2)
                for h in range(4):
                    P.mm(ps[:, bs2 + h // 2, (h % 2) * 256:(h % 2 + 1) * 256], ktl[:, h * 128:(h + 1) * 128],
                         vT[:, ti, h * 256:(h + 1) * 256])
                P.tt(Sf, Sf, ps[:, bs2:bs2 + 2, :].rearrange("p a (h v) -> p (a h) v", h=2, v=256), ALU.add)
                P.tt(Sf, Sf, glt.unsqueeze(2).broadcast_to([128, 4, 256]), ALU.mult)
                P.copy(Sb, Sf, eng="act")
                yield

            for s_ in range(nseq):
                first_touch = [True] * nch
                steps = [(d, (cidx if d == 0 else nch - 1 - cidx)) for d in range(2) for cidx in range(nch)]
                def drive(gens):
                    gens = [g_ for g_ in gens if g_ is not None]
                    while gens:
                        for g_ in list(gens):
                            try:
                                next(g_)
                            except StopIteration:
                                gens.remove(g_)

                drive([prefix(s_, steps[0][0], steps[0][1], bufsets[0])])
                for k_, (d, c) in enumerate(steps):
                    if k_ % nch == 0:
                        if grp == 0:
                            P.memset(Sf, 0.0, eng="pool")
                        else:
                            P.dma(Sf, state_gla[d].rearrange("h k v -> k h v"), q="sp")
                        P.copy(Sb, Sf, eng="pool")
                    nxt = None
                    if k_ + 1 < len(steps):
                        nxt = prefix(s_, steps[k_ + 1][0], steps[k_ + 1][1], bufsets[(k_ + 1) % 2])
                    drive([suffix(s_, d, c, bufsets[k_ % 2], first_touch), nxt])
                    if k_ % nch == nch - 1 and grp == 0:
                        P.dma(nsg[s_, d].rearrange("h k v -> k h v"), Sf, q="sp", is_output=True)
                for h in range(4):
                    for t_ in range(0, L, 512):
                        w = min(512, L - t_)
                        b = nb()
                        srcs = []
                        for half in range(2):
                            fc = h * 2 + half
                            if grp == 0:
                                src = oaccA[:, fc, t_:t_ + w]
                            else:
                                src = (oacc_lo if t_ == 0 else oacc_hi)[:, fc, 0:512]
                            srcs.append(src)
                            sq = ar.view(ZB_ + half * 1024, [128, 512], BF16)
                            P.act(sq[:, 0:w], src, AF.Square)
                            P.mm(ps[:, b, 0:w], onesP, sq[:, 0:w], start=(half == 0), stop=(half == 1))
                        rs = rstd[:, 0, 0:w]
                        P.act(rs, ps[:, b, 0:w], AF.Ln, bias=epsT[:, 0:1], scale=1.0 / 256.0)
                        P.act(rs, rs, AF.Exp, scale=-0.5)
                        for half in range(2):
                            fc = h * 2 + half
                            t_f = ar.view(EG_, [128, 512], F32)
                            P.stt(t_f[:, 0:w], srcs[half], onorm2[:, half:half + 1], rs, ALU.mult, ALU.mult)
                            sl = slice(s_ * L + t_, s_ * L + t_ + w)
                            P.tt(mixT[:, fc, sl], t_f[:, 0:w], gS[:, fc, sl], ALU.mult)
            if "gla" in debug_names and grp == DBG_GRP:
                dump("d_mix", mixT, [128, 8, 1024], BF16)
            wo = ar.view(WPO, [128, KC, 1024], BF16)
            P.dma(wo, w_out[:, :, :], q="pool")
            for n in range(KC):
                for tt in range(2):
                    b = nb()
                    for kc in range(KC):
                        P.mm(bank(b), wo[:, kc, n * 128:(n + 1) * 128], mixT[:, kc, tt * TT:(tt + 1) * TT],
                             start=(kc == 0), stop=(kc == KC - 1))
                    xs = xT[:, n, T0 + tt * TT:T0 + (tt + 1) * TT]
                    P.stt(xs, bank(b), modG[:, l, 1, n, grp:grp + 1], xs, ALU.mult, ALU.add)


    if stage != "all":
        ada_flush()
    if stage == "ffn0":
        ffn(0, 0)
        final_out()
    elif stage == "mix0":
        mixer_even()
        final_out()
    elif stage == "mix1":
        mixer_odd()
        final_out()
    else:
        def pre(l_, s_):
            def f(tt):
                modnorm_tile(l_, s_, tt)
                if tt == NTT - 1:
                    pre_normed[0] = (l_, s_)
            return f
        ffn(0, 0, after_tile=pre(0, 1))
        mixer_even()
        ffn(0, 1, after_tile=pre(1, 0))
        ffn(1, 0, after_tile=pre(1, 1))
        mixer_odd()
        ffn(1, 1, after_tile=final_tile)

    P.emit()
    stack.close()
    return nc, dbg


PT_ADAB = 0
PT_NORMG = PT_ADAB + 144
PT_FINALG = PT_NORMG + 48
PT_CONV = PT_FINALG + 8
PT_SINK = PT_CONV + 60
PT_ALOG = PT_SINK + 4
PT_DTB = PT_ALOG + 8
PT_ONORME = PT_DTB + 8
PT_ONORMO = PT_ONORME + 1
PT_COLS = PT_ONORMO + 2
DBG_GRP = 0
DBG_STEP = (0, 0)

C_IDENT = 0
C_UF, C_UB, C_SEL127, C_SEL0, C_ONES, C_OFFD, C_ML, C_MU, C_RM = [128 * i for i in range(1, 10)]
CST_COLS = 128 * 10
CSTB_COLS = 512 + 9 * 128


def _fm(v):
    v = np.asarray(v, np.float32)
    return np.ascontiguousarray(v.reshape(-1, 128).T)


def make_tables(inputs):
    pt = np.zeros((128, PT_COLS), np.float32)
    for l in range(2):
        pt[:, PT_ADAB + l * 72: PT_ADAB + (l + 1) * 72] = _fm(inputs["ada_b"][l])
        for s in range(3):
            o = PT_NORMG + (l * 3 + s) * 8
            pt[:, o:o + 8] = _fm(inputs["norm_g"][l, s])
    pt[:, PT_FINALG:PT_FINALG + 8] = _fm(inputs["final_g"])
    cv = np.asarray(inputs["even_conv"], np.float32)[0]
    pt[:, PT_CONV:PT_CONV + 60] = cv.T.reshape(12, 128, 5).transpose(1, 0, 2).reshape(128, 60)
    pt[:, PT_SINK:PT_SINK + 4] = np.broadcast_to(np.asarray(inputs["even_sink"], np.float32)[0][None, :], (128, 4))
    pt[:, PT_ALOG:PT_ALOG + 8] = np.broadcast_to(np.asarray(inputs["even_a_log"], np.float32)[0].reshape(1, 8), (128, 8))
    pt[:, PT_DTB:PT_DTB + 8] = np.broadcast_to(np.asarray(inputs["even_dt_bias"], np.float32)[0].reshape(1, 8), (128, 8))
    pt[:, PT_ONORME:PT_ONORME + 1] = np.asarray(inputs["even_onorm"], np.float32)[0].reshape(128, 1)
    pt[:, PT_ONORMO:PT_ONORMO + 2] = _fm(inputs["odd_onorm"][0])
    cst = np.zeros((128, CST_COLS), np.float32)
    cst[:, C_IDENT:C_IDENT + 128] = np.eye(128, dtype=np.float32)
    kk, ii = np.meshgrid(np.arange(128), np.arange(128), indexing="ij")
    NEG = -30000.0
    cst[:, C_UF:C_UF + 128] = (kk <= ii)
    cst[:, C_UB:C_UB + 128] = (kk >= ii)
    cst[127, C_SEL127:C_SEL127 + 128] = 1.0
    cst[0, C_SEL0:C_SEL0 + 128] = 1.0
    cst[:, C_ONES:C_ONES + 128] = 1.0
    cst[:, C_OFFD:C_OFFD + 128] = (kk != ii)
    cst[:, C_ML:C_ML + 128] = np.where(ii <= kk, 0.0, NEG)
    cst[:, C_MU:C_MU + 128] = np.where(ii >= kk, 0.0, NEG)
    rm = np.zeros((128, 128), np.float32)
    for dp in range(128):
        if (dp % 64) < 32:
            rm[dp + 32, dp] = -1.0
        else:
            rm[dp - 32, dp] = 1.0
    cst[:, C_RM:C_RM + 128] = rm
    return pt, cst


def make_bmask():
    i, j = np.meshgrid(np.arange(128), np.arange(128), indexing="ij")
    ms = [(i // 8 == j // 8)]
    for b in (8, 16, 32, 64):
        ms.append((i // (2 * b) == j // (2 * b)) & ((i // b) % 2 == 1) & ((j // b) % 2 == 0))
    for b in (8, 16, 32, 64):
        ms.append((i // (2 * b) == j // (2 * b)) & ((i // b) % 2 == 0) & ((j // b) % 2 == 1))
    return np.ascontiguousarray(np.concatenate([m.astype(np.float32) for m in ms], axis=1))


def make_rope():
    t = np.arange(1024)
    row = (t // 64).astype(np.float64)
    col = (t % 64).astype(np.float64)
    inv = 10000.0 ** (-np.arange(32, dtype=np.float64) / 32.0)
    ang = np.zeros((128, 1024))
    for d in range(128):
        pos = row if d < 64 else col
        ang[d] = pos * np.float32(inv[d % 32])
    ang32 = np.zeros((128, 1024), np.float32)
    inv32 = (np.float32(10000.0) ** (-np.arange(32, dtype=np.float32) / np.float32(32))).astype(np.float32)
    for d in range(128):
        pos = (row if d < 64 else col).astype(np.float32)
        ang32[d] = pos * inv32[d % 32]
    return np.ascontiguousarray(np.stack([np.cos(ang32), np.sin(ang32)], axis=1).astype(np.float32))


def make_in_maps(inputs, stage="all"):
    pt, cst = make_tables(inputs)
    rope_t = make_rope()
    bmask_t = make_bmask()
    wg = np.asarray(inputs["odd_w_gate"], np.float32)[0]
    wgpad_t = np.zeros((33, 2, 512), np.float32)
    wgpad_t[0:16, 0, :] = wg[0]
    wgpad_t[16:32, 1, :] = wg[1]
    wgpad_t[32, :, :] = np.asarray(inputs["odd_gate_bias"], np.float32)[0]
    maps = []
    xp = np.asarray(inputs["x_prompt"], np.float32)
    xs = np.asarray(inputs["x_sample"], np.float32)
    for c in range(8):
        xin = np.concatenate([xp[4 * c:4 * c + 4].reshape(1024, D), xs[c]], axis=0)
        cond = np.stack([np.asarray(inputs["c_ctx"], np.float32), np.asarray(inputs["c"], np.float32)[c]], axis=-1)
        condT = np.ascontiguousarray(cond.reshape(KC, 128, 2).transpose(1, 0, 2))
        m = {"xin": np.ascontiguousarray(xin), "condT": condT, "ptab": pt, "cst": cst,
             "ada_w": np.asarray(inputs["ada_w"], np.float32),
             "even_w_in": np.asarray(inputs["even_w_in"], np.float32),
             "even_w_out": np.asarray(inputs["even_w_out"], np.float32),
             "rope": rope_t, "bmask": bmask_t, "wgpad": wgpad_t,
             "odd_w_in": np.asarray(inputs["odd_w_in"], np.float32),
             "odd_w_out": np.asarray(inputs["odd_w_out"], np.float32),
             "state_gla": np.ascontiguousarray(np.asarray(inputs["state_gla"], np.float32)[c, 0]),
             "cache_k": np.ascontiguousarray(np.asarray(inputs["cache_k"], np.float32)[c, 0]),
             "cache_v": np.ascontiguousarray(np.asarray(inputs["cache_v"], np.float32)[c, 0]),
             "state_delta": np.ascontiguousarray(np.asarray(inputs["state_delta"], np.float32)[c, 0])}
        if stage in ("all", "ffn0"):
            m["ffn_w_gu"] = np.asarray(inputs["ffn_w_gu"], np.float32)
            m["ffn_w_down"] = np.asarray(inputs["ffn_w_down"], np.float32)
        maps.append(m)
    return maps


_CACHE = {}


def kernel(**inputs):
    if "nc" not in _CACHE:
        _CACHE["nc"] = build_program("all")[0]
    nc = _CACHE["nc"]
    maps = make_in_maps(inputs)
    res = run_bass_kernel_spmd(nc, maps, core_ids=list(range(8)))
    rs = res.results
    ys = [np.asarray(r["yout"], np.float32) for r in rs]
    y_prompt = np.concatenate([y[:1024].reshape(4, 256, D) for y in ys], axis=0)
    y_sample = np.stack([y[1024:] for y in ys], axis=0)
    nsd = np.concatenate([np.asarray(r["nsd"], np.float32) for r in rs], axis=0)[:, None]
    nck = np.concatenate([np.asarray(r["nck"], np.float32).reshape(4, 256, 2, 128) for r in rs], axis=0)[:, None]
    ncv = np.concatenate([np.asarray(r["ncv"], np.float32).reshape(4, 256, 2, 128) for r in rs], axis=0)[:, None]
    nsg = np.concatenate([np.asarray(r["nsg"], np.float32) for r in rs], axis=0)[:, None]
    return (y_prompt, y_sample, np.ascontiguousarray(nsd), np.ascontiguousarray(nck),
            np.ascontiguousarray(ncv), np.ascontiguousarray(nsg))
```

```python
import numpy as np
import concourse.bass as bass
import concourse.mybir as mybir

F32 = mybir.dt.float32
BF16 = mybir.dt.bfloat16
AF = mybir.ActivationFunctionType
ALU = mybir.AluOpType
AX = mybir.AxisListType

_DTSZ = {F32: 4, BF16: 2}


def _region(ap):
    sp = str(ap.space)
    if "DRAM" in sp.upper() or "HBM" in sp.upper():
        return None
    sz = _DTSZ[ap.dtype]
    pat = ap.ap
    pstep, pcnt = pat[0]
    off = int(ap.offset)
    if pstep == 0:
        p0, f0 = 0, off
        pstep = 1 << 40
    else:
        p0, f0 = off // pstep, off % pstep
    ext = 1
    for st, cnt in pat[1:]:
        ext += (cnt - 1) * abs(st)
    b0, b1 = f0 * sz, (f0 + ext) * sz
    if "PSUM" in sp.upper():
        b0 = (b0 // 2048) * 2048
        b1 = ((b1 + 2047) // 2048) * 2048
        return (ap.tensor.name, 0, 128, b0, b1)
    return (ap.tensor.name, p0, p0 + pcnt, b0, b1)


def _ovl(a, b):
    return a[0] == b[0] and a[1] < b[2] and b[1] < a[2] and a[3] < b[4] and b[3] < a[4]


def _covers(a, b):
    return a[0] == b[0] and a[1] <= b[1] and a[2] >= b[2] and a[3] <= b[3] and a[4] >= b[4]


class Op:
    __slots__ = ("eng", "fn", "seq", "inc", "waits", "ctr", "is_dma", "val")

    def __init__(self, eng, fn, is_dma=False):
        self.eng = eng
        self.fn = fn
        self.inc = False
        self.waits = []
        self.is_dma = is_dma
        self.ctr = None
        self.seq = 0
        self.val = 0


ENGS = ("pe", "act", "dve", "pool", "sp")
NDMA = 24


class Prog:
    def __init__(self, nc):
        self.nc = nc
        self.streams = {e: [] for e in ENGS}
        self.seqc = {}
        self.known = {e: {} for e in ENGS}
        self.recs = {}
        self.dma_rr = {"h": 0, "s": 0}
        self.dma_last = {}
        self.nops = 0
        self.out_dmas = []

    def _need(self, op, dep):
        if dep is op:
            return
        if dep.ctr == op.ctr and not dep.is_dma:
            pass
        k = self.known[op.eng]
        if k.get(dep.ctr, -1) >= dep.seq:
            return
        k[dep.ctr] = dep.seq
        dep.inc = True
        op.waits.append(dep)

    BK = 2048

    def _buckets(self, r):
        return range(r[3] // self.BK, (r[4] - 1) // self.BK + 1)

    def _track(self, op, reads, writes):
        BK = self.BK
        for ap in reads:
            r = _region(ap)
            if r is None:
                continue
            for b in self._buckets(r):
                lst = self.recs.setdefault((r[0], b), [])
                is_ps = (r[0] == "ps")
                for rec in lst:
                    if _ovl(rec[0], r) and (rec[1] == "w" or (is_ps and rec[2].ctr != op.ctr)):
                        d = rec[2]
                        if d.ctr == op.ctr and op.eng == "pe":
                            continue
                        self._need(op, d)
                for i, rec in enumerate(lst):
                    if rec[1] == "r" and rec[2].ctr == op.ctr and rec[0] == r:
                        lst.pop(i)
                        break
                lst.append([r, "r", op])
        for ap in writes:
            r = _region(ap)
            if r is None:
                continue
            for b in self._buckets(r):
                lst = self.recs.setdefault((r[0], b), [])
                keep = []
                lo, hi = b * BK, (b + 1) * BK
                for rec in lst:
                    if rec[2] is op:
                        keep.append(rec)
                        continue
                    rr = rec[0]
                    if _ovl(rr, r):
                        d = rec[2]
                        if not (d.ctr == op.ctr and op.eng == "pe"):
                            self._need(op, d)
                        if (r[1] <= rr[1] and r[2] >= rr[2]
                                and r[3] <= max(rr[3], lo) and r[4] >= min(rr[4], hi)):
                            continue
                    keep.append(rec)
                keep.append([r, "w", op])
                self.recs[(r[0], b)] = keep

    def _add(self, eng, fn, reads, writes):
        op = Op(eng, fn)
        op.ctr = eng
        op.seq = self.seqc.get(eng, 0)
        self.seqc[eng] = op.seq + 1
        self._track(op, reads, writes)
        self.streams[eng].append(op)
        self.nops += 1
        return op

    def dma(self, out, in_, q="sp", is_output=False):
        op = Op(q, None, is_dma=True)
        kind = "s" if q == "pool" else "h"
        k = self.dma_rr[kind]
        self.dma_rr[kind] = (k + 1) % (NDMA // 2)
        op.ctr = "dma%s%d" % (kind, k)
        op.seq = self.seqc.get(op.ctr, 0)
        self.seqc[op.ctr] = op.seq + 1
        prev = self.dma_last.get(op.ctr)
        if prev is not None:
            self._need(op, prev)
        self.dma_last[op.ctr] = op
        op.inc = True
        self._track(op, [in_], [out])
        op.fn = lambda e, out=out, in_=in_: e.dma_start(out=out, in_=in_)
        self.streams[q].append(op)
        if is_output:
            self.out_dmas.append(op)
        return op

    def mm(self, out, lhsT, rhs, start=True, stop=True):
        return self._add("pe", lambda e: e.matmul(out, lhsT, rhs, start=start, stop=stop),
                         [lhsT, rhs], [out])

    def tr(self, out, in_, ident):
        return self._add("pe", lambda e: e.transpose(out, in_, ident), [in_, ident], [out])

    def act(self, out, in_, func, bias=None, scale=1.0, accum=None):
        rd = [in_]
        if bias is not None and not isinstance(bias, (int, float)):
            rd.append(bias)
        if not isinstance(scale, (int, float)):
            rd.append(scale)
        wr = [out] + ([accum] if accum is not None else [])
        kw = {}
        if bias is not None:
            kw["bias"] = bias
        if accum is not None:
            kw["accum_out"] = accum
        return self._add("act", lambda e: e.activation(out=out, in_=in_, func=func, scale=scale, **kw),
                         rd, wr)

    def tt(self, out, in0, in1, op, eng="dve"):
        return self._add(eng, lambda e: e.tensor_tensor(out=out, in0=in0, in1=in1, op=op),
                         [in0, in1], [out])

    def ts(self, out, in0, s1, op0, s2=None, op1=None, eng="dve", accum=None):
        rd = [in0] + [s for s in (s1, s2) if s is not None and not isinstance(s, (int, float))]
        kw = {}
        if op1 is not None:
            kw["op1"] = op1
        if accum is not None:
            kw["accum_out"] = accum
        wr = [out] + ([accum] if accum is not None else [])
        return self._add(eng, lambda e: e.tensor_scalar(out=out, in0=in0, scalar1=s1, scalar2=s2, op0=op0, **kw),
                         rd, wr)

    def stt(self, out, in0, scalar, in1, op0, op1, eng="dve"):
        rd = [in0, in1] + ([scalar] if not isinstance(scalar, (int, float)) else [])
        return self._add(eng, lambda e: e.scalar_tensor_tensor(out=out, in0=in0, scalar=scalar, in1=in1,
                                                               op0=op0, op1=op1), rd, [out])

    def copy(self, out, in_, eng="dve"):
        if eng == "act":
            return self.act(out, in_, AF.Copy)
        return self._add(eng, lambda e: e.tensor_copy(out=out, in_=in_), [in_], [out])

    def reduce(self, out, in_, op, eng="dve", axis=None):
        axis = axis or AX.X
        return self._add(eng, lambda e: e.tensor_reduce(out=out, in_=in_, axis=axis, op=op), [in_], [out])

    def recip(self, out, in_):
        return self._add("dve", lambda e: e.reciprocal(out=out, in_=in_), [in_], [out])

    def memset(self, ap, val, eng="dve"):
        return self._add(eng, lambda e: e.memset(ap, val), [], [ap])

    def emit(self):
        nc = self.nc
        sems = {}
        import contextlib
        stack = contextlib.ExitStack()
        allops = []
        for e in ENGS:
            allops.extend(self.streams[e])
        ctrs = sorted(set(o.ctr for o in allops))
        for c in ctrs:
            sems[c] = stack.enter_context(nc.semaphore("s_" + c))
        cnt = {c: 0 for c in ctrs}
        byctr = {c: [] for c in ctrs}
        for o in allops:
            byctr[o.ctr].append(o)
        for c in ctrs:
            ops = sorted(byctr[c], key=lambda o: o.seq)
            v = 0
            for o in ops:
                if o.inc:
                    v += 16 if o.is_dma else 1
                o.val = v
        fin = stack.enter_context(nc.semaphore("s_fin"))
        block = stack.enter_context(nc.Block())
        prog = self

        def run_stream(eng_name, e):
            for o in prog.streams[eng_name]:
                for d in o.waits:
                    e.wait_ge(sems[d.ctr], d.val)
                ins = o.fn(e)
                if o.inc:
                    ins.then_inc(sems[o.ctr], 16 if o.is_dma else 1)

        @block.tensor
        def _(e):
            run_stream("pe", e)

        @block.scalar
        def _(e):
            run_stream("act", e)

        @block.vector
        def _(e):
            run_stream("dve", e)

        @block.gpsimd
        def _(e):
            run_stream("pool", e)

        @block.sync
        def _(e):
            run_stream("sp", e)
            for o in prog.out_dmas:
                e.wait_ge(sems[o.ctr], o.val)

        stack.close()
from concourse.bass_utils import run_bass_kernel_spmd

D = 1024
NTOK = 2048
KC = 8
TT = 512
NTT = 4
DFF = 2816
NHC = 22
EPS = 1e-6
EVEN_IN = 3088
ODD_IN = 3104


class Arena:
    def __init__(self, nc, name, nbytes, stack):
        self.t = stack.enter_context(nc.sbuf_tensor(name, [128, nbytes // 4], F32))
        self.nbytes = nbytes

    def view(self, off, shape, dtype):
        sz = 4 if dtype == F32 else 2
        n = 1
        for s in shape[1:]:
            n *= s
        assert off % 4 == 0 and (n * sz) % 4 == 0 and off + n * sz <= self.nbytes, (off, shape, self.nbytes)
        ap = self.t[0:shape[0], off // 4: off // 4 + (n * sz) // 4]
        if dtype != F32:
            ap = ap.bitcast(dtype)
        if len(shape) == 3:
            ap = ap.rearrange("p (a b) -> p a b", a=shape[1], b=shape[2])
        elif len(shape) == 4:
            ap = ap.rearrange("p (a b c) -> p a b c", a=shape[1], b=shape[2], c=shape[3])
        return ap


def build_program(stage="all", debug_names=()):
    import contextlib
    nc = bass.Bass("TRN2", target_bir_lowering=False)
    P = Prog(nc)
    stack = contextlib.ExitStack()

    def din(name, shape, dt=F32):
        return nc.dram_tensor(name, list(shape), dt, kind="ExternalInput").ap()

    def dout(name, shape, dt=F32):
        return nc.dram_tensor(name, list(shape), dt, kind="ExternalOutput").ap()

    xin = din("xin", [NTOK, D])
    condT = din("condT", [128, KC, 2])
    ptab = din("ptab", [128, PT_COLS])
    cst = din("cst", [128, CST_COLS])
    ada_w = din("ada_w", [2, D, 9 * D])
    if stage in ("all", "ffn0"):
        w_gu = din("ffn_w_gu", [2, 2, D, 2 * DFF])
        w_dn = din("ffn_w_down", [2, 2, DFF, D])
    yout = dout("yout", [NTOK, D])
    even_w_in = din("even_w_in", [1, D, EVEN_IN])
    even_w_out = din("even_w_out", [1, D, D])
    rope = din("rope", [128, 2, 1024])
    bmask = din("bmask", [128, 9 * 128])
    odd_w_in = din("odd_w_in", [1, D, ODD_IN])
    odd_w_out = din("odd_w_out", [1, D, D])
    wgpad = din("wgpad", [33, 2, 512])
    state_gla = din("state_gla", [2, 4, 128, 256])
    nsg = dout("nsg", [4, 2, 4, 128, 256])
    cache_k = din("cache_k", [512, 2, 128])
    cache_v = din("cache_v", [512, 2, 128])
    state_delta = din("state_delta", [2, 4, 128, 128])
    nck = dout("nck", [1024, 256])
    ncv = dout("ncv", [1024, 256])
    nsd = dout("nsd", [4, 2, 4, 128, 128])

    dbg = {}
    xT = stack.enter_context(nc.sbuf_tensor("xT", [128, KC, NTOK], F32))
    ptab_sb = stack.enter_context(nc.sbuf_tensor("ptab_sb", [128, PT_COLS], F32))
    cst_sb = stack.enter_context(nc.sbuf_tensor("cst_sb", [128, CST_COLS], F32))
    cstb = stack.enter_context(nc.sbuf_tensor("cstb", [128, CSTB_COLS], BF16))
    modT = stack.enter_context(nc.sbuf_tensor("modT", [128, 2, 72, 2], F32))
    modA = stack.enter_context(nc.sbuf_tensor("modA", [128, 2, 3, KC, 2], F32))
    modG = stack.enter_context(nc.sbuf_tensor("modG", [128, 2, 3, KC, 2], F32))
    scT = stack.enter_context(nc.sbuf_tensor("scT", [128, KC, 2], BF16))
    condsb = stack.enter_context(nc.sbuf_tensor("condsb", [128, KC, 2], F32))
    gates = stack.enter_context(nc.sbuf_tensor("gates", [128, 8, 8, 16], F32))
    epsT = stack.enter_context(nc.sbuf_tensor("epsT", [128, 2], F32))
    rstd = stack.enter_context(nc.sbuf_tensor("rstd", [128, 2, TT], F32))
    ar = Arena(nc, "arena", 126976, stack)
    ps = stack.enter_context(nc.psum_tensor("ps", [128, 8, 512], F32))

    def bank(b):
        return ps[:, b, :]

    identF = cst_sb[:, C_IDENT:C_IDENT + 128]
    identB = cstb[:, 0:128]
    onesP = cstb[:, 128:256]

    P.dma(ptab_sb[:, :], ptab[:, :], q="sp")
    P.dma(cst_sb[:, :], cst[:, :], q="sp")
    P.dma(condsb[:, :, :], condT[:, :, :], q="sp")
    P.copy(identB, identF, eng="dve")
    P.memset(onesP, 1.0, eng="dve")
    P.memset(epsT[:, :], EPS, eng="dve")
    P.memset(gates[:, :, :, :], 0.0, eng="pool")

    STG = 32768
    for i in range(16):
        stg = ar.view(STG + (i % 2) * 4096, [128, D], F32)
        P.dma(stg, xin[i * 128:(i + 1) * 128, :], q="sp" if i % 2 == 0 else "act")
        for half in range(2):
            b = (i * 2 + half) % 4
            for kk in range(4):
                kc = half * 4 + kk
                P.tr(ps[:, b, kk * 128:(kk + 1) * 128], stg[:, kc * 128:(kc + 1) * 128], identF)
            src = ps[:, b, :].rearrange("p (a t) -> p a t", a=4, t=128)
            dst = xT[:, half * 4:half * 4 + 4, i * 128:(i + 1) * 128]
            if half == 0:
                P.copy(dst, src, eng="dve")
            else:
                P.copy(dst, src, eng="act")

    P.act(scT[:, :, :], condsb[:, :, :], AF.Silu)
    ADAW = 40960

    def ada_finish(l, s):
        ng = ptab_sb[:, PT_NORMG + (l * 3 + s) * 8: PT_NORMG + (l * 3 + s + 1) * 8]
        sc = modT[:, l, (3 * s + 1) * 8:(3 * s + 2) * 8, :]
        P.stt(modA[:, l, s, :, :], sc, 1.0, ng.unsqueeze(2).broadcast_to([128, KC, 2]), ALU.add, ALU.mult)
        gt = modT[:, l, (3 * s + 2) * 8:(3 * s + 3) * 8, :]
        P.ts(modG[:, l, s, :, :], gt, 0.5 if s != 1 else 1.0, ALU.mult)

    for i in range(3):
        wb = ar.view(ADAW + (i % 3) * 16384, [128, KC, 1024], BF16)
        src = ada_w[0].rearrange("(kc p) n -> p kc n", p=128)[:, :, i * 1024:(i + 1) * 1024]
        P.dma(wb, src, q="pool")
        for n in range(8):
            j = i * 8 + n
            for kc in range(KC):
                P.mm(ps[:, 4, 2 * j:2 * j + 2], wb[:, kc, n * 128:(n + 1) * 128], scT[:, kc, :],
                     start=(kc == 0), stop=(kc == KC - 1))
    P.tt(modT[:, 0, 0:24, :], ps[:, 4, 0:48].rearrange("p (j c) -> p j c", j=24, c=2),
         ptab_sb[:, PT_ADAB:PT_ADAB + 24].unsqueeze(2).broadcast_to([128, 24, 2]), ALU.add)
    ada_finish(0, 0)
    ada_tasks = [(0, i, q4) for i in range(3, 9) for q4 in range(4)] + \
                [(1, i, q4) for i in range(9) for q4 in range(4)]
    ada_cnt = [0, 0, 0]

    ada_pending = []

    def ada_load():
        l, i, q4 = ada_tasks.pop(0)
        wb = ar.view(TMP + (ada_cnt[0] % 2) * 4096, [128, KC, 256], BF16)
        ada_cnt[0] += 1
        c0 = i * 1024 + q4 * 256
        P.dma(wb, ada_w[l].rearrange("(kc p) n -> p kc n", p=128)[:, :, c0:c0 + 256], q="pool")
        ada_pending.append((l, i, q4, wb))

    def ada_compute():
        l, i, q4, wb = ada_pending.pop(0)
        bank_b = 6 + (ada_cnt[1] % 2)
        ada_cnt[1] += 1
        for nn in range(2):
            for kc in range(KC):
                P.mm(ps[:, bank_b, 2 * nn:2 * nn + 2], wb[:, kc, nn * 128:(nn + 1) * 128], scT[:, kc, :],
                     start=(kc == 0), stop=(kc == KC - 1))
        j0 = i * 8 + q4 * 2
        P.tt(modT[:, l, j0:j0 + 2, :], ps[:, bank_b, 0:4].rearrange("p (j c) -> p j c", j=2, c=2),
             ptab_sb[:, PT_ADAB + l * 72 + j0:PT_ADAB + l * 72 + j0 + 2].unsqueeze(2).broadcast_to([128, 2, 2]),
             ALU.add)
        if q4 == 3 and i % 3 == 2:
            ada_finish(l, i // 3)

    def ada_tick():
        ada_cnt[2] += 1
        if ada_cnt[2] % 3 != 0:
            return
        if len(ada_pending) == 2 or (ada_pending and not ada_tasks):
            ada_compute()
        if ada_tasks and len(ada_pending) < 2:
            ada_load()

    def ada_flush():
        while ada_tasks or ada_pending:
            if ada_tasks and len(ada_pending) < 2:
                ada_load()
            else:
                ada_compute()

    HT = 0
    FW = 32768
    ACTB = FW + 49152
    SG = ACTB + 32768
    TMP = SG + 4096
    SQ = TMP + 4096
    assert SQ + 4096 <= ar.nbytes, SQ + 4096
    hT = ar.view(HT, [128, KC, NTOK], BF16)

    def rms_tile(tt, bank0=6):
        b = bank0 + (tt % 2)
        for kc in range(KC):
            sq = ar.view(SQ + ((tt * KC + kc) % 4) * 1024, [128, TT], BF16)
            P.act(sq, xT[:, kc, tt * TT:(tt + 1) * TT], AF.Square)
            P.mm(bank(b), onesP, sq, start=(kc == 0), stop=(kc == KC - 1))
        rs = rstd[:, tt % 2, :]
        P.act(rs, bank(b), AF.Ln, bias=epsT[:, 0:1], scale=1.0 / 1024.0)
        P.act(rs, rs, AF.Exp, scale=-0.5)
        return rs

    def modnorm_tile(l, s, tt):
        rs = rms_tile(tt)
        c = 0 if tt < 2 else 1
        for kc in range(KC):
            tmp = ar.view(TMP + ((tt * KC + kc) % 2) * 2048, [128, TT], F32)
            P.stt(tmp, xT[:, kc, tt * TT:(tt + 1) * TT], modA[:, l, s, kc, c:c + 1],
                  rs, ALU.mult, ALU.mult)
            P.act(hT[:, kc, tt * TT:(tt + 1) * TT], tmp, AF.Identity,
                  bias=modT[:, l, 3 * s * 8 + kc, c:c + 1])

    pre_normed = [None]

    def modnorm(l, s):
        if pre_normed[0] == (l, s):
            pre_normed[0] = None
            return
        for tt in range(NTT):
            modnorm_tile(l, s, tt)

    GROUPS = [(0, 4), (4, 8), (8, 12), (12, 16), (16, 19), (19, 22)]

    def ffn(l, i, after_tile=None):
        s = 0 if i == 0 else 2
        modnorm(l, s)
        wgu = w_gu[l, i].rearrange("(kc p) n -> p kc n", p=128)
        wdn = w_dn[l, i].rearrange("(g p) n -> p g n", p=128)

        def load(g):
            j0, j1 = GROUPS[g]
            G = j1 - j0
            base = FW + (g % 2) * 24576
            wg = ar.view(base, [128, KC, 512], BF16)
            wu = ar.view(base + 8192, [128, KC, 512], BF16)
            wd = ar.view(base + 16384, [128, 4, 1024], BF16)
            P.dma(wg[:, :, 0:G * 128], wgu[:, :, j0 * 128:j1 * 128], q="pool")
            P.dma(wu[:, :, 0:G * 128], wgu[:, :, DFF + j0 * 128:DFF + j1 * 128], q="pool")
            P.dma(wd[:, 0:G, :], wdn[:, j0:j1, :], q="pool")
            return wg, wu, wd

        pair = [0]

        def gu(g, W):
            j0, j1 = GROUPS[g]
            wg, wu, _ = W
            ab = ar.view(ACTB + (g % 2) * 16384, [128, 4, NTOK], BF16)
            for jj in range(j1 - j0):
                for tt in range(NTT):
                    pb = (pair[0] % 2) * 2
                    pair[0] += 1
                    rhs = None
                    for kc in range(KC):
                        P.mm(bank(pb), wg[:, kc, jj * 128:(jj + 1) * 128], hT[:, kc, tt * TT:(tt + 1) * TT],
                             start=(kc == 0), stop=(kc == KC - 1))
                    for kc in range(KC):
                        P.mm(bank(pb + 1), wu[:, kc, jj * 128:(jj + 1) * 128], hT[:, kc, tt * TT:(tt + 1) * TT],
                             start=(kc == 0), stop=(kc == KC - 1))
                    sg = ar.view(SG + (pair[0] % 2) * 2048, [128, TT], F32)
                    P.act(sg, bank(pb), AF.Silu)
                    P.tt(ab[:, jj, tt * TT:(tt + 1) * TT], sg, bank(pb + 1), ALU.mult)
                    ada_tick()

        ycnt = [0]

        def down(g, W, tile_major=False):
            j0, j1 = GROUPS[g]
            _, _, wd = W
            ab = ar.view(ACTB + (g % 2) * 16384, [128, 4, NTOK], BF16)
            order = [(n, tt) for n in range(KC) for tt in range(NTT)]
            if tile_major:
                order = [(n, tt) for tt in range(NTT) for n in range(KC)]
            for (n, tt) in order:
                if True:
                    c = 0 if tt < 2 else 1
                    yb = 4 + (ycnt[0] % 4)
                    ycnt[0] += 1
                    for jj in range(j1 - j0):
                        P.mm(bank(yb), wd[:, jj, n * 128:(n + 1) * 128], ab[:, jj, tt * TT:(tt + 1) * TT],
                             start=(jj == 0), stop=(jj == j1 - j0 - 1))
                    xs = xT[:, n, tt * TT:(tt + 1) * TT]
                    P.stt(xs, bank(yb), modG[:, l, s, n, c:c + 1], xs, ALU.mult, ALU.add)
                    if tile_major and n == KC - 1 and after_tile is not None:
                        after_tile(tt)

        W = {}
        W[0] = load(0)
        W[1] = load(1)
        gu(0, W[0])
        for g in range(len(GROUPS)):
            if g + 1 < len(GROUPS):
                gu(g + 1, W[g + 1])
            last = (g == len(GROUPS) - 1)
            if last:
                while ada_pending:
                    ada_compute()
            down(g, W[g], tile_major=last)
            if g + 2 < len(GROUPS):
                W[g + 2] = load(g + 2)
        while ada_pending:
            ada_compute()
        if (l, i) == (0, 1):
            ada_flush()

    def final_tile(tt):
        fg = ptab_sb[:, PT_FINALG:PT_FINALG + 8]
        YS = ACTB
        rs = rms_tile(tt)
        for i in range(4 * tt, 4 * tt + 4):
            yt = ar.view(YS + (i % 2) * 4096, [128, KC, 128], F32)
            for kc in range(KC):
                P.stt(yt[:, kc, :], xT[:, kc, i * 128:(i + 1) * 128], fg[:, kc:kc + 1],
                      rs[:, (i % 4) * 128:(i % 4 + 1) * 128], ALU.mult, ALU.mult)
            st = ar.view(YS + 8192 + (i % 2) * 4096, [128, D], F32)
            for half in range(2):
                b = (i * 2 + half) % 4
                for kk in range(4):
                    kc = half * 4 + kk
                    P.tr(ps[:, b, kk * 128:(kk + 1) * 128], yt[:, kc, :], identF)
                if half == 0:
                    P.copy(st[:, 0:512], bank(b), eng="dve")
                else:
                    P.copy(st[:, 512:1024], bank(b), eng="act")
            P.dma(yout[i * 128:(i + 1) * 128, :], st, q="sp", is_output=True)

    def final_out():
        for tt in range(NTT):
            final_tile(tt)

    NEG = -30000.0
    Uf = cst_sb[:, C_UF:C_UF + 128]
    Ub = cst_sb[:, C_UB:C_UB + 128]
    sel127 = cst_sb[:, C_SEL127:C_SEL127 + 128]
    sel0 = cst_sb[:, C_SEL0:C_SEL0 + 128]
    onesF = cst_sb[:, C_ONES:C_ONES + 128]
    offdiag = cst_sb[:, C_OFFD:C_OFFD + 128]
    MLf = cst_sb[:, C_ML:C_ML + 128]
    MUf = cst_sb[:, C_MU:C_MU + 128]
    Rm = cst_sb[:, C_RM:C_RM + 128]
    MLb = cstb[:, 256:384]
    MUb = cstb[:, 384:512]
    P.dma(cstb[:, 512:512 + 9 * 128], bmask[:, :], q="pool")
    P.copy(MLb, MLf, eng="dve")
    P.copy(MUb, MUf, eng="dve")

    def bc_h(m):
        return m.unsqueeze(1).broadcast_to([128, 4, 128])

    def bc_i(v):
        return v.unsqueeze(2).broadcast_to([128, 4, 128])

    def b4(b):
        return ps[:, b, :].rearrange("p (h t) -> p h t", h=4, t=128)

    def bbf(b):
        return ps[:, b, :].bitcast(BF16)

    nbc = [0]

    def nb():
        b = nbc[0] % 8
        nbc[0] += 1
        return b

    def nbk(k):
        c = (nbc[0] + k - 1) // k * k
        nbc[0] = c + k
        return c % 8

    def dump(name, ap, shape, dt):
        d = dout(name, shape, dt)
        P.dma(d, ap, q="sp", is_output=True)
        dbg[name] = (shape, dt)

    QN, KN, VS, GS = 32768, 40960, 49152, 57344
    QPL, QRO, KFM, VTM, KCT, VC = 65536, 73728, 81920, 86016, 90112, 92160
    WP = 94208
    SCR = 110592
    OACC = 65536
    TST = 81920
    SCN = 98304
    SHR = 114688
    STF = 120832
    STB = 124928

    def mixer_even():
        l = 0
        modnorm(l, 1)
        w_in = even_w_in[0].rearrange("(kc p) n -> p kc n", p=128)
        w_out = even_w_out[0].rearrange("(kc p) n -> p kc n", p=128)
        cw = ptab_sb[:, PT_CONV:PT_CONV + 60].rearrange("p (c j) -> p c j", c=12, j=5)
        sink_bc = ptab_sb[:, PT_SINK:PT_SINK + 4]
        wpc = [0]

        def load_piece(c0, c1):
            wp = ar.view(WP + (wpc[0] % 2) * 8192, [128, KC, 512], BF16)
            wpc[0] += 1
            P.dma(wp[:, :, 0:c1 - c0], w_in[:, :, c0:c1], q="pool")
            return wp

        for grp in range(2):
            T0 = grp * 1024
            nseq, L = (4, 256) if grp == 0 else (1, 1024)
            qn = ar.view(QN, [128, 4, 1024], BF16)
            kn = ar.view(KN, [128, 4, 1024], BF16)
            vS = ar.view(VS, [128, 4, 1024], BF16)
            gS = ar.view(GS, [128, 4, 1024], BF16)
            qpl = ar.view(QPL, [128, 4, 1024], BF16)
            qro = ar.view(QRO, [128, 4, 1024], BF16)
            kfm = ar.view(KFM, [128, 2, 1024], BF16)
            vtm = ar.view(VTM, [128, 8, 256], BF16)
            kcT = ar.view(KCT, [128, 2, 512], BF16)
            vc = ar.view(VC, [128, 4, 256], BF16)
            mixT = hT[:, :, T0:T0 + 1024]

            def proj_fm(wp, cc):
                b = nbk(2)
                for tt in range(2):
                    for kc in range(KC):
                        P.mm(bank(b + tt), wp[:, kc, cc * 128:(cc + 1) * 128],
                             hT[:, kc, T0 + tt * TT:T0 + (tt + 1) * TT], start=(kc == 0), stop=(kc == KC - 1))
                return ps[:, b:b + 2, :].rearrange("p a t -> p (a t)")

            SCALE = 128.0 ** -0.5
            if grp == 1:
                ropeT = ar.view(SCR, [128, 2, 1024], F32)
                P.dma(ropeT, rope[:, :, :], q="sp")
                kst = ar.view(SCR + 8192, [128, 4, 256], BF16)
                P.dma(kst, cache_k.rearrange("(kt p) g d -> p kt (g d)", p=128), q="pool")
                P.dma(vc, cache_v.rearrange("(kt p) g d -> p kt (g d)", p=128), q="pool")
                for g in range(2):
                    b = nb()
                    for kt in range(4):
                        P.tr(bbf(b)[:, kt * 128:(kt + 1) * 128], kst[:, kt, g * 128:(g + 1) * 128], identB)
                    P.copy(kcT[:, g, :], bbf(b)[:, 0:512], eng="dve")

            def rope_apply(dst_bf, xf):
                t1 = ar.view(SCR + 12288, [128, 1024], F32)
                for tt in range(2):
                    b = nb()
                    P.mm(bank(b), Rm, xf[:, tt * TT:(tt + 1) * TT])
                    P.tt(t1[:, tt * TT:(tt + 1) * TT], bank(b), ropeT[:, 1, tt * TT:(tt + 1) * TT], ALU.mult)
                P.tt(xf, xf, ropeT[:, 0, :], ALU.mult, eng="pool")
                P.tt(dst_bf, t1, xf, ALU.add)

            wp = load_piece(2064, 2576)
            for h in range(4):
                pp = proj_fm(wp, h)
                P.act(qpl[:, h, :], pp, AF.Copy, scale=SCALE)
                if grp == 1:
                    xf = ar.view(SCR + 8192, [128, 1024], F32)
                    P.ts(xf, pp, SCALE, ALU.mult)
                    rope_apply(qro[:, h, :], xf)
            if "stop_ip1" in debug_names:
                return
            wp = load_piece(2576, 3088)
            for g in range(2):
                pp = proj_fm(wp, g)
                if grp == 0:
                    P.act(kfm[:, g, :], pp, AF.Copy)
                else:
                    xf = ar.view(SCR + 8192, [128, 1024], F32)
                    P.act(xf, pp, AF.Copy)
                    rope_apply(kfm[:, g, :], xf)
            if "stop_ip1b" in debug_names:
                return
            for i in range(8):
                b = nb()
                for kc in range(KC):
                    P.mm(bank(b), hT[:, kc, T0 + i * 128:T0 + (i + 1) * 128], wp[:, kc, 0:512],
                         start=(kc == 0), stop=(kc == KC - 1))
                if grp == 1:
                    P.act(vtm[:, i, :], ps[:, b, 256:512], AF.Copy)
                else:
                    st = ar.view(SCR + (i % 2) * 2048, [128, 512], F32)
                    P.copy(st, bank(b), eng="dve")
                    P.act(vtm[:, i, :], st[:, 256:512], AF.Copy)
                    P.dma(nck[i * 128:(i + 1) * 128, :], st[:, 0:256], q="sp", is_output=True)
                    P.dma(ncv[i * 128:(i + 1) * 128, :], st[:, 256:512], q="act", is_output=True)
            if "stop_ip2" in debug_names:
                return
            wpab = load_piece(2048, 2064)
            abT = gates[:, 0, :, :]
            for i in range(8):
                b = nb()
                for kc in range(KC):
                    P.mm(ps[:, b, 0:16], hT[:, kc, T0 + i * 128:T0 + (i + 1) * 128], wpab[:, kc, 0:16],
                         start=(kc == 0), stop=(kc == KC - 1))
                P.copy(abT[:, i, :], ps[:, b, 0:16], eng="dve")

            if "stop_ip3" in debug_names:
                return
            Lp = L + 4
            xpb = ar.view(SCR, [128, nseq, Lp], BF16)
            dg = ar.view(SCR + 2112, [128, 2, 5, 128], BF16)
            qs = ar.view(SCR + 8256, [128, 1024], F32)
            P.memset(ar.view(SCR, [128, 1040], BF16), 0.0, eng="pool")
            for pc in range(3):
                wp = load_piece(pc * 512, (pc + 1) * 512)
                for hh in range(4):
                    c = pc * 4 + hh
                    pp = proj_fm(wp, hh)
                    P.act(xpb[:, :, 2:2 + L], pp.rearrange("p (s t) -> p s t", s=nseq, t=L), AF.Copy)
                    for j in range(5):
                        P.ts(dg[:, c % 2, j, :], identB, cw[:, c, j:j + 1], ALU.mult)
                    bc = nbk(2)
                    if nseq == 4:
                        for s2 in range(4):
                            for j in range(5):
                                P.mm(ps[:, bc + s2 // 2, (s2 % 2) * 256:(s2 % 2 + 1) * 256], dg[:, c % 2, j, :],
                                     xpb[:, s2, j:j + 256], start=(j == 0), stop=(j == 4))
                    else:
                        for tt in range(2):
                            for j in range(5):
                                P.mm(bank(bc + tt), dg[:, c % 2, j, :], xpb[:, 0, tt * TT + j:tt * TT + j + TT],
                                     start=(j == 0), stop=(j == 4))
                    accf = ps[:, bc:bc + 2, :].rearrange("p a t -> p (a t)")
                    if pc == 2:
                        P.act(vS[:, hh, :], accf, AF.Silu)
                    else:
                        P.act(qs, accf, AF.Silu)
                        dst = (qn if pc == 0 else kn)[:, hh, :]
                        for tt in range(2):
                            sq = ar.view(SCR + 12352 + (tt % 2) * 1024, [128, TT], BF16)
                            P.act(sq, qs[:, tt * TT:(tt + 1) * TT], AF.Square)
                            b = nb()
                            P.mm(bank(b), onesP, sq)
                            rs = rstd[:, tt % 2, :]
                            P.act(rs, bank(b), AF.Ln, bias=epsT[:, 0:1])
                            P.act(rs, rs, AF.Exp, scale=-0.5)
                            P.stt(dst[:, tt * TT:(tt + 1) * TT], qs[:, tt * TT:(tt + 1) * TT],
                                  SCALE if pc == 0 else 1.0, rs, ALU.mult, ALU.mult)
            wp = load_piece(1536, 2048)
            for hh in range(4):
                pp = proj_fm(wp, hh)
                P.act(gS[:, hh, :], pp, AF.Silu)
            if "inproj" in debug_names and grp == DBG_GRP:
                dump("d_qn", qn, [128, 4, 1024], BF16)
                dump("d_kn", kn, [128, 4, 1024], BF16)
                dump("d_vS", vS, [128, 4, 1024], BF16)
                dump("d_gS", gS, [128, 4, 1024], BF16)
                dump("d_qpl", qpl, [128, 4, 1024], BF16)
                dump("d_qro", qro, [128, 4, 1024], BF16)
                dump("d_kfm", kfm, [128, 2, 1024], BF16)
                dump("d_vtm", vtm, [128, 8, 256], BF16)
                dump("d_ab", abT, [128, 8, 16], F32)

            if "stop_inproj" in debug_names:
                return
            ATT = WP
            if grp == 0:
                for s_ in range(4):
                    for qb in range(2):
                        tq = s_ * 256 + qb * 128
                        Pb = ar.view(ATT + ((s_ * 2 + qb) % 2) * 2048, [128, 4, 256], BF16)
                        PTs = ar.view(ATT + 4096 + ((s_ * 2 + qb) % 2) * 2048, [128, 8, 128], BF16)
                        stt_ = ar.view(ATT + 8192 + ((s_ * 2 + qb) % 2) * 256, [128, 16], F32)
                        on = ar.view(ATT + 8704 + ((s_ * 2 + qb) % 2) * 1024, [128, 4, 128], BF16)
                        bS = nbk(2)
                        P.memset(stt_, 0.0, eng="pool")
                        for h in range(4):
                            P.mm(ps[:, bS + h // 2, (h % 2) * 256:(h % 2 + 1) * 256],
                                 qpl[:, h, tq:tq + 128], kfm[:, h // 2, s_ * 256:(s_ + 1) * 256])
                        S4 = ps[:, bS:bS + 2, :].rearrange("p a (h k) -> p (a h) k", h=2, k=256)
                        mx = stt_[:, 0:4]
                        negm = stt_[:, 4:8]
                        rsum = stt_[:, 8:12]
                        es = stt_[:, 12:16]
                        P.reduce(mx, S4, ALU.max)
                        P.tt(mx, mx, sink_bc, ALU.max)
                        P.ts(negm, mx, -1.0, ALU.mult)
                        for h in range(4):
                            P.act(Pb[:, h, :], S4[:, h, :], AF.Exp, bias=negm[:, h:h + 1], accum=rsum[:, h:h + 1])
                        P.tt(es, sink_bc, negm, ALU.add)
                        P.act(es, es, AF.Exp)
                        P.tt(rsum, rsum, es, ALU.add)
                        P.recip(rsum, rsum)
                        bT = nb()
                        for h in range(4):
                            for kt in range(2):
                                P.tr(bbf(bT)[:, (h * 2 + kt) * 128:(h * 2 + kt + 1) * 128],
                                     Pb[:, h, kt * 128:(kt + 1) * 128], identB)
                        P.copy(PTs.rearrange("p a t -> p (a t)"), bbf(bT)[:, 0:1024], eng="act")
                        bO = nb()
                        for h in range(4):
                            for kt in range(2):
                                P.mm(ps[:, bO, h * 128:(h + 1) * 128], PTs[:, h * 2 + kt, :],
                                     vtm[:, s_ * 2 + kt, (h // 2) * 128:(h // 2 + 1) * 128],
                                     start=(kt == 0), stop=(kt == 1))
                        P.tt(on, b4(bO), bc_i(rsum), ALU.mult)
                        bT2 = nb()
                        for h in range(4):
                            P.tr(bbf(bT2)[:, h * 128:(h + 1) * 128], on[:, h, :], identB)
                        P.copy(mixT[:, 4:8, tq:tq + 128],
                               bbf(bT2)[:, 0:512].rearrange("p (h t) -> p h t", h=4, t=128), eng="dve")
            else:
                for qb in range(8):
                    tq = qb * 128
                    blks = [k for k in (qb - 1, qb, qb + 1) if 0 <= k < 8]
                    nl = len(blks)
                    W = 512 + nl * 128
                    on = ar.view(ATT + 16384 + (qb % 2) * 1024, [128, 4, 128], BF16)
                    for hp in range(2):
                        it = qb * 2 + hp
                        Pb = ar.view(ATT + (it % 2) * 4096, [128, 2, 1024], BF16)
                        PTs = ar.view(ATT + 8192 + (it % 2) * 4096, [128, 2, 1024], BF16)
                        stt_ = ar.view(ATT + 18432 + (it % 2) * 256, [128, 16], F32)
                        mx = stt_[:, 0:2]
                        negm = stt_[:, 2:4]
                        rsum = stt_[:, 4:6]
                        es = stt_[:, 6:8]
                        bS = nbk(4)
                        P.memset(stt_, 0.0, eng="pool")
                        for hh in range(2):
                            h = hp * 2 + hh
                            g = hp
                            P.mm(bank(bS + 2 * hh), qpl[:, h, tq:tq + 128], kcT[:, g, :])
                            k0 = blks[0] * 128
                            has_mask = (blks[0] == qb - 1) or (blks[-1] == qb + 1)
                            P.mm(ps[:, bS + 2 * hh + 1, 0:nl * 128], qro[:, h, tq:tq + 128],
                                 kfm[:, g, k0:k0 + nl * 128], start=True, stop=not has_mask)
                            nm = (1 if blks[0] == qb - 1 else 0) + (1 if blks[-1] == qb + 1 else 0)
                            cnt = 0
                            for bi, k in enumerate(blks):
                                if k == qb - 1 or k == qb + 1:
                                    cnt += 1
                                    P.mm(ps[:, bS + 2 * hh + 1, bi * 128:(bi + 1) * 128], identB,
                                         MUb if k == qb - 1 else MLb, start=False, stop=(cnt == nm))
                        S2 = ps[:, bS:bS + 4, :].rearrange("p (h a) t -> p h (a t)", h=2, a=2)[:, :, 0:W]
                        P.reduce(mx, S2, ALU.max)
                        P.tt(mx, mx, sink_bc[:, hp * 2:hp * 2 + 2], ALU.max)
                        P.ts(negm, mx, -1.0, ALU.mult)
                        for hh in range(2):
                            P.act(Pb[:, hh, 0:W], S2[:, hh, :], AF.Exp, bias=negm[:, hh:hh + 1],
                                  accum=rsum[:, hh:hh + 1])
                        P.tt(es, sink_bc[:, hp * 2:hp * 2 + 2], negm, ALU.add)
                        P.act(es, es, AF.Exp)
                        P.tt(rsum, rsum, es, ALU.add)
                        P.recip(rsum, rsum)
                        nblk = 4 + nl
                        for hh in range(2):
                            bT = nb()
                            for bi in range(nblk):
                                P.tr(bbf(bT)[:, bi * 128:(bi + 1) * 128], Pb[:, hh, bi * 128:(bi + 1) * 128], identB)
                            P.copy(PTs[:, hh, 0:W], bbf(bT)[:, 0:W], eng="act" if hh == 0 else "dve")
                        bO = nb()
                        for hh in range(2):
                            g = hp
                            for bi in range(nblk):
                                if bi < 4:
                                    rhs = vc[:, bi, g * 128:(g + 1) * 128]
                                else:
                                    rhs = vtm[:, blks[bi - 4], g * 128:(g + 1) * 128]
                                P.mm(ps[:, bO, hh * 128:(hh + 1) * 128], PTs[:, hh, bi * 128:(bi + 1) * 128], rhs,
                                     start=(bi == 0), stop=(bi == nblk - 1))
                        P.tt(on[:, hp * 2:hp * 2 + 2, :],
                             ps[:, bO, 0:256].rearrange("p (h t) -> p h t", h=2, t=128),
                             rsum.unsqueeze(2).broadcast_to([128, 2, 128]), ALU.mult)
                    bT2 = nb()
                    for h in range(4):
                        P.tr(bbf(bT2)[:, h * 128:(h + 1) * 128], on[:, h, :], identB)
                    P.copy(mixT[:, 4:8, tq:tq + 128],
                           bbf(bT2)[:, 0:512].rearrange("p (h t) -> p h t", h=4, t=128), eng="dve")
            if "attn" in debug_names and grp == DBG_GRP:
                dump("d_oatt", mixT[:, 4:8, :], [128, 4, 1024], BF16)

            if "stop_attn" in debug_names:
                return
            delta_net(grp, T0, nseq, L, qn, kn, vS, gS, mixT)
            if "delta" in debug_names and grp == DBG_GRP:
                dump("d_oa", mixT[:, 0:4, :], [128, 4, 1024], BF16)

            if "stop_delta" in debug_names:
                return
            wo = ar.view(WP, [128, KC, 1024], BF16)
            P.dma(wo, w_out[:, :, :], q="pool")
            for n in range(KC):
                for tt in range(2):
                    b = nb()
                    for kc in range(KC):
                        P.mm(bank(b), wo[:, kc, n * 128:(n + 1) * 128], mixT[:, kc, tt * TT:(tt + 1) * TT],
                             start=(kc == 0), stop=(kc == KC - 1))
                    xs = xT[:, n, T0 + tt * TT:T0 + (tt + 1) * TT]
                    P.stt(xs, bank(b), modG[:, l, 1, n, grp:grp + 1], xs, ALU.mult, ALU.add)

    def delta_net(grp, T0, nseq, L, qn, kn, vS, gS, mixT):
        abT = gates[:, 0, :, :]
        gT = gates[:, 1, :, 0:8]
        beta = gates[:, 2, :, 0:8]
        gc = gates[:, 3, :, 0:8]
        glb = gates[:, 4, :, 0:8]
        egc = gates[:, 5, :, 0:8]
        kdf = gates[:, 6, :, 0:8]
        glast = gates[:, 7, :, 0:8]
        bege = gates[:, 1, :, 8:16]
        tmpA = gates[:, 2, :, 8:16]
        tmpB = gates[:, 3, :, 8:16]
        alog_bc = ptab_sb[:, PT_ALOG:PT_ALOG + 8]
        dtb_bc = ptab_sb[:, PT_DTB:PT_DTB + 8]
        onorm = ptab_sb[:, PT_ONORME:PT_ONORME + 1]
        bc8 = lambda v: v.unsqueeze(1).broadcast_to([128, 8, 8])
        P.act(beta, abT[:, :, 0:8], AF.Exp, scale=-1.0)
        P.ts(beta, beta, 1.0, ALU.add)
        P.recip(beta, beta)
        P.tt(tmpA, abT[:, :, 8:16], bc8(dtb_bc), ALU.add)
        P.act(tmpB, tmpA, AF.Abs)
        P.act(tmpB, tmpB, AF.Exp, scale=-1.0)
        P.act(tmpB, tmpB, AF.Ln, bias=onesF[:, 0:1])
        P.ts(tmpA, tmpA, 0.0, ALU.max)
        P.tt(tmpA, tmpA, tmpB, ALU.add)
        P.act(tmpB[:, 0, :], alog_bc, AF.Exp)
        P.stt(gT, tmpA, -1.0, bc8(tmpB[:, 0, :]), ALU.mult, ALU.mult)
        g64 = gates[:, 1, :, :].rearrange("p a b -> p (a b)")
        b = nb()
        P.mm(ps[:, b, 0:128], Uf, g64)
        P.mm(ps[:, b, 128:256], Ub, g64)
        pv = ps[:, b, 0:256].rearrange("p (d a c) -> p d a c", d=2, a=8, c=16)
        P.copy(gc[:, :, 0:4], pv[:, 0, :, 0:4], eng="dve")
        P.copy(gc[:, :, 4:8], pv[:, 1, :, 4:8], eng="dve")
        gc64 = gates[:, 3, :, :].rearrange("p a b -> p (a b)")
        b = nb()
        P.mm(ps[:, b, 0:128], sel127, gc64)
        P.mm(ps[:, b, 128:256], sel0, gc64)
        pv = ps[:, b, 0:256].rearrange("p (d a c) -> p d a c", d=2, a=8, c=16)
        P.copy(glb[:, :, 0:4], pv[:, 0, :, 0:4], eng="dve")
        P.copy(glb[:, :, 4:8], pv[:, 1, :, 4:8], eng="dve")
        P.act(egc, gc, AF.Exp)
        P.tt(kdf, glb, gc, ALU.subtract)
        P.act(kdf, kdf, AF.Exp)
        P.act(glast, glb, AF.Exp)
        P.tt(bege, beta, egc, ALU.mult)

        if "gates" in debug_names and grp == DBG_GRP:
            dump("d_g", gT, [128, 8, 8], F32)
            dump("d_beta", beta, [128, 8, 8], F32)
            dump("d_gc", gc, [128, 8, 8], F32)
            dump("d_glb", glb, [128, 8, 8], F32)
        DB = 65536
        TSTS = [DB, DB + 12288]
        SCN2 = DB + 24576
        SHR2 = SCN2 + 16384
        STF2 = SHR2 + 8192
        STB2 = STF2 + 4096
        OACCA = STB2 + 2048
        assert OACCA + 4096 <= ar.nbytes
        Sf = [ar.view(STF2 + d * 2048, [128, 4, 128], F32) for d in range(2)]
        Sb = [ar.view(STB2 + d * 1024, [128, 4, 128], BF16) for d in range(2)]
        nch = L // 128
        if grp == 0:
            oaccA = ar.view(OACCA, [128, 4, 256], F32)
        else:
            oaccB = ar.t[:, 0:8192].rearrange("p (k t) -> p k t", k=8, t=1024)[:, :, 0:512].rearrange(
                "p (h a) t -> p h a t", h=4, a=2)

        def oslice(c):
            if grp == 0:
                return oaccA[:, :, c * 128:(c + 1) * 128]
            t0_ = c * 128
            return oaccB[:, :, t0_ // 512, t0_ % 512:t0_ % 512 + 128]

        def tv(off, dt):
            return ar.view(off, [128, 4, 128], dt)

        def mask_b(k_):
            return cstb[:, 512 + k_ * 128: 512 + (k_ + 1) * 128]

        def mm4(lh, rh):
            bb = nb()
            for h in range(4):
                P.mm(ps[:, bb, h * 128:(h + 1) * 128], lh[:, h, :], rh[:, h, :])
            return bb

        def step(s_, c, d, first_touch):
            T_ = TSTS[d]
            dec, decT = tv(T_, F32), tv(T_ + 2048, F32)
            G1, G2 = dec, decT
            Lb, LTb = tv(T_ + 4096, BF16), tv(T_ + 5120, BF16)
            Ab, Bb = tv(T_ + 6144, BF16), tv(T_ + 7168, BF16)
            Cb, Db, Eb, Tb = [tv(T_ + 8192 + 1024 * k_, BF16) for k_ in range(4)]
            B2, B3 = Cb, Db
            S_ = SCN2 + d * 8192
            Xb, qkT, QdT, Vb_, Kbe, kdec, negwT, vnew = [tv(S_ + 1024 * k_, BF16) for k_ in range(8)]
            H_ = SHR2 + d * 4096
            Ktm, Vtm, KKs, KQs = [tv(H_ + 1024 * k_, BF16) for k_ in range(4)]
            ti = s_ * nch + c
            tsl = slice(ti * 128, (ti + 1) * 128)
            dsl = slice(d * 4, d * 4 + 4)
            bK = nb()
            for h in range(4):
                P.tr(bbf(bK)[:, h * 128:(h + 1) * 128], kn[:, h, tsl], identB)
            P.copy(Ktm.rearrange("p h t -> p (h t)"), bbf(bK)[:, 0:512], eng="act")
            bV = nb()
            for h in range(4):
                P.tr(bbf(bV)[:, h * 128:(h + 1) * 128], vS[:, h, tsl], identB)
            P.copy(Vtm.rearrange("p h t -> p (h t)"), bbf(bV)[:, 0:512], eng="act")
            yield
            bKK = nb()
            for h in range(4):
                P.mm(ps[:, bKK, h * 128:(h + 1) * 128], kn[:, h, tsl], kn[:, h, tsl])
            P.tt(KKs, b4(bKK), bc_h(offdiag), ALU.mult)
            bKQ = nb()
            for h in range(4):
                P.mm(ps[:, bKQ, h * 128:(h + 1) * 128], kn[:, h, tsl], qn[:, h, tsl])
            P.copy(KQs.rearrange("p h t -> p (h t)"), bank(bKQ), eng="act")
            yield
            U_ = Uf if d == 0 else Ub
            Mdec, MdecT = (MLf, MUf) if d == 0 else (MUf, MLf)
            for h in range(4):
                P.act(G1[:, h, :], onesF, AF.Copy, scale=gT[:, ti, d * 4 + h:d * 4 + h + 1])
            P.stt(G2, G1, -1.0, bc_h(U_), ALU.mult, ALU.mult)
            bD = nb()
            P.mm(bank(bD), U_, G1.rearrange("p h t -> p (h t)"), start=True, stop=False)
            P.mm(bank(bD), onesF, G2.rearrange("p h t -> p (h t)"), start=False, stop=True)
            yield
            P.tt(dec, b4(bD), bc_h(Mdec), ALU.add)
            P.stt(decT, b4(bD), -1.0, bc_h(MdecT), ALU.mult, ALU.add)
            P.act(dec, dec, AF.Exp)
            P.act(decT, decT, AF.Exp)
            P.tt(B2, bc_h(identB), bc_i(beta[:, ti, dsl]), ALU.mult, eng="pool")
            P.tt(B3, bc_h(identB), bc_i(egc[:, ti, dsl]), ALU.mult, eng="pool")
            bR = nb()
            P.mm(bank(bR), onesP, B2.rearrange("p h t -> p (h t)"))
            bE = nb()
            P.mm(bank(bE), onesP, B3.rearrange("p h t -> p (h t)"))
            yield
            P.tt(Lb, KKs, dec, ALU.mult)
            P.tt(Lb, Lb, bc_i(beta[:, ti, dsl]), ALU.mult)
            P.tt(LTb, KKs, decT, ALU.mult, eng="pool")
            P.tt(LTb, LTb, b4(bR), ALU.mult)
            P.tt(qkT, KQs, decT, ALU.mult, eng="pool")
            P.tt(QdT, qn[:, :, tsl], b4(bE), ALU.mult)
            for h in range(4):
                hc = d * 4 + h
                P.act(Vb_[:, h, :], Vtm[:, h, :], AF.Copy, scale=beta[:, ti, hc:hc + 1])
                P.act(Kbe[:, h, :], Ktm[:, h, :], AF.Copy, scale=bege[:, ti, hc:hc + 1])
                P.act(kdec[:, h, :], Ktm[:, h, :], AF.Copy, scale=kdf[:, ti, hc:hc + 1])
            yield
            mi = (lambda k: k) if d == 0 else (lambda k: (k + 4) if 1 <= k <= 4 else (k - 4 if k >= 5 else k))
            P.tt(Ab, Lb, bc_h(mask_b(0)), ALU.mult)
            P.tt(Bb, LTb, bc_h(mask_b(0)), ALU.mult)
            P.stt(Tb, Ab, -1.0, bc_h(identF), ALU.mult, ALU.add)
            P.stt(Xb, Bb, -1.0, bc_h(identF), ALU.mult, ALU.add)
            yield
            b1 = mm4(Bb, Ab)
            b2 = mm4(Ab, Bb)
            P.copy(Cb, b4(b1), eng="act")
            P.copy(Db, b4(b2), eng="act")
            yield
            bx = mm4(Cb, Xb)
            bt = mm4(Xb, Cb)
            b3 = mm4(Db, Cb)
            P.tt(Xb, Xb, b4(bx), ALU.add)
            P.tt(Tb, Tb, b4(bt), ALU.add)
            P.copy(Eb, b4(b3), eng="act")
            yield
            bx = mm4(Eb, Xb)
            bt = mm4(Xb, Eb)
            P.tt(Xb, Xb, b4(bx), ALU.add)
            P.tt(Tb, Tb, b4(bt), ALU.add)
            yield
            for lv in range(1, 5):
                last = (lv == 4)
                P.tt(Ab, Lb, bc_h(mask_b(mi(lv))), ALU.mult, eng="pool")
                if not last:
                    P.tt(Bb, LTb, bc_h(mask_b(mi(lv + 4))), ALU.mult)
                b1 = mm4(Ab, Xb)
                if not last:
                    b2 = mm4(Bb, Tb)
                P.copy(Cb, b4(b1), eng="act")
                if not last:
                    P.copy(Db, b4(b2), eng="act")
                yield
                bx = mm4(Tb, Cb)
                if not last:
                    bt = mm4(Xb, Db)
                P.tt(Xb, Xb, b4(bx), ALU.subtract)
                if not last:
                    P.tt(Tb, Tb, b4(bt), ALU.subtract)
                yield
            if "step0" in debug_names and grp == DBG_GRP and s_ == 0 and (c, d) == DBG_STEP:
                dump("d_dec", dec, [128, 4, 128], F32)
                dump("d_decT", decT, [128, 4, 128], F32)
                dump("d_X", Xb, [128, 4, 128], BF16)
                dump("d_qkT", qkT, [128, 4, 128], BF16)
                dump("d_QdT", QdT, [128, 4, 128], BF16)
                dump("d_KKs", KKs, [128, 4, 128], BF16)
            bW = mm4(Kbe, Xb)
            P.act(negwT.rearrange("p h t -> p (h t)"), bank(bW), AF.Copy, scale=-1.0)
            yield
            bVn = nb()
            for h in range(4):
                P.mm(ps[:, bVn, h * 128:(h + 1) * 128], Xb[:, h, :], Vb_[:, h, :], start=True, stop=False)
                P.mm(ps[:, bVn, h * 128:(h + 1) * 128], negwT[:, h, :], Sb[d][:, h, :], start=False, stop=True)
            P.copy(vnew.rearrange("p h t -> p (h t)"), bank(bVn), eng="act")
            yield
            bO = nb()
            for h in range(4):
                P.mm(ps[:, bO, h * 128:(h + 1) * 128], Sb[d][:, h, :], QdT[:, h, :], start=True, stop=False)
                P.mm(ps[:, bO, h * 128:(h + 1) * 128], vnew[:, h, :], qkT[:, h, :], start=False, stop=True)
            bS_ = mm4(kdec, vnew)
            oa = oslice(c if grp == 1 else c)
            if first_touch[c]:
                P.copy(oa, b4(bO), eng="dve")
                first_touch[c] = False
            else:
                P.tt(oa, oa, b4(bO), ALU.add)
            P.tt(Sf[d], Sf[d], bc_i(glast[:, ti, dsl]), ALU.mult)
            P.tt(Sf[d], Sf[d], b4(bS_), ALU.add)
            P.copy(Sb[d], Sf[d], eng="act")
            yield

        for s_ in range(nseq):
            for d in range(2):
                if grp == 0:
                    P.memset(Sf[d], 0.0, eng="pool")
                else:
                    P.dma(Sf[d], state_delta[d].rearrange("h k v -> k h v"), q="sp")
                P.copy(Sb[d], Sf[d], eng="pool")
            first_touch = [True] * nch
            for k in range(nch):
                gens = [step(s_, k, 0, first_touch), step(s_, nch - 1 - k, 1, first_touch)]
                while gens:
                    for g_ in list(gens):
                        try:
                            next(g_)
                        except StopIteration:
                            gens.remove(g_)
            if grp == 0:
                for d in range(2):
                    P.dma(nsd[s_, d].rearrange("h k v -> k h v"), Sf[d], q="sp", is_output=True)
            for h in range(4):
                for t_ in range(0, L, 512):
                    w = min(512, L - t_)
                    sl = slice(s_ * L + t_, s_ * L + t_ + w)
                    src = oaccA[:, h, 0:w] if grp == 0 else oaccB[:, h, t_ // 512, 0:512]
                    sq = ar.view(TSTS[0], [128, 512], BF16)
                    P.act(sq[:, 0:w], src, AF.Square)
                    b = nb()
                    P.mm(ps[:, b, 0:w], onesP, sq[:, 0:w])
                    rs = rstd[:, 0, 0:w]
                    P.act(rs, ps[:, b, 0:w], AF.Ln, bias=epsT[:, 0:1], scale=1.0 / 128.0)
                    P.act(rs, rs, AF.Exp, scale=-0.5)
                    t_f = ar.view(TSTS[0] + 2048, [128, 512], F32)
                    P.stt(t_f[:, 0:w], src, onorm[:, 0:1], rs, ALU.mult, ALU.mult)
                    P.tt(mixT[:, h, sl], t_f[:, 0:w], gS[:, h, sl], ALU.mult)

    def mixer_odd():
        l = 1
        modnorm(l, 1)
        w_in = odd_w_in[0].rearrange("(kc p) n -> p kc n", p=128)
        w_out = odd_w_out[0].rearrange("(kc p) n -> p kc n", p=128)
        onorm2 = ptab_sb[:, PT_ONORMO:PT_ONORMO + 2]
        QT_, KT_, VT_, GS_, LRT_, OACC2 = 32768, 40960, 49152, 65536, 81920, 86016
        WPO = 102400
        ZB_, ZE_ = 102400, 104448
        EG_, ENG_ = ZE_, ZB_
        QTL_, KTL_, QTT_, KTT_, ATB_ = 110592, 111616, 112640, 113664, 114688
        SF_, SB_, WG_ = 115712, 119808, 121856
        SC = 128.0 ** -0.5
        wgp = ar.view(WG_, [128, 2, 512], F32)
        P.dma(wgp[0:33, :, :], wgpad[:, :, :], q="sp")
        wpc = [0]

        def load_piece(c0, c1):
            wp = ar.view(WPO + (wpc[0] % 2) * 8192, [128, KC, 512], BF16)
            wpc[0] += 1
            P.dma(wp[:, :, 0:c1 - c0], w_in[:, :, c0:c1], q="pool")
            return wp

        qT = ar.view(QT_, [128, 8, 512], BF16)
        kT = ar.view(KT_, [128, 8, 512], BF16)
        vT = ar.view(VT_, [128, 8, 1024], BF16)
        gS = ar.view(GS_, [128, 8, 1024], BF16)
        lrT = ar.view(LRT_, [128, 1024], F32)
        glt = gates[:, 0, 0, 0:4]
        for grp in range(2):
            T0 = grp * 1024
            nseq, L = (4, 256) if grp == 0 else (1, 1024)
            nch = L // 128
            mixT = hT[:, :, T0:T0 + 1024]
            wp = load_piece(3072, 3104)
            for tt in range(2):
                b = nb()
                for kc in range(KC):
                    P.mm(ps[0:32, b, :], wp[:, kc, 0:32], hT[:, kc, T0 + tt * TT:T0 + (tt + 1) * TT],
                         start=(kc == 0), stop=(kc == KC - 1))
                P.copy(lrT[0:32, tt * TT:(tt + 1) * TT], ps[0:32, b, :], eng="dve")
            P.memset(lrT[32:33, :], 1.0, eng="dve")
            for (c0, dst, col0, scl) in ((0, qT, 0, SC), (512, kT, 0, 1.0), (1024, vT, 0, 1.0), (1536, vT, 512, 1.0)):
                wp = load_piece(c0, c0 + 512)
                for i in range(8):
                    b = nb()
                    for kc in range(KC):
                        P.mm(bank(b), hT[:, kc, T0 + i * 128:T0 + (i + 1) * 128], wp[:, kc, 0:512],
                             start=(kc == 0), stop=(kc == KC - 1))
                    if i % 2 == 0:
                        P.act(dst[:, i, col0:col0 + 512], bank(b), AF.Copy, scale=scl)
                    else:
                        P.ts(dst[:, i, col0:col0 + 512], bank(b), scl, ALU.mult)
            for pc in range(2):
                wp = load_piece(2048 + pc * 512, 2560 + pc * 512)
                for cc in range(4):
                    b = nbk(2)
                    for tt in range(2):
                        for kc in range(KC):
                            P.mm(bank(b + tt), wp[:, kc, cc * 128:(cc + 1) * 128],
                                 hT[:, kc, T0 + tt * TT:T0 + (tt + 1) * TT], start=(kc == 0), stop=(kc == KC - 1))
                    P.act(gS[:, pc * 4 + cc, :], ps[:, b:b + 2, :].rearrange("p a t -> p (a t)"), AF.Silu)
            if "oinproj" in debug_names and grp == DBG_GRP:
                dump("d_qT", qT, [128, 8, 512], BF16)
                dump("d_kT", kT, [128, 8, 512], BF16)
                dump("d_vT", vT, [128, 8, 1024], BF16)
                dump("d_gS", gS, [128, 8, 1024], BF16)
                dump("d_lrT", lrT[0:33, :], [33, 1024], F32)

            zb = ar.view(ZB_, [128, 512], F32)
            ze = ar.view(ZE_, [128, 512], F32)
            eg = ar.view(EG_, [128, 512], F32)
            eng_ = ar.view(ENG_, [128, 512], F32)
            qtl = ar.view(QTL_, [128, 512], BF16)
            ktl = ar.view(KTL_, [128, 512], BF16)
            qtt = ar.view(QTT_, [128, 4, 128], BF16)
            ktt = ar.view(KTT_, [128, 4, 128], BF16)
            atb = ar.view(ATB_, [128, 4, 128], BF16)
            Sf = ar.view(SF_, [128, 4, 256], F32)
            Sb = ar.view(SB_, [128, 4, 256], BF16)
            if grp == 1:
                oacc_lo = ar.t[:, 0:8192].rearrange("p (k t) -> p k t", k=8, t=1024)[:, :, 0:512]
                oacc_hi = ar.view(OACC2, [128, 8, 512], F32)
            else:
                oaccA = ar.view(OACC2, [128, 8, 256], F32)

            def oslice(c, fc0, fc1):
                if grp == 0:
                    return oaccA[:, fc0:fc1, c * 128:(c + 1) * 128]
                if c < 4:
                    return oacc_lo[:, fc0:fc1, c * 128:(c + 1) * 128]
                return oacc_hi[:, fc0:fc1, (c - 4) * 128:(c - 3) * 128]

            bufsets = []
            for k_ in range(2):
                if k_ == 0:
                    offs = (QTL_, KTL_, QTT_, KTT_, ATB_)
                else:
                    offs = (106496, 107520, 108544, 109568, 125952)
                bufsets.append((ar.view(offs[0], [128, 512], BF16), ar.view(offs[1], [128, 512], BF16),
                                ar.view(offs[2], [128, 4, 128], BF16), ar.view(offs[3], [128, 4, 128], BF16),
                                ar.view(offs[4], [128, 4, 128], BF16), gates[:, 0, k_, 0:4]))

            def prefix(s_, d, c, bs_):
                qtl, ktl, qtt, ktt, atb, glt = bs_
                U_ = Uf if d == 0 else Ub
                elast = identF[:, 127:128] if d == 0 else identF[:, 0:1]
                ti = s_ * nch + c
                tsl = slice(ti * 128, (ti + 1) * 128)
                bz = nb()
                P.mm(bank(bz), lrT[0:33, tsl], wgp[0:33, d, :])
                yield
                P.ts(ze, bank(bz), -80.0, ALU.max)
                P.act(ze, ze, AF.Exp, scale=-1.0)
                yield
                P.act(zb, ze, AF.Ln, bias=onesF[:, 0:1])
                yield
                bg = nb()
                P.mm(bank(bg), U_, zb)
                yield
                P.act(eg, bank(bg), AF.Exp, scale=-1.0 / 16.0)
                P.act(eng_, bank(bg), AF.Exp, scale=1.0 / 16.0)
                P.tt(qtl, qT[:, ti, :], eg, ALU.mult)
                P.tt(ktl, kT[:, ti, :], eng_, ALU.mult)
                yield
                bl = nb()
                for h in range(4):
                    P.mm(ps[:, bl, h:h + 1], eg[:, h * 128:(h + 1) * 128], elast)
                P.copy(glt, ps[:, bl, 0:4], eng="dve")
                bq = nb()
                for h in range(4):
                    P.tr(bbf(bq)[:, h * 128:(h + 1) * 128], qtl[:, h * 128:(h + 1) * 128], identB)
                P.copy(qtt.rearrange("p h t -> p (h t)"), bbf(bq)[:, 0:512], eng="act")
                bk = nb()
                for h in range(4):
                    P.tr(bbf(bk)[:, h * 128:(h + 1) * 128], ktl[:, h * 128:(h + 1) * 128], identB)
                P.copy(ktt.rearrange("p h t -> p (h t)"), bbf(bk)[:, 0:512], eng="dve")
                yield
                ba = nb()
                for h in range(4):
                    P.mm(ps[:, ba, h * 128:(h + 1) * 128], ktt[:, h, :], qtt[:, h, :])
                P.tt(atb, b4(ba), bc_h(U_), ALU.mult)
                yield

            def suffix(s_, d, c, bs_, first_touch):
                qtl, ktl, qtt, ktt, atb, glt = bs_
                ti = s_ * nch + c
                bo = nbk(2)
                for h in range(4):
                    for half in range(2):
                        fc = h * 2 + half
                        dstp = ps[:, bo + fc // 4, (fc % 4) * 128:(fc % 4 + 1) * 128]
                        P.mm(dstp, vT[:, ti, h * 256 + half * 128:h * 256 + (half + 1) * 128], atb[:, h, :],
                             start=True, stop=False)
                        P.mm(dstp, Sb[:, h, half * 128:(half + 1) * 128], qtt[:, h, :], start=False, stop=True)
                for k2 in range(2):
                    oa = oslice(c, k2 * 4, k2 * 4 + 4)
                    if first_touch[c]:
                        P.copy(oa, b4(bo + k2), eng="dve" if k2 == 0 else "act")
                    else:
                        P.tt(oa, oa, b4(bo + k2), ALU.add)
                first_touch[c] = False
                yield
                bs2 = nbk(2)
                for h in range(4):
                    P.mm(ps[:, bs2 + h // 2, (h % 2) * 256:(h % 2 + 1) * 256], ktl[:, h * 128:(h + 1) * 128],
                         vT[:, ti, h * 256:(h + 1) * 256])
                P.tt(Sf, Sf, ps[:, bs2:bs2 + 2, :].rearrange("p a (h v) -> p (a h) v", h=2, v=256), ALU.add)
                P.tt(Sf, Sf, glt.unsqueeze(2).broadcast_to([128, 4, 256]), ALU.mult)
                P.copy(Sb, Sf, eng="act")
                yield

            for s_ in range(nseq):
                first_touch = [True] * nch
                steps = [(d, (cidx if d == 0 else nch - 1 - cidx)) for d in range(2) for cidx in range(nch)]
                def drive(gens):
                    gens = [g_ for g_ in gens if g_ is not None]
                    while gens:
                        for g_ in list(gens):
                            try:
                                next(g_)
                            except StopIteration:
                                gens.remove(g_)

                drive([prefix(s_, steps[0][0], steps[0][1], bufsets[0])])
                for k_, (d, c) in enumerate(steps):
                    if k_ % nch == 0:
                        if grp == 0:
                            P.memset(Sf, 0.0, eng="pool")
                        else:
                            P.dma(Sf, state_gla[d].rearrange("h k v -> k h v"), q="sp")
                        P.copy(Sb, Sf, eng="pool")
                    nxt = None
                    if k_ + 1 < len(steps):
                        nxt = prefix(s_, steps[k_ + 1][0], steps[k_ + 1][1], bufsets[(k_ + 1) % 2])
                    drive([suffix(s_, d, c, bufsets[k_ % 2], first_touch), nxt])
                    if k_ % nch == nch - 1 and grp == 0:
                        P.dma(nsg[s_, d].rearrange("h k v -> k h v"), Sf, q="sp", is_output=True)
                for h in range(4):
                    for t_ in range(0, L, 512):
                        w = min(512, L - t_)
                        b = nb()
                        srcs = []
                        for half in range(2):
                            fc = h * 2 + half
                            if grp == 0:
                                src = oaccA[:, fc, t_:t_ + w]
                            else:
                                src = (oacc_lo if t_ == 0 else oacc_hi)[:, fc, 0:512]
                            srcs.append(src)
                            sq = ar.view(ZB_ + half * 1024, [128, 512], BF16)
                            P.act(sq[:, 0:w], src, AF.Square)
                            P.mm(ps[:, b, 0:w], onesP, sq[:, 0:w], start=(half == 0), stop=(half == 1))
                        rs = rstd[:, 0, 0:w]
                        P.act(rs, ps[:, b, 0:w], AF.Ln, bias=epsT[:, 0:1], scale=1.0 / 256.0)
                        P.act(rs, rs, AF.Exp, scale=-0.5)
                        for half in range(2):
                            fc = h * 2 + half
                            t_f = ar.view(EG_, [128, 512], F32)
                            P.stt(t_f[:, 0:w], srcs[half], onorm2[:, half:half + 1], rs, ALU.mult, ALU.mult)
                            sl = slice(s_ * L + t_, s_ * L + t_ + w)
                            P.tt(mixT[:, fc, sl], t_f[:, 0:w], gS[:, fc, sl], ALU.mult)
            if "gla" in debug_names and grp == DBG_GRP:
                dump("d_mix", mixT, [128, 8, 1024], BF16)
            wo = ar.view(WPO, [128, KC, 1024], BF16)
            P.dma(wo, w_out[:, :, :], q="pool")
            for n in range(KC):
                for tt in range(2):
                    b = nb()
                    for kc in range(KC):
                        P.mm(bank(b), wo[:, kc, n * 128:(n + 1) * 128], mixT[:, kc, tt * TT:(tt + 1) * TT],
                             start=(kc == 0), stop=(kc == KC - 1))
                    xs = xT[:, n, T0 + tt * TT:T0 + (tt + 1) * TT]
                    P.stt(xs, bank(b), modG[:, l, 1, n, grp:grp + 1], xs, ALU.mult, ALU.add)


    if stage != "all":
        ada_flush()
    if stage == "ffn0":
        ffn(0, 0)
        final_out()
    elif stage == "mix0":
        mixer_even()
        final_out()
    elif stage == "mix1":
        mixer_odd()
        final_out()
    else:
        def pre(l_, s_):
            def f(tt):
                modnorm_tile(l_, s_, tt)
                if tt == NTT - 1:
                    pre_normed[0] = (l_, s_)
            return f
        ffn(0, 0, after_tile=pre(0, 1))
        mixer_even()
        ffn(0, 1, after_tile=pre(1, 0))
        ffn(1, 0, after_tile=pre(1, 1))
        mixer_odd()
        ffn(1, 1, after_tile=final_tile)

    P.emit()
    stack.close()
    return nc, dbg


PT_ADAB = 0
PT_NORMG = PT_ADAB + 144
PT_FINALG = PT_NORMG + 48
PT_CONV = PT_FINALG + 8
PT_SINK = PT_CONV + 60
PT_ALOG = PT_SINK + 4
PT_DTB = PT_ALOG + 8
PT_ONORME = PT_DTB + 8
PT_ONORMO = PT_ONORME + 1
PT_COLS = PT_ONORMO + 2
DBG_GRP = 0
DBG_STEP = (0, 0)

C_IDENT = 0
C_UF, C_UB, C_SEL127, C_SEL0, C_ONES, C_OFFD, C_ML, C_MU, C_RM = [128 * i for i in range(1, 10)]
CST_COLS = 128 * 10
CSTB_COLS = 512 + 9 * 128


def _fm(v):
    v = np.asarray(v, np.float32)
    return np.ascontiguousarray(v.reshape(-1, 128).T)


def make_tables(inputs):
    pt = np.zeros((128, PT_COLS), np.float32)
    for l in range(2):
        pt[:, PT_ADAB + l * 72: PT_ADAB + (l + 1) * 72] = _fm(inputs["ada_b"][l])
        for s in range(3):
            o = PT_NORMG + (l * 3 + s) * 8
            pt[:, o:o + 8] = _fm(inputs["norm_g"][l, s])
    pt[:, PT_FINALG:PT_FINALG + 8] = _fm(inputs["final_g"])
    cv = np.asarray(inputs["even_conv"], np.float32)[0]
    pt[:, PT_CONV:PT_CONV + 60] = cv.T.reshape(12, 128, 5).transpose(1, 0, 2).reshape(128, 60)
    pt[:, PT_SINK:PT_SINK + 4] = np.broadcast_to(np.asarray(inputs["even_sink"], np.float32)[0][None, :], (128, 4))
    pt[:, PT_ALOG:PT_ALOG + 8] = np.broadcast_to(np.asarray(inputs["even_a_log"], np.float32)[0].reshape(1, 8), (128, 8))
    pt[:, PT_DTB:PT_DTB + 8] = np.broadcast_to(np.asarray(inputs["even_dt_bias"], np.float32)[0].reshape(1, 8), (128, 8))
    pt[:, PT_ONORME:PT_ONORME + 1] = np.asarray(inputs["even_onorm"], np.float32)[0].reshape(128, 1)
    pt[:, PT_ONORMO:PT_ONORMO + 2] = _fm(inputs["odd_onorm"][0])
    cst = np.zeros((128, CST_COLS), np.float32)
    cst[:, C_IDENT:C_IDENT + 128] = np.eye(128, dtype=np.float32)
    kk, ii = np.meshgrid(np.arange(128), np.arange(128), indexing="ij")
    NEG = -30000.0
    cst[:, C_UF:C_UF + 128] = (kk <= ii)
    cst[:, C_UB:C_UB + 128] = (kk >= ii)
    cst[127, C_SEL127:C_SEL127 + 128] = 1.0
    cst[0, C_SEL0:C_SEL0 + 128] = 1.0
    cst[:, C_ONES:C_ONES + 128] = 1.0
    cst[:, C_OFFD:C_OFFD + 128] = (kk != ii)
    cst[:, C_ML:C_ML + 128] = np.where(ii <= kk, 0.0, NEG)
    cst[:, C_MU:C_MU + 128] = np.where(ii >= kk, 0.0, NEG)
    rm = np.zeros((128, 128), np.float32)
    for dp in range(128):
        if (dp % 64) < 32:
            rm[dp + 32, dp] = -1.0
        else:
            rm[dp - 32, dp] = 1.0
    cst[:, C_RM:C_RM + 128] = rm
    return pt, cst


def make_bmask():
    i, j = np.meshgrid(np.arange(128), np.arange(128), indexing="ij")
    ms = [(i // 8 == j // 8)]
    for b in (8, 16, 32, 64):
        ms.append((i // (2 * b) == j // (2 * b)) & ((i // b) % 2 == 1) & ((j // b) % 2 == 0))
    for b in (8, 16, 32, 64):
        ms.append((i // (2 * b) == j // (2 * b)) & ((i // b) % 2 == 0) & ((j // b) % 2 == 1))
    return np.ascontiguousarray(np.concatenate([m.astype(np.float32) for m in ms], axis=1))


def make_rope():
    t = np.arange(1024)
    row = (t // 64).astype(np.float64)
    col = (t % 64).astype(np.float64)
    inv = 10000.0 ** (-np.arange(32, dtype=np.float64) / 32.0)
    ang = np.zeros((128, 1024))
    for d in range(128):
        pos = row if d < 64 else col
        ang[d] = pos * np.float32(inv[d % 32])
    ang32 = np.zeros((128, 1024), np.float32)
    inv32 = (np.float32(10000.0) ** (-np.arange(32, dtype=np.float32) / np.float32(32))).astype(np.float32)
    for d in range(128):
        pos = (row if d < 64 else col).astype(np.float32)
        ang32[d] = pos * inv32[d % 32]
    return np.ascontiguousarray(np.stack([np.cos(ang32), np.sin(ang32)], axis=1).astype(np.float32))


def make_in_maps(inputs, stage="all"):
    pt, cst = make_tables(inputs)
    rope_t = make_rope()
    bmask_t = make_bmask()
    wg = np.asarray(inputs["odd_w_gate"], np.float32)[0]
    wgpad_t = np.zeros((33, 2, 512), np.float32)
    wgpad_t[0:16, 0, :] = wg[0]
    wgpad_t[16:32, 1, :] = wg[1]
    wgpad_t[32, :, :] = np.asarray(inputs["odd_gate_bias"], np.float32)[0]
    maps = []
    xp = np.asarray(inputs["x_prompt"], np.float32)
    xs = np.asarray(inputs["x_sample"], np.float32)
    for c in range(8):
        xin = np.concatenate([xp[4 * c:4 * c + 4].reshape(1024, D), xs[c]], axis=0)
        cond = np.stack([np.asarray(inputs["c_ctx"], np.float32), np.asarray(inputs["c"], np.float32)[c]], axis=-1)
        condT = np.ascontiguousarray(cond.reshape(KC, 128, 2).transpose(1, 0, 2))
        m = {"xin": np.ascontiguousarray(xin), "condT": condT, "ptab": pt, "cst": cst,
             "ada_w": np.asarray(inputs["ada_w"], np.float32),
             "even_w_in": np.asarray(inputs["even_w_in"], np.float32),
             "even_w_out": np.asarray(inputs["even_w_out"], np.float32),
             "rope": rope_t, "bmask": bmask_t, "wgpad": wgpad_t,
             "odd_w_in": np.asarray(inputs["odd_w_in"], np.float32),
             "odd_w_out": np.asarray(inputs["odd_w_out"], np.float32),
             "state_gla": np.ascontiguousarray(np.asarray(inputs["state_gla"], np.float32)[c, 0]),
             "cache_k": np.ascontiguousarray(np.asarray(inputs["cache_k"], np.float32)[c, 0]),
             "cache_v": np.ascontiguousarray(np.asarray(inputs["cache_v"], np.float32)[c, 0]),
             "state_delta": np.ascontiguousarray(np.asarray(inputs["state_delta"], np.float32)[c, 0])}
        if stage in ("all", "ffn0"):
            m["ffn_w_gu"] = np.asarray(inputs["ffn_w_gu"], np.float32)
            m["ffn_w_down"] = np.asarray(inputs["ffn_w_down"], np.float32)
        maps.append(m)
    return maps


_CACHE = {}


def kernel(**inputs):
    if "nc" not in _CACHE:
        _CACHE["nc"] = build_program("all")[0]
    nc = _CACHE["nc"]
    maps = make_in_maps(inputs)
    res = run_bass_kernel_spmd(nc, maps, core_ids=list(range(8)))
    rs = res.results
    ys = [np.asarray(r["yout"], np.float32) for r in rs]
    y_prompt = np.concatenate([y[:1024].reshape(4, 256, D) for y in ys], axis=0)
    y_sample = np.stack([y[1024:] for y in ys], axis=0)
    nsd = np.concatenate([np.asarray(r["nsd"], np.float32) for r in rs], axis=0)[:, None]
    nck = np.concatenate([np.asarray(r["nck"], np.float32).reshape(4, 256, 2, 128) for r in rs], axis=0)[:, None]
    ncv = np.concatenate([np.asarray(r["ncv"], np.float32).reshape(4, 256, 2, 128) for r in rs], axis=0)[:, None]
    nsg = np.concatenate([np.asarray(r["nsg"], np.float32) for r in rs], axis=0)[:, None]
    return (y_prompt, y_sample, np.ascontiguousarray(nsd), np.ascontiguousarray(nck),
            np.ascontiguousarray(ncv), np.ascontiguousarray(nsg))
```

```python
import numpy as np
import concourse.bass as bass
import concourse.mybir as mybir

F32 = mybir.dt.float32
BF16 = mybir.dt.bfloat16
AF = mybir.ActivationFunctionType
ALU = mybir.AluOpType
AX = mybir.AxisListType

_DTSZ = {F32: 4, BF16: 2}


def _region(ap):
    sp = str(ap.space)
    if "DRAM" in sp.upper() or "HBM" in sp.upper():
        return None
    sz = _DTSZ[ap.dtype]
    pat = ap.ap
    pstep, pcnt = pat[0]
    off = int(ap.offset)
    if pstep == 0:
        p0, f0 = 0, off
        pstep = 1 << 40
    else:
        p0, f0 = off // pstep, off % pstep
    ext = 1
    for st, cnt in pat[1:]:
        ext += (cnt - 1) * abs(st)
    b0, b1 = f0 * sz, (f0 + ext) * sz
    if "PSUM" in sp.upper():
        b0 = (b0 // 2048) * 2048
        b1 = ((b1 + 2047) // 2048) * 2048
        return (ap.tensor.name, 0, 128, b0, b1)
    return (ap.tensor.name, p0, p0 + pcnt, b0, b1)


def _ovl(a, b):
    return a[0] == b[0] and a[1] < b[2] and b[1] < a[2] and a[3] < b[4] and b[3] < a[4]


def _covers(a, b):
    return a[0] == b[0] and a[1] <= b[1] and a[2] >= b[2] and a[3] <= b[3] and a[4] >= b[4]


class Op:
    __slots__ = ("eng", "fn", "seq", "inc", "waits", "ctr", "is_dma", "val")

    def __init__(self, eng, fn, is_dma=False):
        self.eng = eng
        self.fn = fn
        self.inc = False
        self.waits = []
        self.is_dma = is_dma
        self.ctr = None
        self.seq = 0
        self.val = 0


ENGS = ("pe", "act", "dve", "pool", "sp")
NDMA = 24


class Prog:
    def __init__(self, nc):
        self.nc = nc
        self.streams = {e: [] for e in ENGS}
        self.seqc = {}
        self.known = {e: {} for e in ENGS}
        self.recs = {}
        self.dma_rr = {"h": 0, "s": 0}
        self.dma_last = {}
        self.nops = 0
        self.out_dmas = []

    def _need(self, op, dep):
        if dep is op:
            return
        if dep.ctr == op.ctr and not dep.is_dma:
            pass
        k = self.known[op.eng]
        if k.get(dep.ctr, -1) >= dep.seq:
            return
        k[dep.ctr] = dep.seq
        dep.inc = True
        op.waits.append(dep)

    BK = 2048

    def _buckets(self, r):
        return range(r[3] // self.BK, (r[4] - 1) // self.BK + 1)

    def _track(self, op, reads, writes):
        BK = self.BK
        for ap in reads:
            r = _region(ap)
            if r is None:
                continue
            for b in self._buckets(r):
                lst = self.recs.setdefault((r[0], b), [])
                is_ps = (r[0] == "ps")
                for rec in lst:
                    if _ovl(rec[0], r) and (rec[1] == "w" or (is_ps and rec[2].ctr != op.ctr)):
                        d = rec[2]
                        if d.ctr == op.ctr and op.eng == "pe":
                            continue
                        self._need(op, d)
                for i, rec in enumerate(lst):
                    if rec[1] == "r" and rec[2].ctr == op.ctr and rec[0] == r:
                        lst.pop(i)
                        break
                lst.append([r, "r", op])
        for ap in writes:
            r = _region(ap)
            if r is None:
                continue
            for b in self._buckets(r):
                lst = self.recs.setdefault((r[0], b), [])
                keep = []
                lo, hi = b * BK, (b + 1) * BK
                for rec in lst:
                    if rec[2] is op:
                        keep.append(rec)
                        continue
                    rr = rec[0]
                    if _ovl(rr, r):
                        d = rec[2]
                        if not (d.ctr == op.ctr and op.eng == "pe"):
                            self._need(op, d)
                        if (r[1] <= rr[1] and r[2] >= rr[2]
                                and r[3] <= max(rr[3], lo) and r[4] >= min(rr[4], hi)):
                            continue
                    keep.append(rec)
                keep.append([r, "w", op])
                self.recs[(r[0], b)] = keep

    def _add(self, eng, fn, reads, writes):
        op = Op(eng, fn)
        op.ctr = eng
        op.seq = self.seqc.get(eng, 0)
        self.seqc[eng] = op.seq + 1
        self._track(op, reads, writes)
        self.streams[eng].append(op)
        self.nops += 1
        return op

    def dma(self, out, in_, q="sp", is_output=False):
        op = Op(q, None, is_dma=True)
        kind = "s" if q == "pool" else "h"
        k = self.dma_rr[kind]
        self.dma_rr[kind] = (k + 1) % (NDMA // 2)
        op.ctr = "dma%s%d" % (kind, k)
        op.seq = self.seqc.get(op.ctr, 0)
        self.seqc[op.ctr] = op.seq + 1
        prev = self.dma_last.get(op.ctr)
        if prev is not None:
            self._need(op, prev)
        self.dma_last[op.ctr] = op
        op.inc = True
        self._track(op, [in_], [out])
        op.fn = lambda e, out=out, in_=in_: e.dma_start(out=out, in_=in_)
        self.streams[q].append(op)
        if is_output:
            self.out_dmas.append(op)
        return op

    def mm(self, out, lhsT, rhs, start=True, stop=True):
        return self._add("pe", lambda e: e.matmul(out, lhsT, rhs, start=start, stop=stop),
                         [lhsT, rhs], [out])

    def tr(self, out, in_, ident):
        return self._add("pe", lambda e: e.transpose(out, in_, ident), [in_, ident], [out])

    def act(self, out, in_, func, bias=None, scale=1.0, accum=None):
        rd = [in_]
        if bias is not None and not isinstance(bias, (int, float)):
            rd.append(bias)
        if not isinstance(scale, (int, float)):
            rd.append(scale)
        wr = [out] + ([accum] if accum is not None else [])
        kw = {}
        if bias is not None:
            kw["bias"] = bias
        if accum is not None:
            kw["accum_out"] = accum
        return self._add("act", lambda e: e.activation(out=out, in_=in_, func=func, scale=scale, **kw),
                         rd, wr)

    def tt(self, out, in0, in1, op, eng="dve"):
        return self._add(eng, lambda e: e.tensor_tensor(out=out, in0=in0, in1=in1, op=op),
                         [in0, in1], [out])

    def ts(self, out, in0, s1, op0, s2=None, op1=None, eng="dve", accum=None):
        rd = [in0] + [s for s in (s1, s2) if s is not None and not isinstance(s, (int, float))]
        kw = {}
        if op1 is not None:
            kw["op1"] = op1
        if accum is not None:
            kw["accum_out"] = accum
        wr = [out] + ([accum] if accum is not None else [])
        return self._add(eng, lambda e: e.tensor_scalar(out=out, in0=in0, scalar1=s1, scalar2=s2, op0=op0, **kw),
                         rd, wr)

    def stt(self, out, in0, scalar, in1, op0, op1, eng="dve"):
        rd = [in0, in1] + ([scalar] if not isinstance(scalar, (int, float)) else [])
        return self._add(eng, lambda e: e.scalar_tensor_tensor(out=out, in0=in0, scalar=scalar, in1=in1,
                                                               op0=op0, op1=op1), rd, [out])

    def copy(self, out, in_, eng="dve"):
        if eng == "act":
            return self.act(out, in_, AF.Copy)
        return self._add(eng, lambda e: e.tensor_copy(out=out, in_=in_), [in_], [out])

    def reduce(self, out, in_, op, eng="dve", axis=None):
        axis = axis or AX.X
        return self._add(eng, lambda e: e.tensor_reduce(out=out, in_=in_, axis=axis, op=op), [in_], [out])

    def recip(self, out, in_):
        return self._add("dve", lambda e: e.reciprocal(out=out, in_=in_), [in_], [out])

    def memset(self, ap, val, eng="dve"):
        return self._add(eng, lambda e: e.memset(ap, val), [], [ap])

    def emit(self):
        nc = self.nc
        sems = {}
        import contextlib
        stack = contextlib.ExitStack()
        allops = []
        for e in ENGS:
            allops.extend(self.streams[e])
        ctrs = sorted(set(o.ctr for o in allops))
        for c in ctrs:
            sems[c] = stack.enter_context(nc.semaphore("s_" + c))
        cnt = {c: 0 for c in ctrs}
        byctr = {c: [] for c in ctrs}
        for o in allops:
            byctr[o.ctr].append(o)
        for c in ctrs:
            ops = sorted(byctr[c], key=lambda o: o.seq)
            v = 0
            for o in ops:
                if o.inc:
                    v += 16 if o.is_dma else 1
                o.val = v
        fin = stack.enter_context(nc.semaphore("s_fin"))
        block = stack.enter_context(nc.Block())
        prog = self

        def run_stream(eng_name, e):
            for o in prog.streams[eng_name]:
                for d in o.waits:
                    e.wait_ge(sems[d.ctr], d.val)
                ins = o.fn(e)
                if o.inc:
                    ins.then_inc(sems[o.ctr], 16 if o.is_dma else 1)

        @block.tensor
        def _(e):
            run_stream("pe", e)

        @block.scalar
        def _(e):
            run_stream("act", e)

        @block.vector
        def _(e):
            run_stream("dve", e)

        @block.gpsimd
        def _(e):
            run_stream("pool", e)

        @block.sync
        def _(e):
            run_stream("sp", e)
            for o in prog.out_dmas:
                e.wait_ge(sems[o.ctr], o.val)

        stack.close()
from concourse.bass_utils import run_bass_kernel_spmd

D = 1024
NTOK = 2048
KC = 8
TT = 512
NTT = 4
DFF = 2816
NHC = 22
EPS = 1e-6
EVEN_IN = 3088
ODD_IN = 3104


class Arena:
    def __init__(self, nc, name, nbytes, stack):
        self.t = stack.enter_context(nc.sbuf_tensor(name, [128, nbytes // 4], F32))
        self.nbytes = nbytes

    def view(self, off, shape, dtype):
        sz = 4 if dtype == F32 else 2
        n = 1
        for s in shape[1:]:
            n *= s
        assert off % 4 == 0 and (n * sz) % 4 == 0 and off + n * sz <= self.nbytes, (off, shape, self.nbytes)
        ap = self.t[0:shape[0], off // 4: off // 4 + (n * sz) // 4]
        if dtype != F32:
            ap = ap.bitcast(dtype)
        if len(shape) == 3:
            ap = ap.rearrange("p (a b) -> p a b", a=shape[1], b=shape[2])
        elif len(shape) == 4:
            ap = ap.rearrange("p (a b c) -> p a b c", a=shape[1], b=shape[2], c=shape[3])
        return ap


def build_program(stage="all", debug_names=()):
    import contextlib
    nc = bass.Bass("TRN2", target_bir_lowering=False)
    P = Prog(nc)
    stack = contextlib.ExitStack()

    def din(name, shape, dt=F32):
        return nc.dram_tensor(name, list(shape), dt, kind="ExternalInput").ap()

    def dout(name, shape, dt=F32):
        return nc.dram_tensor(name, list(shape), dt, kind="ExternalOutput").ap()

    xin = din("xin", [NTOK, D])
    condT = din("condT", [128, KC, 2])
    ptab = din("ptab", [128, PT_COLS])
    cst = din("cst", [128, CST_COLS])
    ada_w = din("ada_w", [2, D, 9 * D])
    if stage in ("all", "ffn0"):
        w_gu = din("ffn_w_gu", [2, 2, D, 2 * DFF])
        w_dn = din("ffn_w_down", [2, 2, DFF, D])
    yout = dout("yout", [NTOK, D])
    even_w_in = din("even_w_in", [1, D, EVEN_IN])
    even_w_out = din("even_w_out", [1, D, D])
    rope = din("rope", [128, 2, 1024])
    bmask = din("bmask", [128, 9 * 128])
    odd_w_in = din("odd_w_in", [1, D, ODD_IN])
    odd_w_out = din("odd_w_out", [1, D, D])
    wgpad = din("wgpad", [33, 2, 512])
    state_gla = din("state_gla", [2, 4, 128, 256])
    nsg = dout("nsg", [4, 2, 4, 128, 256])
    cache_k = din("cache_k", [512, 2, 128])
    cache_v = din("cache_v", [512, 2, 128])
    state_delta = din("state_delta", [2, 4, 128, 128])
    nck = dout("nck", [1024, 256])
    ncv = dout("ncv", [1024, 256])
    nsd = dout("nsd", [4, 2, 4, 128, 128])

    dbg = {}
    xT = stack.enter_context(nc.sbuf_tensor("xT", [128, KC, NTOK], F32))
    ptab_sb = stack.enter_context(nc.sbuf_tensor("ptab_sb", [128, PT_COLS], F32))
    cst_sb = stack.enter_context(nc.sbuf_tensor("cst_sb", [128, CST_COLS], F32))
    cstb = stack.enter_context(nc.sbuf_tensor("cstb", [128, CSTB_COLS], BF16))
    modT = stack.enter_context(nc.sbuf_tensor("modT", [128, 2, 72, 2], F32))
    modA = stack.enter_context(nc.sbuf_tensor("modA", [128, 2, 3, KC, 2], F32))
    modG = stack.enter_context(nc.sbuf_tensor("modG", [128, 2, 3, KC, 2], F32))
    scT = stack.enter_context(nc.sbuf_tensor("scT", [128, KC, 2], BF16))
    condsb = stack.enter_context(nc.sbuf_tensor("condsb", [128, KC, 2], F32))
    gates = stack.enter_context(nc.sbuf_tensor("gates", [128, 8, 8, 16], F32))
    epsT = stack.enter_context(nc.sbuf_tensor("epsT", [128, 2], F32))
    rstd = stack.enter_context(nc.sbuf_tensor("rstd", [128, 2, TT], F32))
    ar = Arena(nc, "arena", 126976, stack)
    ps = stack.enter_context(nc.psum_tensor("ps", [128, 8, 512], F32))

    def bank(b):
        return ps[:, b, :]

    identF = cst_sb[:, C_IDENT:C_IDENT + 128]
    identB = cstb[:, 0:128]
    onesP = cstb[:, 128:256]

    P.dma(ptab_sb[:, :], ptab[:, :], q="sp")
    P.dma(cst_sb[:, :], cst[:, :], q="sp")
    P.dma(condsb[:, :, :], condT[:, :, :], q="sp")
    P.copy(identB, identF, eng="dve")
    P.memset(onesP, 1.0, eng="dve")
    P.memset(epsT[:, :], EPS, eng="dve")
    P.memset(gates[:, :, :, :], 0.0, eng="pool")

    STG = 32768
    for i in range(16):
        stg = ar.view(STG + (i % 2) * 4096, [128, D], F32)
        P.dma(stg, xin[i * 128:(i + 1) * 128, :], q="sp" if i % 2 == 0 else "act")
        for half in range(2):
            b = (i * 2 + half) % 4
            for kk in range(4):
                kc = half * 4 + kk
                P.tr(ps[:, b, kk * 128:(kk + 1) * 128], stg[:, kc * 128:(kc + 1) * 128], identF)
            src = ps[:, b, :].rearrange("p (a t) -> p a t", a=4, t=128)
            dst = xT[:, half * 4:half * 4 + 4, i * 128:(i + 1) * 128]
            if half == 0:
                P.copy(dst, src, eng="dve")
            else:
                P.copy(dst, src, eng="act")

    P.act(scT[:, :, :], condsb[:, :, :], AF.Silu)
    ADAW = 40960

    def ada_finish(l, s):
        ng = ptab_sb[:, PT_NORMG + (l * 3 + s) * 8: PT_NORMG + (l * 3 + s + 1) * 8]
        sc = modT[:, l, (3 * s + 1) * 8:(3 * s + 2) * 8, :]
        P.stt(modA[:, l, s, :, :], sc, 1.0, ng.unsqueeze(2).broadcast_to([128, KC, 2]), ALU.add, ALU.mult)
        gt = modT[:, l, (3 * s + 2) * 8:(3 * s + 3) * 8, :]
        P.ts(modG[:, l, s, :, :], gt, 0.5 if s != 1 else 1.0, ALU.mult)

    for i in range(3):
        wb = ar.view(ADAW + (i % 3) * 16384, [128, KC, 1024], BF16)
        src = ada_w[0].rearrange("(kc p) n -> p kc n", p=128)[:, :, i * 1024:(i + 1) * 1024]
        P.dma(wb, src, q="pool")
        for n in range(8):
            j = i * 8 + n
            for kc in range(KC):
                P.mm(ps[:, 4, 2 * j:2 * j + 2], wb[:, kc, n * 128:(n + 1) * 128], scT[:, kc, :],
                     start=(kc == 0), stop=(kc == KC - 1))
    P.tt(modT[:, 0, 0:24, :], ps[:, 4, 0:48].rearrange("p (j c) -> p j c", j=24, c=2),
         ptab_sb[:, PT_ADAB:PT_ADAB + 24].unsqueeze(2).broadcast_to([128, 24, 2]), ALU.add)
    ada_finish(0, 0)
    ada_tasks = [(0, i, q4) for i in range(3, 9) for q4 in range(4)] + \
                [(1, i, q4) for i in range(9) for q4 in range(4)]
    ada_cnt = [0, 0, 0]

    ada_pending = []

    def ada_load():
        l, i, q4 = ada_tasks.pop(0)
        wb = ar.view(TMP + (ada_cnt[0] % 2) * 4096, [128, KC, 256], BF16)
        ada_cnt[0] += 1
        c0 = i * 1024 + q4 * 256
        P.dma(wb, ada_w[l].rearrange("(kc p) n -> p kc n", p=128)[:, :, c0:c0 + 256], q="pool")
        ada_pending.append((l, i, q4, wb))

    def ada_compute():
        l, i, q4, wb = ada_pending.pop(0)
        bank_b = 6 + (ada_cnt[1] % 2)
        ada_cnt[1] += 1
        for nn in range(2):
            for kc in range(KC):
                P.mm(ps[:, bank_b, 2 * nn:2 * nn + 2], wb[:, kc, nn * 128:(nn + 1) * 128], scT[:, kc, :],
                     start=(kc == 0), stop=(kc == KC - 1))
        j0 = i * 8 + q4 * 2
        P.tt(modT[:, l, j0:j0 + 2, :], ps[:, bank_b, 0:4].rearrange("p (j c) -> p j c", j=2, c=2),
             ptab_sb[:, PT_ADAB + l * 72 + j0:PT_ADAB + l * 72 + j0 + 2].unsqueeze(2).broadcast_to([128, 2, 2]),
             ALU.add)
        if q4 == 3 and i % 3 == 2:
            ada_finish(l, i // 3)

    def ada_tick():
        ada_cnt[2] += 1
        if ada_cnt[2] % 3 != 0:
            return
        if len(ada_pending) == 2 or (ada_pending and not ada_tasks):
            ada_compute()
        if ada_tasks and len(ada_pending) < 2:
            ada_load()

    def ada_flush():
        while ada_tasks or ada_pending:
            if ada_tasks and len(ada_pending) < 2:
                ada_load()
            else:
                ada_compute()

    HT = 0
    FW = 32768
    ACTB = FW + 49152
    SG = ACTB + 32768
    TMP = SG + 4096
    SQ = TMP + 4096
    assert SQ + 4096 <= ar.nbytes, SQ + 4096
    hT = ar.view(HT, [128, KC, NTOK], BF16)

    def rms_tile(tt, bank0=6):
        b = bank0 + (tt % 2)
        for kc in range(KC):
            sq = ar.view(SQ + ((tt * KC + kc) % 4) * 1024, [128, TT], BF16)
            P.act(sq, xT[:, kc, tt * TT:(tt + 1) * TT], AF.Square)
            P.mm(bank(b), onesP, sq, start=(kc == 0), stop=(kc == KC - 1))
        rs = rstd[:, tt % 2, :]
        P.act(rs, bank(b), AF.Ln, bias=epsT[:, 0:1], scale=1.0 / 1024.0)
        P.act(rs, rs, AF.Exp, scale=-0.5)
        return rs

    def modnorm_tile(l, s, tt):
        rs = rms_tile(tt)
        c = 0 if tt < 2 else 1
        for kc in range(KC):
            tmp = ar.view(TMP + ((tt * KC + kc) % 2) * 2048, [128, TT], F32)
            P.stt(tmp, xT[:, kc, tt * TT:(tt + 1) * TT], modA[:, l, s, kc, c:c + 1],
                  rs, ALU.mult, ALU.mult)
            P.act(hT[:, kc, tt * TT:(tt + 1) * TT], tmp, AF.Identity,
                  bias=modT[:, l, 3 * s * 8 + kc, c:c + 1])

    pre_normed = [None]

    def modnorm(l, s):
        if pre_normed[0] == (l, s):
            pre_normed[0] = None
            return
        for tt in range(NTT):
            modnorm_tile(l, s, tt)

    GROUPS = [(0, 4), (4, 8), (8, 12), (12, 16), (16, 19), (19, 22)]

    def ffn(l, i, after_tile=None):
        s = 0 if i == 0 else 2
        modnorm(l, s)
        wgu = w_gu[l, i].rearrange("(kc p) n -> p kc n", p=128)
        wdn = w_dn[l, i].rearrange("(g p) n -> p g n", p=128)

        def load(g):
            j0, j1 = GROUPS[g]
            G = j1 - j0
            base = FW + (g % 2) * 24576
            wg = ar.view(base, [128, KC, 512], BF16)
            wu = ar.view(base + 8192, [128, KC, 512], BF16)
            wd = ar.view(base + 16384, [128, 4, 1024], BF16)
            P.dma(wg[:, :, 0:G * 128], wgu[:, :, j0 * 128:j1 * 128], q="pool")
            P.dma(wu[:, :, 0:G * 128], wgu[:, :, DFF + j0 * 128:DFF + j1 * 128], q="pool")
            P.dma(wd[:, 0:G, :], wdn[:, j0:j1, :], q="pool")
            return wg, wu, wd

        pair = [0]

        def gu(g, W):
            j0, j1 = GROUPS[g]
            wg, wu, _ = W
            ab = ar.view(ACTB + (g % 2) * 16384, [128, 4, NTOK], BF16)
            for jj in range(j1 - j0):
                for tt in range(NTT):
                    pb = (pair[0] % 2) * 2
                    pair[0] += 1
                    rhs = None
                    for kc in range(KC):
                        P.mm(bank(pb), wg[:, kc, jj * 128:(jj + 1) * 128], hT[:, kc, tt * TT:(tt + 1) * TT],
                             start=(kc == 0), stop=(kc == KC - 1))
                    for kc in range(KC):
                        P.mm(bank(pb + 1), wu[:, kc, jj * 128:(jj + 1) * 128], hT[:, kc, tt * TT:(tt + 1) * TT],
                             start=(kc == 0), stop=(kc == KC - 1))
                    sg = ar.view(SG + (pair[0] % 2) * 2048, [128, TT], F32)
                    P.act(sg, bank(pb), AF.Silu)
                    P.tt(ab[:, jj, tt * TT:(tt + 1) * TT], sg, bank(pb + 1), ALU.mult)
                    ada_tick()

        ycnt = [0]

        def down(g, W, tile_major=False):
            j0, j1 = GROUPS[g]
            _, _, wd = W
            ab = ar.view(ACTB + (g % 2) * 16384, [128, 4, NTOK], BF16)
            order = [(n, tt) for n in range(KC) for tt in range(NTT)]
            if tile_major:
                order = [(n, tt) for tt in range(NTT) for n in range(KC)]
            for (n, tt) in order:
                if True:
                    c = 0 if tt < 2 else 1
                    yb = 4 + (ycnt[0] % 4)
                    ycnt[0] += 1
                    for jj in range(j1 - j0):
                        P.mm(bank(yb), wd[:, jj, n * 128:(n + 1) * 128], ab[:, jj, tt * TT:(tt + 1) * TT],
                             start=(jj == 0), stop=(jj == j1 - j0 - 1))
                    xs = xT[:, n, tt * TT:(tt + 1) * TT]
                    P.stt(xs, bank(yb), modG[:, l, s, n, c:c + 1], xs, ALU.mult, ALU.add)
                    if tile_major and n == KC - 1 and after_tile is not None:
                        after_tile(tt)

        W = {}
        W[0] = load(0)
        W[1] = load(1)
        gu(0, W[0])
        for g in range(len(GROUPS)):
            if g + 1 < len(GROUPS):
                gu(g + 1, W[g + 1])
            last = (g == len(GROUPS) - 1)
            if last:
                while ada_pending:
                    ada_compute()
            down(g, W[g], tile_major=last)
            if g + 2 < len(GROUPS):
                W[g + 2] = load(g + 2)
        while ada_pending:
            ada_compute()
        if (l, i) == (0, 1):
            ada_flush()

    def final_tile(tt):
        fg = ptab_sb[:, PT_FINALG:PT_FINALG + 8]
        YS = ACTB
        rs = rms_tile(tt)
        for i in range(4 * tt, 4 * tt + 4):
            yt = ar.view(YS + (i % 2) * 4096, [128, KC, 128], F32)
            for kc in range(KC):
                P.stt(yt[:, kc, :], xT[:, kc, i * 128:(i + 1) * 128], fg[:, kc:kc + 1],
                      rs[:, (i % 4) * 128:(i % 4 + 1) * 128], ALU.mult, ALU.mult)
            st = ar.view(YS + 8192 + (i % 2) * 4096, [128, D], F32)
            for half in range(2):
                b = (i * 2 + half) % 4
                for kk in range(4):
                    kc = half * 4 + kk
                    P.tr(ps[:, b, kk * 128:(kk + 1) * 128], yt[:, kc, :], identF)
                if half == 0:
                    P.copy(st[:, 0:512], bank(b), eng="dve")
                else:
                    P.copy(st[:, 512:1024], bank(b), eng="act")
            P.dma(yout[i * 128:(i + 1) * 128, :], st, q="sp", is_output=True)

    def final_out():
        for tt in range(NTT):
            final_tile(tt)

    NEG = -30000.0
    Uf = cst_sb[:, C_UF:C_UF + 128]
    Ub = cst_sb[:, C_UB:C_UB + 128]
    sel127 = cst_sb[:, C_SEL127:C_SEL127 + 128]
    sel0 = cst_sb[:, C_SEL0:C_SEL0 + 128]
    onesF = cst_sb[:, C_ONES:C_ONES + 128]
    offdiag = cst_sb[:, C_OFFD:C_OFFD + 128]
    MLf = cst_sb[:, C_ML:C_ML + 128]
    MUf = cst_sb[:, C_MU:C_MU + 128]
    Rm = cst_sb[:, C_RM:C_RM + 128]
    MLb = cstb[:, 256:384]
    MUb = cstb[:, 384:512]
    P.dma(cstb[:, 512:512 + 9 * 128], bmask[:, :], q="pool")
    P.copy(MLb, MLf, eng="dve")
    P.copy(MUb, MUf, eng="dve")

    def bc_h(m):
        return m.unsqueeze(1).broadcast_to([128, 4, 128])

    def bc_i(v):
        return v.unsqueeze(2).broadcast_to([128, 4, 128])

    def b4(b):
        return ps[:, b, :].rearrange("p (h t) -> p h t", h=4, t=128)

    def bbf(b):
        return ps[:, b, :].bitcast(BF16)

    nbc = [0]

    def nb():
        b = nbc[0] % 8
        nbc[0] += 1
        return b

    def nbk(k):
        c = (nbc[0] + k - 1) // k * k
        nbc[0] = c + k
        return c % 8

    def dump(name, ap, shape, dt):
        d = dout(name, shape, dt)
        P.dma(d, ap, q="sp", is_output=True)
        dbg[name] = (shape, dt)

    QN, KN, VS, GS = 32768, 40960, 49152, 57344
    QPL, QRO, KFM, VTM, KCT, VC = 65536, 73728, 81920, 86016, 90112, 92160
    WP = 94208
    SCR = 110592
    OACC = 65536
    TST = 81920
    SCN = 98304
    SHR = 114688
    STF = 120832
    STB = 124928

    def mixer_even():
        l = 0
        modnorm(l, 1)
        w_in = even_w_in[0].rearrange("(kc p) n -> p kc n", p=128)
        w_out = even_w_out[0].rearrange("(kc p) n -> p kc n", p=128)
        cw = ptab_sb[:, PT_CONV:PT_CONV + 60].rearrange("p (c j) -> p c j", c=12, j=5)
        sink_bc = ptab_sb[:, PT_SINK:PT_SINK + 4]
        wpc = [0]

        def load_piece(c0, c1):
            wp = ar.view(WP + (wpc[0] % 2) * 8192, [128, KC, 512], BF16)
            wpc[0] += 1
            P.dma(wp[:, :, 0:c1 - c0], w_in[:, :, c0:c1], q="pool")
            return wp

        for grp in range(2):
            T0 = grp * 1024
            nseq, L = (4, 256) if grp == 0 else (1, 1024)
            qn = ar.view(QN, [128, 4, 1024], BF16)
            kn = ar.view(KN, [128, 4, 1024], BF16)
            vS = ar.view(VS, [128, 4, 1024], BF16)
            gS = ar.view(GS, [128, 4, 1024], BF16)
            qpl = ar.view(QPL, [128, 4, 1024], BF16)
            qro = ar.view(QRO, [128, 4, 1024], BF16)
            kfm = ar.view(KFM, [128, 2, 1024], BF16)
            vtm = ar.view(VTM, [128, 8, 256], BF16)
            kcT = ar.view(KCT, [128, 2, 512], BF16)
            vc = ar.view(VC, [128, 4, 256], BF16)
            mixT = hT[:, :, T0:T0 + 1024]

            def proj_fm(wp, cc):
                b = nbk(2)
                for tt in range(2):
                    for kc in range(KC):
                        P.mm(bank(b + tt), wp[:, kc, cc * 128:(cc + 1) * 128],
                             hT[:, kc, T0 + tt * TT:T0 + (tt + 1) * TT], start=(kc == 0), stop=(kc == KC - 1))
                return ps[:, b:b + 2, :].rearrange("p a t -> p (a t)")

            SCALE = 128.0 ** -0.5
            if grp == 1:
                ropeT = ar.view(SCR, [128, 2, 1024], F32)
                P.dma(ropeT, rope[:, :, :], q="sp")
                kst = ar.view(SCR + 8192, [128, 4, 256], BF16)
                P.dma(kst, cache_k.rearrange("(kt p) g d -> p kt (g d)", p=128), q="pool")
                P.dma(vc, cache_v.rearrange("(kt p) g d -> p kt (g d)", p=128), q="pool")
                for g in range(2):
                    b = nb()
                    for kt in range(4):
                        P.tr(bbf(b)[:, kt * 128:(kt + 1) * 128], kst[:, kt, g * 128:(g + 1) * 128], identB)
                    P.copy(kcT[:, g, :], bbf(b)[:, 0:512], eng="dve")

            def rope_apply(dst_bf, xf):
                t1 = ar.view(SCR + 12288, [128, 1024], F32)
                for tt in range(2):
                    b = nb()
                    P.mm(bank(b), Rm, xf[:, tt * TT:(tt + 1) * TT])
                    P.tt(t1[:, tt * TT:(tt + 1) * TT], bank(b), ropeT[:, 1, tt * TT:(tt + 1) * TT], ALU.mult)
                P.tt(xf, xf, ropeT[:, 0, :], ALU.mult, eng="pool")
                P.tt(dst_bf, t1, xf, ALU.add)

            wp = load_piece(2064, 2576)
            for h in range(4):
                pp = proj_fm(wp, h)
                P.act(qpl[:, h, :], pp, AF.Copy, scale=SCALE)
                if grp == 1:
                    xf = ar.view(SCR + 8192, [128, 1024], F32)
                    P.ts(xf, pp, SCALE, ALU.mult)
                    rope_apply(qro[:, h, :], xf)
            if "stop_ip1" in debug_names:
                return
            wp = load_piece(2576, 3088)
            for g in range(2):
                pp = proj_fm(wp, g)
                if grp == 0:
                    P.act(kfm[:, g, :], pp, AF.Copy)
                else:
                    xf = ar.view(SCR + 8192, [128, 1024], F32)
                    P.act(xf, pp, AF.Copy)
                    rope_apply(kfm[:, g, :], xf)
            if "stop_ip1b" in debug_names:
                return
            for i in range(8):
                b = nb()
                for kc in range(KC):
                    P.mm(bank(b), hT[:, kc, T0 + i * 128:T0 + (i + 1) * 128], wp[:, kc, 0:512],
                         start=(kc == 0), stop=(kc == KC - 1))
                if grp == 1:
                    P.act(vtm[:, i, :], ps[:, b, 256:512], AF.Copy)
                else:
                    st = ar.view(SCR + (i % 2) * 2048, [128, 512], F32)
                    P.copy(st, bank(b), eng="dve")
                    P.act(vtm[:, i, :], st[:, 256:512], AF.Copy)
                    P.dma(nck[i * 128:(i + 1) * 128, :], st[:, 0:256], q="sp", is_output=True)
                    P.dma(ncv[i * 128:(i + 1) * 128, :], st[:, 256:512], q="act", is_output=True)
            if "stop_ip2" in debug_names:
                return
            wpab = load_piece(2048, 2064)
            abT = gates[:, 0, :, :]
            for i in range(8):
                b = nb()
                for kc in range(KC):
                    P.mm(ps[:, b, 0:16], hT[:, kc, T0 + i * 128:T0 + (i + 1) * 128], wpab[:, kc, 0:16],
                         start=(kc == 0), stop=(kc == KC - 1))
                P.copy(abT[:, i, :], ps[:, b, 0:16], eng="dve")

            if "stop_ip3" in debug_names:
                return
            Lp = L + 4
            xpbs = [ar.view(SCR + k_ * 2080, [128, nseq, Lp], BF16) for k_ in range(2)]
            dgs = [ar.view(SCR + 4160 + k_ * 1280, [128, 5, 128], BF16) for k_ in range(2)]
            qss = [ar.view(SCR + 6720 + k_ * 4096, [128, 1024], F32) for k_ in range(2)]
            sqs = [gates[:, 4 + 2 * k_:6 + 2 * k_, :, :].rearrange("p s a b -> p (s a b)").bitcast(BF16) for k_ in range(2)]
            for k_ in range(2):
                P.memset(ar.view(SCR + k_ * 2080, [128, 1040], BF16), 0.0, eng="pool")

            def conv_chunk(c, wp, st):
                pc, hh = c // 4, c % 4
                xpb, dg, qs, sq = xpbs[st], dgs[st], qss[st], sqs[st]
                pp = proj_fm(wp, hh)
                yield
                P.act(xpb[:, :, 2:2 + L], pp.rearrange("p (s t) -> p s t", s=nseq, t=L), AF.Copy)
                for j in range(5):
                    P.ts(dg[:, j, :], identB, cw[:, c, j:j + 1], ALU.mult)
                yield
                bc = nbk(2)
                if nseq == 4:
                    for s2 in range(4):
                        for j in range(5):
                            P.mm(ps[:, bc + s2 // 2, (s2 % 2) * 256:(s2 % 2 + 1) * 256], dg[:, j, :],
                                 xpb[:, s2, j:j + 256], start=(j == 0), stop=(j == 4))
                else:
                    for tt in range(2):
                        for j in range(5):
                            P.mm(bank(bc + tt), dg[:, j, :], xpb[:, 0, tt * TT + j:tt * TT + j + TT],
                                 start=(j == 0), stop=(j == 4))
                accf = ps[:, bc:bc + 2, :].rearrange("p a t -> p (a t)")
                yield
                if pc == 2:
                    P.act(vS[:, hh, :], accf, AF.Silu)
                    yield
                    return
                P.act(qs, accf, AF.Silu)
                dst = (qn if pc == 0 else kn)[:, hh, :]
                for tt in range(2):
                    P.act(sq, qs[:, tt * TT:(tt + 1) * TT], AF.Square)
                    b = nb()
                    P.mm(bank(b), onesP, sq)
                    yield
                    rs = rstd[:, st, :]
                    P.act(rs, bank(b), AF.Ln, bias=epsT[:, 0:1])
                    P.act(rs, rs, AF.Exp, scale=-0.5)
                    P.stt(dst[:, tt * TT:(tt + 1) * TT], qs[:, tt * TT:(tt + 1) * TT],
                          SCALE if pc == 0 else 1.0, rs, ALU.mult, ALU.mult)
                    yield

            wps = {}
            active = []
            nxt_c = 0
            free_st = [0, 1]
            while nxt_c < 12 or active:
                while nxt_c < 12 and len(active) < 2:
                    pc_ = nxt_c // 4
                    if pc_ not in wps:
                        wps[pc_] = load_piece(pc_ * 512, (pc_ + 1) * 512)
                    st_ = free_st.pop(0)
                    active.append((conv_chunk(nxt_c, wps[pc_], st_), st_))
                    nxt_c += 1
                for item in list(active):
                    try:
                        next(item[0])
                    except StopIteration:
                        active.remove(item)
                        free_st.append(item[1])
            wp = load_piece(1536, 2048)
            for hh in range(4):
                pp = proj_fm(wp, hh)
                P.act(gS[:, hh, :], pp, AF.Silu)
            if "inproj" in debug_names and grp == DBG_GRP:
                dump("d_qn", qn, [128, 4, 1024], BF16)
                dump("d_kn", kn, [128, 4, 1024], BF16)
                dump("d_vS", vS, [128, 4, 1024], BF16)
                dump("d_gS", gS, [128, 4, 1024], BF16)
                dump("d_qpl", qpl, [128, 4, 1024], BF16)
                dump("d_qro", qro, [128, 4, 1024], BF16)
                dump("d_kfm", kfm, [128, 2, 1024], BF16)
                dump("d_vtm", vtm, [128, 8, 256], BF16)
                dump("d_ab", abT, [128, 8, 16], F32)

            if "stop_inproj" in debug_names:
                return
            ATT = WP
            if grp == 0:
                for s_ in range(4):
                    for qb in range(2):
                        tq = s_ * 256 + qb * 128
                        Pb = ar.view(ATT + ((s_ * 2 + qb) % 2) * 2048, [128, 4, 256], BF16)
                        PTs = ar.view(ATT + 4096 + ((s_ * 2 + qb) % 2) * 2048, [128, 8, 128], BF16)
                        stt_ = ar.view(ATT + 8192 + ((s_ * 2 + qb) % 2) * 256, [128, 16], F32)
                        on = ar.view(ATT + 8704 + ((s_ * 2 + qb) % 2) * 1024, [128, 4, 128], BF16)
                        bS = nbk(2)
                        P.memset(stt_, 0.0, eng="pool")
                        for h in range(4):
                            P.mm(ps[:, bS + h // 2, (h % 2) * 256:(h % 2 + 1) * 256],
                                 qpl[:, h, tq:tq + 128], kfm[:, h // 2, s_ * 256:(s_ + 1) * 256])
                        S4 = ps[:, bS:bS + 2, :].rearrange("p a (h k) -> p (a h) k", h=2, k=256)
                        mx = stt_[:, 0:4]
                        negm = stt_[:, 4:8]
                        rsum = stt_[:, 8:12]
                        es = stt_[:, 12:16]
                        P.reduce(mx, S4, ALU.max)
                        P.tt(mx, mx, sink_bc, ALU.max)
                        P.ts(negm, mx, -1.0, ALU.mult)
                        for h in range(4):
                            P.act(Pb[:, h, :], S4[:, h, :], AF.Exp, bias=negm[:, h:h + 1], accum=rsum[:, h:h + 1])
                        P.tt(es, sink_bc, negm, ALU.add)
                        P.act(es, es, AF.Exp)
                        P.tt(rsum, rsum, es, ALU.add)
                        P.recip(rsum, rsum)
                        bT = nb()
                        for h in range(4):
                            for kt in range(2):
                                P.tr(bbf(bT)[:, (h * 2 + kt) * 128:(h * 2 + kt + 1) * 128],
                                     Pb[:, h, kt * 128:(kt + 1) * 128], identB)
                        P.copy(PTs.rearrange("p a t -> p (a t)"), bbf(bT)[:, 0:1024], eng="act")
                        bO = nb()
                        for h in range(4):
                            for kt in range(2):
                                P.mm(ps[:, bO, h * 128:(h + 1) * 128], PTs[:, h * 2 + kt, :],
                                     vtm[:, s_ * 2 + kt, (h // 2) * 128:(h // 2 + 1) * 128],
                                     start=(kt == 0), stop=(kt == 1))
                        P.tt(on, b4(bO), bc_i(rsum), ALU.mult)
                        bT2 = nb()
                        for h in range(4):
                            P.tr(bbf(bT2)[:, h * 128:(h + 1) * 128], on[:, h, :], identB)
                        P.copy(mixT[:, 4:8, tq:tq + 128],
                               bbf(bT2)[:, 0:512].rearrange("p (h t) -> p h t", h=4, t=128), eng="dve")
            else:
                for qb in range(8):
                    tq = qb * 128
                    blks = [k for k in (qb - 1, qb, qb + 1) if 0 <= k < 8]
                    nl = len(blks)
                    W = 512 + nl * 128
                    on = ar.view(ATT + 16384 + (qb % 2) * 1024, [128, 4, 128], BF16)
                    for hp in range(2):
                        it = qb * 2 + hp
                        Pb = ar.view(ATT + (it % 2) * 4096, [128, 2, 1024], BF16)
                        PTs = ar.view(ATT + 8192 + (it % 2) * 4096, [128, 2, 1024], BF16)
                        stt_ = ar.view(ATT + 18432 + (it % 2) * 256, [128, 16], F32)
                        mx = stt_[:, 0:2]
                        negm = stt_[:, 2:4]
                        rsum = stt_[:, 4:6]
                        es = stt_[:, 6:8]
                        bS = nbk(4)
                        P.memset(stt_, 0.0, eng="pool")
                        for hh in range(2):
                            h = hp * 2 + hh
                            g = hp
                            P.mm(bank(bS + 2 * hh), qpl[:, h, tq:tq + 128], kcT[:, g, :])
                            k0 = blks[0] * 128
                            has_mask = (blks[0] == qb - 1) or (blks[-1] == qb + 1)
                            P.mm(ps[:, bS + 2 * hh + 1, 0:nl * 128], qro[:, h, tq:tq + 128],
                                 kfm[:, g, k0:k0 + nl * 128], start=True, stop=not has_mask)
                            nm = (1 if blks[0] == qb - 1 else 0) + (1 if blks[-1] == qb + 1 else 0)
                            cnt = 0
                            for bi, k in enumerate(blks):
                                if k == qb - 1 or k == qb + 1:
                                    cnt += 1
                                    P.mm(ps[:, bS + 2 * hh + 1, bi * 128:(bi + 1) * 128], identB,
                                         MUb if k == qb - 1 else MLb, start=False, stop=(cnt == nm))
                        S2 = ps[:, bS:bS + 4, :].rearrange("p (h a) t -> p h (a t)", h=2, a=2)[:, :, 0:W]
                        P.reduce(mx, S2, ALU.max)
                        P.tt(mx, mx, sink_bc[:, hp * 2:hp * 2 + 2], ALU.max)
                        P.ts(negm, mx, -1.0, ALU.mult)
                        for hh in range(2):
                            P.act(Pb[:, hh, 0:W], S2[:, hh, :], AF.Exp, bias=negm[:, hh:hh + 1],
                                  accum=rsum[:, hh:hh + 1])
                        P.tt(es, sink_bc[:, hp * 2:hp * 2 + 2], negm, ALU.add)
                        P.act(es, es, AF.Exp)
                        P.tt(rsum, rsum, es, ALU.add)
                        P.recip(rsum, rsum)
                        nblk = 4 + nl
                        for hh in range(2):
                            bT = nb()
                            for bi in range(nblk):
                                P.tr(bbf(bT)[:, bi * 128:(bi + 1) * 128], Pb[:, hh, bi * 128:(bi + 1) * 128], identB)
                            P.copy(PTs[:, hh, 0:W], bbf(bT)[:, 0:W], eng="act" if hh == 0 else "dve")
                        bO = nb()
                        for hh in range(2):
                            g = hp
                            for bi in range(nblk):
                                if bi < 4:
                                    rhs = vc[:, bi, g * 128:(g + 1) * 128]
                                else:
                                    rhs = vtm[:, blks[bi - 4], g * 128:(g + 1) * 128]
                                P.mm(ps[:, bO, hh * 128:(hh + 1) * 128], PTs[:, hh, bi * 128:(bi + 1) * 128], rhs,
                                     start=(bi == 0), stop=(bi == nblk - 1))
                        P.tt(on[:, hp * 2:hp * 2 + 2, :],
                             ps[:, bO, 0:256].rearrange("p (h t) -> p h t", h=2, t=128),
                             rsum.unsqueeze(2).broadcast_to([128, 2, 128]), ALU.mult)
                    bT2 = nb()
                    for h in range(4):
                        P.tr(bbf(bT2)[:, h * 128:(h + 1) * 128], on[:, h, :], identB)
                    P.copy(mixT[:, 4:8, tq:tq + 128],
                           bbf(bT2)[:, 0:512].rearrange("p (h t) -> p h t", h=4, t=128), eng="dve")
            if "attn" in debug_names and grp == DBG_GRP:
                dump("d_oatt", mixT[:, 4:8, :], [128, 4, 1024], BF16)

            if "stop_attn" in debug_names:
                return
            delta_net(grp, T0, nseq, L, qn, kn, vS, gS, mixT)
            if "delta" in debug_names and grp == DBG_GRP:
                dump("d_oa", mixT[:, 0:4, :], [128, 4, 1024], BF16)

            if "stop_delta" in debug_names:
                return
            wo = ar.view(WP, [128, KC, 1024], BF16)
            P.dma(wo, w_out[:, :, :], q="pool")
            for n in range(KC):
                for tt in range(2):
                    b = nb()
                    for kc in range(KC):
                        P.mm(bank(b), wo[:, kc, n * 128:(n + 1) * 128], mixT[:, kc, tt * TT:(tt + 1) * TT],
                             start=(kc == 0), stop=(kc == KC - 1))
                    xs = xT[:, n, T0 + tt * TT:T0 + (tt + 1) * TT]
                    P.stt(xs, bank(b), modG[:, l, 1, n, grp:grp + 1], xs, ALU.mult, ALU.add)

    def delta_net(grp, T0, nseq, L, qn, kn, vS, gS, mixT):
        abT = gates[:, 0, :, :]
        gT = gates[:, 1, :, 0:8]
        beta = gates[:, 2, :, 0:8]
        gc = gates[:, 3, :, 0:8]
        glb = gates[:, 4, :, 0:8]
        egc = gates[:, 5, :, 0:8]
        kdf = gates[:, 6, :, 0:8]
        glast = gates[:, 7, :, 0:8]
        bege = gates[:, 1, :, 8:16]
        tmpA = gates[:, 2, :, 8:16]
        ngT = gates[:, 4, :, 8:16]
        tmpB = gates[:, 3, :, 8:16]
        alog_bc = ptab_sb[:, PT_ALOG:PT_ALOG + 8]
        dtb_bc = ptab_sb[:, PT_DTB:PT_DTB + 8]
        onorm = ptab_sb[:, PT_ONORME:PT_ONORME + 1]
        bc8 = lambda v: v.unsqueeze(1).broadcast_to([128, 8, 8])
        P.act(beta, abT[:, :, 0:8], AF.Exp, scale=-1.0)
        P.ts(beta, beta, 1.0, ALU.add)
        P.recip(beta, beta)
        P.tt(tmpA, abT[:, :, 8:16], bc8(dtb_bc), ALU.add)
        P.act(tmpB, tmpA, AF.Abs)
        P.act(tmpB, tmpB, AF.Exp, scale=-1.0)
        P.act(tmpB, tmpB, AF.Ln, bias=onesF[:, 0:1])
        P.ts(tmpA, tmpA, 0.0, ALU.max)
        P.tt(tmpA, tmpA, tmpB, ALU.add)
        P.act(tmpB[:, 0, :], alog_bc, AF.Exp)
        P.stt(gT, tmpA, -1.0, bc8(tmpB[:, 0, :]), ALU.mult, ALU.mult)
        P.ts(ngT, gT, -1.0, ALU.mult)
        g64 = gates[:, 1, :, :].rearrange("p a b -> p (a b)")
        b = nb()
        P.mm(ps[:, b, 0:128], Uf, g64)
        P.mm(ps[:, b, 128:256], Ub, g64)
        pv = ps[:, b, 0:256].rearrange("p (d a c) -> p d a c", d=2, a=8, c=16)
        P.copy(gc[:, :, 0:4], pv[:, 0, :, 0:4], eng="dve")
        P.copy(gc[:, :, 4:8], pv[:, 1, :, 4:8], eng="dve")
        gc64 = gates[:, 3, :, :].rearrange("p a b -> p (a b)")
        b = nb()
        P.mm(ps[:, b, 0:128], sel127, gc64)
        P.mm(ps[:, b, 128:256], sel0, gc64)
        pv = ps[:, b, 0:256].rearrange("p (d a c) -> p d a c", d=2, a=8, c=16)
        P.copy(glb[:, :, 0:4], pv[:, 0, :, 0:4], eng="dve")
        P.copy(glb[:, :, 4:8], pv[:, 1, :, 4:8], eng="dve")
        P.act(egc, gc, AF.Exp)
        P.tt(kdf, glb, gc, ALU.subtract)
        P.act(kdf, kdf, AF.Exp)
        P.act(glast, glb, AF.Exp)
        P.tt(bege, beta, egc, ALU.mult)

        if "gates" in debug_names and grp == DBG_GRP:
            dump("d_g", gT, [128, 8, 8], F32)
            dump("d_beta", beta, [128, 8, 8], F32)
            dump("d_gc", gc, [128, 8, 8], F32)
            dump("d_glb", glb, [128, 8, 8], F32)
        DB = 65536
        TSTS = [DB, DB + 12288]
        SCN2 = DB + 24576
        SHR2 = SCN2 + 16384
        STF2 = SHR2 + 8192
        STB2 = STF2 + 4096
        OACCA = STB2 + 2048
        assert OACCA + 4096 <= ar.nbytes
        Sf = [ar.view(STF2 + d * 2048, [128, 4, 128], F32) for d in range(2)]
        Sb = [ar.view(STB2 + d * 1024, [128, 4, 128], BF16) for d in range(2)]
        nch = L // 128
        if grp == 0:
            oaccA = ar.view(OACCA, [128, 4, 256], F32)
        else:
            oaccB = ar.t[:, 0:8192].rearrange("p (k t) -> p k t", k=8, t=1024)[:, :, 0:512].rearrange(
                "p (h a) t -> p h a t", h=4, a=2)

        def oslice(c):
            if grp == 0:
                return oaccA[:, :, c * 128:(c + 1) * 128]
            t0_ = c * 128
            return oaccB[:, :, t0_ // 512, t0_ % 512:t0_ % 512 + 128]

        def tv(off, dt):
            return ar.view(off, [128, 4, 128], dt)

        def mask_b(k_):
            return cstb[:, 512 + k_ * 128: 512 + (k_ + 1) * 128]

        def mm4(lh, rh):
            bb = nb()
            for h in range(4):
                P.mm(ps[:, bb, h * 128:(h + 1) * 128], lh[:, h, :], rh[:, h, :])
            return bb

        def step(s_, c, d, first_touch):
            T_ = TSTS[d]
            dec, decT = tv(T_, F32), tv(T_ + 2048, F32)
            G1, G2 = dec, decT
            Lb, LTb = tv(T_ + 4096, BF16), tv(T_ + 5120, BF16)
            Ab, Bb = tv(T_ + 6144, BF16), tv(T_ + 7168, BF16)
            Cb, Db, Eb, Tb = [tv(T_ + 8192 + 1024 * k_, BF16) for k_ in range(4)]
            B2, B3 = Cb, Db
            S_ = SCN2 + d * 8192
            Xb, qkT, QdT, Vb_, Kbe, kdec, negwT, vnew = [tv(S_ + 1024 * k_, BF16) for k_ in range(8)]
            H_ = SHR2 + d * 4096
            Ktm, Vtm, KKs, KQs = [tv(H_ + 1024 * k_, BF16) for k_ in range(4)]
            ti = s_ * nch + c
            tsl = slice(ti * 128, (ti + 1) * 128)
            dsl = slice(d * 4, d * 4 + 4)
            bK = nb()
            for h in range(4):
                P.tr(bbf(bK)[:, h * 128:(h + 1) * 128], kn[:, h, tsl], identB)
            P.copy(Ktm.rearrange("p h t -> p (h t)"), bbf(bK)[:, 0:512], eng="act")
            bV = nb()
            for h in range(4):
                P.tr(bbf(bV)[:, h * 128:(h + 1) * 128], vS[:, h, tsl], identB)
            P.copy(Vtm.rearrange("p h t -> p (h t)"), bbf(bV)[:, 0:512], eng="act")
            yield
            bKK = nb()
            for h in range(4):
                P.mm(ps[:, bKK, h * 128:(h + 1) * 128], kn[:, h, tsl], kn[:, h, tsl])
            P.copy(KKs.rearrange("p h t -> p (h t)"), bank(bKK), eng="act")
            bKQ = nb()
            for h in range(4):
                P.mm(ps[:, bKQ, h * 128:(h + 1) * 128], kn[:, h, tsl], qn[:, h, tsl])
            P.copy(KQs.rearrange("p h t -> p (h t)"), bank(bKQ), eng="act")
            yield
            U_ = Uf if d == 0 else Ub
            Mdec, MdecT = (MLf, MUf) if d == 0 else (MUf, MLf)
            for h in range(4):
                P.act(G1[:, h, :], onesF, AF.Copy, scale=gT[:, ti, d * 4 + h:d * 4 + h + 1])
            for h in range(4):
                P.act(G2[:, h, :], U_, AF.Copy, scale=ngT[:, ti, d * 4 + h:d * 4 + h + 1])
            bD = nb()
            P.mm(bank(bD), U_, G1.rearrange("p h t -> p (h t)"), start=True, stop=False)
            P.mm(bank(bD), onesF, G2.rearrange("p h t -> p (h t)"), start=False, stop=True)
            yield
            P.tt(dec, b4(bD), bc_h(Mdec), ALU.add)
            P.stt(decT, b4(bD), -1.0, bc_h(MdecT), ALU.mult, ALU.add)
            P.act(dec, dec, AF.Exp)
            P.act(decT, decT, AF.Exp)
            P.tt(B2, bc_h(identB), bc_i(beta[:, ti, dsl]), ALU.mult, eng="pool")
            P.tt(B3, bc_h(identB), bc_i(egc[:, ti, dsl]), ALU.mult, eng="pool")
            bR = nb()
            P.mm(bank(bR), onesP, B2.rearrange("p h t -> p (h t)"))
            bE = nb()
            P.mm(bank(bE), onesP, B3.rearrange("p h t -> p (h t)"))
            yield
            P.tt(Lb, KKs, dec, ALU.mult)
            P.tt(Lb, Lb, bc_i(beta[:, ti, dsl]), ALU.mult)
            P.tt(LTb, KKs, decT, ALU.mult, eng="pool")
            P.tt(LTb, LTb, b4(bR), ALU.mult)
            P.tt(qkT, KQs, decT, ALU.mult, eng="pool")
            P.tt(QdT, qn[:, :, tsl], b4(bE), ALU.mult)
            for h in range(4):
                hc = d * 4 + h
                P.act(Vb_[:, h, :], Vtm[:, h, :], AF.Copy, scale=beta[:, ti, hc:hc + 1])
                P.act(Kbe[:, h, :], Ktm[:, h, :], AF.Copy, scale=bege[:, ti, hc:hc + 1])
                P.act(kdec[:, h, :], Ktm[:, h, :], AF.Copy, scale=kdf[:, ti, hc:hc + 1])
            yield
            mi = (lambda k: k) if d == 0 else (lambda k: (k + 4) if 1 <= k <= 4 else (k - 4 if k >= 5 else k))
            P.tt(Ab, Lb, bc_h(mask_b(0)), ALU.mult)
            P.tt(Bb, LTb, bc_h(mask_b(0)), ALU.mult)
            P.stt(Tb, Ab, -1.0, bc_h(identF), ALU.mult, ALU.add)
            P.stt(Xb, Bb, -1.0, bc_h(identF), ALU.mult, ALU.add)
            yield
            b1 = mm4(Bb, Ab)
            b2 = mm4(Ab, Bb)
            P.copy(Cb, b4(b1), eng="act")
            P.copy(Db, b4(b2), eng="act")
            yield
            bx = mm4(Cb, Xb)
            bt = mm4(Xb, Cb)
            b3 = mm4(Db, Cb)
            P.tt(Xb, Xb, b4(bx), ALU.add)
            P.tt(Tb, Tb, b4(bt), ALU.add)
            P.copy(Eb, b4(b3), eng="act")
            yield
            bx = mm4(Eb, Xb)
            bt = mm4(Xb, Eb)
            P.tt(Xb, Xb, b4(bx), ALU.add)
            P.tt(Tb, Tb, b4(bt), ALU.add)
            yield
            for lv in range(1, 5):
                last = (lv == 4)
                P.tt(Ab, Lb, bc_h(mask_b(mi(lv))), ALU.mult, eng="pool")
                if not last:
                    P.tt(Bb, LTb, bc_h(mask_b(mi(lv + 4))), ALU.mult)
                b1 = mm4(Ab, Xb)
                if not last:
                    b2 = mm4(Bb, Tb)
                P.copy(Cb, b4(b1), eng="act")
                if not last:
                    P.copy(Db, b4(b2), eng="act")
                yield
                bx = mm4(Tb, Cb)
                if not last:
                    bt = mm4(Xb, Db)
                P.tt(Xb, Xb, b4(bx), ALU.subtract)
                if not last:
                    P.tt(Tb, Tb, b4(bt), ALU.subtract)
                yield
            if "step0" in debug_names and grp == DBG_GRP and s_ == 0 and (c, d) == DBG_STEP:
                dump("d_dec", dec, [128, 4, 128], F32)
                dump("d_decT", decT, [128, 4, 128], F32)
                dump("d_X", Xb, [128, 4, 128], BF16)
                dump("d_qkT", qkT, [128, 4, 128], BF16)
                dump("d_QdT", QdT, [128, 4, 128], BF16)
                dump("d_KKs", KKs, [128, 4, 128], BF16)
            bW = mm4(Kbe, Xb)
            P.act(negwT.rearrange("p h t -> p (h t)"), bank(bW), AF.Copy, scale=-1.0)
            yield
            bVn = nb()
            for h in range(4):
                P.mm(ps[:, bVn, h * 128:(h + 1) * 128], Xb[:, h, :], Vb_[:, h, :], start=True, stop=False)
                P.mm(ps[:, bVn, h * 128:(h + 1) * 128], negwT[:, h, :], Sb[d][:, h, :], start=False, stop=True)
            P.copy(vnew.rearrange("p h t -> p (h t)"), bank(bVn), eng="act")
            yield
            bO = nb()
            for h in range(4):
                P.mm(ps[:, bO, h * 128:(h + 1) * 128], Sb[d][:, h, :], QdT[:, h, :], start=True, stop=False)
                P.mm(ps[:, bO, h * 128:(h + 1) * 128], vnew[:, h, :], qkT[:, h, :], start=False, stop=True)
            bS_ = mm4(kdec, vnew)
            oa = oslice(c if grp == 1 else c)
            if first_touch[c]:
                P.copy(oa, b4(bO), eng="dve")
                first_touch[c] = False
            else:
                P.tt(oa, oa, b4(bO), ALU.add)
            P.tt(Sf[d], Sf[d], bc_i(glast[:, ti, dsl]), ALU.mult)
            P.tt(Sf[d], Sf[d], b4(bS_), ALU.add)
            P.copy(Sb[d], Sf[d], eng="act")
            yield

        for s_ in range(nseq):
            for d in range(2):
                if grp == 0:
                    P.memset(Sf[d], 0.0, eng="pool")
                else:
                    P.dma(Sf[d], state_delta[d].rearrange("h k v -> k h v"), q="sp")
                P.copy(Sb[d], Sf[d], eng="pool")
            first_touch = [True] * nch
            for k in range(nch):
                gens = [step(s_, k, 0, first_touch), step(s_, nch - 1 - k, 1, first_touch)]
                while gens:
                    for g_ in list(gens):
                        try:
                            next(g_)
                        except StopIteration:
                            gens.remove(g_)
            if grp == 0:
                for d in range(2):
                    P.dma(nsd[s_, d].rearrange("h k v -> k h v"), Sf[d], q="sp", is_output=True)
            for h in range(4):
                for t_ in range(0, L, 512):
                    w = min(512, L - t_)
                    sl = slice(s_ * L + t_, s_ * L + t_ + w)
                    src = oaccA[:, h, 0:w] if grp == 0 else oaccB[:, h, t_ // 512, 0:512]
                    sq = ar.view(TSTS[0], [128, 512], BF16)
                    P.act(sq[:, 0:w], src, AF.Square)
                    b = nb()
                    P.mm(ps[:, b, 0:w], onesP, sq[:, 0:w])
                    rs = rstd[:, 0, 0:w]
                    P.act(rs, ps[:, b, 0:w], AF.Ln, bias=epsT[:, 0:1], scale=1.0 / 128.0)
                    P.act(rs, rs, AF.Exp, scale=-0.5)
                    t_f = ar.view(TSTS[0] + 2048, [128, 512], F32)
                    P.stt(t_f[:, 0:w], src, onorm[:, 0:1], rs, ALU.mult, ALU.mult)
                    P.tt(mixT[:, h, sl], t_f[:, 0:w], gS[:, h, sl], ALU.mult)

    def mixer_odd():
        l = 1
        modnorm(l, 1)
        w_in = odd_w_in[0].rearrange("(kc p) n -> p kc n", p=128)
        w_out = odd_w_out[0].rearrange("(kc p) n -> p kc n", p=128)
        onorm2 = ptab_sb[:, PT_ONORMO:PT_ONORMO + 2]
        QT_, KT_, VT_, GS_, LRT_, OACC2 = 32768, 40960, 49152, 65536, 81920, 86016
        WPO = 102400
        ZB_, ZE_ = 102400, 104448
        EG_, ENG_ = ZE_, ZB_
        QTL_, KTL_, QTT_, KTT_, ATB_ = 110592, 111616, 112640, 113664, 114688
        SF_, SB_, WG_ = 115712, 119808, 121856
        SC = 128.0 ** -0.5
        wgp = ar.view(WG_, [128, 2, 512], F32)
        P.dma(wgp[0:33, :, :], wgpad[:, :, :], q="sp")
        wpc = [0]

        def load_piece(c0, c1):
            wp = ar.view(WPO + (wpc[0] % 2) * 8192, [128, KC, 512], BF16)
            wpc[0] += 1
            P.dma(wp[:, :, 0:c1 - c0], w_in[:, :, c0:c1], q="pool")
            return wp

        qT = ar.view(QT_, [128, 8, 512], BF16)
        kT = ar.view(KT_, [128, 8, 512], BF16)
        vT = ar.view(VT_, [128, 8, 1024], BF16)
        gS = ar.view(GS_, [128, 8, 1024], BF16)
        lrT = ar.view(LRT_, [128, 1024], F32)
        glt = gates[:, 0, 0, 0:4]
        for grp in range(2):
            T0 = grp * 1024
            nseq, L = (4, 256) if grp == 0 else (1, 1024)
            nch = L // 128
            mixT = hT[:, :, T0:T0 + 1024]
            wp = load_piece(3072, 3104)
            for tt in range(2):
                b = nb()
                for kc in range(KC):
                    P.mm(ps[0:32, b, :], wp[:, kc, 0:32], hT[:, kc, T0 + tt * TT:T0 + (tt + 1) * TT],
                         start=(kc == 0), stop=(kc == KC - 1))
                P.copy(lrT[0:32, tt * TT:(tt + 1) * TT], ps[0:32, b, :], eng="dve")
            P.memset(lrT[32:33, :], 1.0, eng="dve")
            for (c0, dst, col0, scl) in ((0, qT, 0, SC), (512, kT, 0, 1.0), (1024, vT, 0, 1.0), (1536, vT, 512, 1.0)):
                wp = load_piece(c0, c0 + 512)
                for i in range(8):
                    b = nb()
                    for kc in range(KC):
                        P.mm(bank(b), hT[:, kc, T0 + i * 128:T0 + (i + 1) * 128], wp[:, kc, 0:512],
                             start=(kc == 0), stop=(kc == KC - 1))
                    if i % 2 == 0:
                        P.act(dst[:, i, col0:col0 + 512], bank(b), AF.Copy, scale=scl)
                    else:
                        P.ts(dst[:, i, col0:col0 + 512], bank(b), scl, ALU.mult)
            for pc in range(2):
                wp = load_piece(2048 + pc * 512, 2560 + pc * 512)
                for cc in range(4):
                    b = nbk(2)
                    for tt in range(2):
                        for kc in range(KC):
                            P.mm(bank(b + tt), wp[:, kc, cc * 128:(cc + 1) * 128],
                                 hT[:, kc, T0 + tt * TT:T0 + (tt + 1) * TT], start=(kc == 0), stop=(kc == KC - 1))
                    P.act(gS[:, pc * 4 + cc, :], ps[:, b:b + 2, :].rearrange("p a t -> p (a t)"), AF.Silu)
            if "oinproj" in debug_names and grp == DBG_GRP:
                dump("d_qT", qT, [128, 8, 512], BF16)
                dump("d_kT", kT, [128, 8, 512], BF16)
                dump("d_vT", vT, [128, 8, 1024], BF16)
                dump("d_gS", gS, [128, 8, 1024], BF16)
                dump("d_lrT", lrT[0:33, :], [33, 1024], F32)

            zb = ar.view(ZB_, [128, 512], F32)
            ze = ar.view(ZE_, [128, 512], F32)
            eg = ar.view(EG_, [128, 512], F32)
            eng_ = ar.view(ENG_, [128, 512], F32)
            qtl = ar.view(QTL_, [128, 512], BF16)
            ktl = ar.view(KTL_, [128, 512], BF16)
            qtt = ar.view(QTT_, [128, 4, 128], BF16)
            ktt = ar.view(KTT_, [128, 4, 128], BF16)
            atb = ar.view(ATB_, [128, 4, 128], BF16)
            Sf = ar.view(SF_, [128, 4, 256], F32)
            Sb = ar.view(SB_, [128, 4, 256], BF16)
            if grp == 1:
                oacc_lo = ar.t[:, 0:8192].rearrange("p (k t) -> p k t", k=8, t=1024)[:, :, 0:512]
                oacc_hi = ar.view(OACC2, [128, 8, 512], F32)
            else:
                oaccA = ar.view(OACC2, [128, 8, 256], F32)

            def oslice(c, fc0, fc1):
                if grp == 0:
                    return oaccA[:, fc0:fc1, c * 128:(c + 1) * 128]
                if c < 4:
                    return oacc_lo[:, fc0:fc1, c * 128:(c + 1) * 128]
                return oacc_hi[:, fc0:fc1, (c - 4) * 128:(c - 3) * 128]

            bufsets = []
            for k_ in range(2):
                if k_ == 0:
                    offs = (QTL_, KTL_, QTT_, KTT_, ATB_)
                else:
                    offs = (106496, 107520, 108544, 109568, 125952)
                bufsets.append((ar.view(offs[0], [128, 512], BF16), ar.view(offs[1], [128, 512], BF16),
                                ar.view(offs[2], [128, 4, 128], BF16), ar.view(offs[3], [128, 4, 128], BF16),
                                ar.view(offs[4], [128, 4, 128], BF16), gates[:, 0, k_, 0:4]))

            def prefix(s_, d, c, bs_):
                qtl, ktl, qtt, ktt, atb, glt = bs_
                U_ = Uf if d == 0 else Ub
                elast = identF[:, 127:128] if d == 0 else identF[:, 0:1]
                ti = s_ * nch + c
                tsl = slice(ti * 128, (ti + 1) * 128)
                bz = nb()
                P.mm(bank(bz), lrT[0:33, tsl], wgp[0:33, d, :])
                yield
                P.ts(ze, bank(bz), -80.0, ALU.max)
                P.act(ze, ze, AF.Exp, scale=-1.0)
                yield
                P.act(zb, ze, AF.Ln, bias=onesF[:, 0:1])
                yield
                bg = nb()
                P.mm(bank(bg), U_, zb)
                yield
                P.act(eg, bank(bg), AF.Exp, scale=-1.0 / 16.0)
                P.act(eng_, bank(bg), AF.Exp, scale=1.0 / 16.0)
                P.tt(qtl, qT[:, ti, :], eg, ALU.mult)
                P.tt(ktl, kT[:, ti, :], eng_, ALU.mult)
                yield
                bl = nb()
                for h in range(4):
                    P.mm(ps[:, bl, h:h + 1], eg[:, h * 128:(h + 1) * 128], elast)
                P.copy(glt, ps[:, bl, 0:4], eng="dve")
                bq = nb()
                for h in range(4):
                    P.tr(bbf(bq)[:, h * 128:(h + 1) * 128], qtl[:, h * 128:(h + 1) * 128], identB)
                P.copy(qtt.rearrange("p h t -> p (h t)"), bbf(bq)[:, 0:512], eng="act")
                bk = nb()
                for h in range(4):
                    P.tr(bbf(bk)[:, h * 128:(h + 1) * 128], ktl[:, h * 128:(h + 1) * 128], identB)
                P.copy(ktt.rearrange("p h t -> p (h t)"), bbf(bk)[:, 0:512], eng="dve")
                yield
                ba = nb()
                for h in range(4):
                    P.mm(ps[:, ba, h * 128:(h + 1) * 128], ktt[:, h, :], qtt[:, h, :])
                P.tt(atb, b4(ba), bc_h(U_), ALU.mult)
                yield

            def suffix(s_, d, c, bs_, first_touch):
                qtl, ktl, qtt, ktt, atb, glt = bs_
                ti = s_ * nch + c
                bo = nbk(2)
                for h in range(4):
                    for half in range(2):
                        fc = h * 2 + half
                        dstp = ps[:, bo + fc // 4, (fc % 4) * 128:(fc % 4 + 1) * 128]
                        P.mm(dstp, vT[:, ti, h * 256 + half * 128:h * 256 + (half + 1) * 128], atb[:, h, :],
                             start=True, stop=False)
                        P.mm(dstp, Sb[:, h, half * 128:(half + 1) * 128], qtt[:, h, :], start=False, stop=True)
                for k2 in range(2):
                    oa = oslice(c, k2 * 4, k2 * 4 + 4)
                    if first_touch[c]:
                        P.copy(oa, b4(bo + k2), eng="dve" if k2 == 0 else "act")
                    else:
                        P.tt(oa, oa, b4(bo + k2), ALU.add)
                first_touch[c] = False
                yield
                bs2 = nbk(2)
                for h in range(4):
                    P.mm(ps[:, bs2 + h // 2, (h % 2) * 256:(h % 2 + 1) * 256], ktl[:, h * 128:(h + 1) * 128],
                         vT[:, ti, h * 256:(h + 1) * 256])
                P.tt(Sf, Sf, ps[:, bs2:bs2 + 2, :].rearrange("p a (h v) -> p (a h) v", h=2, v=256), ALU.add)
                P.tt(Sf, Sf, glt.unsqueeze(2).broadcast_to([128, 4, 256]), ALU.mult)
                P.copy(Sb, Sf, eng="act")
                yield

            for s_ in range(nseq):
                first_touch = [True] * nch
                steps = [(d, (cidx if d == 0 else nch - 1 - cidx)) for d in range(2) for cidx in range(nch)]
                def drive(gens):
                    gens = [g_ for g_ in gens if g_ is not None]
                    while gens:
                        for g_ in list(gens):
                            try:
                                next(g_)
                            except StopIteration:
                                gens.remove(g_)

                drive([prefix(s_, steps[0][0], steps[0][1], bufsets[0])])
                for k_, (d, c) in enumerate(steps):
                    if k_ % nch == 0:
                        if grp == 0:
                            P.memset(Sf, 0.0, eng="pool")
                        else:
                            P.dma(Sf, state_gla[d].rearrange("h k v -> k h v"), q="sp")
                        P.copy(Sb, Sf, eng="pool")
                    nxt = None
                    if k_ + 1 < len(steps):
                        nxt = prefix(s_, steps[k_ + 1][0], steps[k_ + 1][1], bufsets[(k_ + 1) % 2])
                    drive([suffix(s_, d, c, bufsets[k_ % 2], first_touch), nxt])
                    if k_ % nch == nch - 1 and grp == 0:
                        P.dma(nsg[s_, d].rearrange("h k v -> k h v"), Sf, q="sp", is_output=True)
                for h in range(4):
                    for t_ in range(0, L, 512):
                        w = min(512, L - t_)
                        b = nb()
                        srcs = []
                        for half in range(2):
                            fc = h * 2 + half
                            if grp == 0:
                                src = oaccA[:, fc, t_:t_ + w]
                            else:
                                src = (oacc_lo if t_ == 0 else oacc_hi)[:, fc, 0:512]
                            srcs.append(src)
                            sq = ar.view(ZB_ + half * 1024, [128, 512], BF16)
                            P.act(sq[:, 0:w], src, AF.Square)
                            P.mm(ps[:, b, 0:w], onesP, sq[:, 0:w], start=(half == 0), stop=(half == 1))
                        rs = rstd[:, 0, 0:w]
                        P.act(rs, ps[:, b, 0:w], AF.Ln, bias=epsT[:, 0:1], scale=1.0 / 256.0)
                        P.act(rs, rs, AF.Exp, scale=-0.5)
                        for half in range(2):
                            fc = h * 2 + half
                            t_f = ar.view(EG_, [128, 512], F32)
                            P.stt(t_f[:, 0:w], srcs[half], onorm2[:, half:half + 1], rs, ALU.mult, ALU.mult)
                            sl = slice(s_ * L + t_, s_ * L + t_ + w)
                            P.tt(mixT[:, fc, sl], t_f[:, 0:w], gS[:, fc, sl], ALU.mult)
            if "gla" in debug_names and grp == DBG_GRP:
                dump("d_mix", mixT, [128, 8, 1024], BF16)
            wo = ar.view(WPO, [128, KC, 1024], BF16)
            P.dma(wo, w_out[:, :, :], q="pool")
            for n in range(KC):
                for tt in range(2):
                    b = nb()
                    for kc in range(KC):
                        P.mm(bank(b), wo[:, kc, n * 128:(n + 1) * 128], mixT[:, kc, tt * TT:(tt + 1) * TT],
                             start=(kc == 0), stop=(kc == KC - 1))
                    xs = xT[:, n, T0 + tt * TT:T0 + (tt + 1) * TT]
                    P.stt(xs, bank(b), modG[:, l, 1, n, grp:grp + 1], xs, ALU.mult, ALU.add)


    if stage != "all":
        ada_flush()
    if stage == "ffn0":
        ffn(0, 0)
        final_out()
    elif stage == "mix0":
        mixer_even()
        final_out()
    elif stage == "mix1":
        mixer_odd()
        final_out()
    else:
        def pre(l_, s_):
            def f(tt):
                modnorm_tile(l_, s_, tt)
                if tt == NTT - 1:
                    pre_normed[0] = (l_, s_)
            return f
        ffn(0, 0, after_tile=pre(0, 1))
        mixer_even()
        ffn(0, 1, after_tile=pre(1, 0))
        ffn(1, 0, after_tile=pre(1, 1))
        mixer_odd()
        ffn(1, 1, after_tile=final_tile)

    P.emit()
    stack.close()
    return nc, dbg


PT_ADAB = 0
PT_NORMG = PT_ADAB + 144
PT_FINALG = PT_NORMG + 48
PT_CONV = PT_FINALG + 8
PT_SINK = PT_CONV + 60
PT_ALOG = PT_SINK + 4
PT_DTB = PT_ALOG + 8
PT_ONORME = PT_DTB + 8
PT_ONORMO = PT_ONORME + 1
PT_COLS = PT_ONORMO + 2
DBG_GRP = 0
DBG_STEP = (0, 0)

C_IDENT = 0
C_UF, C_UB, C_SEL127, C_SEL0, C_ONES, C_OFFD, C_ML, C_MU, C_RM = [128 * i for i in range(1, 10)]
CST_COLS = 128 * 10
CSTB_COLS = 512 + 9 * 128


def _fm(v):
    v = np.asarray(v, np.float32)
    return np.ascontiguousarray(v.reshape(-1, 128).T)


def make_tables(inputs):
    pt = np.zeros((128, PT_COLS), np.float32)
    for l in range(2):
        pt[:, PT_ADAB + l * 72: PT_ADAB + (l + 1) * 72] = _fm(inputs["ada_b"][l])
        for s in range(3):
            o = PT_NORMG + (l * 3 + s) * 8
            pt[:, o:o + 8] = _fm(inputs["norm_g"][l, s])
    pt[:, PT_FINALG:PT_FINALG + 8] = _fm(inputs["final_g"])
    cv = np.asarray(inputs["even_conv"], np.float32)[0]
    pt[:, PT_CONV:PT_CONV + 60] = cv.T.reshape(12, 128, 5).transpose(1, 0, 2).reshape(128, 60)
    pt[:, PT_SINK:PT_SINK + 4] = np.broadcast_to(np.asarray(inputs["even_sink"], np.float32)[0][None, :], (128, 4))
    pt[:, PT_ALOG:PT_ALOG + 8] = np.broadcast_to(np.asarray(inputs["even_a_log"], np.float32)[0].reshape(1, 8), (128, 8))
    pt[:, PT_DTB:PT_DTB + 8] = np.broadcast_to(np.asarray(inputs["even_dt_bias"], np.float32)[0].reshape(1, 8), (128, 8))
    pt[:, PT_ONORME:PT_ONORME + 1] = np.asarray(inputs["even_onorm"], np.float32)[0].reshape(128, 1)
    pt[:, PT_ONORMO:PT_ONORMO + 2] = _fm(inputs["odd_onorm"][0])
    cst = np.zeros((128, CST_COLS), np.float32)
    cst[:, C_IDENT:C_IDENT + 128] = np.eye(128, dtype=np.float32)
    kk, ii = np.meshgrid(np.arange(128), np.arange(128), indexing="ij")
    NEG = -30000.0
    cst[:, C_UF:C_UF + 128] = (kk <= ii)
    cst[:, C_UB:C_UB + 128] = (kk >= ii)
    cst[127, C_SEL127:C_SEL127 + 128] = 1.0
    cst[0, C_SEL0:C_SEL0 + 128] = 1.0
    cst[:, C_ONES:C_ONES + 128] = 1.0
    cst[:, C_OFFD:C_OFFD + 128] = (kk != ii)
    cst[:, C_ML:C_ML + 128] = np.where(ii <= kk, 0.0, NEG)
    cst[:, C_MU:C_MU + 128] = np.where(ii >= kk, 0.0, NEG)
    rm = np.zeros((128, 128), np.float32)
    for dp in range(128):
        if (dp % 64) < 32:
            rm[dp + 32, dp] = -1.0
        else:
            rm[dp - 32, dp] = 1.0
    cst[:, C_RM:C_RM + 128] = rm
    return pt, cst


def make_bmask():
    i, j = np.meshgrid(np.arange(128), np.arange(128), indexing="ij")
    ms = [(i // 8 == j // 8) & (i != j)]
    for b in (8, 16, 32, 64):
        ms.append((i // (2 * b) == j // (2 * b)) & ((i // b) % 2 == 1) & ((j // b) % 2 == 0))
    for b in (8, 16, 32, 64):
        ms.append((i // (2 * b) == j // (2 * b)) & ((i // b) % 2 == 0) & ((j // b) % 2 == 1))
    return np.ascontiguousarray(np.concatenate([m.astype(np.float32) for m in ms], axis=1))


def make_rope():
    t = np.arange(1024)
    row = (t // 64).astype(np.float64)
    col = (t % 64).astype(np.float64)
    inv = 10000.0 ** (-np.arange(32, dtype=np.float64) / 32.0)
    ang = np.zeros((128, 1024))
    for d in range(128):
        pos = row if d < 64 else col
        ang[d] = pos * np.float32(inv[d % 32])
    ang32 = np.zeros((128, 1024), np.float32)
    inv32 = (np.float32(10000.0) ** (-np.arange(32, dtype=np.float32) / np.float32(32))).astype(np.float32)
    for d in range(128):
        pos = (row if d < 64 else col).astype(np.float32)
        ang32[d] = pos * inv32[d % 32]
    return np.ascontiguousarray(np.stack([np.cos(ang32), np.sin(ang32)], axis=1).astype(np.float32))


def make_in_maps(inputs, stage="all"):
    pt, cst = make_tables(inputs)
    rope_t = make_rope()
    bmask_t = make_bmask()
    wg = np.asarray(inputs["odd_w_gate"], np.float32)[0]
    wgpad_t = np.zeros((33, 2, 512), np.float32)
    wgpad_t[0:16, 0, :] = wg[0]
    wgpad_t[16:32, 1, :] = wg[1]
    wgpad_t[32, :, :] = np.asarray(inputs["odd_gate_bias"], np.float32)[0]
    maps = []
    xp = np.asarray(inputs["x_prompt"], np.float32)
    xs = np.asarray(inputs["x_sample"], np.float32)
    for c in range(8):
        xin = np.concatenate([xp[4 * c:4 * c + 4].reshape(1024, D), xs[c]], axis=0)
        cond = np.stack([np.asarray(inputs["c_ctx"], np.float32), np.asarray(inputs["c"], np.float32)[c]], axis=-1)
        condT = np.ascontiguousarray(cond.reshape(KC, 128, 2).transpose(1, 0, 2))
        m = {"xin": np.ascontiguousarray(xin), "condT": condT, "ptab": pt, "cst": cst,
             "ada_w": np.asarray(inputs["ada_w"], np.float32),
             "even_w_in": np.asarray(inputs["even_w_in"], np.float32),
             "even_w_out": np.asarray(inputs["even_w_out"], np.float32),
             "rope": rope_t, "bmask": bmask_t, "wgpad": wgpad_t,
             "odd_w_in": np.asarray(inputs["odd_w_in"], np.float32),
             "odd_w_out": np.asarray(inputs["odd_w_out"], np.float32),
             "state_gla": np.ascontiguousarray(np.asarray(inputs["state_gla"], np.float32)[c, 0]),
             "cache_k": np.ascontiguousarray(np.asarray(inputs["cache_k"], np.float32)[c, 0]),
             "cache_v": np.ascontiguousarray(np.asarray(inputs["cache_v"], np.float32)[c, 0]),
             "state_delta": np.ascontiguousarray(np.asarray(inputs["state_delta"], np.float32)[c, 0])}
        if stage in ("all", "ffn0"):
            m["ffn_w_gu"] = np.asarray(inputs["ffn_w_gu"], np.float32)
            m["ffn_w_down"] = np.asarray(inputs["ffn_w_down"], np.float32)
        maps.append(m)
    return maps


_CACHE = {}


def kernel(**inputs):
    if "nc" not in _CACHE:
        _CACHE["nc"] = build_program("all")[0]
    nc = _CACHE["nc"]
    maps = make_in_maps(inputs)
    res = run_bass_kernel_spmd(nc, maps, core_ids=list(range(8)))
    rs = res.results
    ys = [np.asarray(r["yout"], np.float32) for r in rs]
    y_prompt = np.concatenate([y[:1024].reshape(4, 256, D) for y in ys], axis=0)
    y_sample = np.stack([y[1024:] for y in ys], axis=0)
    nsd = np.concatenate([np.asarray(r["nsd"], np.float32) for r in rs], axis=0)[:, None]
    nck = np.concatenate([np.asarray(r["nck"], np.float32).reshape(4, 256, 2, 128) for r in rs], axis=0)[:, None]
    ncv = np.concatenate([np.asarray(r["ncv"], np.float32).reshape(4, 256, 2, 128) for r in rs], axis=0)[:, None]
    nsg = np.concatenate([np.asarray(r["nsg"], np.float32) for r in rs], axis=0)[:, None]
    return (y_prompt, y_sample, np.ascontiguousarray(nsd), np.ascontiguousarray(nck),
            np.ascontiguousarray(ncv), np.ascontiguousarray(nsg))
```

```python
import numpy as np
import concourse.bass as bass
import concourse.mybir as mybir

F32 = mybir.dt.float32
BF16 = mybir.dt.bfloat16
AF = mybir.ActivationFunctionType
ALU = mybir.AluOpType
AX = mybir.AxisListType

_DTSZ = {F32: 4, BF16: 2}


def _region(ap):
    sp = str(ap.space)
    if "DRAM" in sp.upper() or "HBM" in sp.upper():
        return None
    sz = _DTSZ[ap.dtype]
    pat = ap.ap
    pstep, pcnt = pat[0]
    off = int(ap.offset)
    if pstep == 0:
        p0, f0 = 0, off
        pstep = 1 << 40
    else:
        p0, f0 = off // pstep, off % pstep
    ext = 1
    for st, cnt in pat[1:]:
        ext += (cnt - 1) * abs(st)
    b0, b1 = f0 * sz, (f0 + ext) * sz
    if "PSUM" in sp.upper():
        b0 = (b0 // 2048) * 2048
        b1 = ((b1 + 2047) // 2048) * 2048
        return (ap.tensor.name, 0, 128, b0, b1)
    return (ap.tensor.name, p0, p0 + pcnt, b0, b1)


def _ovl(a, b):
    return a[0] == b[0] and a[1] < b[2] and b[1] < a[2] and a[3] < b[4] and b[3] < a[4]


def _covers(a, b):
    return a[0] == b[0] and a[1] <= b[1] and a[2] >= b[2] and a[3] <= b[3] and a[4] >= b[4]


class Op:
    __slots__ = ("eng", "fn", "seq", "inc", "waits", "ctr", "is_dma", "val")

    def __init__(self, eng, fn, is_dma=False):
        self.eng = eng
        self.fn = fn
        self.inc = False
        self.waits = []
        self.is_dma = is_dma
        self.ctr = None
        self.seq = 0
        self.val = 0


ENGS = ("pe", "act", "dve", "pool", "sp")
NDMA = 24


class Prog:
    def __init__(self, nc):
        self.nc = nc
        self.streams = {e: [] for e in ENGS}
        self.seqc = {}
        self.known = {e: {} for e in ENGS}
        self.recs = {}
        self.dma_rr = {"h": 0, "s": 0}
        self.dma_last = {}
        self.nops = 0
        self.out_dmas = []

    def _need(self, op, dep):
        if dep is op:
            return
        if dep.ctr == op.ctr and not dep.is_dma:
            pass
        k = self.known[op.eng]
        if k.get(dep.ctr, -1) >= dep.seq:
            return
        k[dep.ctr] = dep.seq
        dep.inc = True
        op.waits.append(dep)

    BK = 2048

    def _buckets(self, r):
        return range(r[3] // self.BK, (r[4] - 1) // self.BK + 1)

    def _track(self, op, reads, writes):
        BK = self.BK
        for ap in reads:
            r = _region(ap)
            if r is None:
                continue
            for b in self._buckets(r):
                lst = self.recs.setdefault((r[0], b), [])
                is_ps = (r[0] == "ps")
                for rec in lst:
                    if _ovl(rec[0], r) and (rec[1] == "w" or (is_ps and rec[2].ctr != op.ctr)):
                        d = rec[2]
                        if d.ctr == op.ctr and op.eng == "pe":
                            continue
                        self._need(op, d)
                for i, rec in enumerate(lst):
                    if rec[1] == "r" and rec[2].ctr == op.ctr and rec[0] == r:
                        lst.pop(i)
                        break
                lst.append([r, "r", op])
        for ap in writes:
            r = _region(ap)
            if r is None:
                continue
            for b in self._buckets(r):
                lst = self.recs.setdefault((r[0], b), [])
                keep = []
                lo, hi = b * BK, (b + 1) * BK
                for rec in lst:
                    if rec[2] is op:
                        keep.append(rec)
                        continue
                    rr = rec[0]
                    if _ovl(rr, r):
                        d = rec[2]
                        if not (d.ctr == op.ctr and op.eng == "pe"):
                            self._need(op, d)
                        if (r[1] <= rr[1] and r[2] >= rr[2]
                                and r[3] <= max(rr[3], lo) and r[4] >= min(rr[4], hi)):
                            continue
                    keep.append(rec)
                keep.append([r, "w", op])
                self.recs[(r[0], b)] = keep

    def _add(self, eng, fn, reads, writes):
        op = Op(eng, fn)
        op.ctr = eng
        op.seq = self.seqc.get(eng, 0)
        self.seqc[eng] = op.seq + 1
        self._track(op, reads, writes)
        self.streams[eng].append(op)
        self.nops += 1
        return op

    def dma(self, out, in_, q="sp", is_output=False):
        op = Op(q, None, is_dma=True)
        kind = "s" if q == "pool" else "h"
        k = self.dma_rr[kind]
        self.dma_rr[kind] = (k + 1) % (NDMA // 2)
        op.ctr = "dma%s%d" % (kind, k)
        op.seq = self.seqc.get(op.ctr, 0)
        self.seqc[op.ctr] = op.seq + 1
        prev = self.dma_last.get(op.ctr)
        if prev is not None:
            self._need(op, prev)
        self.dma_last[op.ctr] = op
        op.inc = True
        self._track(op, [in_], [out])
        op.fn = lambda e, out=out, in_=in_: e.dma_start(out=out, in_=in_)
        self.streams[q].append(op)
        if is_output:
            self.out_dmas.append(op)
        return op

    def mm(self, out, lhsT, rhs, start=True, stop=True):
        return self._add("pe", lambda e: e.matmul(out, lhsT, rhs, start=start, stop=stop),
                         [lhsT, rhs], [out])

    def tr(self, out, in_, ident):
        return self._add("pe", lambda e: e.transpose(out, in_, ident), [in_, ident], [out])

    def act(self, out, in_, func, bias=None, scale=1.0, accum=None):
        rd = [in_]
        if bias is not None and not isinstance(bias, (int, float)):
            rd.append(bias)
        if not isinstance(scale, (int, float)):
            rd.append(scale)
        wr = [out] + ([accum] if accum is not None else [])
        kw = {}
        if bias is not None:
            kw["bias"] = bias
        if accum is not None:
            kw["accum_out"] = accum
        return self._add("act", lambda e: e.activation(out=out, in_=in_, func=func, scale=scale, **kw),
                         rd, wr)

    def tt(self, out, in0, in1, op, eng="dve"):
        return self._add(eng, lambda e: e.tensor_tensor(out=out, in0=in0, in1=in1, op=op),
                         [in0, in1], [out])

    def ts(self, out, in0, s1, op0, s2=None, op1=None, eng="dve", accum=None):
        rd = [in0] + [s for s in (s1, s2) if s is not None and not isinstance(s, (int, float))]
        kw = {}
        if op1 is not None:
            kw["op1"] = op1
        if accum is not None:
            kw["accum_out"] = accum
        wr = [out] + ([accum] if accum is not None else [])
        return self._add(eng, lambda e: e.tensor_scalar(out=out, in0=in0, scalar1=s1, scalar2=s2, op0=op0, **kw),
                         rd, wr)

    def stt(self, out, in0, scalar, in1, op0, op1, eng="dve"):
        rd = [in0, in1] + ([scalar] if not isinstance(scalar, (int, float)) else [])
        return self._add(eng, lambda e: e.scalar_tensor_tensor(out=out, in0=in0, scalar=scalar, in1=in1,
                                                               op0=op0, op1=op1), rd, [out])

    def copy(self, out, in_, eng="dve"):
        if eng == "act":
            return self.act(out, in_, AF.Copy)
        return self._add(eng, lambda e: e.tensor_copy(out=out, in_=in_), [in_], [out])

    def reduce(self, out, in_, op, eng="dve", axis=None):
        axis = axis or AX.X
        return self._add(eng, lambda e: e.tensor_reduce(out=out, in_=in_, axis=axis, op=op), [in_], [out])

    def recip(self, out, in_):
        return self._add("dve", lambda e: e.reciprocal(out=out, in_=in_), [in_], [out])

    def memset(self, ap, val, eng="dve"):
        return self._add(eng, lambda e: e.memset(ap, val), [], [ap])

    def emit(self):
        nc = self.nc
        sems = {}
        import contextlib
        stack = contextlib.ExitStack()
        allops = []
        for e in ENGS:
            allops.extend(self.streams[e])
        ctrs = sorted(set(o.ctr for o in allops))
        for c in ctrs:
            sems[c] = stack.enter_context(nc.semaphore("s_" + c))
        cnt = {c: 0 for c in ctrs}
        byctr = {c: [] for c in ctrs}
        for o in allops:
            byctr[o.ctr].append(o)
        for c in ctrs:
            ops = sorted(byctr[c], key=lambda o: o.seq)
            v = 0
            for o in ops:
                if o.inc:
                    v += 16 if o.is_dma else 1
                o.val = v
        fin = stack.enter_context(nc.semaphore("s_fin"))
        block = stack.enter_context(nc.Block())
        prog = self

        def run_stream(eng_name, e):
            for o in prog.streams[eng_name]:
                for d in o.waits:
                    e.wait_ge(sems[d.ctr], d.val)
                ins = o.fn(e)
                if o.inc:
                    ins.then_inc(sems[o.ctr], 16 if o.is_dma else 1)

        @block.tensor
        def _(e):
            run_stream("pe", e)

        @block.scalar
        def _(e):
            run_stream("act", e)

        @block.vector
        def _(e):
            run_stream("dve", e)

        @block.gpsimd
        def _(e):
            run_stream("pool", e)

        @block.sync
        def _(e):
            run_stream("sp", e)
            for o in prog.out_dmas:
                e.wait_ge(sems[o.ctr], o.val)

        stack.close()
from concourse.bass_utils import run_bass_kernel_spmd

D = 1024
NTOK = 2048
KC = 8
TT = 512
NTT = 4
DFF = 2816
NHC = 22
EPS = 1e-6
EVEN_IN = 3088
ODD_IN = 3104


class Arena:
    def __init__(self, nc, name, nbytes, stack):
        self.t = stack.enter_context(nc.sbuf_tensor(name, [128, nbytes // 4], F32))
        self.nbytes = nbytes

    def view(self, off, shape, dtype):
        sz = 4 if dtype == F32 else 2
        n = 1
        for s in shape[1:]:
            n *= s
        assert off % 4 == 0 and (n * sz) % 4 == 0 and off + n * sz <= self.nbytes, (off, shape, self.nbytes)
        ap = self.t[0:shape[0], off // 4: off // 4 + (n * sz) // 4]
        if dtype != F32:
            ap = ap.bitcast(dtype)
        if len(shape) == 3:
            ap = ap.rearrange("p (a b) -> p a b", a=shape[1], b=shape[2])
        elif len(shape) == 4:
            ap = ap.rearrange("p (a b c) -> p a b c", a=shape[1], b=shape[2], c=shape[3])
        return ap


def build_program(stage="all", debug_names=()):
    import contextlib
    nc = bass.Bass("TRN2", target_bir_lowering=False)
    P = Prog(nc)
    stack = contextlib.ExitStack()

    def din(name, shape, dt=F32):
        return nc.dram_tensor(name, list(shape), dt, kind="ExternalInput").ap()

    def dout(name, shape, dt=F32):
        return nc.dram_tensor(name, list(shape), dt, kind="ExternalOutput").ap()

    xin = din("xin", [NTOK, D])
    condT = din("condT", [128, KC, 2])
    ptab = din("ptab", [128, PT_COLS])
    cst = din("cst", [128, CST_COLS])
    ada_w = din("ada_w", [2, D, 9 * D])
    if stage in ("all", "ffn0"):
        w_gu = din("ffn_w_gu", [2, 2, D, 2 * DFF])
        w_dn = din("ffn_w_down", [2, 2, DFF, D])
    yout = dout("yout", [NTOK, D])
    even_w_in = din("even_w_in", [1, D, EVEN_IN])
    even_w_out = din("even_w_out", [1, D, D])
    rope = din("rope", [128, 2, 1024])
    bmask = din("bmask", [128, 9 * 128])
    odd_w_in = din("odd_w_in", [1, D, ODD_IN])
    odd_w_out = din("odd_w_out", [1, D, D])
    wgpad = din("wgpad", [33, 2, 512])
    state_gla = din("state_gla", [2, 4, 128, 256])
    nsg = dout("nsg", [4, 2, 4, 128, 256])
    cache_k = din("cache_k", [512, 2, 128])
    cache_v = din("cache_v", [512, 2, 128])
    state_delta = din("state_delta", [2, 4, 128, 128])
    nck = dout("nck", [1024, 256])
    ncv = dout("ncv", [1024, 256])
    nsd = dout("nsd", [4, 2, 4, 128, 128])

    dbg = {}
    xT = stack.enter_context(nc.sbuf_tensor("xT", [128, KC, NTOK], F32))
    ptab_sb = stack.enter_context(nc.sbuf_tensor("ptab_sb", [128, PT_COLS], F32))
    cst_sb = stack.enter_context(nc.sbuf_tensor("cst_sb", [128, CST_COLS], F32))
    cstb = stack.enter_context(nc.sbuf_tensor("cstb", [128, CSTB_COLS], BF16))
    modT = stack.enter_context(nc.sbuf_tensor("modT", [128, 2, 72, 2], F32))
    modA = stack.enter_context(nc.sbuf_tensor("modA", [128, 2, 3, KC, 2], F32))
    modG = stack.enter_context(nc.sbuf_tensor("modG", [128, 2, 3, KC, 2], F32))
    scT = stack.enter_context(nc.sbuf_tensor("scT", [128, KC, 2], BF16))
    condsb = stack.enter_context(nc.sbuf_tensor("condsb", [128, KC, 2], F32))
    gates = stack.enter_context(nc.sbuf_tensor("gates", [128, 8, 8, 16], F32))
    epsT = stack.enter_context(nc.sbuf_tensor("epsT", [128, 2], F32))
    rstd = stack.enter_context(nc.sbuf_tensor("rstd", [128, 2, TT], F32))
    ar = Arena(nc, "arena", 126976, stack)
    ps = stack.enter_context(nc.psum_tensor("ps", [128, 8, 512], F32))

    def bank(b):
        return ps[:, b, :]

    identF = cst_sb[:, C_IDENT:C_IDENT + 128]
    identB = cstb[:, 0:128]
    onesP = cstb[:, 128:256]

    P.dma(ptab_sb[:, :], ptab[:, :], q="sp")
    P.dma(cst_sb[:, :], cst[:, :], q="sp")
    P.dma(condsb[:, :, :], condT[:, :, :], q="sp")
    P.copy(identB, identF, eng="dve")
    P.memset(onesP, 1.0, eng="dve")
    P.memset(epsT[:, :], EPS, eng="dve")
    P.memset(gates[:, :, :, :], 0.0, eng="pool")

    STG = 32768
    for i in range(16):
        stg = ar.view(STG + (i % 2) * 4096, [128, D], F32)
        P.dma(stg, xin[i * 128:(i + 1) * 128, :], q="sp" if i % 2 == 0 else "act")
        for half in range(2):
            b = (i * 2 + half) % 4
            for kk in range(4):
                kc = half * 4 + kk
                P.tr(ps[:, b, kk * 128:(kk + 1) * 128], stg[:, kc * 128:(kc + 1) * 128], identF)
            src = ps[:, b, :].rearrange("p (a t) -> p a t", a=4, t=128)
            dst = xT[:, half * 4:half * 4 + 4, i * 128:(i + 1) * 128]
            if half == 0:
                P.copy(dst, src, eng="dve")
            else:
                P.copy(dst, src, eng="act")

    P.act(scT[:, :, :], condsb[:, :, :], AF.Silu)
    ADAW = 40960

    def ada_finish(l, s):
        ng = ptab_sb[:, PT_NORMG + (l * 3 + s) * 8: PT_NORMG + (l * 3 + s + 1) * 8]
        sc = modT[:, l, (3 * s + 1) * 8:(3 * s + 2) * 8, :]
        P.stt(modA[:, l, s, :, :], sc, 1.0, ng.unsqueeze(2).broadcast_to([128, KC, 2]), ALU.add, ALU.mult)
        gt = modT[:, l, (3 * s + 2) * 8:(3 * s + 3) * 8, :]
        P.ts(modG[:, l, s, :, :], gt, 0.5 if s != 1 else 1.0, ALU.mult)

    for i in range(3):
        wb = ar.view(ADAW + (i % 3) * 16384, [128, KC, 1024], BF16)
        src = ada_w[0].rearrange("(kc p) n -> p kc n", p=128)[:, :, i * 1024:(i + 1) * 1024]
        P.dma(wb, src, q="pool")
        for n in range(8):
            j = i * 8 + n
            for kc in range(KC):
                P.mm(ps[:, 4, 2 * j:2 * j + 2], wb[:, kc, n * 128:(n + 1) * 128], scT[:, kc, :],
                     start=(kc == 0), stop=(kc == KC - 1))
    P.tt(modT[:, 0, 0:24, :], ps[:, 4, 0:48].rearrange("p (j c) -> p j c", j=24, c=2),
         ptab_sb[:, PT_ADAB:PT_ADAB + 24].unsqueeze(2).broadcast_to([128, 24, 2]), ALU.add)
    ada_finish(0, 0)
    ada_tasks = [(0, i, q4) for i in range(3, 9) for q4 in range(4)] + \
                [(1, i, q4) for i in range(9) for q4 in range(4)]
    ada_cnt = [0, 0, 0]

    ada_pending = []

    def ada_load():
        l, i, q4 = ada_tasks.pop(0)
        wb = ar.view(TMP + (ada_cnt[0] % 2) * 4096, [128, KC, 256], BF16)
        ada_cnt[0] += 1
        c0 = i * 1024 + q4 * 256
        P.dma(wb, ada_w[l].rearrange("(kc p) n -> p kc n", p=128)[:, :, c0:c0 + 256], q="pool")
        ada_pending.append((l, i, q4, wb))

    def ada_compute():
        l, i, q4, wb = ada_pending.pop(0)
        bank_b = 6 + (ada_cnt[1] % 2)
        ada_cnt[1] += 1
        for nn in range(2):
            for kc in range(KC):
                P.mm(ps[:, bank_b, 2 * nn:2 * nn + 2], wb[:, kc, nn * 128:(nn + 1) * 128], scT[:, kc, :],
                     start=(kc == 0), stop=(kc == KC - 1))
        j0 = i * 8 + q4 * 2
        P.tt(modT[:, l, j0:j0 + 2, :], ps[:, bank_b, 0:4].rearrange("p (j c) -> p j c", j=2, c=2),
             ptab_sb[:, PT_ADAB + l * 72 + j0:PT_ADAB + l * 72 + j0 + 2].unsqueeze(2).broadcast_to([128, 2, 2]),
             ALU.add)
        if q4 == 3 and i % 3 == 2:
            ada_finish(l, i // 3)

    def ada_tick():
        ada_cnt[2] += 1
        if ada_cnt[2] % 3 != 0:
            return
        if len(ada_pending) == 2 or (ada_pending and not ada_tasks):
            ada_compute()
        if ada_tasks and len(ada_pending) < 2:
            ada_load()

    def ada_flush():
        while ada_tasks or ada_pending:
            if ada_tasks and len(ada_pending) < 2:
                ada_load()
            else:
                ada_compute()

    HT = 0
    FW = 32768
    ACTB = FW + 49152
    SG = ACTB + 32768
    TMP = SG + 4096
    SQ = TMP + 4096
    assert SQ + 4096 <= ar.nbytes, SQ + 4096
    hT = ar.view(HT, [128, KC, NTOK], BF16)

    def rms_tile(tt, bank0=6):
        b = bank0 + (tt % 2)
        for kc in range(KC):
            sq = ar.view(SQ + ((tt * KC + kc) % 4) * 1024, [128, TT], BF16)
            P.act(sq, xT[:, kc, tt * TT:(tt + 1) * TT], AF.Square)
            P.mm(bank(b), onesP, sq, start=(kc == 0), stop=(kc == KC - 1))
        rs = rstd[:, tt % 2, :]
        P.act(rs, bank(b), AF.Ln, bias=epsT[:, 0:1], scale=1.0 / 1024.0)
        P.act(rs, rs, AF.Exp, scale=-0.5)
        return rs

    def modnorm_tile(l, s, tt):
        rs = rms_tile(tt)
        c = 0 if tt < 2 else 1
        for kc in range(KC):
            tmp = ar.view(TMP + ((tt * KC + kc) % 2) * 2048, [128, TT], F32)
            P.stt(tmp, xT[:, kc, tt * TT:(tt + 1) * TT], modA[:, l, s, kc, c:c + 1],
                  rs, ALU.mult, ALU.mult)
            P.act(hT[:, kc, tt * TT:(tt + 1) * TT], tmp, AF.Identity,
                  bias=modT[:, l, 3 * s * 8 + kc, c:c + 1])

    pre_normed = [None]

    def modnorm(l, s):
        if pre_normed[0] == (l, s):
            pre_normed[0] = None
            return
        for tt in range(NTT):
            modnorm_tile(l, s, tt)

    GROUPS = [(0, 4), (4, 8), (8, 12), (12, 16), (16, 19), (19, 22)]

    def ffn(l, i, after_tile=None):
        s = 0 if i == 0 else 2
        modnorm(l, s)
        wgu = w_gu[l, i].rearrange("(kc p) n -> p kc n", p=128)
        wdn = w_dn[l, i].rearrange("(g p) n -> p g n", p=128)

        def load(g):
            j0, j1 = GROUPS[g]
            G = j1 - j0
            base = FW + (g % 2) * 24576
            wg = ar.view(base, [128, KC, 512], BF16)
            wu = ar.view(base + 8192, [128, KC, 512], BF16)
            wd = ar.view(base + 16384, [128, 4, 1024], BF16)
            P.dma(wg[:, :, 0:G * 128], wgu[:, :, j0 * 128:j1 * 128], q="pool")
            P.dma(wu[:, :, 0:G * 128], wgu[:, :, DFF + j0 * 128:DFF + j1 * 128], q="pool")
            P.dma(wd[:, 0:G, :], wdn[:, j0:j1, :], q="pool")
            return wg, wu, wd

        pair = [0]

        def gu(g, W):
            j0, j1 = GROUPS[g]
            wg, wu, _ = W
            ab = ar.view(ACTB + (g % 2) * 16384, [128, 4, NTOK], BF16)
            for jj in range(j1 - j0):
                for tt in range(NTT):
                    pb = (pair[0] % 2) * 2
                    pair[0] += 1
                    rhs = None
                    for kc in range(KC):
                        P.mm(bank(pb), wg[:, kc, jj * 128:(jj + 1) * 128], hT[:, kc, tt * TT:(tt + 1) * TT],
                             start=(kc == 0), stop=(kc == KC - 1))
                    for kc in range(KC):
                        P.mm(bank(pb + 1), wu[:, kc, jj * 128:(jj + 1) * 128], hT[:, kc, tt * TT:(tt + 1) * TT],
                             start=(kc == 0), stop=(kc == KC - 1))
                    sg = ar.view(SG + (pair[0] % 2) * 2048, [128, TT], F32)
                    P.act(sg, bank(pb), AF.Silu)
                    P.tt(ab[:, jj, tt * TT:(tt + 1) * TT], sg, bank(pb + 1), ALU.mult)
                    ada_tick()

        ycnt = [0]

        def down(g, W, tile_major=False):
            j0, j1 = GROUPS[g]
            _, _, wd = W
            ab = ar.view(ACTB + (g % 2) * 16384, [128, 4, NTOK], BF16)
            order = [(n, tt) for n in range(KC) for tt in range(NTT)]
            if tile_major:
                order = [(n, tt) for tt in range(NTT) for n in range(KC)]
            for (n, tt) in order:
                if True:
                    c = 0 if tt < 2 else 1
                    yb = 4 + (ycnt[0] % 4)
                    ycnt[0] += 1
                    for jj in range(j1 - j0):
                        P.mm(bank(yb), wd[:, jj, n * 128:(n + 1) * 128], ab[:, jj, tt * TT:(tt + 1) * TT],
                             start=(jj == 0), stop=(jj == j1 - j0 - 1))
                    xs = xT[:, n, tt * TT:(tt + 1) * TT]
                    P.stt(xs, bank(yb), modG[:, l, s, n, c:c + 1], xs, ALU.mult, ALU.add)
                    if tile_major and n == KC - 1 and after_tile is not None:
                        after_tile(tt)

        W = {}
        W[0] = load(0)
        W[1] = load(1)
        gu(0, W[0])
        for g in range(len(GROUPS)):
            if g + 1 < len(GROUPS):
                gu(g + 1, W[g + 1])
            last = (g == len(GROUPS) - 1)
            if last:
                while ada_pending:
                    ada_compute()
            down(g, W[g], tile_major=last)
            if g + 2 < len(GROUPS):
                W[g + 2] = load(g + 2)
        while ada_pending:
            ada_compute()
        if (l, i) == (0, 1):
            ada_flush()

    def final_tile(tt):
        fg = ptab_sb[:, PT_FINALG:PT_FINALG + 8]
        YS = ACTB
        rs = rms_tile(tt)
        for i in range(4 * tt, 4 * tt + 4):
            yt = ar.view(YS + (i % 2) * 4096, [128, KC, 128], F32)
            for kc in range(KC):
                P.stt(yt[:, kc, :], xT[:, kc, i * 128:(i + 1) * 128], fg[:, kc:kc + 1],
                      rs[:, (i % 4) * 128:(i % 4 + 1) * 128], ALU.mult, ALU.mult)
            st = ar.view(YS + 8192 + (i % 2) * 4096, [128, D], F32)
            for half in range(2):
                b = (i * 2 + half) % 4
                for kk in range(4):
                    kc = half * 4 + kk
                    P.tr(ps[:, b, kk * 128:(kk + 1) * 128], yt[:, kc, :], identF)
                if half == 0:
                    P.copy(st[:, 0:512], bank(b), eng="dve")
                else:
                    P.copy(st[:, 512:1024], bank(b), eng="act")
            P.dma(yout[i * 128:(i + 1) * 128, :], st, q="sp", is_output=True)

    def final_out():
        for tt in range(NTT):
            final_tile(tt)

    NEG = -30000.0
    Uf = cst_sb[:, C_UF:C_UF + 128]
    Ub = cst_sb[:, C_UB:C_UB + 128]
    sel127 = cst_sb[:, C_SEL127:C_SEL127 + 128]
    sel0 = cst_sb[:, C_SEL0:C_SEL0 + 128]
    onesF = cst_sb[:, C_ONES:C_ONES + 128]
    offdiag = cst_sb[:, C_OFFD:C_OFFD + 128]
    MLf = cst_sb[:, C_ML:C_ML + 128]
    MUf = cst_sb[:, C_MU:C_MU + 128]
    Rm = cst_sb[:, C_RM:C_RM + 128]
    MLb = cstb[:, 256:384]
    MUb = cstb[:, 384:512]
    P.dma(cstb[:, 512:512 + 9 * 128], bmask[:, :], q="pool")
    P.copy(MLb, MLf, eng="dve")
    P.copy(MUb, MUf, eng="dve")

    def bc_h(m):
        return m.unsqueeze(1).broadcast_to([128, 4, 128])

    def bc_i(v):
        return v.unsqueeze(2).broadcast_to([128, 4, 128])

    def b4(b):
        return ps[:, b, :].rearrange("p (h t) -> p h t", h=4, t=128)

    def bbf(b):
        return ps[:, b, :].bitcast(BF16)

    nbc = [0]

    def nb():
        b = nbc[0] % 8
        nbc[0] += 1
        return b

    def nbk(k):
        c = (nbc[0] + k - 1) // k * k
        nbc[0] = c + k
        return c % 8

    def dump(name, ap, shape, dt):
        d = dout(name, shape, dt)
        P.dma(d, ap, q="sp", is_output=True)
        dbg[name] = (shape, dt)

    QN, KN, VS, GS = 32768, 40960, 49152, 57344
    QPL, QRO, KFM, VTM, KCT, VC = 65536, 73728, 81920, 86016, 90112, 92160
    WP = 94208
    SCR = 110592
    OACC = 65536
    TST = 81920
    SCN = 98304
    SHR = 114688
    STF = 120832
    STB = 124928

    def mixer_even():
        l = 0
        modnorm(l, 1)
        w_in = even_w_in[0].rearrange("(kc p) n -> p kc n", p=128)
        w_out = even_w_out[0].rearrange("(kc p) n -> p kc n", p=128)
        cw = ptab_sb[:, PT_CONV:PT_CONV + 60].rearrange("p (c j) -> p c j", c=12, j=5)
        sink_bc = ptab_sb[:, PT_SINK:PT_SINK + 4]
        wpc = [0]

        def load_piece(c0, c1):
            wp = ar.view(WP + (wpc[0] % 2) * 8192, [128, KC, 512], BF16)
            wpc[0] += 1
            P.dma(wp[:, :, 0:c1 - c0], w_in[:, :, c0:c1], q="pool")
            return wp

        for grp in range(2):
            T0 = grp * 1024
            nseq, L = (4, 256) if grp == 0 else (1, 1024)
            qn = ar.view(QN, [128, 4, 1024], BF16)
            kn = ar.view(KN, [128, 4, 1024], BF16)
            vS = ar.view(VS, [128, 4, 1024], BF16)
            gS = ar.view(GS, [128, 4, 1024], BF16)
            qpl = ar.view(QPL, [128, 4, 1024], BF16)
            qro = ar.view(QRO, [128, 4, 1024], BF16)
            kfm = ar.view(KFM, [128, 2, 1024], BF16)
            vtm = ar.view(VTM, [128, 8, 256], BF16)
            kcT = ar.view(KCT, [128, 2, 512], BF16)
            vc = ar.view(VC, [128, 4, 256], BF16)
            mixT = hT[:, :, T0:T0 + 1024]

            def proj_fm(wp, cc):
                b = nbk(2)
                for tt in range(2):
                    for kc in range(KC):
                        P.mm(bank(b + tt), wp[:, kc, cc * 128:(cc + 1) * 128],
                             hT[:, kc, T0 + tt * TT:T0 + (tt + 1) * TT], start=(kc == 0), stop=(kc == KC - 1))
                return ps[:, b:b + 2, :].rearrange("p a t -> p (a t)")

            SCALE = 128.0 ** -0.5
            if grp == 1:
                ropeT = ar.view(SCR, [128, 2, 1024], F32)
                P.dma(ropeT, rope[:, :, :], q="sp")
                kst = ar.view(SCR + 8192, [128, 4, 256], BF16)
                P.dma(kst, cache_k.rearrange("(kt p) g d -> p kt (g d)", p=128), q="pool")
                P.dma(vc, cache_v.rearrange("(kt p) g d -> p kt (g d)", p=128), q="pool")
                for g in range(2):
                    b = nb()
                    for kt in range(4):
                        P.tr(bbf(b)[:, kt * 128:(kt + 1) * 128], kst[:, kt, g * 128:(g + 1) * 128], identB)
                    P.copy(kcT[:, g, :], bbf(b)[:, 0:512], eng="dve")

            def rope_apply(dst_bf, xf):
                t1 = ar.view(SCR + 12288, [128, 1024], F32)
                for tt in range(2):
                    b = nb()
                    P.mm(bank(b), Rm, xf[:, tt * TT:(tt + 1) * TT])
                    P.tt(t1[:, tt * TT:(tt + 1) * TT], bank(b), ropeT[:, 1, tt * TT:(tt + 1) * TT], ALU.mult)
                P.tt(xf, xf, ropeT[:, 0, :], ALU.mult, eng="pool")
                P.tt(dst_bf, t1, xf, ALU.add)

            wp = load_piece(2064, 2576)
            for h in range(4):
                pp = proj_fm(wp, h)
                P.act(qpl[:, h, :], pp, AF.Copy, scale=SCALE)
                if grp == 1:
                    xf = ar.view(SCR + 8192, [128, 1024], F32)
                    P.ts(xf, pp, SCALE, ALU.mult)
                    rope_apply(qro[:, h, :], xf)
            if "stop_ip1" in debug_names:
                return
            wp = load_piece(2576, 3088)
            for g in range(2):
                pp = proj_fm(wp, g)
                if grp == 0:
                    P.act(kfm[:, g, :], pp, AF.Copy)
                else:
                    xf = ar.view(SCR + 8192, [128, 1024], F32)
                    P.act(xf, pp, AF.Copy)
                    rope_apply(kfm[:, g, :], xf)
            if "stop_ip1b" in debug_names:
                return
            for i in range(8):
                b = nb()
                for kc in range(KC):
                    P.mm(bank(b), hT[:, kc, T0 + i * 128:T0 + (i + 1) * 128], wp[:, kc, 0:512],
                         start=(kc == 0), stop=(kc == KC - 1))
                if grp == 1:
                    P.act(vtm[:, i, :], ps[:, b, 256:512], AF.Copy)
                else:
                    st = ar.view(SCR + (i % 2) * 2048, [128, 512], F32)
                    P.copy(st, bank(b), eng="dve")
                    P.act(vtm[:, i, :], st[:, 256:512], AF.Copy)
                    P.dma(nck[i * 128:(i + 1) * 128, :], st[:, 0:256], q="sp", is_output=True)
                    P.dma(ncv[i * 128:(i + 1) * 128, :], st[:, 256:512], q="act", is_output=True)
            if "stop_ip2" in debug_names:
                return
            wpab = load_piece(2048, 2064)
            abT = gates[:, 0, :, :]
            for i in range(8):
                b = nb()
                for kc in range(KC):
                    P.mm(ps[:, b, 0:16], hT[:, kc, T0 + i * 128:T0 + (i + 1) * 128], wpab[:, kc, 0:16],
                         start=(kc == 0), stop=(kc == KC - 1))
                P.copy(abT[:, i, :], ps[:, b, 0:16], eng="dve")

            if "stop_ip3" in debug_names:
                return
            Lp = L + 4
            xpbs = [ar.view(SCR + k_ * 2080, [128, nseq, Lp], BF16) for k_ in range(2)]
            dgs = [ar.view(SCR + 4160 + k_ * 1280, [128, 5, 128], BF16) for k_ in range(2)]
            qss = [ar.view(SCR + 6720 + k_ * 4096, [128, 1024], F32) for k_ in range(2)]
            sqs = [gates[:, 4 + 2 * k_:6 + 2 * k_, :, :].rearrange("p s a b -> p (s a b)").bitcast(BF16) for k_ in range(2)]
            for k_ in range(2):
                P.memset(ar.view(SCR + k_ * 2080, [128, 1040], BF16), 0.0, eng="pool")

            def conv_chunk(c, wp, st):
                pc, hh = c // 4, c % 4
                xpb, dg, qs, sq = xpbs[st], dgs[st], qss[st], sqs[st]
                pp = proj_fm(wp, hh)
                yield
                P.act(xpb[:, :, 2:2 + L], pp.rearrange("p (s t) -> p s t", s=nseq, t=L), AF.Copy)
                for j in range(5):
                    P.ts(dg[:, j, :], identB, cw[:, c, j:j + 1], ALU.mult)
                yield
                bc = nbk(2)
                if nseq == 4:
                    for s2 in range(4):
                        for j in range(5):
                            P.mm(ps[:, bc + s2 // 2, (s2 % 2) * 256:(s2 % 2 + 1) * 256], dg[:, j, :],
                                 xpb[:, s2, j:j + 256], start=(j == 0), stop=(j == 4))
                else:
                    for tt in range(2):
                        for j in range(5):
                            P.mm(bank(bc + tt), dg[:, j, :], xpb[:, 0, tt * TT + j:tt * TT + j + TT],
                                 start=(j == 0), stop=(j == 4))
                accf = ps[:, bc:bc + 2, :].rearrange("p a t -> p (a t)")
                yield
                if pc == 2:
                    P.act(vS[:, hh, :], accf, AF.Silu)
                    yield
                    return
                P.act(qs, accf, AF.Silu)
                dst = (qn if pc == 0 else kn)[:, hh, :]
                for tt in range(2):
                    P.act(sq, qs[:, tt * TT:(tt + 1) * TT], AF.Square)
                    b = nb()
                    P.mm(bank(b), onesP, sq)
                    yield
                    rs = rstd[:, st, :]
                    P.act(rs, bank(b), AF.Ln, bias=epsT[:, 0:1])
                    P.act(rs, rs, AF.Exp, scale=-0.5)
                    P.stt(dst[:, tt * TT:(tt + 1) * TT], qs[:, tt * TT:(tt + 1) * TT],
                          SCALE if pc == 0 else 1.0, rs, ALU.mult, ALU.mult)
                    yield

            wps = {}
            active = []
            nxt_c = 0
            free_st = [0, 1]
            while nxt_c < 12 or active:
                while nxt_c < 12 and len(active) < 2:
                    pc_ = nxt_c // 4
                    if pc_ not in wps:
                        wps[pc_] = load_piece(pc_ * 512, (pc_ + 1) * 512)
                    st_ = free_st.pop(0)
                    active.append((conv_chunk(nxt_c, wps[pc_], st_), st_))
                    nxt_c += 1
                for item in list(active):
                    try:
                        next(item[0])
                    except StopIteration:
                        active.remove(item)
                        free_st.append(item[1])
            wp = load_piece(1536, 2048)
            for hh in range(4):
                pp = proj_fm(wp, hh)
                P.act(gS[:, hh, :], pp, AF.Silu)
            if "inproj" in debug_names and grp == DBG_GRP:
                dump("d_qn", qn, [128, 4, 1024], BF16)
                dump("d_kn", kn, [128, 4, 1024], BF16)
                dump("d_vS", vS, [128, 4, 1024], BF16)
                dump("d_gS", gS, [128, 4, 1024], BF16)
                dump("d_qpl", qpl, [128, 4, 1024], BF16)
                dump("d_qro", qro, [128, 4, 1024], BF16)
                dump("d_kfm", kfm, [128, 2, 1024], BF16)
                dump("d_vtm", vtm, [128, 8, 256], BF16)
                dump("d_ab", abT, [128, 8, 16], F32)

            if "stop_inproj" in debug_names:
                return
            ATT = WP
            if grp == 0:
                for s_ in range(4):
                    for qb in range(2):
                        tq = s_ * 256 + qb * 128
                        Pb = ar.view(ATT + ((s_ * 2 + qb) % 2) * 2048, [128, 4, 256], BF16)
                        PTs = ar.view(ATT + 4096 + ((s_ * 2 + qb) % 2) * 2048, [128, 8, 128], BF16)
                        stt_ = ar.view(ATT + 8192 + ((s_ * 2 + qb) % 2) * 256, [128, 16], F32)
                        on = ar.view(ATT + 8704 + ((s_ * 2 + qb) % 2) * 1024, [128, 4, 128], BF16)
                        bS = nbk(2)
                        P.memset(stt_, 0.0, eng="pool")
                        for h in range(4):
                            P.mm(ps[:, bS + h // 2, (h % 2) * 256:(h % 2 + 1) * 256],
                                 qpl[:, h, tq:tq + 128], kfm[:, h // 2, s_ * 256:(s_ + 1) * 256])
                        S4 = ps[:, bS:bS + 2, :].rearrange("p a (h k) -> p (a h) k", h=2, k=256)
                        mx = stt_[:, 0:4]
                        negm = stt_[:, 4:8]
                        rsum = stt_[:, 8:12]
                        es = stt_[:, 12:16]
                        P.reduce(mx, S4, ALU.max)
                        P.tt(mx, mx, sink_bc, ALU.max)
                        P.ts(negm, mx, -1.0, ALU.mult)
                        for h in range(4):
                            P.act(Pb[:, h, :], S4[:, h, :], AF.Exp, bias=negm[:, h:h + 1], accum=rsum[:, h:h + 1])
                        P.tt(es, sink_bc, negm, ALU.add)
                        P.act(es, es, AF.Exp)
                        P.tt(rsum, rsum, es, ALU.add)
                        P.recip(rsum, rsum)
                        bT = nb()
                        for h in range(4):
                            for kt in range(2):
                                P.tr(bbf(bT)[:, (h * 2 + kt) * 128:(h * 2 + kt + 1) * 128],
                                     Pb[:, h, kt * 128:(kt + 1) * 128], identB)
                        P.copy(PTs.rearrange("p a t -> p (a t)"), bbf(bT)[:, 0:1024], eng="act")
                        bO = nb()
                        for h in range(4):
                            for kt in range(2):
                                P.mm(ps[:, bO, h * 128:(h + 1) * 128], PTs[:, h * 2 + kt, :],
                                     vtm[:, s_ * 2 + kt, (h // 2) * 128:(h // 2 + 1) * 128],
                                     start=(kt == 0), stop=(kt == 1))
                        P.tt(on, b4(bO), bc_i(rsum), ALU.mult)
                        bT2 = nb()
                        for h in range(4):
                            P.tr(bbf(bT2)[:, h * 128:(h + 1) * 128], on[:, h, :], identB)
                        P.copy(mixT[:, 4:8, tq:tq + 128],
                               bbf(bT2)[:, 0:512].rearrange("p (h t) -> p h t", h=4, t=128), eng="dve")
            else:
                for qb in range(8):
                    tq = qb * 128
                    blks = [k for k in (qb - 1, qb, qb + 1) if 0 <= k < 8]
                    nl = len(blks)
                    W = 512 + nl * 128
                    on = ar.view(ATT + 16384 + (qb % 2) * 1024, [128, 4, 128], BF16)
                    for hp in range(2):
                        it = qb * 2 + hp
                        Pb = ar.view(ATT + (it % 2) * 4096, [128, 2, 1024], BF16)
                        PTs = ar.view(ATT + 8192 + (it % 2) * 4096, [128, 2, 1024], BF16)
                        stt_ = ar.view(ATT + 18432 + (it % 2) * 256, [128, 16], F32)
                        mx = stt_[:, 0:2]
                        negm = stt_[:, 2:4]
                        rsum = stt_[:, 4:6]
                        es = stt_[:, 6:8]
                        bS = nbk(4)
                        P.memset(stt_, 0.0, eng="pool")
                        for hh in range(2):
                            h = hp * 2 + hh
                            g = hp
                            P.mm(bank(bS + 2 * hh), qpl[:, h, tq:tq + 128], kcT[:, g, :])
                            k0 = blks[0] * 128
                            has_mask = (blks[0] == qb - 1) or (blks[-1] == qb + 1)
                            P.mm(ps[:, bS + 2 * hh + 1, 0:nl * 128], qro[:, h, tq:tq + 128],
                                 kfm[:, g, k0:k0 + nl * 128], start=True, stop=not has_mask)
                            nm = (1 if blks[0] == qb - 1 else 0) + (1 if blks[-1] == qb + 1 else 0)
                            cnt = 0
                            for bi, k in enumerate(blks):
                                if k == qb - 1 or k == qb + 1:
                                    cnt += 1
                                    P.mm(ps[:, bS + 2 * hh + 1, bi * 128:(bi + 1) * 128], identB,
                                         MUb if k == qb - 1 else MLb, start=False, stop=(cnt == nm))
                        S2 = ps[:, bS:bS + 4, :].rearrange("p (h a) t -> p h (a t)", h=2, a=2)[:, :, 0:W]
                        P.reduce(mx, S2, ALU.max)
                        P.tt(mx, mx, sink_bc[:, hp * 2:hp * 2 + 2], ALU.max)
                        P.ts(negm, mx, -1.0, ALU.mult)
                        for hh in range(2):
                            P.act(Pb[:, hh, 0:W], S2[:, hh, :], AF.Exp, bias=negm[:, hh:hh + 1],
                                  accum=rsum[:, hh:hh + 1])
                        P.tt(es, sink_bc[:, hp * 2:hp * 2 + 2], negm, ALU.add)
                        P.act(es, es, AF.Exp)
                        P.tt(rsum, rsum, es, ALU.add)
                        P.recip(rsum, rsum)
                        nblk = 4 + nl
                        for hh in range(2):
                            bT = nb()
                            for bi in range(nblk):
                                P.tr(bbf(bT)[:, bi * 128:(bi + 1) * 128], Pb[:, hh, bi * 128:(bi + 1) * 128], identB)
                            P.copy(PTs[:, hh, 0:W], bbf(bT)[:, 0:W], eng="act" if hh == 0 else "dve")
                        bO = nb()
                        for hh in range(2):
                            g = hp
                            for bi in range(nblk):
                                if bi < 4:
                                    rhs = vc[:, bi, g * 128:(g + 1) * 128]
                                else:
                                    rhs = vtm[:, blks[bi - 4], g * 128:(g + 1) * 128]
                                P.mm(ps[:, bO, hh * 128:(hh + 1) * 128], PTs[:, hh, bi * 128:(bi + 1) * 128], rhs,
                                     start=(bi == 0), stop=(bi == nblk - 1))
                        P.tt(on[:, hp * 2:hp * 2 + 2, :],
                             ps[:, bO, 0:256].rearrange("p (h t) -> p h t", h=2, t=128),
                             rsum.unsqueeze(2).broadcast_to([128, 2, 128]), ALU.mult)
                    bT2 = nb()
                    for h in range(4):
                        P.tr(bbf(bT2)[:, h * 128:(h + 1) * 128], on[:, h, :], identB)
                    P.copy(mixT[:, 4:8, tq:tq + 128],
                           bbf(bT2)[:, 0:512].rearrange("p (h t) -> p h t", h=4, t=128), eng="dve")
            if "attn" in debug_names and grp == DBG_GRP:
                dump("d_oatt", mixT[:, 4:8, :], [128, 4, 1024], BF16)

            if "stop_attn" in debug_names:
                return
            delta_net(grp, T0, nseq, L, qn, kn, vS, gS, mixT)
            if "delta" in debug_names and grp == DBG_GRP:
                dump("d_oa", mixT[:, 0:4, :], [128, 4, 1024], BF16)

            if "stop_delta" in debug_names:
                return
            wo = ar.view(WP, [128, KC, 1024], BF16)
            P.dma(wo, w_out[:, :, :], q="pool")
            for n in range(KC):
                for tt in range(2):
                    b = nb()
                    for kc in range(KC):
                        P.mm(bank(b), wo[:, kc, n * 128:(n + 1) * 128], mixT[:, kc, tt * TT:(tt + 1) * TT],
                             start=(kc == 0), stop=(kc == KC - 1))
                    xs = xT[:, n, T0 + tt * TT:T0 + (tt + 1) * TT]
                    P.stt(xs, bank(b), modG[:, l, 1, n, grp:grp + 1], xs, ALU.mult, ALU.add)

    def delta_net(grp, T0, nseq, L, qn, kn, vS, gS, mixT):
        abT = gates[:, 0, :, :]
        gT = gates[:, 1, :, 0:8]
        beta = gates[:, 2, :, 0:8]
        gc = gates[:, 3, :, 0:8]
        glb = gates[:, 4, :, 0:8]
        egc = gates[:, 5, :, 0:8]
        kdf = gates[:, 6, :, 0:8]
        glast = gates[:, 7, :, 0:8]
        bege = gates[:, 1, :, 8:16]
        tmpA = gates[:, 2, :, 8:16]
        ngT = gates[:, 4, :, 8:16]
        tmpB = gates[:, 3, :, 8:16]
        alog_bc = ptab_sb[:, PT_ALOG:PT_ALOG + 8]
        dtb_bc = ptab_sb[:, PT_DTB:PT_DTB + 8]
        onorm = ptab_sb[:, PT_ONORME:PT_ONORME + 1]
        bc8 = lambda v: v.unsqueeze(1).broadcast_to([128, 8, 8])
        P.act(beta, abT[:, :, 0:8], AF.Exp, scale=-1.0)
        P.ts(beta, beta, 1.0, ALU.add)
        P.recip(beta, beta)
        P.tt(tmpA, abT[:, :, 8:16], bc8(dtb_bc), ALU.add)
        P.act(tmpB, tmpA, AF.Abs)
        P.act(tmpB, tmpB, AF.Exp, scale=-1.0)
        P.act(tmpB, tmpB, AF.Ln, bias=onesF[:, 0:1])
        P.ts(tmpA, tmpA, 0.0, ALU.max)
        P.tt(tmpA, tmpA, tmpB, ALU.add)
        P.act(tmpB[:, 0, :], alog_bc, AF.Exp)
        P.stt(gT, tmpA, -1.0, bc8(tmpB[:, 0, :]), ALU.mult, ALU.mult)
        P.ts(ngT, gT, -1.0, ALU.mult)
        g64 = gates[:, 1, :, :].rearrange("p a b -> p (a b)")
        b = nb()
        P.mm(ps[:, b, 0:128], Uf, g64)
        P.mm(ps[:, b, 128:256], Ub, g64)
        pv = ps[:, b, 0:256].rearrange("p (d a c) -> p d a c", d=2, a=8, c=16)
        P.copy(gc[:, :, 0:4], pv[:, 0, :, 0:4], eng="dve")
        P.copy(gc[:, :, 4:8], pv[:, 1, :, 4:8], eng="dve")
        gc64 = gates[:, 3, :, :].rearrange("p a b -> p (a b)")
        b = nb()
        P.mm(ps[:, b, 0:128], sel127, gc64)
        P.mm(ps[:, b, 128:256], sel0, gc64)
        pv = ps[:, b, 0:256].rearrange("p (d a c) -> p d a c", d=2, a=8, c=16)
        P.copy(glb[:, :, 0:4], pv[:, 0, :, 0:4], eng="dve")
        P.copy(glb[:, :, 4:8], pv[:, 1, :, 4:8], eng="dve")
        P.act(egc, gc, AF.Exp)
        P.tt(kdf, glb, gc, ALU.subtract)
        P.act(kdf, kdf, AF.Exp)
        P.act(glast, glb, AF.Exp)
        P.tt(bege, beta, egc, ALU.mult)

        if "gates" in debug_names and grp == DBG_GRP:
            dump("d_g", gT, [128, 8, 8], F32)
            dump("d_beta", beta, [128, 8, 8], F32)
            dump("d_gc", gc, [128, 8, 8], F32)
            dump("d_glb", glb, [128, 8, 8], F32)
        DB = 65536
        TSTS = [DB, DB + 12288]
        SCN2 = DB + 24576
        SHR2 = SCN2 + 16384
        STF2 = SHR2 + 8192
        STB2 = STF2 + 4096
        OACCA = STB2 + 2048
        assert OACCA + 4096 <= ar.nbytes
        Sf = [ar.view(STF2 + d * 2048, [128, 4, 128], F32) for d in range(2)]
        Sb = [ar.view(STB2 + d * 1024, [128, 4, 128], BF16) for d in range(2)]
        nch = L // 128
        if grp == 0:
            oaccA = ar.view(OACCA, [128, 4, 256], F32)
        else:
            oaccB = ar.t[:, 0:8192].rearrange("p (k t) -> p k t", k=8, t=1024)[:, :, 0:512].rearrange(
                "p (h a) t -> p h a t", h=4, a=2)

        def oslice(c):
            if grp == 0:
                return oaccA[:, :, c * 128:(c + 1) * 128]
            t0_ = c * 128
            return oaccB[:, :, t0_ // 512, t0_ % 512:t0_ % 512 + 128]

        def tv(off, dt):
            return ar.view(off, [128, 4, 128], dt)

        def mask_b(k_):
            return cstb[:, 512 + k_ * 128: 512 + (k_ + 1) * 128]

        def mm4(lh, rh):
            bb = nb()
            for h in range(4):
                P.mm(ps[:, bb, h * 128:(h + 1) * 128], lh[:, h, :], rh[:, h, :])
            return bb

        def step(s_, c, d, first_touch):
            T_ = TSTS[d]
            dec, decT = tv(T_, F32), tv(T_ + 2048, F32)
            G1, G2 = dec, decT
            Lb, LTb = tv(T_ + 4096, BF16), tv(T_ + 5120, BF16)
            Ab, Bb = tv(T_ + 6144, BF16), tv(T_ + 7168, BF16)
            Cb, Db, Eb, Tb = [tv(T_ + 8192 + 1024 * k_, BF16) for k_ in range(4)]
            B2, B3 = Cb, Db
            S_ = SCN2 + d * 8192
            Xb, qkT, QdT, Vb_, Kbe, kdec, negwT, vnew = [tv(S_ + 1024 * k_, BF16) for k_ in range(8)]
            H_ = SHR2 + d * 4096
            Ktm, Vtm, KKs, KQs = [tv(H_ + 1024 * k_, BF16) for k_ in range(4)]
            ti = s_ * nch + c
            tsl = slice(ti * 128, (ti + 1) * 128)
            dsl = slice(d * 4, d * 4 + 4)
            bK = nb()
            for h in range(4):
                P.tr(bbf(bK)[:, h * 128:(h + 1) * 128], kn[:, h, tsl], identB)
            P.copy(Ktm.rearrange("p h t -> p (h t)"), bbf(bK)[:, 0:512], eng="act")
            bV = nb()
            for h in range(4):
                P.tr(bbf(bV)[:, h * 128:(h + 1) * 128], vS[:, h, tsl], identB)
            P.copy(Vtm.rearrange("p h t -> p (h t)"), bbf(bV)[:, 0:512], eng="act")
            yield
            bKK = nb()
            for h in range(4):
                P.mm(ps[:, bKK, h * 128:(h + 1) * 128], kn[:, h, tsl], kn[:, h, tsl])
            P.copy(KKs.rearrange("p h t -> p (h t)"), bank(bKK), eng="act")
            bKQ = nb()
            for h in range(4):
                P.mm(ps[:, bKQ, h * 128:(h + 1) * 128], kn[:, h, tsl], qn[:, h, tsl])
            P.copy(KQs.rearrange("p h t -> p (h t)"), bank(bKQ), eng="act")
            yield
            U_ = Uf if d == 0 else Ub
            Mdec, MdecT = (MLf, MUf) if d == 0 else (MUf, MLf)
            for h in range(4):
                P.act(G1[:, h, :], onesF, AF.Copy, scale=gT[:, ti, d * 4 + h:d * 4 + h + 1])
            for h in range(4):
                P.act(G2[:, h, :], U_, AF.Copy, scale=ngT[:, ti, d * 4 + h:d * 4 + h + 1])
            bD = nb()
            P.mm(bank(bD), U_, G1.rearrange("p h t -> p (h t)"), start=True, stop=False)
            P.mm(bank(bD), onesF, G2.rearrange("p h t -> p (h t)"), start=False, stop=True)
            yield
            P.tt(dec, b4(bD), bc_h(Mdec), ALU.add)
            P.stt(decT, b4(bD), -1.0, bc_h(MdecT), ALU.mult, ALU.add)
            P.act(dec, dec, AF.Exp)
            P.act(decT, decT, AF.Exp)
            P.tt(B2, bc_h(identB), bc_i(beta[:, ti, dsl]), ALU.mult, eng="pool")
            P.tt(B3, bc_h(identB), bc_i(egc[:, ti, dsl]), ALU.mult, eng="pool")
            bR = nb()
            P.mm(bank(bR), onesP, B2.rearrange("p h t -> p (h t)"))
            bE = nb()
            P.mm(bank(bE), onesP, B3.rearrange("p h t -> p (h t)"))
            yield
            P.tt(Lb, KKs, dec, ALU.mult)
            P.tt(Lb, Lb, bc_i(beta[:, ti, dsl]), ALU.mult)
            P.tt(LTb, KKs, decT, ALU.mult, eng="pool")
            P.tt(LTb, LTb, b4(bR), ALU.mult)
            P.tt(qkT, KQs, decT, ALU.mult, eng="pool")
            P.tt(QdT, qn[:, :, tsl], b4(bE), ALU.mult)
            for h in range(4):
                hc = d * 4 + h
                P.act(Vb_[:, h, :], Vtm[:, h, :], AF.Copy, scale=beta[:, ti, hc:hc + 1])
                P.act(Kbe[:, h, :], Ktm[:, h, :], AF.Copy, scale=bege[:, ti, hc:hc + 1])
                P.act(kdec[:, h, :], Ktm[:, h, :], AF.Copy, scale=kdf[:, ti, hc:hc + 1])
            yield
            mi = (lambda k: k) if d == 0 else (lambda k: (k + 4) if 1 <= k <= 4 else (k - 4 if k >= 5 else k))
            P.tt(Ab, Lb, bc_h(mask_b(0)), ALU.mult)
            P.tt(Bb, LTb, bc_h(mask_b(0)), ALU.mult)
            P.stt(Tb, Ab, -1.0, bc_h(identF), ALU.mult, ALU.add)
            P.stt(Xb, Bb, -1.0, bc_h(identF), ALU.mult, ALU.add)
            yield
            b1 = mm4(Bb, Ab)
            b2 = mm4(Ab, Bb)
            P.copy(Cb, b4(b1), eng="act")
            P.copy(Db, b4(b2), eng="act")
            yield
            bx = mm4(Cb, Xb)
            bt = mm4(Xb, Cb)
            b3 = mm4(Db, Cb)
            P.tt(Xb, Xb, b4(bx), ALU.add)
            P.tt(Tb, Tb, b4(bt), ALU.add)
            P.copy(Eb, b4(b3), eng="act")
            yield
            bx = mm4(Eb, Xb)
            bt = mm4(Xb, Eb)
            P.tt(Xb, Xb, b4(bx), ALU.add)
            P.tt(Tb, Tb, b4(bt), ALU.add)
            yield
            for lv in range(1, 5):
                last = (lv == 4)
                P.tt(Ab, Lb, bc_h(mask_b(mi(lv))), ALU.mult, eng="pool")
                if not last:
                    P.tt(Bb, LTb, bc_h(mask_b(mi(lv + 4))), ALU.mult)
                b1 = mm4(Ab, Xb)
                if not last:
                    b2 = mm4(Bb, Tb)
                P.copy(Cb, b4(b1), eng="act")
                if not last:
                    P.copy(Db, b4(b2), eng="act")
                yield
                bx = mm4(Tb, Cb)
                if not last:
                    bt = mm4(Xb, Db)
                P.tt(Xb, Xb, b4(bx), ALU.subtract)
                if not last:
                    P.tt(Tb, Tb, b4(bt), ALU.subtract)
                yield
            if "step0" in debug_names and grp == DBG_GRP and s_ == 0 and (c, d) == DBG_STEP:
                dump("d_dec", dec, [128, 4, 128], F32)
                dump("d_decT", decT, [128, 4, 128], F32)
                dump("d_X", Xb, [128, 4, 128], BF16)
                dump("d_qkT", qkT, [128, 4, 128], BF16)
                dump("d_QdT", QdT, [128, 4, 128], BF16)
                dump("d_KKs", KKs, [128, 4, 128], BF16)
            bW = mm4(Kbe, Xb)
            P.act(negwT.rearrange("p h t -> p (h t)"), bank(bW), AF.Copy, scale=-1.0)
            yield
            bVn = nb()
            for h in range(4):
                P.mm(ps[:, bVn, h * 128:(h + 1) * 128], Xb[:, h, :], Vb_[:, h, :], start=True, stop=False)
                P.mm(ps[:, bVn, h * 128:(h + 1) * 128], negwT[:, h, :], Sb[d][:, h, :], start=False, stop=True)
            P.copy(vnew.rearrange("p h t -> p (h t)"), bank(bVn), eng="act")
            yield
            bO = nb()
            for h in range(4):
                P.mm(ps[:, bO, h * 128:(h + 1) * 128], Sb[d][:, h, :], QdT[:, h, :], start=True, stop=False)
                P.mm(ps[:, bO, h * 128:(h + 1) * 128], vnew[:, h, :], qkT[:, h, :], start=False, stop=True)
            bS_ = mm4(kdec, vnew)
            oa = oslice(c if grp == 1 else c)
            if first_touch[c]:
                P.copy(oa, b4(bO), eng="dve")
                first_touch[c] = False
            else:
                P.tt(oa, oa, b4(bO), ALU.add)
            P.tt(Sf[d], Sf[d], bc_i(glast[:, ti, dsl]), ALU.mult)
            P.tt(Sf[d], Sf[d], b4(bS_), ALU.add)
            P.copy(Sb[d], Sf[d], eng="act")
            yield

        for s_ in range(nseq):
            for d in range(2):
                if grp == 0:
                    P.memset(Sf[d], 0.0, eng="pool")
                else:
                    P.dma(Sf[d], state_delta[d].rearrange("h k v -> k h v"), q="sp")
                P.copy(Sb[d], Sf[d], eng="pool")
            first_touch = [True] * nch
            for k in range(nch):
                gens = [step(s_, k, 0, first_touch), step(s_, nch - 1 - k, 1, first_touch)]
                while gens:
                    for g_ in list(gens):
                        try:
                            next(g_)
                        except StopIteration:
                            gens.remove(g_)
            if grp == 0:
                for d in range(2):
                    P.dma(nsd[s_, d].rearrange("h k v -> k h v"), Sf[d], q="sp", is_output=True)
            for h in range(4):
                for t_ in range(0, L, 512):
                    w = min(512, L - t_)
                    sl = slice(s_ * L + t_, s_ * L + t_ + w)
                    src = oaccA[:, h, 0:w] if grp == 0 else oaccB[:, h, t_ // 512, 0:512]
                    sq = ar.view(TSTS[0], [128, 512], BF16)
                    P.act(sq[:, 0:w], src, AF.Square)
                    b = nb()
                    P.mm(ps[:, b, 0:w], onesP, sq[:, 0:w])
                    rs = rstd[:, 0, 0:w]
                    P.act(rs, ps[:, b, 0:w], AF.Ln, bias=epsT[:, 0:1], scale=1.0 / 128.0)
                    P.act(rs, rs, AF.Exp, scale=-0.5)
                    t_f = ar.view(TSTS[0] + 2048, [128, 512], F32)
                    P.stt(t_f[:, 0:w], src, onorm[:, 0:1], rs, ALU.mult, ALU.mult)
                    P.tt(mixT[:, h, sl], t_f[:, 0:w], gS[:, h, sl], ALU.mult)

    def mixer_odd():
        l = 1
        modnorm(l, 1)
        w_in = odd_w_in[0].rearrange("(kc p) n -> p kc n", p=128)
        w_out = odd_w_out[0].rearrange("(kc p) n -> p kc n", p=128)
        onorm2 = ptab_sb[:, PT_ONORMO:PT_ONORMO + 2]
        QT_, KT_, VT_, GS_, LRT_, OACC2 = 32768, 40960, 49152, 65536, 81920, 86016
        WPO = 102400
        ZB_, ZE_ = 102400, 104448
        EG_, ENG_ = ZE_, ZB_
        QTL_, KTL_, QTT_, KTT_, ATB_ = 110592, 111616, 112640, 113664, 114688
        SF_, SB_, WG_ = 115712, 119808, 121856
        SC = 128.0 ** -0.5
        wgp = ar.view(WG_, [128, 2, 512], F32)
        P.dma(wgp[0:33, :, :], wgpad[:, :, :], q="sp")
        wpc = [0]

        def load_piece(c0, c1):
            wp = ar.view(WPO + (wpc[0] % 2) * 8192, [128, KC, 512], BF16)
            wpc[0] += 1
            P.dma(wp[:, :, 0:c1 - c0], w_in[:, :, c0:c1], q="pool")
            return wp

        qT = ar.view(QT_, [128, 8, 512], BF16)
        kT = ar.view(KT_, [128, 8, 512], BF16)
        vT = ar.view(VT_, [128, 8, 1024], BF16)
        gS = ar.view(GS_, [128, 8, 1024], BF16)
        lrT = ar.view(LRT_, [128, 1024], F32)
        glt = gates[:, 0, 0, 0:4]
        for grp in range(2):
            T0 = grp * 1024
            nseq, L = (4, 256) if grp == 0 else (1, 1024)
            nch = L // 128
            mixT = hT[:, :, T0:T0 + 1024]
            wp = load_piece(3072, 3104)
            for tt in range(2):
                b = nb()
                for kc in range(KC):
                    P.mm(ps[0:32, b, :], wp[:, kc, 0:32], hT[:, kc, T0 + tt * TT:T0 + (tt + 1) * TT],
                         start=(kc == 0), stop=(kc == KC - 1))
                P.copy(lrT[0:32, tt * TT:(tt + 1) * TT], ps[0:32, b, :], eng="dve")
            P.memset(lrT[32:33, :], 1.0, eng="dve")
            for (c0, dst, col0, scl) in ((0, qT, 0, SC), (512, kT, 0, 1.0), (1024, vT, 0, 1.0), (1536, vT, 512, 1.0)):
                wp = load_piece(c0, c0 + 512)
                for i in range(8):
                    b = nb()
                    for kc in range(KC):
                        P.mm(bank(b), hT[:, kc, T0 + i * 128:T0 + (i + 1) * 128], wp[:, kc, 0:512],
                             start=(kc == 0), stop=(kc == KC - 1))
                    if i % 2 == 0:
                        P.act(dst[:, i, col0:col0 + 512], bank(b), AF.Copy, scale=scl)
                    else:
                        P.ts(dst[:, i, col0:col0 + 512], bank(b), scl, ALU.mult)
            for pc in range(2):
                wp = load_piece(2048 + pc * 512, 2560 + pc * 512)
                for cc in range(4):
                    b = nbk(2)
                    for tt in range(2):
                        for kc in range(KC):
                            P.mm(bank(b + tt), wp[:, kc, cc * 128:(cc + 1) * 128],
                                 hT[:, kc, T0 + tt * TT:T0 + (tt + 1) * TT], start=(kc == 0), stop=(kc == KC - 1))
                    P.act(gS[:, pc * 4 + cc, :], ps[:, b:b + 2, :].rearrange("p a t -> p (a t)"), AF.Silu)
            if "oinproj" in debug_names and grp == DBG_GRP:
                dump("d_qT", qT, [128, 8, 512], BF16)
                dump("d_kT", kT, [128, 8, 512], BF16)
                dump("d_vT", vT, [128, 8, 1024], BF16)
                dump("d_gS", gS, [128, 8, 1024], BF16)
                dump("d_lrT", lrT[0:33, :], [33, 1024], F32)

            zb = ar.view(ZB_, [128, 512], F32)
            ze = ar.view(ZE_, [128, 512], F32)
            eg = ar.view(EG_, [128, 512], F32)
            eng_ = ar.view(ENG_, [128, 512], F32)
            qtl = ar.view(QTL_, [128, 512], BF16)
            ktl = ar.view(KTL_, [128, 512], BF16)
            qtt = ar.view(QTT_, [128, 4, 128], BF16)
            ktt = ar.view(KTT_, [128, 4, 128], BF16)
            atb = ar.view(ATB_, [128, 4, 128], BF16)
            Sf = ar.view(SF_, [128, 4, 256], F32)
            Sb = ar.view(SB_, [128, 4, 256], BF16)
            if grp == 1:
                oacc_lo = ar.t[:, 0:8192].rearrange("p (k t) -> p k t", k=8, t=1024)[:, :, 0:512]
                oacc_hi = ar.view(OACC2, [128, 8, 512], F32)
            else:
                oaccA = ar.view(OACC2, [128, 8, 256], F32)

            def oslice(c, fc0, fc1):
                if grp == 0:
                    return oaccA[:, fc0:fc1, c * 128:(c + 1) * 128]
                if c < 4:
                    return oacc_lo[:, fc0:fc1, c * 128:(c + 1) * 128]
                return oacc_hi[:, fc0:fc1, (c - 4) * 128:(c - 3) * 128]

            bufsets = []
            for k_ in range(2):
                if k_ == 0:
                    offs = (QTL_, KTL_, QTT_, KTT_, ATB_)
                else:
                    offs = (106496, 107520, 108544, 109568, 125952)
                bufsets.append((ar.view(offs[0], [128, 512], BF16), ar.view(offs[1], [128, 512], BF16),
                                ar.view(offs[2], [128, 4, 128], BF16), ar.view(offs[3], [128, 4, 128], BF16),
                                ar.view(offs[4], [128, 4, 128], BF16), gates[:, 0, k_, 0:4]))

            rb_ = rstd[:, :, :].rearrange("p a t -> p (a t)").bitcast(BF16)
            egs = [rb_[:, k_ * 512:(k_ + 1) * 512] for k_ in range(2)]
            engs = [rb_[:, 1024 + k_ * 512:1024 + (k_ + 1) * 512] for k_ in range(2)]
            glts = [gates[:, 0, k_, 0:4] for k_ in range(3)]
            gsum = gates[:, 0, 3, 0:4]

            def prefixA(s_, d, c, ka):
                eg_b, eng_b, glt = egs[ka % 2], engs[ka % 2], glts[ka % 3]
                U_ = Uf if d == 0 else Ub
                ti = s_ * nch + c
                tsl = slice(ti * 128, (ti + 1) * 128)
                bz = nb()
                P.mm(bank(bz), lrT[0:33, tsl], wgp[0:33, d, :])
                yield
                P.ts(ze, bank(bz), -80.0, ALU.max)
                P.act(ze, ze, AF.Exp, scale=-1.0)
                yield
                P.act(zb, ze, AF.Ln, bias=onesF[:, 0:1])
                yield
                bg = nb()
                P.mm(bank(bg), U_, zb)
                bl = nb()
                for h in range(4):
                    P.mm(ps[:, bl, h:h + 1], zb[:, h * 128:(h + 1) * 128], onesF[:, 0:1])
                yield
                P.act(eg_b, bank(bg), AF.Exp, scale=-1.0 / 16.0)
                P.act(eng_b, bank(bg), AF.Exp, scale=1.0 / 16.0)
                P.act(glt, ps[:, bl, 0:4], AF.Exp, scale=-1.0 / 16.0)
                yield

            def prefixB(s_, d, c, bs_, ka):
                qtl, ktl, qtt, ktt, atb, _ = bs_
                eg_b, eng_b = egs[ka % 2], engs[ka % 2]
                U_ = Uf if d == 0 else Ub
                ti = s_ * nch + c
                P.tt(qtl, qT[:, ti, :], eg_b, ALU.mult)
                P.tt(ktl, kT[:, ti, :], eng_b, ALU.mult)
                yield
                bq = nb()
                for h in range(4):
                    P.tr(bbf(bq)[:, h * 128:(h + 1) * 128], qtl[:, h * 128:(h + 1) * 128], identB)
                P.copy(qtt.rearrange("p h t -> p (h t)"), bbf(bq)[:, 0:512], eng="act")
                bk = nb()
                for h in range(4):
                    P.tr(bbf(bk)[:, h * 128:(h + 1) * 128], ktl[:, h * 128:(h + 1) * 128], identB)
                P.copy(ktt.rearrange("p h t -> p (h t)"), bbf(bk)[:, 0:512], eng="dve")
                yield
                ba = nb()
                for h in range(4):
                    P.mm(ps[:, ba, h * 128:(h + 1) * 128], ktt[:, h, :], qtt[:, h, :])
                P.tt(atb, b4(ba), bc_h(U_), ALU.mult)
                yield

            def suffix(s_, d, c, bs_, first_touch, ka):
                qtl, ktl, qtt, ktt, atb, _ = bs_
                glt = glts[ka % 3]
                ti = s_ * nch + c
                bo = nbk(2)
                for h in range(4):
                    for half in range(2):
                        fc = h * 2 + half
                        dstp = ps[:, bo + fc // 4, (fc % 4) * 128:(fc % 4 + 1) * 128]
                        P.mm(dstp, vT[:, ti, h * 256 + half * 128:h * 256 + (half + 1) * 128], atb[:, h, :],
                             start=True, stop=False)
                        P.mm(dstp, Sb[:, h, half * 128:(half + 1) * 128], qtt[:, h, :], start=False, stop=True)
                for k2 in range(2):
                    oa = oslice(c, k2 * 4, k2 * 4 + 4)
                    if first_touch[c]:
                        P.copy(oa, b4(bo + k2), eng="dve" if k2 == 0 else "act")
                    else:
                        P.tt(oa, oa, b4(bo + k2), ALU.add)
                first_touch[c] = False
                yield
                bs2 = nbk(2)
                for h in range(4):
                    P.mm(ps[:, bs2 + h // 2, (h % 2) * 256:(h % 2 + 1) * 256], ktl[:, h * 128:(h + 1) * 128],
                         vT[:, ti, h * 256:(h + 1) * 256])
                P.tt(Sf, Sf, ps[:, bs2:bs2 + 2, :].rearrange("p a (h v) -> p (a h) v", h=2, v=256), ALU.add)
                P.tt(Sf, Sf, glt.unsqueeze(2).broadcast_to([128, 4, 256]), ALU.mult)
                P.copy(Sb, Sf, eng="act")
                yield

            for s_ in range(nseq):
                first_touch = [True] * nch
                steps = [(d, (cidx if d == 0 else nch - 1 - cidx)) for d in range(2) for cidx in range(nch)]
                def drive(gens):
                    gens = [g_ for g_ in gens if g_ is not None]
                    while gens:
                        for g_ in list(gens):
                            try:
                                next(g_)
                            except StopIteration:
                                gens.remove(g_)

                ns = len(steps)

                def gA(k2):
                    return prefixA(s_, steps[k2][0], steps[k2][1], k2) if k2 < ns else None

                def gB(k2):
                    return prefixB(s_, steps[k2][0], steps[k2][1], bufsets[k2 % 2], k2) if k2 < ns else None

                drive([gA(0)])
                drive([gB(0), gA(1)])
                for k_, (d, c) in enumerate(steps):
                    if k_ % nch == 0:
                        if grp == 0:
                            P.memset(Sf, 0.0, eng="pool")
                        else:
                            P.dma(Sf, state_gla[d].rearrange("h k v -> k h v"), q="sp")
                        P.copy(Sb, Sf, eng="pool")
                    drive([suffix(s_, d, c, bufsets[k_ % 2], first_touch, k_), gB(k_ + 1), gA(k_ + 2)])
                    if k_ % nch == nch - 1 and grp == 0:
                        P.dma(nsg[s_, d].rearrange("h k v -> k h v"), Sf, q="sp", is_output=True)
                for h in range(4):
                    for t_ in range(0, L, 512):
                        w = min(512, L - t_)
                        b = nb()
                        srcs = []
                        for half in range(2):
                            fc = h * 2 + half
                            if grp == 0:
                                src = oaccA[:, fc, t_:t_ + w]
                            else:
                                src = (oacc_lo if t_ == 0 else oacc_hi)[:, fc, 0:512]
                            srcs.append(src)
                            sq = ar.view(ZB_ + half * 1024, [128, 512], BF16)
                            P.act(sq[:, 0:w], src, AF.Square)
                            P.mm(ps[:, b, 0:w], onesP, sq[:, 0:w], start=(half == 0), stop=(half == 1))
                        rs = rstd[:, 0, 0:w]
                        P.act(rs, ps[:, b, 0:w], AF.Ln, bias=epsT[:, 0:1], scale=1.0 / 256.0)
                        P.act(rs, rs, AF.Exp, scale=-0.5)
                        for half in range(2):
                            fc = h * 2 + half
                            t_f = ar.view(EG_, [128, 512], F32)
                            P.stt(t_f[:, 0:w], srcs[half], onorm2[:, half:half + 1], rs, ALU.mult, ALU.mult)
                            sl = slice(s_ * L + t_, s_ * L + t_ + w)
                            P.tt(mixT[:, fc, sl], t_f[:, 0:w], gS[:, fc, sl], ALU.mult)
            if "gla" in debug_names and grp == DBG_GRP:
                dump("d_mix", mixT, [128, 8, 1024], BF16)
            wo = ar.view(WPO, [128, KC, 1024], BF16)
            P.dma(wo, w_out[:, :, :], q="pool")
            for n in range(KC):
                for tt in range(2):
                    b = nb()
                    for kc in range(KC):
                        P.mm(bank(b), wo[:, kc, n * 128:(n + 1) * 128], mixT[:, kc, tt * TT:(tt + 1) * TT],
                             start=(kc == 0), stop=(kc == KC - 1))
                    xs = xT[:, n, T0 + tt * TT:T0 + (tt + 1) * TT]
                    P.stt(xs, bank(b), modG[:, l, 1, n, grp:grp + 1], xs, ALU.mult, ALU.add)


    if stage != "all":
        ada_flush()
    if stage == "ffn0":
        ffn(0, 0)
        final_out()
    elif stage == "mix0":
        mixer_even()
        final_out()
    elif stage == "mix1":
        mixer_odd()
        final_out()
    else:
        def pre(l_, s_):
            def f(tt):
                modnorm_tile(l_, s_, tt)
                if tt == NTT - 1:
                    pre_normed[0] = (l_, s_)
            return f
        ffn(0, 0, after_tile=pre(0, 1))
        mixer_even()
        ffn(0, 1, after_tile=pre(1, 0))
        ffn(1, 0, after_tile=pre(1, 1))
        mixer_odd()
        ffn(1, 1, after_tile=final_tile)

    P.emit()
    stack.close()
    return nc, dbg


PT_ADAB = 0
PT_NORMG = PT_ADAB + 144
PT_FINALG = PT_NORMG + 48
PT_CONV = PT_FINALG + 8
PT_SINK = PT_CONV + 60
PT_ALOG = PT_SINK + 4
PT_DTB = PT_ALOG + 8
PT_ONORME = PT_DTB + 8
PT_ONORMO = PT_ONORME + 1
PT_COLS = PT_ONORMO + 2
DBG_GRP = 0
DBG_STEP = (0, 0)

C_IDENT = 0
C_UF, C_UB, C_SEL127, C_SEL0, C_ONES, C_OFFD, C_ML, C_MU, C_RM = [128 * i for i in range(1, 10)]
CST_COLS = 128 * 10
CSTB_COLS = 512 + 9 * 128


def _fm(v):
    v = np.asarray(v, np.float32)
    return np.ascontiguousarray(v.reshape(-1, 128).T)


def make_tables(inputs):
    pt = np.zeros((128, PT_COLS), np.float32)
    for l in range(2):
        pt[:, PT_ADAB + l * 72: PT_ADAB + (l + 1) * 72] = _fm(inputs["ada_b"][l])
        for s in range(3):
            o = PT_NORMG + (l * 3 + s) * 8
            pt[:, o:o + 8] = _fm(inputs["norm_g"][l, s])
    pt[:, PT_FINALG:PT_FINALG + 8] = _fm(inputs["final_g"])
    cv = np.asarray(inputs["even_conv"], np.float32)[0]
    pt[:, PT_CONV:PT_CONV + 60] = cv.T.reshape(12, 128, 5).transpose(1, 0, 2).reshape(128, 60)
    pt[:, PT_SINK:PT_SINK + 4] = np.broadcast_to(np.asarray(inputs["even_sink"], np.float32)[0][None, :], (128, 4))
    pt[:, PT_ALOG:PT_ALOG + 8] = np.broadcast_to(np.asarray(inputs["even_a_log"], np.float32)[0].reshape(1, 8), (128, 8))
    pt[:, PT_DTB:PT_DTB + 8] = np.broadcast_to(np.asarray(inputs["even_dt_bias"], np.float32)[0].reshape(1, 8), (128, 8))
    pt[:, PT_ONORME:PT_ONORME + 1] = np.asarray(inputs["even_onorm"], np.float32)[0].reshape(128, 1)
    pt[:, PT_ONORMO:PT_ONORMO + 2] = _fm(inputs["odd_onorm"][0])
    cst = np.zeros((128, CST_COLS), np.float32)
    cst[:, C_IDENT:C_IDENT + 128] = np.eye(128, dtype=np.float32)
    kk, ii = np.meshgrid(np.arange(128), np.arange(128), indexing="ij")
    NEG = -30000.0
    cst[:, C_UF:C_UF + 128] = (kk <= ii)
    cst[:, C_UB:C_UB + 128] = (kk >= ii)
    cst[127, C_SEL127:C_SEL127 + 128] = 1.0
    cst[0, C_SEL0:C_SEL0 + 128] = 1.0
    cst[:, C_ONES:C_ONES + 128] = 1.0
    cst[:, C_OFFD:C_OFFD + 128] = (kk != ii)
    cst[:, C_ML:C_ML + 128] = np.where(ii <= kk, 0.0, NEG)
    cst[:, C_MU:C_MU + 128] = np.where(ii >= kk, 0.0, NEG)
    rm = np.zeros((128, 128), np.float32)
    for dp in range(128):
        if (dp % 64) < 32:
            rm[dp + 32, dp] = -1.0
        else:
            rm[dp - 32, dp] = 1.0
    cst[:, C_RM:C_RM + 128] = rm
    return pt, cst


def make_bmask():
    i, j = np.meshgrid(np.arange(128), np.arange(128), indexing="ij")
    ms = [(i // 8 == j // 8) & (i != j)]
    for b in (8, 16, 32, 64):
        ms.append((i // (2 * b) == j // (2 * b)) & ((i // b) % 2 == 1) & ((j // b) % 2 == 0))
    for b in (8, 16, 32, 64):
        ms.append((i // (2 * b) == j // (2 * b)) & ((i // b) % 2 == 0) & ((j // b) % 2 == 1))
    return np.ascontiguousarray(np.concatenate([m.astype(np.float32) for m in ms], axis=1))


def make_rope():
    t = np.arange(1024)
    row = (t // 64).astype(np.float64)
    col = (t % 64).astype(np.float64)
    inv = 10000.0 ** (-np.arange(32, dtype=np.float64) / 32.0)
    ang = np.zeros((128, 1024))
    for d in range(128):
        pos = row if d < 64 else col
        ang[d] = pos * np.float32(inv[d % 32])
    ang32 = np.zeros((128, 1024), np.float32)
    inv32 = (np.float32(10000.0) ** (-np.arange(32, dtype=np.float32) / np.float32(32))).astype(np.float32)
    for d in range(128):
        pos = (row if d < 64 else col).astype(np.float32)
        ang32[d] = pos * inv32[d % 32]
    return np.ascontiguousarray(np.stack([np.cos(ang32), np.sin(ang32)], axis=1).astype(np.float32))


def make_in_maps(inputs, stage="all"):
    pt, cst = make_tables(inputs)
    rope_t = make_rope()
    bmask_t = make_bmask()
    wg = np.asarray(inputs["odd_w_gate"], np.float32)[0]
    wgpad_t = np.zeros((33, 2, 512), np.float32)
    wgpad_t[0:16, 0, :] = wg[0]
    wgpad_t[16:32, 1, :] = wg[1]
    wgpad_t[32, :, :] = np.asarray(inputs["odd_gate_bias"], np.float32)[0]
    maps = []
    xp = np.asarray(inputs["x_prompt"], np.float32)
    xs = np.asarray(inputs["x_sample"], np.float32)
    for c in range(8):
        xin = np.concatenate([xp[4 * c:4 * c + 4].reshape(1024, D), xs[c]], axis=0)
        cond = np.stack([np.asarray(inputs["c_ctx"], np.float32), np.asarray(inputs["c"], np.float32)[c]], axis=-1)
        condT = np.ascontiguousarray(cond.reshape(KC, 128, 2).transpose(1, 0, 2))
        m = {"xin": np.ascontiguousarray(xin), "condT": condT, "ptab": pt, "cst": cst,
             "ada_w": np.asarray(inputs["ada_w"], np.float32),
             "even_w_in": np.asarray(inputs["even_w_in"], np.float32),
             "even_w_out": np.asarray(inputs["even_w_out"], np.float32),
             "rope": rope_t, "bmask": bmask_t, "wgpad": wgpad_t,
             "odd_w_in": np.asarray(inputs["odd_w_in"], np.float32),
             "odd_w_out": np.asarray(inputs["odd_w_out"], np.float32),
             "state_gla": np.ascontiguousarray(np.asarray(inputs["state_gla"], np.float32)[c, 0]),
             "cache_k": np.ascontiguousarray(np.asarray(inputs["cache_k"], np.float32)[c, 0]),
             "cache_v": np.ascontiguousarray(np.asarray(inputs["cache_v"], np.float32)[c, 0]),
             "state_delta": np.ascontiguousarray(np.asarray(inputs["state_delta"], np.float32)[c, 0])}
        if stage in ("all", "ffn0"):
            m["ffn_w_gu"] = np.asarray(inputs["ffn_w_gu"], np.float32)
            m["ffn_w_down"] = np.asarray(inputs["ffn_w_down"], np.float32)
        maps.append(m)
    return maps


_CACHE = {}


def kernel(**inputs):
    if "nc" not in _CACHE:
        _CACHE["nc"] = build_program("all")[0]
    nc = _CACHE["nc"]
    maps = make_in_maps(inputs)
    res = run_bass_kernel_spmd(nc, maps, core_ids=list(range(8)))
    rs = res.results
    ys = [np.asarray(r["yout"], np.float32) for r in rs]
    y_prompt = np.concatenate([y[:1024].reshape(4, 256, D) for y in ys], axis=0)
    y_sample = np.stack([y[1024:] for y in ys], axis=0)
    nsd = np.concatenate([np.asarray(r["nsd"], np.float32) for r in rs], axis=0)[:, None]
    nck = np.concatenate([np.asarray(r["nck"], np.float32).reshape(4, 256, 2, 128) for r in rs], axis=0)[:, None]
    ncv = np.concatenate([np.asarray(r["ncv"], np.float32).reshape(4, 256, 2, 128) for r in rs], axis=0)[:, None]
    nsg = np.concatenate([np.asarray(r["nsg"], np.float32) for r in rs], axis=0)[:, None]
    return (y_prompt, y_sample, np.ascontiguousarray(nsd), np.ascontiguousarray(nck),
            np.ascontiguousarray(ncv), np.ascontiguousarray(nsg))
```

```python
import numpy as np
import concourse.bass as bass
import concourse.mybir as mybir

F32 = mybir.dt.float32
BF16 = mybir.dt.bfloat16
AF = mybir.ActivationFunctionType
ALU = mybir.AluOpType
AX = mybir.AxisListType

_DTSZ = {F32: 4, BF16: 2}


def _region(ap):
    sp = str(ap.space)
    if "DRAM" in sp.upper() or "HBM" in sp.upper():
        return None
    sz = _DTSZ[ap.dtype]
    pat = ap.ap
    pstep, pcnt = pat[0]
    off = int(ap.offset)
    if pstep == 0:
        p0, f0 = 0, off
        pstep = 1 << 40
    else:
        p0, f0 = off // pstep, off % pstep
    ext = 1
    for st, cnt in pat[1:]:
        ext += (cnt - 1) * abs(st)
    b0, b1 = f0 * sz, (f0 + ext) * sz
    if "PSUM" in sp.upper():
        b0 = (b0 // 2048) * 2048
        b1 = ((b1 + 2047) // 2048) * 2048
        return (ap.tensor.name, 0, 128, b0, b1)
    return (ap.tensor.name, p0, p0 + pcnt, b0, b1)


def _ovl(a, b):
    return a[0] == b[0] and a[1] < b[2] and b[1] < a[2] and a[3] < b[4] and b[3] < a[4]


def _covers(a, b):
    return a[0] == b[0] and a[1] <= b[1] and a[2] >= b[2] and a[3] <= b[3] and a[4] >= b[4]


class Op:
    __slots__ = ("eng", "fn", "seq", "inc", "waits", "ctr", "is_dma", "val")

    def __init__(self, eng, fn, is_dma=False):
        self.eng = eng
        self.fn = fn
        self.inc = False
        self.waits = []
        self.is_dma = is_dma
        self.ctr = None
        self.seq = 0
        self.val = 0


ENGS = ("pe", "act", "dve", "pool", "sp")
NDMA = 24


class Prog:
    def __init__(self, nc):
        self.nc = nc
        self.streams = {e: [] for e in ENGS}
        self.seqc = {}
        self.known = {e: {} for e in ENGS}
        self.recs = {}
        self.dma_rr = {"h": 0, "s": 0}
        self.dma_last = {}
        self.nops = 0
        self.out_dmas = []

    def _need(self, op, dep):
        if dep is op:
            return
        if dep.ctr == op.ctr and not dep.is_dma:
            pass
        k = self.known[op.eng]
        if k.get(dep.ctr, -1) >= dep.seq:
            return
        k[dep.ctr] = dep.seq
        dep.inc = True
        op.waits.append(dep)

    BK = 2048

    def _buckets(self, r):
        return range(r[3] // self.BK, (r[4] - 1) // self.BK + 1)

    def _track(self, op, reads, writes):
        BK = self.BK
        for ap in reads:
            r = _region(ap)
            if r is None:
                continue
            for b in self._buckets(r):
                lst = self.recs.setdefault((r[0], b), [])
                is_ps = (r[0] == "ps")
                for rec in lst:
                    if _ovl(rec[0], r) and (rec[1] == "w" or (is_ps and rec[2].ctr != op.ctr)):
                        d = rec[2]
                        if d.ctr == op.ctr and op.eng == "pe":
                            continue
                        self._need(op, d)
                for i, rec in enumerate(lst):
                    if rec[1] == "r" and rec[2].ctr == op.ctr and rec[0] == r:
                        lst.pop(i)
                        break
                lst.append([r, "r", op])
        for ap in writes:
            r = _region(ap)
            if r is None:
                continue
            for b in self._buckets(r):
                lst = self.recs.setdefault((r[0], b), [])
                keep = []
                lo, hi = b * BK, (b + 1) * BK
                for rec in lst:
                    if rec[2] is op:
                        keep.append(rec)
                        continue
                    rr = rec[0]
                    if _ovl(rr, r):
                        d = rec[2]
                        if not (d.ctr == op.ctr and op.eng == "pe"):
                            self._need(op, d)
                        if (r[1] <= rr[1] and r[2] >= rr[2]
                                and r[3] <= max(rr[3], lo) and r[4] >= min(rr[4], hi)):
                            continue
                    keep.append(rec)
                keep.append([r, "w", op])
                self.recs[(r[0], b)] = keep

    def _add(self, eng, fn, reads, writes):
        op = Op(eng, fn)
        op.ctr = eng
        op.seq = self.seqc.get(eng, 0)
        self.seqc[eng] = op.seq + 1
        self._track(op, reads, writes)
        self.streams[eng].append(op)
        self.nops += 1
        return op

    def dma(self, out, in_, q="sp", is_output=False):
        op = Op(q, None, is_dma=True)
        kind = "s" if q == "pool" else "h"
        k = self.dma_rr[kind]
        self.dma_rr[kind] = (k + 1) % (NDMA // 2)
        op.ctr = "dma%s%d" % (kind, k)
        op.seq = self.seqc.get(op.ctr, 0)
        self.seqc[op.ctr] = op.seq + 1
        prev = self.dma_last.get(op.ctr)
        if prev is not None:
            self._need(op, prev)
        self.dma_last[op.ctr] = op
        op.inc = True
        self._track(op, [in_], [out])
        op.fn = lambda e, out=out, in_=in_: e.dma_start(out=out, in_=in_)
        self.streams[q].append(op)
        if is_output:
            self.out_dmas.append(op)
        return op

    def mm(self, out, lhsT, rhs, start=True, stop=True):
        return self._add("pe", lambda e: e.matmul(out, lhsT, rhs, start=start, stop=stop),
                         [lhsT, rhs], [out])

    def tr(self, out, in_, ident):
        return self._add("pe", lambda e: e.transpose(out, in_, ident), [in_, ident], [out])

    def act(self, out, in_, func, bias=None, scale=1.0, accum=None):
        rd = [in_]
        if bias is not None and not isinstance(bias, (int, float)):
            rd.append(bias)
        if not isinstance(scale, (int, float)):
            rd.append(scale)
        wr = [out] + ([accum] if accum is not None else [])
        kw = {}
        if bias is not None:
            kw["bias"] = bias
        if accum is not None:
            kw["accum_out"] = accum
        return self._add("act", lambda e: e.activation(out=out, in_=in_, func=func, scale=scale, **kw),
                         rd, wr)

    def tt(self, out, in0, in1, op, eng="dve"):
        return self._add(eng, lambda e: e.tensor_tensor(out=out, in0=in0, in1=in1, op=op),
                         [in0, in1], [out])

    def ts(self, out, in0, s1, op0, s2=None, op1=None, eng="dve", accum=None):
        rd = [in0] + [s for s in (s1, s2) if s is not None and not isinstance(s, (int, float))]
        kw = {}
        if op1 is not None:
            kw["op1"] = op1
        if accum is not None:
            kw["accum_out"] = accum
        wr = [out] + ([accum] if accum is not None else [])
        return self._add(eng, lambda e: e.tensor_scalar(out=out, in0=in0, scalar1=s1, scalar2=s2, op0=op0, **kw),
                         rd, wr)

    def stt(self, out, in0, scalar, in1, op0, op1, eng="dve"):
        rd = [in0, in1] + ([scalar] if not isinstance(scalar, (int, float)) else [])
        return self._add(eng, lambda e: e.scalar_tensor_tensor(out=out, in0=in0, scalar=scalar, in1=in1,
                                                               op0=op0, op1=op1), rd, [out])

    def copy(self, out, in_, eng="dve"):
        if eng == "act":
            return self.act(out, in_, AF.Copy)
        return self._add(eng, lambda e: e.tensor_copy(out=out, in_=in_), [in_], [out])

    def reduce(self, out, in_, op, eng="dve", axis=None):
        axis = axis or AX.X
        return self._add(eng, lambda e: e.tensor_reduce(out=out, in_=in_, axis=axis, op=op), [in_], [out])

    def recip(self, out, in_):
        return self._add("dve", lambda e: e.reciprocal(out=out, in_=in_), [in_], [out])

    def memset(self, ap, val, eng="dve"):
        return self._add(eng, lambda e: e.memset(ap, val), [], [ap])

    def emit(self):
        nc = self.nc
        sems = {}
        import contextlib
        stack = contextlib.ExitStack()
        allops = []
        for e in ENGS:
            allops.extend(self.streams[e])
        ctrs = sorted(set(o.ctr for o in allops))
        for c in ctrs:
            sems[c] = stack.enter_context(nc.semaphore("s_" + c))
        cnt = {c: 0 for c in ctrs}
        byctr = {c: [] for c in ctrs}
        for o in allops:
            byctr[o.ctr].append(o)
        for c in ctrs:
            ops = sorted(byctr[c], key=lambda o: o.seq)
            v = 0
            for o in ops:
                if o.inc:
                    v += 16 if o.is_dma else 1
                o.val = v
        fin = stack.enter_context(nc.semaphore("s_fin"))
        block = stack.enter_context(nc.Block())
        prog = self

        def run_stream(eng_name, e):
            for o in prog.streams[eng_name]:
                for d in o.waits:
                    e.wait_ge(sems[d.ctr], d.val)
                ins = o.fn(e)
                if o.inc:
                    ins.then_inc(sems[o.ctr], 16 if o.is_dma else 1)

        @block.tensor
        def _(e):
            run_stream("pe", e)

        @block.scalar
        def _(e):
            run_stream("act", e)

        @block.vector
        def _(e):
            run_stream("dve", e)

        @block.gpsimd
        def _(e):
            run_stream("pool", e)

        @block.sync
        def _(e):
            run_stream("sp", e)
            for o in prog.out_dmas:
                e.wait_ge(sems[o.ctr], o.val)

        stack.close()
from concourse.bass_utils import run_bass_kernel_spmd

D = 1024
NTOK = 2048
KC = 8
TT = 512
NTT = 4
DFF = 2816
NHC = 22
EPS = 1e-6
EVEN_IN = 3088
ODD_IN = 3104


class Arena:
    def __init__(self, nc, name, nbytes, stack):
        self.t = stack.enter_context(nc.sbuf_tensor(name, [128, nbytes // 4], F32))
        self.nbytes = nbytes

    def view(self, off, shape, dtype):
        sz = 4 if dtype == F32 else 2
        n = 1
        for s in shape[1:]:
            n *= s
        assert off % 4 == 0 and (n * sz) % 4 == 0 and off + n * sz <= self.nbytes, (off, shape, self.nbytes)
        ap = self.t[0:shape[0], off // 4: off // 4 + (n * sz) // 4]
        if dtype != F32:
            ap = ap.bitcast(dtype)
        if len(shape) == 3:
            ap = ap.rearrange("p (a b) -> p a b", a=shape[1], b=shape[2])
        elif len(shape) == 4:
            ap = ap.rearrange("p (a b c) -> p a b c", a=shape[1], b=shape[2], c=shape[3])
        return ap


def build_program(stage="all", debug_names=()):
    import contextlib
    nc = bass.Bass("TRN2", target_bir_lowering=False)
    P = Prog(nc)
    stack = contextlib.ExitStack()

    def din(name, shape, dt=F32):
        return nc.dram_tensor(name, list(shape), dt, kind="ExternalInput").ap()

    def dout(name, shape, dt=F32):
        return nc.dram_tensor(name, list(shape), dt, kind="ExternalOutput").ap()

    xin = din("xin", [NTOK, D])
    condT = din("condT", [128, KC, 2])
    ptab = din("ptab", [128, PT_COLS])
    cst = din("cst", [128, CST_COLS])
    ada_w = din("ada_w", [2, D, 9 * D])
    if stage in ("all", "ffn0"):
        w_gu = din("ffn_w_gu", [2, 2, D, 2 * DFF])
        w_dn = din("ffn_w_down", [2, 2, DFF, D])
    yout = dout("yout", [NTOK, D])
    even_w_in = din("even_w_in", [1, D, EVEN_IN])
    even_w_out = din("even_w_out", [1, D, D])
    rope = din("rope", [128, 2, 1024])
    bmask = din("bmask", [128, 9 * 128])
    odd_w_in = din("odd_w_in", [1, D, ODD_IN])
    odd_w_out = din("odd_w_out", [1, D, D])
    wgpad = din("wgpad", [33, 2, 512])
    state_gla = din("state_gla", [2, 4, 128, 256])
    nsg = dout("nsg", [4, 2, 4, 128, 256])
    cache_k = din("cache_k", [512, 2, 128])
    cache_v = din("cache_v", [512, 2, 128])
    state_delta = din("state_delta", [2, 4, 128, 128])
    nck = dout("nck", [1024, 256])
    ncv = dout("ncv", [1024, 256])
    nsd = dout("nsd", [4, 2, 4, 128, 128])

    dbg = {}
    xT = stack.enter_context(nc.sbuf_tensor("xT", [128, KC, NTOK], F32))
    ptab_sb = stack.enter_context(nc.sbuf_tensor("ptab_sb", [128, PT_COLS], F32))
    cst_sb = stack.enter_context(nc.sbuf_tensor("cst_sb", [128, CST_COLS], F32))
    cstb = stack.enter_context(nc.sbuf_tensor("cstb", [128, CSTB_COLS], BF16))
    modT = stack.enter_context(nc.sbuf_tensor("modT", [128, 2, 72, 2], F32))
    modA = stack.enter_context(nc.sbuf_tensor("modA", [128, 2, 3, KC, 2], F32))
    modG = stack.enter_context(nc.sbuf_tensor("modG", [128, 2, 3, KC, 2], F32))
    scT = stack.enter_context(nc.sbuf_tensor("scT", [128, KC, 2], BF16))
    condsb = stack.enter_context(nc.sbuf_tensor("condsb", [128, KC, 2], F32))
    gates = stack.enter_context(nc.sbuf_tensor("gates", [128, 8, 8, 16], F32))
    epsT = stack.enter_context(nc.sbuf_tensor("epsT", [128, 2], F32))
    rstd = stack.enter_context(nc.sbuf_tensor("rstd", [128, 2, TT], F32))
    ar = Arena(nc, "arena", 126976, stack)
    ps = stack.enter_context(nc.psum_tensor("ps", [128, 8, 512], F32))

    def bank(b):
        return ps[:, b, :]

    identF = cst_sb[:, C_IDENT:C_IDENT + 128]
    identB = cstb[:, 0:128]
    onesP = cstb[:, 128:256]

    P.dma(ptab_sb[:, :], ptab[:, :], q="sp")
    P.dma(cst_sb[:, :], cst[:, :], q="sp")
    P.dma(condsb[:, :, :], condT[:, :, :], q="sp")
    P.copy(identB, identF, eng="dve")
    P.memset(onesP, 1.0, eng="dve")
    P.memset(epsT[:, :], EPS, eng="dve")
    P.memset(gates[:, :, :, :], 0.0, eng="pool")

    STG = 32768
    for i in range(16):
        stg = ar.view(STG + (i % 2) * 4096, [128, D], F32)
        P.dma(stg, xin[i * 128:(i + 1) * 128, :], q="sp" if i % 2 == 0 else "act")
        for half in range(2):
            b = (i * 2 + half) % 4
            for kk in range(4):
                kc = half * 4 + kk
                P.tr(ps[:, b, kk * 128:(kk + 1) * 128], stg[:, kc * 128:(kc + 1) * 128], identF)
            src = ps[:, b, :].rearrange("p (a t) -> p a t", a=4, t=128)
            dst = xT[:, half * 4:half * 4 + 4, i * 128:(i + 1) * 128]
            if half == 0:
                P.copy(dst, src, eng="dve")
            else:
                P.copy(dst, src, eng="act")

    P.act(scT[:, :, :], condsb[:, :, :], AF.Silu)
    ADAW = 40960

    def ada_finish(l, s):
        ng = ptab_sb[:, PT_NORMG + (l * 3 + s) * 8: PT_NORMG + (l * 3 + s + 1) * 8]
        sc = modT[:, l, (3 * s + 1) * 8:(3 * s + 2) * 8, :]
        P.stt(modA[:, l, s, :, :], sc, 1.0, ng.unsqueeze(2).broadcast_to([128, KC, 2]), ALU.add, ALU.mult)
        gt = modT[:, l, (3 * s + 2) * 8:(3 * s + 3) * 8, :]
        P.ts(modG[:, l, s, :, :], gt, 0.5 if s != 1 else 1.0, ALU.mult)

    for i in range(3):
        wb = ar.view(ADAW + (i % 3) * 16384, [128, KC, 1024], BF16)
        src = ada_w[0].rearrange("(kc p) n -> p kc n", p=128)[:, :, i * 1024:(i + 1) * 1024]
        P.dma(wb, src, q="pool")
        for n in range(8):
            j = i * 8 + n
            for kc in range(KC):
                P.mm(ps[:, 4, 2 * j:2 * j + 2], wb[:, kc, n * 128:(n + 1) * 128], scT[:, kc, :],
                     start=(kc == 0), stop=(kc == KC - 1))
    P.tt(modT[:, 0, 0:24, :], ps[:, 4, 0:48].rearrange("p (j c) -> p j c", j=24, c=2),
         ptab_sb[:, PT_ADAB:PT_ADAB + 24].unsqueeze(2).broadcast_to([128, 24, 2]), ALU.add)
    ada_finish(0, 0)
    ada_tasks = [(0, i, q4) for i in range(3, 9) for q4 in range(4)] + \
                [(1, i, q4) for i in range(9) for q4 in range(4)]
    ada_cnt = [0, 0, 0]

    ada_pending = []

    def ada_load():
        l, i, q4 = ada_tasks.pop(0)
        wb = ar.view(TMP + (ada_cnt[0] % 2) * 4096, [128, KC, 256], BF16)
        ada_cnt[0] += 1
        c0 = i * 1024 + q4 * 256
        P.dma(wb, ada_w[l].rearrange("(kc p) n -> p kc n", p=128)[:, :, c0:c0 + 256], q="pool")
        ada_pending.append((l, i, q4, wb))

    def ada_compute():
        l, i, q4, wb = ada_pending.pop(0)
        bank_b = 6 + (ada_cnt[1] % 2)
        ada_cnt[1] += 1
        for nn in range(2):
            for kc in range(KC):
                P.mm(ps[:, bank_b, 2 * nn:2 * nn + 2], wb[:, kc, nn * 128:(nn + 1) * 128], scT[:, kc, :],
                     start=(kc == 0), stop=(kc == KC - 1))
        j0 = i * 8 + q4 * 2
        P.tt(modT[:, l, j0:j0 + 2, :], ps[:, bank_b, 0:4].rearrange("p (j c) -> p j c", j=2, c=2),
             ptab_sb[:, PT_ADAB + l * 72 + j0:PT_ADAB + l * 72 + j0 + 2].unsqueeze(2).broadcast_to([128, 2, 2]),
             ALU.add)
        if q4 == 3 and i % 3 == 2:
            ada_finish(l, i // 3)

    def ada_tick():
        ada_cnt[2] += 1
        if ada_cnt[2] % 3 != 0:
            return
        if len(ada_pending) == 2 or (ada_pending and not ada_tasks):
            ada_compute()
        if ada_tasks and len(ada_pending) < 2:
            ada_load()

    def ada_flush():
        while ada_tasks or ada_pending:
            if ada_tasks and len(ada_pending) < 2:
                ada_load()
            else:
                ada_compute()

    HT = 0
    FW = 32768
    ACTB = FW + 49152
    SG = ACTB + 32768
    TMP = SG + 4096
    SQ = TMP + 4096
    assert SQ + 4096 <= ar.nbytes, SQ + 4096
    hT = ar.view(HT, [128, KC, NTOK], BF16)

    def rms_tile(tt, bank0=6):
        b = bank0 + (tt % 2)
        for kc in range(KC):
            sq = ar.view(SQ + ((tt * KC + kc) % 4) * 1024, [128, TT], BF16)
            P.act(sq, xT[:, kc, tt * TT:(tt + 1) * TT], AF.Square)
            P.mm(bank(b), onesP, sq, start=(kc == 0), stop=(kc == KC - 1))
        rs = rstd[:, tt % 2, :]
        P.act(rs, bank(b), AF.Ln, bias=epsT[:, 0:1], scale=1.0 / 1024.0)
        P.act(rs, rs, AF.Exp, scale=-0.5)
        return rs

    def modnorm_tile(l, s, tt):
        rs = rms_tile(tt)
        c = 0 if tt < 2 else 1
        for kc in range(KC):
            tmp = ar.view(TMP + ((tt * KC + kc) % 2) * 2048, [128, TT], F32)
            P.stt(tmp, xT[:, kc, tt * TT:(tt + 1) * TT], modA[:, l, s, kc, c:c + 1],
                  rs, ALU.mult, ALU.mult)
            P.act(hT[:, kc, tt * TT:(tt + 1) * TT], tmp, AF.Identity,
                  bias=modT[:, l, 3 * s * 8 + kc, c:c + 1])

    pre_normed = [None]

    def modnorm(l, s):
        if pre_normed[0] == (l, s):
            pre_normed[0] = None
            return
        for tt in range(NTT):
            modnorm_tile(l, s, tt)

    GROUPS = [(0, 4), (4, 8), (8, 12), (12, 16), (16, 19), (19, 22)]

    def ffn(l, i, after_tile=None):
        s = 0 if i == 0 else 2
        modnorm(l, s)
        wgu = w_gu[l, i].rearrange("(kc p) n -> p kc n", p=128)
        wdn = w_dn[l, i].rearrange("(g p) n -> p g n", p=128)

        def load(g):
            j0, j1 = GROUPS[g]
            G = j1 - j0
            base = FW + (g % 2) * 24576
            wg = ar.view(base, [128, KC, 512], BF16)
            wu = ar.view(base + 8192, [128, KC, 512], BF16)
            wd = ar.view(base + 16384, [128, 4, 1024], BF16)
            P.dma(wg[:, :, 0:G * 128], wgu[:, :, j0 * 128:j1 * 128], q="pool")
            P.dma(wu[:, :, 0:G * 128], wgu[:, :, DFF + j0 * 128:DFF + j1 * 128], q="pool")
            P.dma(wd[:, 0:G, :], wdn[:, j0:j1, :], q="pool")
            return wg, wu, wd

        pair = [0]

        def gu(g, W):
            j0, j1 = GROUPS[g]
            wg, wu, _ = W
            ab = ar.view(ACTB + (g % 2) * 16384, [128, 4, NTOK], BF16)
            for jj in range(j1 - j0):
                for tt in range(NTT):
                    pb = (pair[0] % 2) * 2
                    pair[0] += 1
                    rhs = None
                    for kc in range(KC):
                        P.mm(bank(pb), wg[:, kc, jj * 128:(jj + 1) * 128], hT[:, kc, tt * TT:(tt + 1) * TT],
                             start=(kc == 0), stop=(kc == KC - 1))
                    for kc in range(KC):
                        P.mm(bank(pb + 1), wu[:, kc, jj * 128:(jj + 1) * 128], hT[:, kc, tt * TT:(tt + 1) * TT],
                             start=(kc == 0), stop=(kc == KC - 1))
                    sg = ar.view(SG + (pair[0] % 2) * 2048, [128, TT], F32)
                    P.act(sg, bank(pb), AF.Silu)
                    P.tt(ab[:, jj, tt * TT:(tt + 1) * TT], sg, bank(pb + 1), ALU.mult)
                    ada_tick()

        ycnt = [0]

        def down(g, W, tile_major=False):
            j0, j1 = GROUPS[g]
            _, _, wd = W
            ab = ar.view(ACTB + (g % 2) * 16384, [128, 4, NTOK], BF16)
            order = [(n, tt) for n in range(KC) for tt in range(NTT)]
            if tile_major:
                order = [(n, tt) for tt in range(NTT) for n in range(KC)]
            for (n, tt) in order:
                if True:
                    c = 0 if tt < 2 else 1
                    yb = 4 + (ycnt[0] % 4)
                    ycnt[0] += 1
                    for jj in range(j1 - j0):
                        P.mm(bank(yb), wd[:, jj, n * 128:(n + 1) * 128], ab[:, jj, tt * TT:(tt + 1) * TT],
                             start=(jj == 0), stop=(jj == j1 - j0 - 1))
                    xs = xT[:, n, tt * TT:(tt + 1) * TT]
                    P.stt(xs, bank(yb), modG[:, l, s, n, c:c + 1], xs, ALU.mult, ALU.add)
                    if tile_major and n == KC - 1 and after_tile is not None:
                        after_tile(tt)

        W = {}
        W[0] = load(0)
        W[1] = load(1)
        gu(0, W[0])
        for g in range(len(GROUPS)):
            if g + 1 < len(GROUPS):
                gu(g + 1, W[g + 1])
            last = (g == len(GROUPS) - 1)
            if last:
                while ada_pending:
                    ada_compute()
            down(g, W[g], tile_major=last)
            if g + 2 < len(GROUPS):
                W[g + 2] = load(g + 2)
        while ada_pending:
            ada_compute()
        if (l, i) == (0, 1):
            ada_flush()

    def final_tile(tt):
        fg = ptab_sb[:, PT_FINALG:PT_FINALG + 8]
        YS = ACTB
        rs = rms_tile(tt)
        for i in range(4 * tt, 4 * tt + 4):
            yt = ar.view(YS + (i % 2) * 4096, [128, KC, 128], F32)
            for kc in range(KC):
                P.stt(yt[:, kc, :], xT[:, kc, i * 128:(i + 1) * 128], fg[:, kc:kc + 1],
                      rs[:, (i % 4) * 128:(i % 4 + 1) * 128], ALU.mult, ALU.mult)
            st = ar.view(YS + 8192 + (i % 2) * 4096, [128, D], F32)
            for half in range(2):
                b = (i * 2 + half) % 4
                for kk in range(4):
                    kc = half * 4 + kk
                    P.tr(ps[:, b, kk * 128:(kk + 1) * 128], yt[:, kc, :], identF)
                if half == 0:
                    P.copy(st[:, 0:512], bank(b), eng="dve")
                else:
                    P.copy(st[:, 512:1024], bank(b), eng="act")
            P.dma(yout[i * 128:(i + 1) * 128, :], st, q="sp", is_output=True)

    def final_out():
        for tt in range(NTT):
            final_tile(tt)

    NEG = -30000.0
    Uf = cst_sb[:, C_UF:C_UF + 128]
    Ub = cst_sb[:, C_UB:C_UB + 128]
    sel127 = cst_sb[:, C_SEL127:C_SEL127 + 128]
    sel0 = cst_sb[:, C_SEL0:C_SEL0 + 128]
    onesF = cst_sb[:, C_ONES:C_ONES + 128]
    offdiag = cst_sb[:, C_OFFD:C_OFFD + 128]
    MLf = cst_sb[:, C_ML:C_ML + 128]
    MUf = cst_sb[:, C_MU:C_MU + 128]
    Rm = cst_sb[:, C_RM:C_RM + 128]
    MLb = cstb[:, 256:384]
    MUb = cstb[:, 384:512]
    P.dma(cstb[:, 512:512 + 9 * 128], bmask[:, :], q="pool")
    P.copy(MLb, MLf, eng="dve")
    P.copy(MUb, MUf, eng="dve")

    def bc_h(m):
        return m.unsqueeze(1).broadcast_to([128, 4, 128])

    def bc_i(v):
        return v.unsqueeze(2).broadcast_to([128, 4, 128])

    def b4(b):
        return ps[:, b, :].rearrange("p (h t) -> p h t", h=4, t=128)

    def bbf(b):
        return ps[:, b, :].bitcast(BF16)

    nbc = [0]

    def nb():
        b = nbc[0] % 8
        nbc[0] += 1
        return b

    def nbk(k):
        c = (nbc[0] + k - 1) // k * k
        nbc[0] = c + k
        return c % 8

    def dump(name, ap, shape, dt):
        d = dout(name, shape, dt)
        P.dma(d, ap, q="sp", is_output=True)
        dbg[name] = (shape, dt)

    QN, KN, VS, GS = 32768, 40960, 49152, 57344
    QPL, QRO, KFM, VTM, KCT, VC = 65536, 73728, 81920, 86016, 90112, 92160
    WP = 94208
    SCR = 110592
    OACC = 65536
    TST = 81920
    SCN = 98304
    SHR = 114688
    STF = 120832
    STB = 124928

    def mixer_even():
        l = 0
        modnorm(l, 1)
        w_in = even_w_in[0].rearrange("(kc p) n -> p kc n", p=128)
        w_out = even_w_out[0].rearrange("(kc p) n -> p kc n", p=128)
        cw = ptab_sb[:, PT_CONV:PT_CONV + 60].rearrange("p (c j) -> p c j", c=12, j=5)
        sink_bc = ptab_sb[:, PT_SINK:PT_SINK + 4]
        wpc = [0]

        def load_piece(c0, c1):
            wp = ar.view(WP + (wpc[0] % 2) * 8192, [128, KC, 512], BF16)
            wpc[0] += 1
            P.dma(wp[:, :, 0:c1 - c0], w_in[:, :, c0:c1], q="pool")
            return wp

        for grp in range(2):
            T0 = grp * 1024
            nseq, L = (4, 256) if grp == 0 else (1, 1024)
            qn = ar.view(QN, [128, 4, 1024], BF16)
            kn = ar.view(KN, [128, 4, 1024], BF16)
            vS = ar.view(VS, [128, 4, 1024], BF16)
            gS = ar.view(GS, [128, 4, 1024], BF16)
            qpl = ar.view(QPL, [128, 4, 1024], BF16)
            qro = ar.view(QRO, [128, 4, 1024], BF16)
            kfm = ar.view(KFM, [128, 2, 1024], BF16)
            vtm = ar.view(VTM, [128, 8, 256], BF16)
            kcT = ar.view(KCT, [128, 2, 512], BF16)
            vc = ar.view(VC, [128, 4, 256], BF16)
            mixT = hT[:, :, T0:T0 + 1024]

            def proj_fm(wp, cc):
                b = nbk(2)
                for tt in range(2):
                    for kc in range(KC):
                        P.mm(bank(b + tt), wp[:, kc, cc * 128:(cc + 1) * 128],
                             hT[:, kc, T0 + tt * TT:T0 + (tt + 1) * TT], start=(kc == 0), stop=(kc == KC - 1))
                return ps[:, b:b + 2, :].rearrange("p a t -> p (a t)")

            SCALE = 128.0 ** -0.5
            if grp == 1:
                ropeT = ar.view(SCR, [128, 2, 1024], F32)
                P.dma(ropeT, rope[:, :, :], q="sp")
                kst = ar.view(SCR + 8192, [128, 4, 256], BF16)
                P.dma(kst, cache_k.rearrange("(kt p) g d -> p kt (g d)", p=128), q="pool")
                P.dma(vc, cache_v.rearrange("(kt p) g d -> p kt (g d)", p=128), q="pool")
                for g in range(2):
                    b = nb()
                    for kt in range(4):
                        P.tr(bbf(b)[:, kt * 128:(kt + 1) * 128], kst[:, kt, g * 128:(g + 1) * 128], identB)
                    P.copy(kcT[:, g, :], bbf(b)[:, 0:512], eng="dve")

            def rope_apply(dst_bf, xf):
                t1 = ar.view(SCR + 12288, [128, 1024], F32)
                for tt in range(2):
                    b = nb()
                    P.mm(bank(b), Rm, xf[:, tt * TT:(tt + 1) * TT])
                    P.tt(t1[:, tt * TT:(tt + 1) * TT], bank(b), ropeT[:, 1, tt * TT:(tt + 1) * TT], ALU.mult)
                P.tt(xf, xf, ropeT[:, 0, :], ALU.mult, eng="pool")
                P.tt(dst_bf, t1, xf, ALU.add)

            wp = load_piece(2064, 2576)
            for h in range(4):
                pp = proj_fm(wp, h)
                P.act(qpl[:, h, :], pp, AF.Copy, scale=SCALE)
                if grp == 1:
                    xf = ar.view(SCR + 8192, [128, 1024], F32)
                    P.ts(xf, pp, SCALE, ALU.mult)
                    rope_apply(qro[:, h, :], xf)
            if "stop_ip1" in debug_names:
                return
            wp = load_piece(2576, 3088)
            for g in range(2):
                pp = proj_fm(wp, g)
                if grp == 0:
                    P.act(kfm[:, g, :], pp, AF.Copy)
                else:
                    xf = ar.view(SCR + 8192, [128, 1024], F32)
                    P.act(xf, pp, AF.Copy)
                    rope_apply(kfm[:, g, :], xf)
            if "stop_ip1b" in debug_names:
                return
            for i in range(8):
                b = nb()
                for kc in range(KC):
                    P.mm(bank(b), hT[:, kc, T0 + i * 128:T0 + (i + 1) * 128], wp[:, kc, 0:512],
                         start=(kc == 0), stop=(kc == KC - 1))
                if grp == 1:
                    P.act(vtm[:, i, :], ps[:, b, 256:512], AF.Copy)
                else:
                    st = ar.view(SCR + (i % 2) * 2048, [128, 512], F32)
                    P.copy(st, bank(b), eng="dve")
                    P.act(vtm[:, i, :], st[:, 256:512], AF.Copy)
                    P.dma(nck[i * 128:(i + 1) * 128, :], st[:, 0:256], q="sp", is_output=True)
                    P.dma(ncv[i * 128:(i + 1) * 128, :], st[:, 256:512], q="act", is_output=True)
            if "stop_ip2" in debug_names:
                return
            wpab = load_piece(2048, 2064)
            abT = gates[:, 0, :, :]
            for i in range(8):
                b = nb()
                for kc in range(KC):
                    P.mm(ps[:, b, 0:16], hT[:, kc, T0 + i * 128:T0 + (i + 1) * 128], wpab[:, kc, 0:16],
                         start=(kc == 0), stop=(kc == KC - 1))
                P.copy(abT[:, i, :], ps[:, b, 0:16], eng="dve")

            if "stop_ip3" in debug_names:
                return
            Lp = L + 4
            xpbs = [ar.view(SCR + k_ * 2080, [128, nseq, Lp], BF16) for k_ in range(2)]
            dgs = [ar.view(SCR + 4160 + k_ * 1280, [128, 5, 128], BF16) for k_ in range(2)]
            qss = [ar.view(SCR + 6720 + k_ * 4096, [128, 1024], F32) for k_ in range(2)]
            sqs = [gates[:, 4 + 2 * k_:6 + 2 * k_, :, :].rearrange("p s a b -> p (s a b)").bitcast(BF16) for k_ in range(2)]
            for k_ in range(2):
                P.memset(ar.view(SCR + k_ * 2080, [128, 1040], BF16), 0.0, eng="pool")

            def conv_chunk(c, wp, st):
                pc, hh = c // 4, c % 4
                xpb, dg, qs, sq = xpbs[st], dgs[st], qss[st], sqs[st]
                pp = proj_fm(wp, hh)
                yield
                P.act(xpb[:, :, 2:2 + L], pp.rearrange("p (s t) -> p s t", s=nseq, t=L), AF.Copy)
                for j in range(5):
                    P.ts(dg[:, j, :], identB, cw[:, c, j:j + 1], ALU.mult)
                yield
                bc = nbk(2)
                if nseq == 4:
                    for s2 in range(4):
                        for j in range(5):
                            P.mm(ps[:, bc + s2 // 2, (s2 % 2) * 256:(s2 % 2 + 1) * 256], dg[:, j, :],
                                 xpb[:, s2, j:j + 256], start=(j == 0), stop=(j == 4))
                else:
                    for tt in range(2):
                        for j in range(5):
                            P.mm(bank(bc + tt), dg[:, j, :], xpb[:, 0, tt * TT + j:tt * TT + j + TT],
                                 start=(j == 0), stop=(j == 4))
                accf = ps[:, bc:bc + 2, :].rearrange("p a t -> p (a t)")
                yield
                if pc == 2:
                    P.act(vS[:, hh, :], accf, AF.Silu)
                    yield
                    return
                P.act(qs, accf, AF.Silu)
                dst = (qn if pc == 0 else kn)[:, hh, :]
                for tt in range(2):
                    P.act(sq, qs[:, tt * TT:(tt + 1) * TT], AF.Square)
                    b = nb()
                    P.mm(bank(b), onesP, sq)
                    yield
                    rs = rstd[:, st, :]
                    P.act(rs, bank(b), AF.Ln, bias=epsT[:, 0:1])
                    P.act(rs, rs, AF.Exp, scale=-0.5)
                    P.stt(dst[:, tt * TT:(tt + 1) * TT], qs[:, tt * TT:(tt + 1) * TT],
                          SCALE if pc == 0 else 1.0, rs, ALU.mult, ALU.mult)
                    yield

            wps = {}
            active = []
            nxt_c = 0
            free_st = [0, 1]
            while nxt_c < 12 or active:
                while nxt_c < 12 and len(active) < 2:
                    pc_ = nxt_c // 4
                    if pc_ not in wps:
                        wps[pc_] = load_piece(pc_ * 512, (pc_ + 1) * 512)
                    st_ = free_st.pop(0)
                    active.append((conv_chunk(nxt_c, wps[pc_], st_), st_))
                    nxt_c += 1
                for item in list(active):
                    try:
                        next(item[0])
                    except StopIteration:
                        active.remove(item)
                        free_st.append(item[1])
            wp = load_piece(1536, 2048)
            for hh in range(4):
                pp = proj_fm(wp, hh)
                P.act(gS[:, hh, :], pp, AF.Silu)
            if "inproj" in debug_names and grp == DBG_GRP:
                dump("d_qn", qn, [128, 4, 1024], BF16)
                dump("d_kn", kn, [128, 4, 1024], BF16)
                dump("d_vS", vS, [128, 4, 1024], BF16)
                dump("d_gS", gS, [128, 4, 1024], BF16)
                dump("d_qpl", qpl, [128, 4, 1024], BF16)
                dump("d_qro", qro, [128, 4, 1024], BF16)
                dump("d_kfm", kfm, [128, 2, 1024], BF16)
                dump("d_vtm", vtm, [128, 8, 256], BF16)
                dump("d_ab", abT, [128, 8, 16], F32)

            if "stop_inproj" in debug_names:
                return
            ATT = WP
            if grp == 0:
                for s_ in range(4):
                    for qb in range(2):
                        tq = s_ * 256 + qb * 128
                        Pb = ar.view(ATT + ((s_ * 2 + qb) % 2) * 2048, [128, 4, 256], BF16)
                        PTs = ar.view(ATT + 4096 + ((s_ * 2 + qb) % 2) * 2048, [128, 8, 128], BF16)
                        stt_ = ar.view(ATT + 8192 + ((s_ * 2 + qb) % 2) * 256, [128, 16], F32)
                        on = ar.view(ATT + 8704 + ((s_ * 2 + qb) % 2) * 1024, [128, 4, 128], BF16)
                        bS = nbk(2)
                        P.memset(stt_, 0.0, eng="pool")
                        for h in range(4):
                            P.mm(ps[:, bS + h // 2, (h % 2) * 256:(h % 2 + 1) * 256],
                                 qpl[:, h, tq:tq + 128], kfm[:, h // 2, s_ * 256:(s_ + 1) * 256])
                        S4 = ps[:, bS:bS + 2, :].rearrange("p a (h k) -> p (a h) k", h=2, k=256)
                        mx = stt_[:, 0:4]
                        negm = stt_[:, 4:8]
                        rsum = stt_[:, 8:12]
                        es = stt_[:, 12:16]
                        P.reduce(mx, S4, ALU.max)
                        P.tt(mx, mx, sink_bc, ALU.max)
                        P.ts(negm, mx, -1.0, ALU.mult)
                        for h in range(4):
                            P.act(Pb[:, h, :], S4[:, h, :], AF.Exp, bias=negm[:, h:h + 1], accum=rsum[:, h:h + 1])
                        P.tt(es, sink_bc, negm, ALU.add)
                        P.act(es, es, AF.Exp)
                        P.tt(rsum, rsum, es, ALU.add)
                        P.recip(rsum, rsum)
                        bT = nb()
                        for h in range(4):
                            for kt in range(2):
                                P.tr(bbf(bT)[:, (h * 2 + kt) * 128:(h * 2 + kt + 1) * 128],
                                     Pb[:, h, kt * 128:(kt + 1) * 128], identB)
                        P.copy(PTs.rearrange("p a t -> p (a t)"), bbf(bT)[:, 0:1024], eng="act")
                        bO = nb()
                        for h in range(4):
                            for kt in range(2):
                                P.mm(ps[:, bO, h * 128:(h + 1) * 128], PTs[:, h * 2 + kt, :],
                                     vtm[:, s_ * 2 + kt, (h // 2) * 128:(h // 2 + 1) * 128],
                                     start=(kt == 0), stop=(kt == 1))
                        P.tt(on, b4(bO), bc_i(rsum), ALU.mult)
                        bT2 = nb()
                        for h in range(4):
                            P.tr(bbf(bT2)[:, h * 128:(h + 1) * 128], on[:, h, :], identB)
                        P.copy(mixT[:, 4:8, tq:tq + 128],
                               bbf(bT2)[:, 0:512].rearrange("p (h t) -> p h t", h=4, t=128), eng="dve")
            else:
                for qb in range(8):
                    tq = qb * 128
                    blks = [k for k in (qb - 1, qb, qb + 1) if 0 <= k < 8]
                    nl = len(blks)
                    W = 512 + nl * 128
                    on = ar.view(ATT + 16384 + (qb % 2) * 1024, [128, 4, 128], BF16)
                    for hp in range(2):
                        it = qb * 2 + hp
                        Pb = ar.view(ATT + (it % 2) * 4096, [128, 2, 1024], BF16)
                        PTs = ar.view(ATT + 8192 + (it % 2) * 4096, [128, 2, 1024], BF16)
                        stt_ = ar.view(ATT + 18432 + (it % 2) * 256, [128, 16], F32)
                        mx = stt_[:, 0:2]
                        negm = stt_[:, 2:4]
                        rsum = stt_[:, 4:6]
                        es = stt_[:, 6:8]
                        bS = nbk(4)
                        P.memset(stt_, 0.0, eng="pool")
                        for hh in range(2):
                            h = hp * 2 + hh
                            g = hp
                            P.mm(bank(bS + 2 * hh), qpl[:, h, tq:tq + 128], kcT[:, g, :])
                            k0 = blks[0] * 128
                            has_mask = (blks[0] == qb - 1) or (blks[-1] == qb + 1)
                            P.mm(ps[:, bS + 2 * hh + 1, 0:nl * 128], qro[:, h, tq:tq + 128],
                                 kfm[:, g, k0:k0 + nl * 128], start=True, stop=not has_mask)
                            nm = (1 if blks[0] == qb - 1 else 0) + (1 if blks[-1] == qb + 1 else 0)
                            cnt = 0
                            for bi, k in enumerate(blks):
                                if k == qb - 1 or k == qb + 1:
                                    cnt += 1
                                    P.mm(ps[:, bS + 2 * hh + 1, bi * 128:(bi + 1) * 128], identB,
                                         MUb if k == qb - 1 else MLb, start=False, stop=(cnt == nm))
                        S2 = ps[:, bS:bS + 4, :].rearrange("p (h a) t -> p h (a t)", h=2, a=2)[:, :, 0:W]
                        P.reduce(mx, S2, ALU.max)
                        P.tt(mx, mx, sink_bc[:, hp * 2:hp * 2 + 2], ALU.max)
                        P.ts(negm, mx, -1.0, ALU.mult)
                        for hh in range(2):
                            P.act(Pb[:, hh, 0:W], S2[:, hh, :], AF.Exp, bias=negm[:, hh:hh + 1],
                                  accum=rsum[:, hh:hh + 1])
                        P.tt(es, sink_bc[:, hp * 2:hp * 2 + 2], negm, ALU.add)
                        P.act(es, es, AF.Exp)
                        P.tt(rsum, rsum, es, ALU.add)
                        P.recip(rsum, rsum)
                        nblk = 4 + nl
                        for hh in range(2):
                            bT = nb()
                            for bi in range(nblk):
                                P.tr(bbf(bT)[:, bi * 128:(bi + 1) * 128], Pb[:, hh, bi * 128:(bi + 1) * 128], identB)
                            P.copy(PTs[:, hh, 0:W], bbf(bT)[:, 0:W], eng="act" if hh == 0 else "dve")
                        bO = nb()
                        for hh in range(2):
                            g = hp
                            for bi in range(nblk):
                                if bi < 4:
                                    rhs = vc[:, bi, g * 128:(g + 1) * 128]
                                else:
                                    rhs = vtm[:, blks[bi - 4], g * 128:(g + 1) * 128]
                                P.mm(ps[:, bO, hh * 128:(hh + 1) * 128], PTs[:, hh, bi * 128:(bi + 1) * 128], rhs,
                                     start=(bi == 0), stop=(bi == nblk - 1))
                        P.tt(on[:, hp * 2:hp * 2 + 2, :],
                             ps[:, bO, 0:256].rearrange("p (h t) -> p h t", h=2, t=128),
                             rsum.unsqueeze(2).broadcast_to([128, 2, 128]), ALU.mult)
                    bT2 = nb()
                    for h in range(4):
                        P.tr(bbf(bT2)[:, h * 128:(h + 1) * 128], on[:, h, :], identB)
                    P.copy(mixT[:, 4:8, tq:tq + 128],
                           bbf(bT2)[:, 0:512].rearrange("p (h t) -> p h t", h=4, t=128), eng="dve")
            if "attn" in debug_names and grp == DBG_GRP:
                dump("d_oatt", mixT[:, 4:8, :], [128, 4, 1024], BF16)

            if "stop_attn" in debug_names:
                return
            delta_net(grp, T0, nseq, L, qn, kn, vS, gS, mixT)
            if "delta" in debug_names and grp == DBG_GRP:
                dump("d_oa", mixT[:, 0:4, :], [128, 4, 1024], BF16)

            if "stop_delta" in debug_names:
                return
            wo = ar.view(WP, [128, KC, 1024], BF16)
            P.dma(wo, w_out[:, :, :], q="pool")
            for n in range(KC):
                for tt in range(2):
                    b = nb()
                    for kc in range(KC):
                        P.mm(bank(b), wo[:, kc, n * 128:(n + 1) * 128], mixT[:, kc, tt * TT:(tt + 1) * TT],
                             start=(kc == 0), stop=(kc == KC - 1))
                    xs = xT[:, n, T0 + tt * TT:T0 + (tt + 1) * TT]
                    P.stt(xs, bank(b), modG[:, l, 1, n, grp:grp + 1], xs, ALU.mult, ALU.add)

    def delta_net(grp, T0, nseq, L, qn, kn, vS, gS, mixT):
        abT = gates[:, 0, :, :]
        gT = gates[:, 1, :, 0:8]
        beta = gates[:, 2, :, 0:8]
        gc = gates[:, 3, :, 0:8]
        glb = gates[:, 4, :, 0:8]
        egc = gates[:, 5, :, 0:8]
        kdf = gates[:, 6, :, 0:8]
        glast = gates[:, 7, :, 0:8]
        bege = gates[:, 1, :, 8:16]
        tmpA = gates[:, 2, :, 8:16]
        ngT = gates[:, 4, :, 8:16]
        tmpB = gates[:, 3, :, 8:16]
        alog_bc = ptab_sb[:, PT_ALOG:PT_ALOG + 8]
        dtb_bc = ptab_sb[:, PT_DTB:PT_DTB + 8]
        onorm = ptab_sb[:, PT_ONORME:PT_ONORME + 1]
        bc8 = lambda v: v.unsqueeze(1).broadcast_to([128, 8, 8])
        P.act(beta, abT[:, :, 0:8], AF.Exp, scale=-1.0)
        P.ts(beta, beta, 1.0, ALU.add)
        P.recip(beta, beta)
        P.tt(tmpA, abT[:, :, 8:16], bc8(dtb_bc), ALU.add)
        P.act(tmpB, tmpA, AF.Abs)
        P.act(tmpB, tmpB, AF.Exp, scale=-1.0)
        P.act(tmpB, tmpB, AF.Ln, bias=onesF[:, 0:1])
        P.ts(tmpA, tmpA, 0.0, ALU.max)
        P.tt(tmpA, tmpA, tmpB, ALU.add)
        P.act(tmpB[:, 0, :], alog_bc, AF.Exp)
        P.stt(gT, tmpA, -1.0, bc8(tmpB[:, 0, :]), ALU.mult, ALU.mult)
        P.ts(ngT, gT, -1.0, ALU.mult)
        g64 = gates[:, 1, :, :].rearrange("p a b -> p (a b)")
        b = nb()
        P.mm(ps[:, b, 0:128], Uf, g64)
        P.mm(ps[:, b, 128:256], Ub, g64)
        pv = ps[:, b, 0:256].rearrange("p (d a c) -> p d a c", d=2, a=8, c=16)
        P.copy(gc[:, :, 0:4], pv[:, 0, :, 0:4], eng="dve")
        P.copy(gc[:, :, 4:8], pv[:, 1, :, 4:8], eng="dve")
        gc64 = gates[:, 3, :, :].rearrange("p a b -> p (a b)")
        b = nb()
        P.mm(ps[:, b, 0:128], sel127, gc64)
        P.mm(ps[:, b, 128:256], sel0, gc64)
        pv = ps[:, b, 0:256].rearrange("p (d a c) -> p d a c", d=2, a=8, c=16)
        P.copy(glb[:, :, 0:4], pv[:, 0, :, 0:4], eng="dve")
        P.copy(glb[:, :, 4:8], pv[:, 1, :, 4:8], eng="dve")
        P.act(egc, gc, AF.Exp)
        P.tt(kdf, glb, gc, ALU.subtract)
        P.act(kdf, kdf, AF.Exp)
        P.act(glast, glb, AF.Exp)
        P.tt(bege, beta, egc, ALU.mult)

        if "gates" in debug_names and grp == DBG_GRP:
            dump("d_g", gT, [128, 8, 8], F32)
            dump("d_beta", beta, [128, 8, 8], F32)
            dump("d_gc", gc, [128, 8, 8], F32)
            dump("d_glb", glb, [128, 8, 8], F32)
        DB = 65536
        TSTS = [DB, DB + 12288]
        SCN2 = DB + 24576
        SHR2 = SCN2 + 16384
        STF2 = SHR2 + 8192
        STB2 = STF2 + 4096
        OACCA = STB2 + 2048
        assert OACCA + 4096 <= ar.nbytes
        Sf = [ar.view(STF2 + d * 2048, [128, 4, 128], F32) for d in range(2)]
        Sb = [ar.view(STB2 + d * 1024, [128, 4, 128], BF16) for d in range(2)]
        nch = L // 128
        if grp == 0:
            oaccA = ar.view(OACCA, [128, 4, 256], F32)
        else:
            oaccB = ar.t[:, 0:8192].rearrange("p (k t) -> p k t", k=8, t=1024)[:, :, 0:512].rearrange(
                "p (h a) t -> p h a t", h=4, a=2)

        def oslice(c):
            if grp == 0:
                return oaccA[:, :, c * 128:(c + 1) * 128]
            t0_ = c * 128
            return oaccB[:, :, t0_ // 512, t0_ % 512:t0_ % 512 + 128]

        def tv(off, dt):
            return ar.view(off, [128, 4, 128], dt)

        def mask_b(k_):
            return cstb[:, 512 + k_ * 128: 512 + (k_ + 1) * 128]

        def mm4(lh, rh):
            bb = nb()
            for h in range(4):
                P.mm(ps[:, bb, h * 128:(h + 1) * 128], lh[:, h, :], rh[:, h, :])
            return bb

        def step(s_, c, d, first_touch):
            T_ = TSTS[d]
            dec, decT = tv(T_, F32), tv(T_ + 2048, F32)
            G1, G2 = dec, decT
            Lb, LTb = tv(T_ + 4096, BF16), tv(T_ + 5120, BF16)
            Ab, Bb = tv(T_ + 6144, BF16), tv(T_ + 7168, BF16)
            Cb, Db, Eb, Tb = [tv(T_ + 8192 + 1024 * k_, BF16) for k_ in range(4)]
            B2, B3 = Cb, Db
            S_ = SCN2 + d * 8192
            Xb, qkT, QdT, Vb_, Kbe, kdec, negwT, vnew = [tv(S_ + 1024 * k_, BF16) for k_ in range(8)]
            H_ = SHR2 + d * 4096
            Ktm, Vtm, KKs, KQs = [tv(H_ + 1024 * k_, BF16) for k_ in range(4)]
            ti = s_ * nch + c
            tsl = slice(ti * 128, (ti + 1) * 128)
            dsl = slice(d * 4, d * 4 + 4)
            bK = nb()
            for h in range(4):
                P.tr(bbf(bK)[:, h * 128:(h + 1) * 128], kn[:, h, tsl], identB)
            P.copy(Ktm.rearrange("p h t -> p (h t)"), bbf(bK)[:, 0:512], eng="act")
            bV = nb()
            for h in range(4):
                P.tr(bbf(bV)[:, h * 128:(h + 1) * 128], vS[:, h, tsl], identB)
            P.copy(Vtm.rearrange("p h t -> p (h t)"), bbf(bV)[:, 0:512], eng="act")
            yield
            bKK = nb()
            for h in range(4):
                P.mm(ps[:, bKK, h * 128:(h + 1) * 128], kn[:, h, tsl], kn[:, h, tsl])
            P.copy(KKs.rearrange("p h t -> p (h t)"), bank(bKK), eng="act")
            bKQ = nb()
            for h in range(4):
                P.mm(ps[:, bKQ, h * 128:(h + 1) * 128], kn[:, h, tsl], qn[:, h, tsl])
            P.copy(KQs.rearrange("p h t -> p (h t)"), bank(bKQ), eng="act")
            yield
            U_ = Uf if d == 0 else Ub
            Mdec, MdecT = (MLf, MUf) if d == 0 else (MUf, MLf)
            for h in range(4):
                P.act(G1[:, h, :], onesF, AF.Copy, scale=gT[:, ti, d * 4 + h:d * 4 + h + 1])
            for h in range(4):
                P.act(G2[:, h, :], U_, AF.Copy, scale=ngT[:, ti, d * 4 + h:d * 4 + h + 1])
            bD = nb()
            P.mm(bank(bD), U_, G1.rearrange("p h t -> p (h t)"), start=True, stop=False)
            P.mm(bank(bD), onesF, G2.rearrange("p h t -> p (h t)"), start=False, stop=True)
            yield
            P.tt(dec, b4(bD), bc_h(Mdec), ALU.add)
            P.stt(decT, b4(bD), -1.0, bc_h(MdecT), ALU.mult, ALU.add)
            P.act(dec, dec, AF.Exp)
            P.act(decT, decT, AF.Exp)
            P.tt(B2, bc_h(identB), bc_i(beta[:, ti, dsl]), ALU.mult, eng="pool")
            P.tt(B3, bc_h(identB), bc_i(egc[:, ti, dsl]), ALU.mult, eng="pool")
            bR = nb()
            P.mm(bank(bR), onesP, B2.rearrange("p h t -> p (h t)"))
            bE = nb()
            P.mm(bank(bE), onesP, B3.rearrange("p h t -> p (h t)"))
            yield
            P.tt(Lb, KKs, dec, ALU.mult)
            P.tt(Lb, Lb, bc_i(beta[:, ti, dsl]), ALU.mult)
            P.tt(LTb, KKs, decT, ALU.mult, eng="pool")
            P.tt(LTb, LTb, b4(bR), ALU.mult)
            P.tt(qkT, KQs, decT, ALU.mult, eng="pool")
            P.tt(QdT, qn[:, :, tsl], b4(bE), ALU.mult)
            for h in range(4):
                hc = d * 4 + h
                P.act(Vb_[:, h, :], Vtm[:, h, :], AF.Copy, scale=beta[:, ti, hc:hc + 1])
                P.act(Kbe[:, h, :], Ktm[:, h, :], AF.Copy, scale=bege[:, ti, hc:hc + 1])
                P.act(kdec[:, h, :], Ktm[:, h, :], AF.Copy, scale=kdf[:, ti, hc:hc + 1])
            yield
            mi = (lambda k: k) if d == 0 else (lambda k: (k + 4) if 1 <= k <= 4 else (k - 4 if k >= 5 else k))
            P.tt(Ab, Lb, bc_h(mask_b(0)), ALU.mult)
            P.tt(Bb, LTb, bc_h(mask_b(0)), ALU.mult)
            P.stt(Tb, Ab, -1.0, bc_h(identF), ALU.mult, ALU.add)
            P.stt(Xb, Bb, -1.0, bc_h(identF), ALU.mult, ALU.add)
            yield
            b1 = mm4(Bb, Ab)
            b2 = mm4(Ab, Bb)
            P.copy(Cb, b4(b1), eng="act")
            P.copy(Db, b4(b2), eng="act")
            yield
            bx = mm4(Cb, Xb)
            bt = mm4(Xb, Cb)
            b3 = mm4(Db, Cb)
            P.tt(Xb, Xb, b4(bx), ALU.add)
            P.tt(Tb, Tb, b4(bt), ALU.add)
            P.copy(Eb, b4(b3), eng="act")
            yield
            bx = mm4(Eb, Xb)
            bt = mm4(Xb, Eb)
            P.tt(Xb, Xb, b4(bx), ALU.add)
            P.tt(Tb, Tb, b4(bt), ALU.add)
            yield
            for lv in range(1, 5):
                last = (lv == 4)
                P.tt(Ab, Lb, bc_h(mask_b(mi(lv))), ALU.mult, eng="pool")
                if not last:
                    P.tt(Bb, LTb, bc_h(mask_b(mi(lv + 4))), ALU.mult)
                b1 = mm4(Ab, Xb)
                if not last:
                    b2 = mm4(Bb, Tb)
                P.copy(Cb, b4(b1), eng="act")
                if not last:
                    P.copy(Db, b4(b2), eng="act")
                yield
                bx = mm4(Tb, Cb)
                if not last:
                    bt = mm4(Xb, Db)
                P.tt(Xb, Xb, b4(bx), ALU.subtract)
                if not last:
                    P.tt(Tb, Tb, b4(bt), ALU.subtract)
                yield
            if "step0" in debug_names and grp == DBG_GRP and s_ == 0 and (c, d) == DBG_STEP:
                dump("d_dec", dec, [128, 4, 128], F32)
                dump("d_decT", decT, [128, 4, 128], F32)
                dump("d_X", Xb, [128, 4, 128], BF16)
                dump("d_qkT", qkT, [128, 4, 128], BF16)
                dump("d_QdT", QdT, [128, 4, 128], BF16)
                dump("d_KKs", KKs, [128, 4, 128], BF16)
            bW = mm4(Kbe, Xb)
            P.act(negwT.rearrange("p h t -> p (h t)"), bank(bW), AF.Copy, scale=-1.0)
            yield
            bVn = nb()
            for h in range(4):
                P.mm(ps[:, bVn, h * 128:(h + 1) * 128], Xb[:, h, :], Vb_[:, h, :], start=True, stop=False)
                P.mm(ps[:, bVn, h * 128:(h + 1) * 128], negwT[:, h, :], Sb[d][:, h, :], start=False, stop=True)
            P.copy(vnew.rearrange("p h t -> p (h t)"), bank(bVn), eng="act")
            yield
            bO = nb()
            for h in range(4):
                P.mm(ps[:, bO, h * 128:(h + 1) * 128], Sb[d][:, h, :], QdT[:, h, :], start=True, stop=False)
                P.mm(ps[:, bO, h * 128:(h + 1) * 128], vnew[:, h, :], qkT[:, h, :], start=False, stop=True)
            bS_ = mm4(kdec, vnew)
            oa = oslice(c if grp == 1 else c)
            if first_touch[c]:
                P.copy(oa, b4(bO), eng="dve")
                first_touch[c] = False
            else:
                P.tt(oa, oa, b4(bO), ALU.add)
            P.tt(Sf[d], Sf[d], bc_i(glast[:, ti, dsl]), ALU.mult)
            P.tt(Sf[d], Sf[d], b4(bS_), ALU.add)
            P.copy(Sb[d], Sf[d], eng="act")
            yield

        for s_ in range(nseq):
            for d in range(2):
                if grp == 0:
                    P.memset(Sf[d], 0.0, eng="pool")
                else:
                    P.dma(Sf[d], state_delta[d].rearrange("h k v -> k h v"), q="sp")
                P.copy(Sb[d], Sf[d], eng="pool")
            first_touch = [True] * nch
            for k in range(nch):
                gens = [step(s_, k, 0, first_touch), step(s_, nch - 1 - k, 1, first_touch)]
                while gens:
                    for g_ in list(gens):
                        try:
                            next(g_)
                        except StopIteration:
                            gens.remove(g_)
            if grp == 0:
                for d in range(2):
                    P.dma(nsd[s_, d].rearrange("h k v -> k h v"), Sf[d], q="sp", is_output=True)
            for h in range(4):
                for t_ in range(0, L, 512):
                    w = min(512, L - t_)
                    sl = slice(s_ * L + t_, s_ * L + t_ + w)
                    src = oaccA[:, h, 0:w] if grp == 0 else oaccB[:, h, t_ // 512, 0:512]
                    sq = ar.view(TSTS[0], [128, 512], BF16)
                    P.act(sq[:, 0:w], src, AF.Square)
                    b = nb()
                    P.mm(ps[:, b, 0:w], onesP, sq[:, 0:w])
                    rs = rstd[:, 0, 0:w]
                    P.act(rs, ps[:, b, 0:w], AF.Ln, bias=epsT[:, 0:1], scale=1.0 / 128.0)
                    P.act(rs, rs, AF.Exp, scale=-0.5)
                    t_f = ar.view(TSTS[0] + 2048, [128, 512], F32)
                    P.stt(t_f[:, 0:w], src, onorm[:, 0:1], rs, ALU.mult, ALU.mult)
                    P.tt(mixT[:, h, sl], t_f[:, 0:w], gS[:, h, sl], ALU.mult)

    def mixer_odd():
        l = 1
        modnorm(l, 1)
        w_in = odd_w_in[0].rearrange("(kc p) n -> p kc n", p=128)
        w_out = odd_w_out[0].rearrange("(kc p) n -> p kc n", p=128)
        onorm2 = ptab_sb[:, PT_ONORMO:PT_ONORMO + 2]
        QT_, KT_, VT_, GS_, LRT_, OACC2 = 32768, 40960, 49152, 65536, 81920, 86016
        WPO = 102400
        ZB_, ZE_ = 102400, 104448
        EG_, ENG_ = ZE_, ZB_
        QTL_, KTL_, QTT_, KTT_, ATB_ = 110592, 111616, 112640, 113664, 114688
        SF_, SB_, WG_ = 115712, 119808, 121856
        SC = 128.0 ** -0.5
        wgp = ar.view(WG_, [128, 2, 512], BF16)
        P.dma(wgp[0:33, :, :], wgpad[:, :, :], q="pool")
        wpc = [0]

        def load_piece(c0, c1):
            wp = ar.view(WPO + (wpc[0] % 2) * 8192, [128, KC, 512], BF16)
            wpc[0] += 1
            P.dma(wp[:, :, 0:c1 - c0], w_in[:, :, c0:c1], q="pool")
            return wp

        qT = ar.view(QT_, [128, 8, 512], BF16)
        kT = ar.view(KT_, [128, 8, 512], BF16)
        vT = ar.view(VT_, [128, 8, 1024], BF16)
        gS = ar.view(GS_, [128, 8, 1024], BF16)
        lrT = ar.view(LRT_, [128, 1024], BF16)
        glt = gates[:, 0, 0, 0:4]
        for grp in range(2):
            T0 = grp * 1024
            nseq, L = (4, 256) if grp == 0 else (1, 1024)
            nch = L // 128
            mixT = hT[:, :, T0:T0 + 1024]
            wp = load_piece(3072, 3104)
            for tt in range(2):
                b = nb()
                for kc in range(KC):
                    P.mm(ps[0:32, b, :], wp[:, kc, 0:32], hT[:, kc, T0 + tt * TT:T0 + (tt + 1) * TT],
                         start=(kc == 0), stop=(kc == KC - 1))
                P.copy(lrT[0:32, tt * TT:(tt + 1) * TT], ps[0:32, b, :], eng="dve")
            P.memset(lrT[32:33, :], 1.0, eng="dve")
            for (c0, dst, col0, scl) in ((0, qT, 0, SC), (512, kT, 0, 1.0), (1024, vT, 0, 1.0), (1536, vT, 512, 1.0)):
                wp = load_piece(c0, c0 + 512)
                for i in range(8):
                    b = nb()
                    for kc in range(KC):
                        P.mm(bank(b), hT[:, kc, T0 + i * 128:T0 + (i + 1) * 128], wp[:, kc, 0:512],
                             start=(kc == 0), stop=(kc == KC - 1))
                    if i % 2 == 0:
                        P.act(dst[:, i, col0:col0 + 512], bank(b), AF.Copy, scale=scl)
                    else:
                        P.ts(dst[:, i, col0:col0 + 512], bank(b), scl, ALU.mult)
            for pc in range(2):
                wp = load_piece(2048 + pc * 512, 2560 + pc * 512)
                for cc in range(4):
                    b = nbk(2)
                    for tt in range(2):
                        for kc in range(KC):
                            P.mm(bank(b + tt), wp[:, kc, cc * 128:(cc + 1) * 128],
                                 hT[:, kc, T0 + tt * TT:T0 + (tt + 1) * TT], start=(kc == 0), stop=(kc == KC - 1))
                    P.act(gS[:, pc * 4 + cc, :], ps[:, b:b + 2, :].rearrange("p a t -> p (a t)"), AF.Silu)
            if "oinproj" in debug_names and grp == DBG_GRP:
                dump("d_qT", qT, [128, 8, 512], BF16)
                dump("d_kT", kT, [128, 8, 512], BF16)
                dump("d_vT", vT, [128, 8, 1024], BF16)
                dump("d_gS", gS, [128, 8, 1024], BF16)
                dump("d_lrT", lrT[0:33, :], [33, 1024], F32)

            zb = ar.view(ZB_, [128, 512], F32)
            ze = ar.view(ZE_, [128, 512], F32)
            eg = ar.view(EG_, [128, 512], F32)
            eng_ = ar.view(ENG_, [128, 512], F32)
            qtl = ar.view(QTL_, [128, 512], BF16)
            ktl = ar.view(KTL_, [128, 512], BF16)
            qtt = ar.view(QTT_, [128, 4, 128], BF16)
            ktt = ar.view(KTT_, [128, 4, 128], BF16)
            atb = ar.view(ATB_, [128, 4, 128], BF16)
            Sf = ar.view(SF_, [128, 4, 256], F32)
            Sb = ar.view(SB_, [128, 4, 256], BF16)
            if grp == 1:
                oacc_lo = ar.t[:, 0:8192].rearrange("p (k t) -> p k t", k=8, t=1024)[:, :, 0:512]
                oacc_hi = ar.view(OACC2, [128, 8, 512], F32)
            else:
                oaccA = ar.view(OACC2, [128, 8, 256], F32)

            def oslice(c, fc0, fc1):
                if grp == 0:
                    return oaccA[:, fc0:fc1, c * 128:(c + 1) * 128]
                if c < 4:
                    return oacc_lo[:, fc0:fc1, c * 128:(c + 1) * 128]
                return oacc_hi[:, fc0:fc1, (c - 4) * 128:(c - 3) * 128]

            bufsets = []
            for k_ in range(2):
                if k_ == 0:
                    offs = (QTL_, KTL_, QTT_, KTT_, ATB_)
                else:
                    offs = (106496, 107520, 108544, 109568, 125952)
                bufsets.append((ar.view(offs[0], [128, 512], BF16), ar.view(offs[1], [128, 512], BF16),
                                ar.view(offs[2], [128, 4, 128], BF16), ar.view(offs[3], [128, 4, 128], BF16),
                                ar.view(offs[4], [128, 4, 128], BF16), gates[:, 0, k_, 0:4]))

            rb_ = rstd[:, :, :].rearrange("p a t -> p (a t)").bitcast(BF16)
            egs = [rb_[:, k_ * 512:(k_ + 1) * 512] for k_ in range(2)]
            engs = [rb_[:, 1024 + k_ * 512:1024 + (k_ + 1) * 512] for k_ in range(2)]
            glts = [gates[:, 0, k_, 0:4] for k_ in range(3)]
            gsum = gates[:, 0, 3, 0:4]

            def prefixA(s_, d, c, ka):
                eg_b, eng_b, glt = egs[ka % 2], engs[ka % 2], glts[ka % 3]
                U_ = Uf if d == 0 else Ub
                ti = s_ * nch + c
                tsl = slice(ti * 128, (ti + 1) * 128)
                bz = nb()
                P.mm(bank(bz), lrT[0:33, tsl], wgp[0:33, d, :])
                yield
                P.ts(ze, bank(bz), -80.0, ALU.max)
                P.act(ze, ze, AF.Exp, scale=-1.0)
                yield
                P.act(zb, ze, AF.Ln, bias=onesF[:, 0:1])
                yield
                bg = nb()
                P.mm(bank(bg), U_, zb)
                bl = nb()
                for h in range(4):
                    P.mm(ps[:, bl, h:h + 1], zb[:, h * 128:(h + 1) * 128], onesF[:, 0:1])
                yield
                P.act(eg_b, bank(bg), AF.Exp, scale=-1.0 / 16.0)
                P.act(eng_b, bank(bg), AF.Exp, scale=1.0 / 16.0)
                P.act(glt, ps[:, bl, 0:4], AF.Exp, scale=-1.0 / 16.0)
                yield

            def prefixB(s_, d, c, bs_, ka):
                qtl, ktl, qtt, ktt, atb, _ = bs_
                eg_b, eng_b = egs[ka % 2], engs[ka % 2]
                U_ = Uf if d == 0 else Ub
                ti = s_ * nch + c
                P.tt(qtl, qT[:, ti, :], eg_b, ALU.mult)
                P.tt(ktl, kT[:, ti, :], eng_b, ALU.mult)
                yield
                bq = nb()
                for h in range(4):
                    P.tr(bbf(bq)[:, h * 128:(h + 1) * 128], qtl[:, h * 128:(h + 1) * 128], identB)
                P.copy(qtt.rearrange("p h t -> p (h t)"), bbf(bq)[:, 0:512], eng="act")
                bk = nb()
                for h in range(4):
                    P.tr(bbf(bk)[:, h * 128:(h + 1) * 128], ktl[:, h * 128:(h + 1) * 128], identB)
                P.copy(ktt.rearrange("p h t -> p (h t)"), bbf(bk)[:, 0:512], eng="dve")
                yield
                ba = nb()
                for h in range(4):
                    P.mm(ps[:, ba, h * 128:(h + 1) * 128], ktt[:, h, :], qtt[:, h, :])
                P.tt(atb, b4(ba), bc_h(U_), ALU.mult)
                yield

            def suffix(s_, d, c, bs_, first_touch, ka):
                qtl, ktl, qtt, ktt, atb, _ = bs_
                glt = glts[ka % 3]
                ti = s_ * nch + c
                bo = nbk(2)
                for h in range(4):
                    for half in range(2):
                        fc = h * 2 + half
                        dstp = ps[:, bo + fc // 4, (fc % 4) * 128:(fc % 4 + 1) * 128]
                        P.mm(dstp, vT[:, ti, h * 256 + half * 128:h * 256 + (half + 1) * 128], atb[:, h, :],
                             start=True, stop=False)
                        P.mm(dstp, Sb[:, h, half * 128:(half + 1) * 128], qtt[:, h, :], start=False, stop=True)
                for k2 in range(2):
                    oa = oslice(c, k2 * 4, k2 * 4 + 4)
                    if first_touch[c]:
                        P.copy(oa, b4(bo + k2), eng="dve" if k2 == 0 else "act")
                    else:
                        P.tt(oa, oa, b4(bo + k2), ALU.add)
                first_touch[c] = False
                yield
                bs2 = nbk(2)
                for h in range(4):
                    P.mm(ps[:, bs2 + h // 2, (h % 2) * 256:(h % 2 + 1) * 256], ktl[:, h * 128:(h + 1) * 128],
                         vT[:, ti, h * 256:(h + 1) * 256])
                P.tt(Sf, Sf, ps[:, bs2:bs2 + 2, :].rearrange("p a (h v) -> p (a h) v", h=2, v=256), ALU.add)
                P.tt(Sf, Sf, glt.unsqueeze(2).broadcast_to([128, 4, 256]), ALU.mult)
                P.copy(Sb, Sf, eng="act")
                yield

            for s_ in range(nseq):
                first_touch = [True] * nch
                steps = [(d, (cidx if d == 0 else nch - 1 - cidx)) for d in range(2) for cidx in range(nch)]
                def drive(gens):
                    gens = [g_ for g_ in gens if g_ is not None]
                    while gens:
                        for g_ in list(gens):
                            try:
                                next(g_)
                            except StopIteration:
                                gens.remove(g_)

                ns = len(steps)

                def gA(k2):
                    return prefixA(s_, steps[k2][0], steps[k2][1], k2) if k2 < ns else None

                def gB(k2):
                    return prefixB(s_, steps[k2][0], steps[k2][1], bufsets[k2 % 2], k2) if k2 < ns else None

                drive([gA(0)])
                drive([gB(0), gA(1)])
                for k_, (d, c) in enumerate(steps):
                    if k_ % nch == 0:
                        if grp == 0:
                            P.memset(Sf, 0.0, eng="pool")
                        else:
                            P.dma(Sf, state_gla[d].rearrange("h k v -> k h v"), q="sp")
                        P.copy(Sb, Sf, eng="pool")
                    drive([gA(k_ + 2), gB(k_ + 1), suffix(s_, d, c, bufsets[k_ % 2], first_touch, k_)])
                    if k_ % nch == nch - 1 and grp == 0:
                        P.dma(nsg[s_, d].rearrange("h k v -> k h v"), Sf, q="sp", is_output=True)
                for h in range(4):
                    for t_ in range(0, L, 512):
                        w = min(512, L - t_)
                        b = nb()
                        srcs = []
                        for half in range(2):
                            fc = h * 2 + half
                            if grp == 0:
                                src = oaccA[:, fc, t_:t_ + w]
                            else:
                                src = (oacc_lo if t_ == 0 else oacc_hi)[:, fc, 0:512]
                            srcs.append(src)
                            sq = ar.view(ZB_ + half * 1024, [128, 512], BF16)
                            P.act(sq[:, 0:w], src, AF.Square)
                            P.mm(ps[:, b, 0:w], onesP, sq[:, 0:w], start=(half == 0), stop=(half == 1))
                        rs = rstd[:, 0, 0:w]
                        P.act(rs, ps[:, b, 0:w], AF.Ln, bias=epsT[:, 0:1], scale=1.0 / 256.0)
                        P.act(rs, rs, AF.Exp, scale=-0.5)
                        for half in range(2):
                            fc = h * 2 + half
                            t_f = ar.view(EG_, [128, 512], F32)
                            P.stt(t_f[:, 0:w], srcs[half], onorm2[:, half:half + 1], rs, ALU.mult, ALU.mult)
                            sl = slice(s_ * L + t_, s_ * L + t_ + w)
                            P.tt(mixT[:, fc, sl], t_f[:, 0:w], gS[:, fc, sl], ALU.mult)
            if "gla" in debug_names and grp == DBG_GRP:
                dump("d_mix", mixT, [128, 8, 1024], BF16)
            wo = ar.view(WPO, [128, KC, 1024], BF16)
            P.dma(wo, w_out[:, :, :], q="pool")
            for n in range(KC):
                for tt in range(2):
                    b = nb()
                    for kc in range(KC):
                        P.mm(bank(b), wo[:, kc, n * 128:(n + 1) * 128], mixT[:, kc, tt * TT:(tt + 1) * TT],
                             start=(kc == 0), stop=(kc == KC - 1))
                    xs = xT[:, n, T0 + tt * TT:T0 + (tt + 1) * TT]
                    P.stt(xs, bank(b), modG[:, l, 1, n, grp:grp + 1], xs, ALU.mult, ALU.add)


    if stage != "all":
        ada_flush()
    if stage == "ffn0":
        ffn(0, 0)
        final_out()
    elif stage == "mix0":
        mixer_even()
        final_out()
    elif stage == "mix1":
        mixer_odd()
        final_out()
    else:
        def pre(l_, s_):
            def f(tt):
                modnorm_tile(l_, s_, tt)
                if tt == NTT - 1:
                    pre_normed[0] = (l_, s_)
            return f
        ffn(0, 0, after_tile=pre(0, 1))
        mixer_even()
        ffn(0, 1, after_tile=pre(1, 0))
        ffn(1, 0, after_tile=pre(1, 1))
        mixer_odd()
        ffn(1, 1, after_tile=final_tile)

    P.emit()
    stack.close()
    return nc, dbg


PT_ADAB = 0
PT_NORMG = PT_ADAB + 144
PT_FINALG = PT_NORMG + 48
PT_CONV = PT_FINALG + 8
PT_SINK = PT_CONV + 60
PT_ALOG = PT_SINK + 4
PT_DTB = PT_ALOG + 8
PT_ONORME = PT_DTB + 8
PT_ONORMO = PT_ONORME + 1
PT_COLS = PT_ONORMO + 2
DBG_GRP = 0
DBG_STEP = (0, 0)

C_IDENT = 0
C_UF, C_UB, C_SEL127, C_SEL0, C_ONES, C_OFFD, C_ML, C_MU, C_RM = [128 * i for i in range(1, 10)]
CST_COLS = 128 * 10
CSTB_COLS = 512 + 9 * 128


def _fm(v):
    v = np.asarray(v, np.float32)
    return np.ascontiguousarray(v.reshape(-1, 128).T)


def make_tables(inputs):
    pt = np.zeros((128, PT_COLS), np.float32)
    for l in range(2):
        pt[:, PT_ADAB + l * 72: PT_ADAB + (l + 1) * 72] = _fm(inputs["ada_b"][l])
        for s in range(3):
            o = PT_NORMG + (l * 3 + s) * 8
            pt[:, o:o + 8] = _fm(inputs["norm_g"][l, s])
    pt[:, PT_FINALG:PT_FINALG + 8] = _fm(inputs["final_g"])
    cv = np.asarray(inputs["even_conv"], np.float32)[0]
    pt[:, PT_CONV:PT_CONV + 60] = cv.T.reshape(12, 128, 5).transpose(1, 0, 2).reshape(128, 60)
    pt[:, PT_SINK:PT_SINK + 4] = np.broadcast_to(np.asarray(inputs["even_sink"], np.float32)[0][None, :], (128, 4))
    pt[:, PT_ALOG:PT_ALOG + 8] = np.broadcast_to(np.asarray(inputs["even_a_log"], np.float32)[0].reshape(1, 8), (128, 8))
    pt[:, PT_DTB:PT_DTB + 8] = np.broadcast_to(np.asarray(inputs["even_dt_bias"], np.float32)[0].reshape(1, 8), (128, 8))
    pt[:, PT_ONORME:PT_ONORME + 1] = np.asarray(inputs["even_onorm"], np.float32)[0].reshape(128, 1)
    pt[:, PT_ONORMO:PT_ONORMO + 2] = _fm(inputs["odd_onorm"][0])
    cst = np.zeros((128, CST_COLS), np.float32)
    cst[:, C_IDENT:C_IDENT + 128] = np.eye(128, dtype=np.float32)
    kk, ii = np.meshgrid(np.arange(128), np.arange(128), indexing="ij")
    NEG = -30000.0
    cst[:, C_UF:C_UF + 128] = (kk <= ii)
    cst[:, C_UB:C_UB + 128] = (kk >= ii)
    cst[127, C_SEL127:C_SEL127 + 128] = 1.0
    cst[0, C_SEL0:C_SEL0 + 128] = 1.0
    cst[:, C_ONES:C_ONES + 128] = 1.0
    cst[:, C_OFFD:C_OFFD + 128] = (kk != ii)
    cst[:, C_ML:C_ML + 128] = np.where(ii <= kk, 0.0, NEG)
    cst[:, C_MU:C_MU + 128] = np.where(ii >= kk, 0.0, NEG)
    rm = np.zeros((128, 128), np.float32)
    for dp in range(128):
        if (dp % 64) < 32:
            rm[dp + 32, dp] = -1.0
        else:
            rm[dp - 32, dp] = 1.0
    cst[:, C_RM:C_RM + 128] = rm
    return pt, cst


def make_bmask():
    i, j = np.meshgrid(np.arange(128), np.arange(128), indexing="ij")
    ms = [(i // 8 == j // 8) & (i != j)]
    for b in (8, 16, 32, 64):
        ms.append((i // (2 * b) == j // (2 * b)) & ((i // b) % 2 == 1) & ((j // b) % 2 == 0))
    for b in (8, 16, 32, 64):
        ms.append((i // (2 * b) == j // (2 * b)) & ((i // b) % 2 == 0) & ((j // b) % 2 == 1))
    return np.ascontiguousarray(np.concatenate([m.astype(np.float32) for m in ms], axis=1))


def make_rope():
    t = np.arange(1024)
    row = (t // 64).astype(np.float64)
    col = (t % 64).astype(np.float64)
    inv = 10000.0 ** (-np.arange(32, dtype=np.float64) / 32.0)
    ang = np.zeros((128, 1024))
    for d in range(128):
        pos = row if d < 64 else col
        ang[d] = pos * np.float32(inv[d % 32])
    ang32 = np.zeros((128, 1024), np.float32)
    inv32 = (np.float32(10000.0) ** (-np.arange(32, dtype=np.float32) / np.float32(32))).astype(np.float32)
    for d in range(128):
        pos = (row if d < 64 else col).astype(np.float32)
        ang32[d] = pos * inv32[d % 32]
    return np.ascontiguousarray(np.stack([np.cos(ang32), np.sin(ang32)], axis=1).astype(np.float32))


def make_in_maps(inputs, stage="all"):
    pt, cst = make_tables(inputs)
    rope_t = make_rope()
    bmask_t = make_bmask()
    wg = np.asarray(inputs["odd_w_gate"], np.float32)[0]
    wgpad_t = np.zeros((33, 2, 512), np.float32)
    wgpad_t[0:16, 0, :] = wg[0]
    wgpad_t[16:32, 1, :] = wg[1]
    wgpad_t[32, :, :] = np.asarray(inputs["odd_gate_bias"], np.float32)[0]
    maps = []
    xp = np.asarray(inputs["x_prompt"], np.float32)
    xs = np.asarray(inputs["x_sample"], np.float32)
    for c in range(8):
        xin = np.concatenate([xp[4 * c:4 * c + 4].reshape(1024, D), xs[c]], axis=0)
        cond = np.stack([np.asarray(inputs["c_ctx"], np.float32), np.asarray(inputs["c"], np.float32)[c]], axis=-1)
        condT = np.ascontiguousarray(cond.reshape(KC, 128, 2).transpose(1, 0, 2))
        m = {"xin": np.ascontiguousarray(xin), "condT": condT, "ptab": pt, "cst": cst,
             "ada_w": np.asarray(inputs["ada_w"], np.float32),
             "even_w_in": np.asarray(inputs["even_w_in"], np.float32),
             "even_w_out": np.asarray(inputs["even_w_out"], np.float32),
             "rope": rope_t, "bmask": bmask_t, "wgpad": wgpad_t,
             "odd_w_in": np.asarray(inputs["odd_w_in"], np.float32),
             "odd_w_out": np.asarray(inputs["odd_w_out"], np.float32),
             "state_gla": np.ascontiguousarray(np.asarray(inputs["state_gla"], np.float32)[c, 0]),
             "cache_k": np.ascontiguousarray(np.asarray(inputs["cache_k"], np.float32)[c, 0]),
             "cache_v": np.ascontiguousarray(np.asarray(inputs["cache_v"], np.float32)[c, 0]),
             "state_delta": np.ascontiguousarray(np.asarray(inputs["state_delta"], np.float32)[c, 0])}
        if stage in ("all", "ffn0"):
            m["ffn_w_gu"] = np.asarray(inputs["ffn_w_gu"], np.float32)
            m["ffn_w_down"] = np.asarray(inputs["ffn_w_down"], np.float32)
        maps.append(m)
    return maps


_CACHE = {}


def kernel(**inputs):
    if "nc" not in _CACHE:
        _CACHE["nc"] = build_program("all")[0]
    nc = _CACHE["nc"]
    maps = make_in_maps(inputs)
    res = run_bass_kernel_spmd(nc, maps, core_ids=list(range(8)))
    rs = res.results
    ys = [np.asarray(r["yout"], np.float32) for r in rs]
    y_prompt = np.concatenate([y[:1024].reshape(4, 256, D) for y in ys], axis=0)
    y_sample = np.stack([y[1024:] for y in ys], axis=0)
    nsd = np.concatenate([np.asarray(r["nsd"], np.float32) for r in rs], axis=0)[:, None]
    nck = np.concatenate([np.asarray(r["nck"], np.float32).reshape(4, 256, 2, 128) for r in rs], axis=0)[:, None]
    ncv = np.concatenate([np.asarray(r["ncv"], np.float32).reshape(4, 256, 2, 128) for r in rs], axis=0)[:, None]
    nsg = np.concatenate([np.asarray(r["nsg"], np.float32) for r in rs], axis=0)[:, None]
    return (y_prompt, y_sample, np.ascontiguousarray(nsd), np.ascontiguousarray(nck),
            np.ascontiguousarray(ncv), np.ascontiguousarray(nsg))
```

```python
import numpy as np
import concourse.bass as bass
import concourse.mybir as mybir

F32 = mybir.dt.float32
BF16 = mybir.dt.bfloat16
AF = mybir.ActivationFunctionType
ALU = mybir.AluOpType
AX = mybir.AxisListType

_DTSZ = {F32: 4, BF16: 2}


def _region(ap):
    sp = str(ap.space)
    if "DRAM" in sp.upper() or "HBM" in sp.upper():
        return None
    sz = _DTSZ[ap.dtype]
    pat = ap.ap
    pstep, pcnt = pat[0]
    off = int(ap.offset)
    if pstep == 0:
        p0, f0 = 0, off
        pstep = 1 << 40
    else:
        p0, f0 = off // pstep, off % pstep
    ext = 1
    for st, cnt in pat[1:]:
        ext += (cnt - 1) * abs(st)
    b0, b1 = f0 * sz, (f0 + ext) * sz
    if "PSUM" in sp.upper():
        b0 = (b0 // 2048) * 2048
        b1 = ((b1 + 2047) // 2048) * 2048
        return (ap.tensor.name, 0, 128, b0, b1)
    return (ap.tensor.name, p0, p0 + pcnt, b0, b1)


def _ovl(a, b):
    return a[0] == b[0] and a[1] < b[2] and b[1] < a[2] and a[3] < b[4] and b[3] < a[4]


def _covers(a, b):
    return a[0] == b[0] and a[1] <= b[1] and a[2] >= b[2] and a[3] <= b[3] and a[4] >= b[4]


class Op:
    __slots__ = ("eng", "fn", "seq", "inc", "waits", "ctr", "is_dma", "val")

    def __init__(self, eng, fn, is_dma=False):
        self.eng = eng
        self.fn = fn
        self.inc = False
        self.waits = []
        self.is_dma = is_dma
        self.ctr = None
        self.seq = 0
        self.val = 0


ENGS = ("pe", "act", "dve", "pool", "sp")
NDMA = 24


class Prog:
    def __init__(self, nc):
        self.nc = nc
        self.streams = {e: [] for e in ENGS}
        self.seqc = {}
        self.known = {e: {} for e in ENGS}
        self.recs = {}
        self.dma_rr = {"h": 0, "s": 0}
        self.dma_last = {}
        self.nops = 0
        self.out_dmas = []

    def _need(self, op, dep):
        if dep is op:
            return
        if dep.ctr == op.ctr and not dep.is_dma:
            pass
        k = self.known[op.eng]
        if k.get(dep.ctr, -1) >= dep.seq:
            return
        k[dep.ctr] = dep.seq
        dep.inc = True
        op.waits.append(dep)

    BK = 2048

    def _buckets(self, r):
        return range(r[3] // self.BK, (r[4] - 1) // self.BK + 1)

    def _track(self, op, reads, writes):
        BK = self.BK
        for ap in reads:
            r = _region(ap)
            if r is None:
                continue
            for b in self._buckets(r):
                lst = self.recs.setdefault((r[0], b), [])
                is_ps = (r[0] == "ps")
                for rec in lst:
                    if _ovl(rec[0], r) and (rec[1] == "w" or (is_ps and rec[2].ctr != op.ctr)):
                        d = rec[2]
                        if d.ctr == op.ctr and op.eng == "pe":
                            continue
                        self._need(op, d)
                for i, rec in enumerate(lst):
                    if rec[1] == "r" and rec[2].ctr == op.ctr and rec[0] == r:
                        lst.pop(i)
                        break
                lst.append([r, "r", op])
        for ap in writes:
            r = _region(ap)
            if r is None:
                continue
            for b in self._buckets(r):
                lst = self.recs.setdefault((r[0], b), [])
                keep = []
                lo, hi = b * BK, (b + 1) * BK
                for rec in lst:
                    if rec[2] is op:
                        keep.append(rec)
                        continue
                    rr = rec[0]
                    if _ovl(rr, r):
                        d = rec[2]
                        if not (d.ctr == op.ctr and op.eng == "pe"):
                            self._need(op, d)
                        if (r[1] <= rr[1] and r[2] >= rr[2]
                                and r[3] <= max(rr[3], lo) and r[4] >= min(rr[4], hi)):
                            continue
                    keep.append(rec)
                keep.append([r, "w", op])
                self.recs[(r[0], b)] = keep

    def _add(self, eng, fn, reads, writes):
        op = Op(eng, fn)
        op.ctr = eng
        op.seq = self.seqc.get(eng, 0)
        self.seqc[eng] = op.seq + 1
        self._track(op, reads, writes)
        self.streams[eng].append(op)
        self.nops += 1
        return op

    def dma(self, out, in_, q="sp", is_output=False):
        op = Op(q, None, is_dma=True)
        kind = "s" if q == "pool" else "h"
        k = self.dma_rr[kind]
        self.dma_rr[kind] = (k + 1) % (NDMA // 2)
        op.ctr = "dma%s%d" % (kind, k)
        op.seq = self.seqc.get(op.ctr, 0)
        self.seqc[op.ctr] = op.seq + 1
        prev = self.dma_last.get(op.ctr)
        if prev is not None:
            self._need(op, prev)
        self.dma_last[op.ctr] = op
        op.inc = True
        self._track(op, [in_], [out])
        op.fn = lambda e, out=out, in_=in_: e.dma_start(out=out, in_=in_)
        self.streams[q].append(op)
        if is_output:
            self.out_dmas.append(op)
        return op

    def mm(self, out, lhsT, rhs, start=True, stop=True):
        return self._add("pe", lambda e: e.matmul(out, lhsT, rhs, start=start, stop=stop),
                         [lhsT, rhs], [out])

    def tr(self, out, in_, ident):
        return self._add("pe", lambda e: e.transpose(out, in_, ident), [in_, ident], [out])

    def act(self, out, in_, func, bias=None, scale=1.0, accum=None):
        rd = [in_]
        if bias is not None and not isinstance(bias, (int, float)):
            rd.append(bias)
        if not isinstance(scale, (int, float)):
            rd.append(scale)
        wr = [out] + ([accum] if accum is not None else [])
        kw = {}
        if bias is not None:
            kw["bias"] = bias
        if accum is not None:
            kw["accum_out"] = accum
        return self._add("act", lambda e: e.activation(out=out, in_=in_, func=func, scale=scale, **kw),
                         rd, wr)

    def tt(self, out, in0, in1, op, eng="dve"):
        return self._add(eng, lambda e: e.tensor_tensor(out=out, in0=in0, in1=in1, op=op),
                         [in0, in1], [out])

    def ts(self, out, in0, s1, op0, s2=None, op1=None, eng="dve", accum=None):
        rd = [in0] + [s for s in (s1, s2) if s is not None and not isinstance(s, (int, float))]
        kw = {}
        if op1 is not None:
            kw["op1"] = op1
        if accum is not None:
            kw["accum_out"] = accum
        wr = [out] + ([accum] if accum is not None else [])
        return self._add(eng, lambda e: e.tensor_scalar(out=out, in0=in0, scalar1=s1, scalar2=s2, op0=op0, **kw),
                         rd, wr)

    def stt(self, out, in0, scalar, in1, op0, op1, eng="dve"):
        rd = [in0, in1] + ([scalar] if not isinstance(scalar, (int, float)) else [])
        return self._add(eng, lambda e: e.scalar_tensor_tensor(out=out, in0=in0, scalar=scalar, in1=in1,
                                                               op0=op0, op1=op1), rd, [out])

    def copy(self, out, in_, eng="dve"):
        if eng == "act":
            return self.act(out, in_, AF.Copy)
        return self._add(eng, lambda e: e.tensor_copy(out=out, in_=in_), [in_], [out])

    def reduce(self, out, in_, op, eng="dve", axis=None):
        axis = axis or AX.X
        return self._add(eng, lambda e: e.tensor_reduce(out=out, in_=in_, axis=axis, op=op), [in_], [out])

    def recip(self, out, in_):
        return self._add("dve", lambda e: e.reciprocal(out=out, in_=in_), [in_], [out])

    def memset(self, ap, val, eng="dve"):
        return self._add(eng, lambda e: e.memset(ap, val), [], [ap])

    def emit(self):
        nc = self.nc
        sems = {}
        import contextlib
        stack = contextlib.ExitStack()
        allops = []
        for e in ENGS:
            allops.extend(self.streams[e])
        ctrs = sorted(set(o.ctr for o in allops))
        for c in ctrs:
            sems[c] = stack.enter_context(nc.semaphore("s_" + c))
        cnt = {c: 0 for c in ctrs}
        byctr = {c: [] for c in ctrs}
        for o in allops:
            byctr[o.ctr].append(o)
        for c in ctrs:
            ops = sorted(byctr[c], key=lambda o: o.seq)
            v = 0
            for o in ops:
                if o.inc:
                    v += 16 if o.is_dma else 1
                o.val = v
        fin = stack.enter_context(nc.semaphore("s_fin"))
        block = stack.enter_context(nc.Block())
        prog = self

        def run_stream(eng_name, e):
            for o in prog.streams[eng_name]:
                for d in o.waits:
                    e.wait_ge(sems[d.ctr], d.val)
                ins = o.fn(e)
                if o.inc:
                    ins.then_inc(sems[o.ctr], 16 if o.is_dma else 1)

        @block.tensor
        def _(e):
            run_stream("pe", e)

        @block.scalar
        def _(e):
            run_stream("act", e)

        @block.vector
        def _(e):
            run_stream("dve", e)

        @block.gpsimd
        def _(e):
            run_stream("pool", e)

        @block.sync
        def _(e):
            run_stream("sp", e)
            for o in prog.out_dmas:
                e.wait_ge(sems[o.ctr], o.val)

        stack.close()
from concourse.bass_utils import run_bass_kernel_spmd

D = 1024
NTOK = 2048
KC = 8
TT = 512
NTT = 4
DFF = 2816
NHC = 22
EPS = 1e-6
EVEN_IN = 3088
ODD_IN = 3104


class Arena:
    def __init__(self, nc, name, nbytes, stack):
        self.t = stack.enter_context(nc.sbuf_tensor(name, [128, nbytes // 4], F32))
        self.nbytes = nbytes

    def view(self, off, shape, dtype):
        sz = 4 if dtype == F32 else 2
        n = 1
        for s in shape[1:]:
            n *= s
        assert off % 4 == 0 and (n * sz) % 4 == 0 and off + n * sz <= self.nbytes, (off, shape, self.nbytes)
        ap = self.t[0:shape[0], off // 4: off // 4 + (n * sz) // 4]
        if dtype != F32:
            ap = ap.bitcast(dtype)
        if len(shape) == 3:
            ap = ap.rearrange("p (a b) -> p a b", a=shape[1], b=shape[2])
        elif len(shape) == 4:
            ap = ap.rearrange("p (a b c) -> p a b c", a=shape[1], b=shape[2], c=shape[3])
        return ap


def build_program(stage="all", debug_names=()):
    import contextlib
    nc = bass.Bass("TRN2", target_bir_lowering=False)
    P = Prog(nc)
    stack = contextlib.ExitStack()

    def din(name, shape, dt=F32):
        return nc.dram_tensor(name, list(shape), dt, kind="ExternalInput").ap()

    def dout(name, shape, dt=F32):
        return nc.dram_tensor(name, list(shape), dt, kind="ExternalOutput").ap()

    xin = din("xin", [NTOK, D])
    condT = din("condT", [128, KC, 2])
    ptab = din("ptab", [128, PT_COLS])
    cst = din("cst", [128, CST_COLS])
    ada_w = din("ada_w", [2, D, 9 * D])
    if stage in ("all", "ffn0"):
        w_gu = din("ffn_w_gu", [2, 2, D, 2 * DFF])
        w_dn = din("ffn_w_down", [2, 2, DFF, D])
    yout = dout("yout", [NTOK, D])
    even_w_in = din("even_w_in", [1, D, EVEN_IN])
    even_w_out = din("even_w_out", [1, D, D])
    rope = din("rope", [128, 2, 1024])
    bmask = din("bmask", [128, 9 * 128])
    odd_w_in = din("odd_w_in", [1, D, ODD_IN])
    odd_w_out = din("odd_w_out", [1, D, D])
    wgpad = din("wgpad", [33, 2, 512])
    state_gla = din("state_gla", [2, 4, 128, 256])
    nsg = dout("nsg", [4, 2, 4, 128, 256])
    cache_k = din("cache_k", [512, 2, 128])
    cache_v = din("cache_v", [512, 2, 128])
    state_delta = din("state_delta", [2, 4, 128, 128])
    nck = dout("nck", [1024, 256])
    ncv = dout("ncv", [1024, 256])
    nsd = dout("nsd", [4, 2, 4, 128, 128])

    dbg = {}
    xT = stack.enter_context(nc.sbuf_tensor("xT", [128, KC, NTOK], F32))
    ptab_sb = stack.enter_context(nc.sbuf_tensor("ptab_sb", [128, PT_COLS], F32))
    cst_sb = stack.enter_context(nc.sbuf_tensor("cst_sb", [128, CST_COLS], F32))
    cstb = stack.enter_context(nc.sbuf_tensor("cstb", [128, CSTB_COLS], BF16))
    modT = stack.enter_context(nc.sbuf_tensor("modT", [128, 2, 72, 2], F32))
    modA = stack.enter_context(nc.sbuf_tensor("modA", [128, 2, 3, KC, 2], F32))
    modG = stack.enter_context(nc.sbuf_tensor("modG", [128, 2, 3, KC, 2], F32))
    scT = stack.enter_context(nc.sbuf_tensor("scT", [128, KC, 2], BF16))
    condsb = stack.enter_context(nc.sbuf_tensor("condsb", [128, KC, 2], F32))
    gates = stack.enter_context(nc.sbuf_tensor("gates", [128, 8, 8, 16], F32))
    epsT = stack.enter_context(nc.sbuf_tensor("epsT", [128, 2], F32))
    rstd = stack.enter_context(nc.sbuf_tensor("rstd", [128, 2, TT], F32))
    ar = Arena(nc, "arena", 126976, stack)
    ps = stack.enter_context(nc.psum_tensor("ps", [128, 8, 512], F32))

    def bank(b):
        return ps[:, b, :]

    identF = cst_sb[:, C_IDENT:C_IDENT + 128]
    identB = cstb[:, 0:128]
    onesP = cstb[:, 128:256]

    P.dma(ptab_sb[:, :], ptab[:, :], q="sp")
    P.dma(cst_sb[:, :], cst[:, :], q="sp")
    P.dma(condsb[:, :, :], condT[:, :, :], q="sp")
    P.copy(identB, identF, eng="dve")
    P.memset(onesP, 1.0, eng="dve")
    P.memset(epsT[:, :], EPS, eng="dve")
    P.memset(gates[:, :, :, :], 0.0, eng="pool")

    STG = 32768
    for i in range(16):
        stg = ar.view(STG + (i % 2) * 4096, [128, D], F32)
        P.dma(stg, xin[i * 128:(i + 1) * 128, :], q="sp" if i % 2 == 0 else "act")
        for half in range(2):
            b = (i * 2 + half) % 4
            for kk in range(4):
                kc = half * 4 + kk
                P.tr(ps[:, b, kk * 128:(kk + 1) * 128], stg[:, kc * 128:(kc + 1) * 128], identF)
            src = ps[:, b, :].rearrange("p (a t) -> p a t", a=4, t=128)
            dst = xT[:, half * 4:half * 4 + 4, i * 128:(i + 1) * 128]
            if half == 0:
                P.copy(dst, src, eng="dve")
            else:
                P.copy(dst, src, eng="act")

    P.act(scT[:, :, :], condsb[:, :, :], AF.Silu)
    ADAW = 40960

    def ada_finish(l, s):
        ng = ptab_sb[:, PT_NORMG + (l * 3 + s) * 8: PT_NORMG + (l * 3 + s + 1) * 8]
        sc = modT[:, l, (3 * s + 1) * 8:(3 * s + 2) * 8, :]
        P.stt(modA[:, l, s, :, :], sc, 1.0, ng.unsqueeze(2).broadcast_to([128, KC, 2]), ALU.add, ALU.mult)
        gt = modT[:, l, (3 * s + 2) * 8:(3 * s + 3) * 8, :]
        P.ts(modG[:, l, s, :, :], gt, 0.5 if s != 1 else 1.0, ALU.mult)

    for i in range(3):
        wb = ar.view(ADAW + (i % 3) * 16384, [128, KC, 1024], BF16)
        src = ada_w[0].rearrange("(kc p) n -> p kc n", p=128)[:, :, i * 1024:(i + 1) * 1024]
        P.dma(wb, src, q="pool")
        for n in range(8):
            j = i * 8 + n
            for kc in range(KC):
                P.mm(ps[:, 4, 2 * j:2 * j + 2], wb[:, kc, n * 128:(n + 1) * 128], scT[:, kc, :],
                     start=(kc == 0), stop=(kc == KC - 1))
    P.tt(modT[:, 0, 0:24, :], ps[:, 4, 0:48].rearrange("p (j c) -> p j c", j=24, c=2),
         ptab_sb[:, PT_ADAB:PT_ADAB + 24].unsqueeze(2).broadcast_to([128, 24, 2]), ALU.add)
    ada_finish(0, 0)
    ada_tasks = [(0, i, q4) for i in range(3, 9) for q4 in range(4)] + \
                [(1, i, q4) for i in range(9) for q4 in range(4)]
    ada_cnt = [0, 0, 0]

    ada_pending = []

    def ada_load():
        l, i, q4 = ada_tasks.pop(0)
        wb = ar.view(TMP + (ada_cnt[0] % 2) * 4096, [128, KC, 256], BF16)
        ada_cnt[0] += 1
        c0 = i * 1024 + q4 * 256
        P.dma(wb, ada_w[l].rearrange("(kc p) n -> p kc n", p=128)[:, :, c0:c0 + 256], q="pool")
        ada_pending.append((l, i, q4, wb))

    def ada_compute():
        l, i, q4, wb = ada_pending.pop(0)
        bank_b = 6 + (ada_cnt[1] % 2)
        ada_cnt[1] += 1
        for nn in range(2):
            for kc in range(KC):
                P.mm(ps[:, bank_b, 2 * nn:2 * nn + 2], wb[:, kc, nn * 128:(nn + 1) * 128], scT[:, kc, :],
                     start=(kc == 0), stop=(kc == KC - 1))
        j0 = i * 8 + q4 * 2
        P.tt(modT[:, l, j0:j0 + 2, :], ps[:, bank_b, 0:4].rearrange("p (j c) -> p j c", j=2, c=2),
             ptab_sb[:, PT_ADAB + l * 72 + j0:PT_ADAB + l * 72 + j0 + 2].unsqueeze(2).broadcast_to([128, 2, 2]),
             ALU.add)
        if q4 == 3 and i % 3 == 2:
            ada_finish(l, i // 3)

    def ada_tick():
        ada_cnt[2] += 1
        if ada_cnt[2] % 3 != 0:
            return
        if len(ada_pending) == 2 or (ada_pending and not ada_tasks):
            ada_compute()
        if ada_tasks and len(ada_pending) < 2:
            ada_load()

    def ada_flush():
        while ada_tasks or ada_pending:
            if ada_tasks and len(ada_pending) < 2:
                ada_load()
            else:
                ada_compute()

    HT = 0
    FW = 32768
    ACTB = FW + 49152
    SG = ACTB + 32768
    TMP = SG + 4096
    SQ = TMP + 4096
    assert SQ + 4096 <= ar.nbytes, SQ + 4096
    hT = ar.view(HT, [128, KC, NTOK], BF16)

    def rms_tile(tt, bank0=6):
        b = bank0 + (tt % 2)
        for kc in range(KC):
            sq = ar.view(SQ + ((tt * KC + kc) % 4) * 1024, [128, TT], BF16)
            P.act(sq, xT[:, kc, tt * TT:(tt + 1) * TT], AF.Square)
            P.mm(bank(b), onesP, sq, start=(kc == 0), stop=(kc == KC - 1))
        rs = rstd[:, tt % 2, :]
        P.act(rs, bank(b), AF.Ln, bias=epsT[:, 0:1], scale=1.0 / 1024.0)
        P.act(rs, rs, AF.Exp, scale=-0.5)
        return rs

    def modnorm_tile(l, s, tt):
        rs = rms_tile(tt)
        c = 0 if tt < 2 else 1
        for kc in range(KC):
            tmp = ar.view(TMP + ((tt * KC + kc) % 2) * 2048, [128, TT], F32)
            P.stt(tmp, xT[:, kc, tt * TT:(tt + 1) * TT], modA[:, l, s, kc, c:c + 1],
                  rs, ALU.mult, ALU.mult)
            P.act(hT[:, kc, tt * TT:(tt + 1) * TT], tmp, AF.Identity,
                  bias=modT[:, l, 3 * s * 8 + kc, c:c + 1])

    pre_normed = [None]

    def modnorm(l, s):
        if pre_normed[0] == (l, s):
            pre_normed[0] = None
            return
        for tt in range(NTT):
            modnorm_tile(l, s, tt)

    GROUPS = [(0, 4), (4, 8), (8, 12), (12, 16), (16, 19), (19, 22)]

    def ffn(l, i, after_tile=None):
        s = 0 if i == 0 else 2
        modnorm(l, s)
        wgu = w_gu[l, i].rearrange("(kc p) n -> p kc n", p=128)
        wdn = w_dn[l, i].rearrange("(g p) n -> p g n", p=128)

        def load(g):
            j0, j1 = GROUPS[g]
            G = j1 - j0
            base = FW + (g % 2) * 24576
            wg = ar.view(base, [128, KC, 512], BF16)
            wu = ar.view(base + 8192, [128, KC, 512], BF16)
            wd = ar.view(base + 16384, [128, 4, 1024], BF16)
            P.dma(wg[:, :, 0:G * 128], wgu[:, :, j0 * 128:j1 * 128], q="pool")
            P.dma(wu[:, :, 0:G * 128], wgu[:, :, DFF + j0 * 128:DFF + j1 * 128], q="pool")
            P.dma(wd[:, 0:G, :], wdn[:, j0:j1, :], q="pool")
            return wg, wu, wd

        pair = [0]

        def gu(g, W):
            j0, j1 = GROUPS[g]
            wg, wu, _ = W
            ab = ar.view(ACTB + (g % 2) * 16384, [128, 4, NTOK], BF16)
            for jj in range(j1 - j0):
                for tt in range(NTT):
                    pb = (pair[0] % 2) * 2
                    pair[0] += 1
                    rhs = None
                    for kc in range(KC):
                        P.mm(bank(pb), wg[:, kc, jj * 128:(jj + 1) * 128], hT[:, kc, tt * TT:(tt + 1) * TT],
                             start=(kc == 0), stop=(kc == KC - 1))
                    for kc in range(KC):
                        P.mm(bank(pb + 1), wu[:, kc, jj * 128:(jj + 1) * 128], hT[:, kc, tt * TT:(tt + 1) * TT],
                             start=(kc == 0), stop=(kc == KC - 1))
                    sg = ar.view(SG + (pair[0] % 2) * 2048, [128, TT], F32)
                    P.act(sg, bank(pb), AF.Silu)
                    P.tt(ab[:, jj, tt * TT:(tt + 1) * TT], sg, bank(pb + 1), ALU.mult)
                    ada_tick()

        ycnt = [0]

        def down(g, W, tile_major=False):
            j0, j1 = GROUPS[g]
            _, _, wd = W
            ab = ar.view(ACTB + (g % 2) * 16384, [128, 4, NTOK], BF16)
            order = [(n, tt) for n in range(KC) for tt in range(NTT)]
            if tile_major:
                order = [(n, tt) for tt in range(NTT) for n in range(KC)]
            for (n, tt) in order:
                if True:
                    c = 0 if tt < 2 else 1
                    yb = 4 + (ycnt[0] % 4)
                    ycnt[0] += 1
                    for jj in range(j1 - j0):
                        P.mm(bank(yb), wd[:, jj, n * 128:(n + 1) * 128], ab[:, jj, tt * TT:(tt + 1) * TT],
                             start=(jj == 0), stop=(jj == j1 - j0 - 1))
                    xs = xT[:, n, tt * TT:(tt + 1) * TT]
                    P.stt(xs, bank(yb), modG[:, l, s, n, c:c + 1], xs, ALU.mult, ALU.add)
                    if tile_major and n == KC - 1 and after_tile is not None:
                        after_tile(tt)

        W = {}
        W[0] = load(0)
        W[1] = load(1)
        gu(0, W[0])
        for g in range(len(GROUPS)):
            if g + 1 < len(GROUPS):
                gu(g + 1, W[g + 1])
            last = (g == len(GROUPS) - 1)
            if last:
                while ada_pending:
                    ada_compute()
            down(g, W[g], tile_major=last)
            if g + 2 < len(GROUPS):
                W[g + 2] = load(g + 2)
        while ada_pending:
            ada_compute()
        if (l, i) == (0, 1):
            ada_flush()

    def final_tile(tt):
        fg = ptab_sb[:, PT_FINALG:PT_FINALG + 8]
        YS = ACTB
        rs = rms_tile(tt)
        for i in range(4 * tt, 4 * tt + 4):
            yt = ar.view(YS + (i % 2) * 4096, [128, KC, 128], F32)
            for kc in range(KC):
                P.stt(yt[:, kc, :], xT[:, kc, i * 128:(i + 1) * 128], fg[:, kc:kc + 1],
                      rs[:, (i % 4) * 128:(i % 4 + 1) * 128], ALU.mult, ALU.mult)
            st = ar.view(YS + 8192 + (i % 2) * 4096, [128, D], F32)
            for half in range(2):
                b = (i * 2 + half) % 4
                for kk in range(4):
                    kc = half * 4 + kk
                    P.tr(ps[:, b, kk * 128:(kk + 1) * 128], yt[:, kc, :], identF)
                if half == 0:
                    P.copy(st[:, 0:512], bank(b), eng="dve")
                else:
                    P.copy(st[:, 512:1024], bank(b), eng="act")
            P.dma(yout[i * 128:(i + 1) * 128, :], st, q="sp", is_output=True)

    def final_out():
        for tt in range(NTT):
            final_tile(tt)

    NEG = -30000.0
    Uf = cst_sb[:, C_UF:C_UF + 128]
    Ub = cst_sb[:, C_UB:C_UB + 128]
    sel127 = cst_sb[:, C_SEL127:C_SEL127 + 128]
    sel0 = cst_sb[:, C_SEL0:C_SEL0 + 128]
    onesF = cst_sb[:, C_ONES:C_ONES + 128]
    offdiag = cst_sb[:, C_OFFD:C_OFFD + 128]
    MLf = cst_sb[:, C_ML:C_ML + 128]
    MUf = cst_sb[:, C_MU:C_MU + 128]
    Rm = cst_sb[:, C_RM:C_RM + 128]
    MLb = cstb[:, 256:384]
    MUb = cstb[:, 384:512]
    P.dma(cstb[:, 512:512 + 9 * 128], bmask[:, :], q="pool")
    P.copy(MLb, MLf, eng="dve")
    P.copy(MUb, MUf, eng="dve")

    def bc_h(m):
        return m.unsqueeze(1).broadcast_to([128, 4, 128])

    def bc_i(v):
        return v.unsqueeze(2).broadcast_to([128, 4, 128])

    def b4(b):
        return ps[:, b, :].rearrange("p (h t) -> p h t", h=4, t=128)

    def bbf(b):
        return ps[:, b, :].bitcast(BF16)

    nbc = [0]

    def nb():
        b = nbc[0] % 8
        nbc[0] += 1
        return b

    def nbk(k):
        c = (nbc[0] + k - 1) // k * k
        nbc[0] = c + k
        return c % 8

    def dump(name, ap, shape, dt):
        d = dout(name, shape, dt)
        P.dma(d, ap, q="sp", is_output=True)
        dbg[name] = (shape, dt)

    QN, KN, VS, GS = 32768, 40960, 49152, 57344
    QPL, QRO, KFM, VTM, KCT, VC = 65536, 73728, 81920, 86016, 90112, 92160
    WP = 94208
    SCR = 110592
    OACC = 65536
    TST = 81920
    SCN = 98304
    SHR = 114688
    STF = 120832
    STB = 124928

    def mixer_even():
        l = 0
        modnorm(l, 1)
        w_in = even_w_in[0].rearrange("(kc p) n -> p kc n", p=128)
        w_out = even_w_out[0].rearrange("(kc p) n -> p kc n", p=128)
        cw = ptab_sb[:, PT_CONV:PT_CONV + 60].rearrange("p (c j) -> p c j", c=12, j=5)
        sink_bc = ptab_sb[:, PT_SINK:PT_SINK + 4]
        wpc = [0]

        def load_piece(c0, c1):
            wp = ar.view(WP + (wpc[0] % 2) * 8192, [128, KC, 512], BF16)
            wpc[0] += 1
            P.dma(wp[:, :, 0:c1 - c0], w_in[:, :, c0:c1], q="pool")
            return wp

        for grp in range(2):
            T0 = grp * 1024
            nseq, L = (4, 256) if grp == 0 else (1, 1024)
            qn = ar.view(QN, [128, 4, 1024], BF16)
            kn = ar.view(KN, [128, 4, 1024], BF16)
            vS = ar.view(VS, [128, 4, 1024], BF16)
            gS = ar.view(GS, [128, 4, 1024], BF16)
            qpl = ar.view(QPL, [128, 4, 1024], BF16)
            qro = ar.view(QRO, [128, 4, 1024], BF16)
            kfm = ar.view(KFM, [128, 2, 1024], BF16)
            vtm = ar.view(VTM, [128, 8, 256], BF16)
            kcT = ar.view(KCT, [128, 2, 512], BF16)
            vc = ar.view(VC, [128, 4, 256], BF16)
            mixT = hT[:, :, T0:T0 + 1024]

            def proj_fm(wp, cc):
                b = nbk(2)
                for tt in range(2):
                    for kc in range(KC):
                        P.mm(bank(b + tt), wp[:, kc, cc * 128:(cc + 1) * 128],
                             hT[:, kc, T0 + tt * TT:T0 + (tt + 1) * TT], start=(kc == 0), stop=(kc == KC - 1))
                return ps[:, b:b + 2, :].rearrange("p a t -> p (a t)")

            SCALE = 128.0 ** -0.5
            if grp == 1:
                ropeT = ar.view(SCR, [128, 2, 1024], F32)
                P.dma(ropeT, rope[:, :, :], q="sp")
                kst = ar.view(SCR + 8192, [128, 4, 256], BF16)
                P.dma(kst, cache_k.rearrange("(kt p) g d -> p kt (g d)", p=128), q="pool")
                P.dma(vc, cache_v.rearrange("(kt p) g d -> p kt (g d)", p=128), q="pool")
                for g in range(2):
                    b = nb()
                    for kt in range(4):
                        P.tr(bbf(b)[:, kt * 128:(kt + 1) * 128], kst[:, kt, g * 128:(g + 1) * 128], identB)
                    P.copy(kcT[:, g, :], bbf(b)[:, 0:512], eng="dve")

            def rope_apply(dst_bf, xf):
                t1 = ar.view(SCR + 12288, [128, 1024], F32)
                for tt in range(2):
                    b = nb()
                    P.mm(bank(b), Rm, xf[:, tt * TT:(tt + 1) * TT])
                    P.tt(t1[:, tt * TT:(tt + 1) * TT], bank(b), ropeT[:, 1, tt * TT:(tt + 1) * TT], ALU.mult)
                P.tt(xf, xf, ropeT[:, 0, :], ALU.mult, eng="pool")
                P.tt(dst_bf, t1, xf, ALU.add)

            wp = load_piece(2064, 2576)
            for h in range(4):
                pp = proj_fm(wp, h)
                P.act(qpl[:, h, :], pp, AF.Copy, scale=SCALE)
                if grp == 1:
                    xf = ar.view(SCR + 8192, [128, 1024], F32)
                    P.ts(xf, pp, SCALE, ALU.mult)
                    rope_apply(qro[:, h, :], xf)
            if "stop_ip1" in debug_names:
                return
            wp = load_piece(2576, 3088)
            for g in range(2):
                pp = proj_fm(wp, g)
                if grp == 0:
                    P.act(kfm[:, g, :], pp, AF.Copy)
                else:
                    xf = ar.view(SCR + 8192, [128, 1024], F32)
                    P.act(xf, pp, AF.Copy)
                    rope_apply(kfm[:, g, :], xf)
            if "stop_ip1b" in debug_names:
                return
            for i in range(8):
                b = nb()
                for kc in range(KC):
                    P.mm(bank(b), hT[:, kc, T0 + i * 128:T0 + (i + 1) * 128], wp[:, kc, 0:512],
                         start=(kc == 0), stop=(kc == KC - 1))
                if grp == 1:
                    P.act(vtm[:, i, :], ps[:, b, 256:512], AF.Copy)
                else:
                    st = ar.view(SCR + (i % 2) * 2048, [128, 512], F32)
                    P.copy(st, bank(b), eng="dve")
                    P.act(vtm[:, i, :], st[:, 256:512], AF.Copy)
                    P.dma(nck[i * 128:(i + 1) * 128, :], st[:, 0:256], q="sp", is_output=True)
                    P.dma(ncv[i * 128:(i + 1) * 128, :], st[:, 256:512], q="act", is_output=True)
            if "stop_ip2" in debug_names:
                return
            wpab = load_piece(2048, 2064)
            abT = gates[:, 0, :, :]
            for i in range(8):
                b = nb()
                for kc in range(KC):
                    P.mm(ps[:, b, 0:16], hT[:, kc, T0 + i * 128:T0 + (i + 1) * 128], wpab[:, kc, 0:16],
                         start=(kc == 0), stop=(kc == KC - 1))
                P.copy(abT[:, i, :], ps[:, b, 0:16], eng="dve")

            if "stop_ip3" in debug_names:
                return
            Lp = L + 4
            xpbs = [ar.view(SCR + k_ * 2080, [128, nseq, Lp], BF16) for k_ in range(2)]
            dgs = [ar.view(SCR + 4160 + k_ * 1280, [128, 5, 128], BF16) for k_ in range(2)]
            qss = [ar.view(SCR + 6720 + k_ * 4096, [128, 1024], F32) for k_ in range(2)]
            sqs = [gates[:, 4 + 2 * k_:6 + 2 * k_, :, :].rearrange("p s a b -> p (s a b)").bitcast(BF16) for k_ in range(2)]
            for k_ in range(2):
                P.memset(ar.view(SCR + k_ * 2080, [128, 1040], BF16), 0.0, eng="pool")

            def conv_chunk(c, wp, st):
                pc, hh = c // 4, c % 4
                xpb, dg, qs, sq = xpbs[st], dgs[st], qss[st], sqs[st]
                pp = proj_fm(wp, hh)
                yield
                P.act(xpb[:, :, 2:2 + L], pp.rearrange("p (s t) -> p s t", s=nseq, t=L), AF.Copy)
                for j in range(5):
                    P.ts(dg[:, j, :], identB, cw[:, c, j:j + 1], ALU.mult)
                yield
                bc = nbk(2)
                if nseq == 4:
                    for s2 in range(4):
                        for j in range(5):
                            P.mm(ps[:, bc + s2 // 2, (s2 % 2) * 256:(s2 % 2 + 1) * 256], dg[:, j, :],
                                 xpb[:, s2, j:j + 256], start=(j == 0), stop=(j == 4))
                else:
                    for tt in range(2):
                        for j in range(5):
                            P.mm(bank(bc + tt), dg[:, j, :], xpb[:, 0, tt * TT + j:tt * TT + j + TT],
                                 start=(j == 0), stop=(j == 4))
                accf = ps[:, bc:bc + 2, :].rearrange("p a t -> p (a t)")
                yield
                if pc == 2:
                    P.act(vS[:, hh, :], accf, AF.Silu)
                    yield
                    return
                P.act(qs, accf, AF.Silu)
                dst = (qn if pc == 0 else kn)[:, hh, :]
                for tt in range(2):
                    P.act(sq, qs[:, tt * TT:(tt + 1) * TT], AF.Square)
                    b = nb()
                    P.mm(bank(b), onesP, sq)
                    yield
                    rs = rstd[:, st, :]
                    P.act(rs, bank(b), AF.Ln, bias=epsT[:, 0:1])
                    P.act(rs, rs, AF.Exp, scale=-0.5)
                    P.stt(dst[:, tt * TT:(tt + 1) * TT], qs[:, tt * TT:(tt + 1) * TT],
                          SCALE if pc == 0 else 1.0, rs, ALU.mult, ALU.mult)
                    yield

            wps = {}
            active = []
            nxt_c = 0
            free_st = [0, 1]
            while nxt_c < 12 or active:
                while nxt_c < 12 and len(active) < 2:
                    pc_ = nxt_c // 4
                    if pc_ not in wps:
                        wps[pc_] = load_piece(pc_ * 512, (pc_ + 1) * 512)
                    st_ = free_st.pop(0)
                    active.append((conv_chunk(nxt_c, wps[pc_], st_), st_))
                    nxt_c += 1
                for item in list(active):
                    try:
                        next(item[0])
                    except StopIteration:
                        active.remove(item)
                        free_st.append(item[1])
            wp = load_piece(1536, 2048)
            for hh in range(4):
                pp = proj_fm(wp, hh)
                P.act(gS[:, hh, :], pp, AF.Silu)
            if "inproj" in debug_names and grp == DBG_GRP:
                dump("d_qn", qn, [128, 4, 1024], BF16)
                dump("d_kn", kn, [128, 4, 1024], BF16)
                dump("d_vS", vS, [128, 4, 1024], BF16)
                dump("d_gS", gS, [128, 4, 1024], BF16)
                dump("d_qpl", qpl, [128, 4, 1024], BF16)
                dump("d_qro", qro, [128, 4, 1024], BF16)
                dump("d_kfm", kfm, [128, 2, 1024], BF16)
                dump("d_vtm", vtm, [128, 8, 256], BF16)
                dump("d_ab", abT, [128, 8, 16], F32)

            if "stop_inproj" in debug_names:
                return
            ATT = WP
            if grp == 0:
                for s_ in range(4):
                    for qb in range(2):
                        tq = s_ * 256 + qb * 128
                        Pb = ar.view(ATT + ((s_ * 2 + qb) % 2) * 2048, [128, 4, 256], BF16)
                        PTs = ar.view(ATT + 4096 + ((s_ * 2 + qb) % 2) * 2048, [128, 8, 128], BF16)
                        stt_ = ar.view(ATT + 8192 + ((s_ * 2 + qb) % 2) * 256, [128, 16], F32)
                        on = ar.view(ATT + 8704 + ((s_ * 2 + qb) % 2) * 1024, [128, 4, 128], BF16)
                        bS = nbk(2)
                        P.memset(stt_, 0.0, eng="pool")
                        for h in range(4):
                            P.mm(ps[:, bS + h // 2, (h % 2) * 256:(h % 2 + 1) * 256],
                                 qpl[:, h, tq:tq + 128], kfm[:, h // 2, s_ * 256:(s_ + 1) * 256])
                        S4 = ps[:, bS:bS + 2, :].rearrange("p a (h k) -> p (a h) k", h=2, k=256)
                        mx = stt_[:, 0:4]
                        negm = stt_[:, 4:8]
                        rsum = stt_[:, 8:12]
                        es = stt_[:, 12:16]
                        P.reduce(mx, S4, ALU.max)
                        P.tt(mx, mx, sink_bc, ALU.max)
                        P.ts(negm, mx, -1.0, ALU.mult)
                        for h in range(4):
                            P.act(Pb[:, h, :], S4[:, h, :], AF.Exp, bias=negm[:, h:h + 1], accum=rsum[:, h:h + 1])
                        P.tt(es, sink_bc, negm, ALU.add)
                        P.act(es, es, AF.Exp)
                        P.tt(rsum, rsum, es, ALU.add)
                        P.recip(rsum, rsum)
                        bT = nb()
                        for h in range(4):
                            for kt in range(2):
                                P.tr(bbf(bT)[:, (h * 2 + kt) * 128:(h * 2 + kt + 1) * 128],
                                     Pb[:, h, kt * 128:(kt + 1) * 128], identB)
                        P.copy(PTs.rearrange("p a t -> p (a t)"), bbf(bT)[:, 0:1024], eng="act")
                        bO = nb()
                        for h in range(4):
                            for kt in range(2):
                                P.mm(ps[:, bO, h * 128:(h + 1) * 128], PTs[:, h * 2 + kt, :],
                                     vtm[:, s_ * 2 + kt, (h // 2) * 128:(h // 2 + 1) * 128],
                                     start=(kt == 0), stop=(kt == 1))
                        P.tt(on, b4(bO), bc_i(rsum), ALU.mult)
                        bT2 = nb()
                        for h in range(4):
                            P.tr(bbf(bT2)[:, h * 128:(h + 1) * 128], on[:, h, :], identB)
                        P.copy(mixT[:, 4:8, tq:tq + 128],
                               bbf(bT2)[:, 0:512].rearrange("p (h t) -> p h t", h=4, t=128), eng="dve")
            else:
                for qb in range(8):
                    tq = qb * 128
                    blks = [k for k in (qb - 1, qb, qb + 1) if 0 <= k < 8]
                    nl = len(blks)
                    W = 512 + nl * 128
                    on = ar.view(ATT + 16384 + (qb % 2) * 1024, [128, 4, 128], BF16)
                    for hp in range(2):
                        it = qb * 2 + hp
                        Pb = ar.view(ATT + (it % 2) * 4096, [128, 2, 1024], BF16)
                        PTs = ar.view(ATT + 8192 + (it % 2) * 4096, [128, 2, 1024], BF16)
                        stt_ = ar.view(ATT + 18432 + (it % 2) * 256, [128, 16], F32)
                        mx = stt_[:, 0:2]
                        negm = stt_[:, 2:4]
                        rsum = stt_[:, 4:6]
                        es = stt_[:, 6:8]
                        bS = nbk(4)
                        P.memset(stt_, 0.0, eng="pool")
                        for hh in range(2):
                            h = hp * 2 + hh
                            g = hp
                            P.mm(bank(bS + 2 * hh), qpl[:, h, tq:tq + 128], kcT[:, g, :])
                            k0 = blks[0] * 128
                            has_mask = (blks[0] == qb - 1) or (blks[-1] == qb + 1)
                            P.mm(ps[:, bS + 2 * hh + 1, 0:nl * 128], qro[:, h, tq:tq + 128],
                                 kfm[:, g, k0:k0 + nl * 128], start=True, stop=not has_mask)
                            nm = (1 if blks[0] == qb - 1 else 0) + (1 if blks[-1] == qb + 1 else 0)
                            cnt = 0
                            for bi, k in enumerate(blks):
                                if k == qb - 1 or k == qb + 1:
                                    cnt += 1
                                    P.mm(ps[:, bS + 2 * hh + 1, bi * 128:(bi + 1) * 128], identB,
                                         MUb if k == qb - 1 else MLb, start=False, stop=(cnt == nm))
                        S2 = ps[:, bS:bS + 4, :].rearrange("p (h a) t -> p h (a t)", h=2, a=2)[:, :, 0:W]
                        P.reduce(mx, S2, ALU.max)
                        P.tt(mx, mx, sink_bc[:, hp * 2:hp * 2 + 2], ALU.max)
                        P.ts(negm, mx, -1.0, ALU.mult)
                        for hh in range(2):
                            P.act(Pb[:, hh, 0:W], S2[:, hh, :], AF.Exp, bias=negm[:, hh:hh + 1],
                                  accum=rsum[:, hh:hh + 1])
                        P.tt(es, sink_bc[:, hp * 2:hp * 2 + 2], negm, ALU.add)
                        P.act(es, es, AF.Exp)
                        P.tt(rsum, rsum, es, ALU.add)
                        P.recip(rsum, rsum)
                        nblk = 4 + nl
                        for hh in range(2):
                            bT = nb()
                            for bi in range(nblk):
                                P.tr(bbf(bT)[:, bi * 128:(bi + 1) * 128], Pb[:, hh, bi * 128:(bi + 1) * 128], identB)
                            P.copy(PTs[:, hh, 0:W], bbf(bT)[:, 0:W], eng="act" if hh == 0 else "dve")
                        bO = nb()
                        for hh in range(2):
                            g = hp
                            for bi in range(nblk):
                                if bi < 4:
                                    rhs = vc[:, bi, g * 128:(g + 1) * 128]
                                else:
                                    rhs = vtm[:, blks[bi - 4], g * 128:(g + 1) * 128]
                                P.mm(ps[:, bO, hh * 128:(hh + 1) * 128], PTs[:, hh, bi * 128:(bi + 1) * 128], rhs,
                                     start=(bi == 0), stop=(bi == nblk - 1))
                        P.tt(on[:, hp * 2:hp * 2 + 2, :],
                             ps[:, bO, 0:256].rearrange("p (h t) -> p h t", h=2, t=128),
                             rsum.unsqueeze(2).broadcast_to([128, 2, 128]), ALU.mult)
                    bT2 = nb()
                    for h in range(4):
                        P.tr(bbf(bT2)[:, h * 128:(h + 1) * 128], on[:, h, :], identB)
                    P.copy(mixT[:, 4:8, tq:tq + 128],
                           bbf(bT2)[:, 0:512].rearrange("p (h t) -> p h t", h=4, t=128), eng="dve")
            if "attn" in debug_names and grp == DBG_GRP:
                dump("d_oatt", mixT[:, 4:8, :], [128, 4, 1024], BF16)

            if "stop_attn" in debug_names:
                return
            delta_net(grp, T0, nseq, L, qn, kn, vS, gS, mixT)
            if "delta" in debug_names and grp == DBG_GRP:
                dump("d_oa", mixT[:, 0:4, :], [128, 4, 1024], BF16)

            if "stop_delta" in debug_names:
                return
            wo = ar.view(WP, [128, KC, 1024], BF16)
            P.dma(wo[:, :, 0:512], w_out[:, :, 0:512], q="pool")
            P.dma(wo[:, :, 512:1024], w_out[:, :, 512:1024], q="pool")
            for n in range(KC):
                for tt in range(2):
                    b = nb()
                    for kc in range(KC):
                        P.mm(bank(b), wo[:, kc, n * 128:(n + 1) * 128], mixT[:, kc, tt * TT:(tt + 1) * TT],
                             start=(kc == 0), stop=(kc == KC - 1))
                    xs = xT[:, n, T0 + tt * TT:T0 + (tt + 1) * TT]
                    P.stt(xs, bank(b), modG[:, l, 1, n, grp:grp + 1], xs, ALU.mult, ALU.add)

    def delta_net(grp, T0, nseq, L, qn, kn, vS, gS, mixT):
        abT = gates[:, 0, :, :]
        gT = gates[:, 1, :, 0:8]
        beta = gates[:, 2, :, 0:8]
        gc = gates[:, 3, :, 0:8]
        glb = gates[:, 4, :, 0:8]
        egc = gates[:, 5, :, 0:8]
        kdf = gates[:, 6, :, 0:8]
        glast = gates[:, 7, :, 0:8]
        bege = gates[:, 1, :, 8:16]
        tmpA = gates[:, 2, :, 8:16]
        ngT = gates[:, 4, :, 8:16]
        tmpB = gates[:, 3, :, 8:16]
        alog_bc = ptab_sb[:, PT_ALOG:PT_ALOG + 8]
        dtb_bc = ptab_sb[:, PT_DTB:PT_DTB + 8]
        onorm = ptab_sb[:, PT_ONORME:PT_ONORME + 1]
        bc8 = lambda v: v.unsqueeze(1).broadcast_to([128, 8, 8])
        P.act(beta, abT[:, :, 0:8], AF.Exp, scale=-1.0)
        P.ts(beta, beta, 1.0, ALU.add)
        P.recip(beta, beta)
        P.tt(tmpA, abT[:, :, 8:16], bc8(dtb_bc), ALU.add)
        P.act(tmpB, tmpA, AF.Abs)
        P.act(tmpB, tmpB, AF.Exp, scale=-1.0)
        P.act(tmpB, tmpB, AF.Ln, bias=onesF[:, 0:1])
        P.ts(tmpA, tmpA, 0.0, ALU.max)
        P.tt(tmpA, tmpA, tmpB, ALU.add)
        P.act(tmpB[:, 0, :], alog_bc, AF.Exp)
        P.stt(gT, tmpA, -1.0, bc8(tmpB[:, 0, :]), ALU.mult, ALU.mult)
        P.ts(ngT, gT, -1.0, ALU.mult)
        g64 = gates[:, 1, :, :].rearrange("p a b -> p (a b)")
        b = nb()
        P.mm(ps[:, b, 0:128], Uf, g64)
        P.mm(ps[:, b, 128:256], Ub, g64)
        pv = ps[:, b, 0:256].rearrange("p (d a c) -> p d a c", d=2, a=8, c=16)
        P.copy(gc[:, :, 0:4], pv[:, 0, :, 0:4], eng="dve")
        P.copy(gc[:, :, 4:8], pv[:, 1, :, 4:8], eng="dve")
        gc64 = gates[:, 3, :, :].rearrange("p a b -> p (a b)")
        b = nb()
        P.mm(ps[:, b, 0:128], sel127, gc64)
        P.mm(ps[:, b, 128:256], sel0, gc64)
        pv = ps[:, b, 0:256].rearrange("p (d a c) -> p d a c", d=2, a=8, c=16)
        P.copy(glb[:, :, 0:4], pv[:, 0, :, 0:4], eng="dve")
        P.copy(glb[:, :, 4:8], pv[:, 1, :, 4:8], eng="dve")
        P.act(egc, gc, AF.Exp)
        P.tt(kdf, glb, gc, ALU.subtract)
        P.act(kdf, kdf, AF.Exp)
        P.act(glast, glb, AF.Exp)
        P.tt(bege, beta, egc, ALU.mult)

        if "gates" in debug_names and grp == DBG_GRP:
            dump("d_g", gT, [128, 8, 8], F32)
            dump("d_beta", beta, [128, 8, 8], F32)
            dump("d_gc", gc, [128, 8, 8], F32)
            dump("d_glb", glb, [128, 8, 8], F32)
        DB = 65536
        TSTS = [DB, DB + 12288]
        SCN2 = DB + 24576
        SHR2 = SCN2 + 16384
        STF2 = SHR2 + 8192
        STB2 = STF2 + 4096
        OACCA = STB2 + 2048
        assert OACCA + 4096 <= ar.nbytes
        Sf = [ar.view(STF2 + d * 2048, [128, 4, 128], F32) for d in range(2)]
        Sb = [ar.view(STB2 + d * 1024, [128, 4, 128], BF16) for d in range(2)]
        nch = L // 128
        if grp == 0:
            oaccA = ar.view(OACCA, [128, 4, 256], F32)
        else:
            oaccB = ar.t[:, 0:8192].rearrange("p (k t) -> p k t", k=8, t=1024)[:, :, 0:512].rearrange(
                "p (h a) t -> p h a t", h=4, a=2)

        def oslice(c):
            if grp == 0:
                return oaccA[:, :, c * 128:(c + 1) * 128]
            t0_ = c * 128
            return oaccB[:, :, t0_ // 512, t0_ % 512:t0_ % 512 + 128]

        def tv(off, dt):
            return ar.view(off, [128, 4, 128], dt)

        def mask_b(k_):
            return cstb[:, 512 + k_ * 128: 512 + (k_ + 1) * 128]

        def mm4(lh, rh):
            bb = nb()
            for h in range(4):
                P.mm(ps[:, bb, h * 128:(h + 1) * 128], lh[:, h, :], rh[:, h, :])
            return bb

        def step(s_, c, d, first_touch):
            T_ = TSTS[d]
            dec, decT = tv(T_, F32), tv(T_ + 2048, F32)
            G1, G2 = dec, decT
            Lb, LTb = tv(T_ + 4096, BF16), tv(T_ + 5120, BF16)
            Ab, Bb = tv(T_ + 6144, BF16), tv(T_ + 7168, BF16)
            Cb, Db, Eb, Tb = [tv(T_ + 8192 + 1024 * k_, BF16) for k_ in range(4)]
            B2, B3 = Cb, Db
            S_ = SCN2 + d * 8192
            Xb, qkT, QdT, Vb_, Kbe, kdec, negwT, vnew = [tv(S_ + 1024 * k_, BF16) for k_ in range(8)]
            H_ = SHR2 + d * 4096
            Ktm, Vtm, KKs, KQs = [tv(H_ + 1024 * k_, BF16) for k_ in range(4)]
            ti = s_ * nch + c
            tsl = slice(ti * 128, (ti + 1) * 128)
            dsl = slice(d * 4, d * 4 + 4)
            bK = nb()
            for h in range(4):
                P.tr(bbf(bK)[:, h * 128:(h + 1) * 128], kn[:, h, tsl], identB)
            P.copy(Ktm.rearrange("p h t -> p (h t)"), bbf(bK)[:, 0:512], eng="act")
            bV = nb()
            for h in range(4):
                P.tr(bbf(bV)[:, h * 128:(h + 1) * 128], vS[:, h, tsl], identB)
            P.copy(Vtm.rearrange("p h t -> p (h t)"), bbf(bV)[:, 0:512], eng="act")
            yield
            bKK = nb()
            for h in range(4):
                P.mm(ps[:, bKK, h * 128:(h + 1) * 128], kn[:, h, tsl], kn[:, h, tsl])
            P.copy(KKs.rearrange("p h t -> p (h t)"), bank(bKK), eng="act")
            bKQ = nb()
            for h in range(4):
                P.mm(ps[:, bKQ, h * 128:(h + 1) * 128], kn[:, h, tsl], qn[:, h, tsl])
            P.copy(KQs.rearrange("p h t -> p (h t)"), bank(bKQ), eng="act")
            yield
            U_ = Uf if d == 0 else Ub
            Mdec, MdecT = (MLf, MUf) if d == 0 else (MUf, MLf)
            for h in range(4):
                P.act(G1[:, h, :], onesF, AF.Copy, scale=gT[:, ti, d * 4 + h:d * 4 + h + 1])
            for h in range(4):
                P.act(G2[:, h, :], U_, AF.Copy, scale=ngT[:, ti, d * 4 + h:d * 4 + h + 1])
            bD = nb()
            P.mm(bank(bD), U_, G1.rearrange("p h t -> p (h t)"), start=True, stop=False)
            P.mm(bank(bD), onesF, G2.rearrange("p h t -> p (h t)"), start=False, stop=True)
            yield
            P.tt(dec, b4(bD), bc_h(Mdec), ALU.add)
            P.stt(decT, b4(bD), -1.0, bc_h(MdecT), ALU.mult, ALU.add)
            P.act(dec, dec, AF.Exp)
            P.act(decT, decT, AF.Exp)
            P.tt(B2, bc_h(identB), bc_i(beta[:, ti, dsl]), ALU.mult, eng="pool")
            P.tt(B3, bc_h(identB), bc_i(egc[:, ti, dsl]), ALU.mult, eng="pool")
            bR = nb()
            P.mm(bank(bR), onesP, B2.rearrange("p h t -> p (h t)"))
            bE = nb()
            P.mm(bank(bE), onesP, B3.rearrange("p h t -> p (h t)"))
            yield
            P.tt(Lb, KKs, dec, ALU.mult)
            P.tt(Lb, Lb, bc_i(beta[:, ti, dsl]), ALU.mult)
            P.tt(LTb, KKs, decT, ALU.mult, eng="pool")
            P.tt(LTb, LTb, b4(bR), ALU.mult)
            P.tt(qkT, KQs, decT, ALU.mult, eng="pool")
            P.tt(QdT, qn[:, :, tsl], b4(bE), ALU.mult)
            for h in range(4):
                hc = d * 4 + h
                P.act(Vb_[:, h, :], Vtm[:, h, :], AF.Copy, scale=beta[:, ti, hc:hc + 1])
                P.act(Kbe[:, h, :], Ktm[:, h, :], AF.Copy, scale=bege[:, ti, hc:hc + 1])
                P.act(kdec[:, h, :], Ktm[:, h, :], AF.Copy, scale=kdf[:, ti, hc:hc + 1])
            yield
            mi = (lambda k: k) if d == 0 else (lambda k: (k + 4) if 1 <= k <= 4 else (k - 4 if k >= 5 else k))
            P.tt(Ab, Lb, bc_h(mask_b(0)), ALU.mult)
            P.tt(Bb, LTb, bc_h(mask_b(0)), ALU.mult)
            P.stt(Tb, Ab, -1.0, bc_h(identF), ALU.mult, ALU.add)
            P.stt(Xb, Bb, -1.0, bc_h(identF), ALU.mult, ALU.add)
            yield
            b1 = mm4(Bb, Ab)
            b2 = mm4(Ab, Bb)
            P.copy(Cb, b4(b1), eng="act")
            P.copy(Db, b4(b2), eng="act")
            yield
            bx = mm4(Cb, Xb)
            bt = mm4(Xb, Cb)
            b3 = mm4(Db, Cb)
            P.tt(Xb, Xb, b4(bx), ALU.add)
            P.tt(Tb, Tb, b4(bt), ALU.add)
            P.copy(Eb, b4(b3), eng="act")
            yield
            bx = mm4(Eb, Xb)
            bt = mm4(Xb, Eb)
            P.tt(Xb, Xb, b4(bx), ALU.add)
            P.tt(Tb, Tb, b4(bt), ALU.add)
            yield
            for lv in range(1, 5):
                last = (lv == 4)
                P.tt(Ab, Lb, bc_h(mask_b(mi(lv))), ALU.mult, eng="pool")
                if not last:
                    P.tt(Bb, LTb, bc_h(mask_b(mi(lv + 4))), ALU.mult)
                b1 = mm4(Ab, Xb)
                if not last:
                    b2 = mm4(Bb, Tb)
                P.copy(Cb, b4(b1), eng="act")
                if not last:
                    P.copy(Db, b4(b2), eng="act")
                yield
                bx = mm4(Tb, Cb)
                if not last:
                    bt = mm4(Xb, Db)
                P.tt(Xb, Xb, b4(bx), ALU.subtract)
                if not last:
                    P.tt(Tb, Tb, b4(bt), ALU.subtract)
                yield
            if "step0" in debug_names and grp == DBG_GRP and s_ == 0 and (c, d) == DBG_STEP:
                dump("d_dec", dec, [128, 4, 128], F32)
                dump("d_decT", decT, [128, 4, 128], F32)
                dump("d_X", Xb, [128, 4, 128], BF16)
                dump("d_qkT", qkT, [128, 4, 128], BF16)
                dump("d_QdT", QdT, [128, 4, 128], BF16)
                dump("d_KKs", KKs, [128, 4, 128], BF16)
            bW = mm4(Kbe, Xb)
            P.act(negwT.rearrange("p h t -> p (h t)"), bank(bW), AF.Copy, scale=-1.0)
            yield
            bVn = nb()
            for h in range(4):
                P.mm(ps[:, bVn, h * 128:(h + 1) * 128], Xb[:, h, :], Vb_[:, h, :], start=True, stop=False)
                P.mm(ps[:, bVn, h * 128:(h + 1) * 128], negwT[:, h, :], Sb[d][:, h, :], start=False, stop=True)
            P.copy(vnew.rearrange("p h t -> p (h t)"), bank(bVn), eng="act")
            yield
            bO = nb()
            for h in range(4):
                P.mm(ps[:, bO, h * 128:(h + 1) * 128], Sb[d][:, h, :], QdT[:, h, :], start=True, stop=False)
                P.mm(ps[:, bO, h * 128:(h + 1) * 128], vnew[:, h, :], qkT[:, h, :], start=False, stop=True)
            bS_ = mm4(kdec, vnew)
            oa = oslice(c if grp == 1 else c)
            if first_touch[c]:
                P.copy(oa, b4(bO), eng="dve")
                first_touch[c] = False
            else:
                P.tt(oa, oa, b4(bO), ALU.add)
            P.tt(Sf[d], Sf[d], bc_i(glast[:, ti, dsl]), ALU.mult)
            P.tt(Sf[d], Sf[d], b4(bS_), ALU.add)
            P.copy(Sb[d], Sf[d], eng="act")
            yield

        for s_ in range(nseq):
            for d in range(2):
                if grp == 0:
                    P.memset(Sf[d], 0.0, eng="pool")
                else:
                    P.dma(Sf[d], state_delta[d].rearrange("h k v -> k h v"), q="sp")
                P.copy(Sb[d], Sf[d], eng="pool")
            first_touch = [True] * nch
            for k in range(nch):
                gens = [step(s_, k, 0, first_touch), step(s_, nch - 1 - k, 1, first_touch)]
                while gens:
                    for g_ in list(gens):
                        try:
                            next(g_)
                        except StopIteration:
                            gens.remove(g_)
            if grp == 0:
                for d in range(2):
                    P.dma(nsd[s_, d].rearrange("h k v -> k h v"), Sf[d], q="sp", is_output=True)
            for h in range(4):
                for t_ in range(0, L, 512):
                    w = min(512, L - t_)
                    sl = slice(s_ * L + t_, s_ * L + t_ + w)
                    src = oaccA[:, h, 0:w] if grp == 0 else oaccB[:, h, t_ // 512, 0:512]
                    sq = ar.view(TSTS[0], [128, 512], BF16)
                    P.act(sq[:, 0:w], src, AF.Square)
                    b = nb()
                    P.mm(ps[:, b, 0:w], onesP, sq[:, 0:w])
                    rs = rstd[:, 0, 0:w]
                    P.act(rs, ps[:, b, 0:w], AF.Ln, bias=epsT[:, 0:1], scale=1.0 / 128.0)
                    P.act(rs, rs, AF.Exp, scale=-0.5)
                    t_f = ar.view(TSTS[0] + 2048, [128, 512], F32)
                    P.stt(t_f[:, 0:w], src, onorm[:, 0:1], rs, ALU.mult, ALU.mult)
                    P.tt(mixT[:, h, sl], t_f[:, 0:w], gS[:, h, sl], ALU.mult)

    def mixer_odd():
        l = 1
        modnorm(l, 1)
        w_in = odd_w_in[0].rearrange("(kc p) n -> p kc n", p=128)
        w_out = odd_w_out[0].rearrange("(kc p) n -> p kc n", p=128)
        onorm2 = ptab_sb[:, PT_ONORMO:PT_ONORMO + 2]
        QT_, KT_, VT_, GS_, LRT_, OACC2 = 32768, 40960, 49152, 65536, 81920, 86016
        WPO = 102400
        ZB_, ZE_ = 102400, 104448
        EG_, ENG_ = ZE_, ZB_
        QTL_, KTL_, QTT_, KTT_, ATB_ = 110592, 111616, 112640, 113664, 114688
        SF_, SB_, WG_ = 115712, 119808, 121856
        SC = 128.0 ** -0.5
        wgp = ar.view(WG_, [128, 2, 512], BF16)
        P.dma(wgp[0:33, :, :], wgpad[:, :, :], q="pool")
        wpc = [0]

        def load_piece(c0, c1):
            wp = ar.view(WPO + (wpc[0] % 2) * 8192, [128, KC, 512], BF16)
            wpc[0] += 1
            P.dma(wp[:, :, 0:c1 - c0], w_in[:, :, c0:c1], q="pool")
            return wp

        qT = ar.view(QT_, [128, 8, 512], BF16)
        kT = ar.view(KT_, [128, 8, 512], BF16)
        vT = ar.view(VT_, [128, 8, 1024], BF16)
        gS = ar.view(GS_, [128, 8, 1024], BF16)
        lrT = ar.view(LRT_, [128, 1024], BF16)
        glt = gates[:, 0, 0, 0:4]
        for grp in range(2):
            T0 = grp * 1024
            nseq, L = (4, 256) if grp == 0 else (1, 1024)
            nch = L // 128
            mixT = hT[:, :, T0:T0 + 1024]
            wp = load_piece(3072, 3104)
            for tt in range(2):
                b = nb()
                for kc in range(KC):
                    P.mm(ps[0:32, b, :], wp[:, kc, 0:32], hT[:, kc, T0 + tt * TT:T0 + (tt + 1) * TT],
                         start=(kc == 0), stop=(kc == KC - 1))
                P.copy(lrT[0:32, tt * TT:(tt + 1) * TT], ps[0:32, b, :], eng="dve")
            P.memset(lrT[32:33, :], 1.0, eng="dve")
            for (c0, dst, col0, scl) in ((0, qT, 0, SC), (512, kT, 0, 1.0), (1024, vT, 0, 1.0), (1536, vT, 512, 1.0)):
                wp = load_piece(c0, c0 + 512)
                for i in range(8):
                    b = nb()
                    for kc in range(KC):
                        P.mm(bank(b), hT[:, kc, T0 + i * 128:T0 + (i + 1) * 128], wp[:, kc, 0:512],
                             start=(kc == 0), stop=(kc == KC - 1))
                    if i % 2 == 0:
                        P.act(dst[:, i, col0:col0 + 512], bank(b), AF.Copy, scale=scl)
                    else:
                        P.ts(dst[:, i, col0:col0 + 512], bank(b), scl, ALU.mult)
            for pc in range(2):
                wp = load_piece(2048 + pc * 512, 2560 + pc * 512)
                for cc in range(4):
                    b = nbk(2)
                    for tt in range(2):
                        for kc in range(KC):
                            P.mm(bank(b + tt), wp[:, kc, cc * 128:(cc + 1) * 128],
                                 hT[:, kc, T0 + tt * TT:T0 + (tt + 1) * TT], start=(kc == 0), stop=(kc == KC - 1))
                    P.act(gS[:, pc * 4 + cc, :], ps[:, b:b + 2, :].rearrange("p a t -> p (a t)"), AF.Silu)
            if "oinproj" in debug_names and grp == DBG_GRP:
                dump("d_qT", qT, [128, 8, 512], BF16)
                dump("d_kT", kT, [128, 8, 512], BF16)
                dump("d_vT", vT, [128, 8, 1024], BF16)
                dump("d_gS", gS, [128, 8, 1024], BF16)
                dump("d_lrT", lrT[0:33, :], [33, 1024], F32)

            zb = ar.view(ZB_, [128, 512], F32)
            ze = ar.view(ZE_, [128, 512], F32)
            eg = ar.view(EG_, [128, 512], F32)
            eng_ = ar.view(ENG_, [128, 512], F32)
            qtl = ar.view(QTL_, [128, 512], BF16)
            ktl = ar.view(KTL_, [128, 512], BF16)
            qtt = ar.view(QTT_, [128, 4, 128], BF16)
            ktt = ar.view(KTT_, [128, 4, 128], BF16)
            atb = ar.view(ATB_, [128, 4, 128], BF16)
            Sf = ar.view(SF_, [128, 4, 256], F32)
            Sb = ar.view(SB_, [128, 4, 256], BF16)
            if grp == 1:
                oacc_lo = ar.t[:, 0:8192].rearrange("p (k t) -> p k t", k=8, t=1024)[:, :, 0:512]
                oacc_hi = ar.view(OACC2, [128, 8, 512], F32)
            else:
                oaccA = ar.view(OACC2, [128, 8, 256], F32)

            def oslice(c, fc0, fc1):
                if grp == 0:
                    return oaccA[:, fc0:fc1, c * 128:(c + 1) * 128]
                if c < 4:
                    return oacc_lo[:, fc0:fc1, c * 128:(c + 1) * 128]
                return oacc_hi[:, fc0:fc1, (c - 4) * 128:(c - 3) * 128]

            bufsets = []
            for k_ in range(2):
                if k_ == 0:
                    offs = (QTL_, KTL_, QTT_, KTT_, ATB_)
                else:
                    offs = (106496, 107520, 108544, 109568, 125952)
                bufsets.append((ar.view(offs[0], [128, 512], BF16), ar.view(offs[1], [128, 512], BF16),
                                ar.view(offs[2], [128, 4, 128], BF16), ar.view(offs[3], [128, 4, 128], BF16),
                                ar.view(offs[4], [128, 4, 128], BF16), gates[:, 0, k_, 0:4]))

            rb_ = rstd[:, :, :].rearrange("p a t -> p (a t)").bitcast(BF16)
            egs = [rb_[:, k_ * 512:(k_ + 1) * 512] for k_ in range(2)]
            engs = [rb_[:, 1024 + k_ * 512:1024 + (k_ + 1) * 512] for k_ in range(2)]
            glts = [gates[:, 0, k_, 0:4] for k_ in range(3)]
            gsum = gates[:, 0, 3, 0:4]

            def prefixA(s_, d, c, ka):
                eg_b, eng_b, glt = egs[ka % 2], engs[ka % 2], glts[ka % 3]
                U_ = Uf if d == 0 else Ub
                ti = s_ * nch + c
                tsl = slice(ti * 128, (ti + 1) * 128)
                bz = nb()
                P.mm(bank(bz), lrT[0:33, tsl], wgp[0:33, d, :])
                yield
                P.ts(ze, bank(bz), -80.0, ALU.max)
                P.act(ze, ze, AF.Exp, scale=-1.0)
                yield
                P.act(zb, ze, AF.Ln, bias=onesF[:, 0:1])
                yield
                bg = nb()
                P.mm(bank(bg), U_, zb)
                bl = nb()
                for h in range(4):
                    P.mm(ps[:, bl, h:h + 1], zb[:, h * 128:(h + 1) * 128], onesF[:, 0:1])
                yield
                P.act(eg_b, bank(bg), AF.Exp, scale=-1.0 / 16.0)
                P.act(eng_b, bank(bg), AF.Exp, scale=1.0 / 16.0)
                P.act(glt, ps[:, bl, 0:4], AF.Exp, scale=-1.0 / 16.0)
                yield

            def prefixB(s_, d, c, bs_, ka):
                qtl, ktl, qtt, ktt, atb, _ = bs_
                eg_b, eng_b = egs[ka % 2], engs[ka % 2]
                U_ = Uf if d == 0 else Ub
                ti = s_ * nch + c
                P.tt(qtl, qT[:, ti, :], eg_b, ALU.mult)
                P.tt(ktl, kT[:, ti, :], eng_b, ALU.mult)
                yield
                bq = nb()
                for h in range(4):
                    P.tr(bbf(bq)[:, h * 128:(h + 1) * 128], qtl[:, h * 128:(h + 1) * 128], identB)
                P.copy(qtt.rearrange("p h t -> p (h t)"), bbf(bq)[:, 0:512], eng="act")
                bk = nb()
                for h in range(4):
                    P.tr(bbf(bk)[:, h * 128:(h + 1) * 128], ktl[:, h * 128:(h + 1) * 128], identB)
                P.copy(ktt.rearrange("p h t -> p (h t)"), bbf(bk)[:, 0:512], eng="dve")
                yield
                ba = nb()
                for h in range(4):
                    P.mm(ps[:, ba, h * 128:(h + 1) * 128], ktt[:, h, :], qtt[:, h, :])
                P.tt(atb, b4(ba), bc_h(U_), ALU.mult)
                yield

            def suffix(s_, d, c, bs_, first_touch, ka):
                qtl, ktl, qtt, ktt, atb, _ = bs_
                glt = glts[ka % 3]
                ti = s_ * nch + c
                bo = nbk(2)
                for h in range(4):
                    for half in range(2):
                        fc = h * 2 + half
                        dstp = ps[:, bo + fc // 4, (fc % 4) * 128:(fc % 4 + 1) * 128]
                        P.mm(dstp, vT[:, ti, h * 256 + half * 128:h * 256 + (half + 1) * 128], atb[:, h, :],
                             start=True, stop=False)
                        P.mm(dstp, Sb[:, h, half * 128:(half + 1) * 128], qtt[:, h, :], start=False, stop=True)
                for k2 in range(2):
                    oa = oslice(c, k2 * 4, k2 * 4 + 4)
                    if first_touch[c]:
                        P.copy(oa, b4(bo + k2), eng="dve" if k2 == 0 else "act")
                    else:
                        P.tt(oa, oa, b4(bo + k2), ALU.add)
                first_touch[c] = False
                yield
                bs2 = nbk(2)
                for h in range(4):
                    P.mm(ps[:, bs2 + h // 2, (h % 2) * 256:(h % 2 + 1) * 256], ktl[:, h * 128:(h + 1) * 128],
                         vT[:, ti, h * 256:(h + 1) * 256])
                P.tt(Sf, Sf, ps[:, bs2:bs2 + 2, :].rearrange("p a (h v) -> p (a h) v", h=2, v=256), ALU.add)
                P.tt(Sf, Sf, glt.unsqueeze(2).broadcast_to([128, 4, 256]), ALU.mult)
                P.copy(Sb, Sf, eng="act")
                yield

            for s_ in range(nseq):
                first_touch = [True] * nch
                steps = [(d, (cidx if d == 0 else nch - 1 - cidx)) for d in range(2) for cidx in range(nch)]
                def drive(gens):
                    gens = [g_ for g_ in gens if g_ is not None]
                    while gens:
                        for g_ in list(gens):
                            try:
                                next(g_)
                            except StopIteration:
                                gens.remove(g_)

                ns = len(steps)

                def gA(k2):
                    return prefixA(s_, steps[k2][0], steps[k2][1], k2) if k2 < ns else None

                def gB(k2):
                    return prefixB(s_, steps[k2][0], steps[k2][1], bufsets[k2 % 2], k2) if k2 < ns else None

                drive([gA(0)])
                drive([gB(0), gA(1)])
                for k_, (d, c) in enumerate(steps):
                    if k_ % nch == 0:
                        if grp == 0:
                            P.memset(Sf, 0.0, eng="pool")
                        else:
                            P.dma(Sf, state_gla[d].rearrange("h k v -> k h v"), q="sp")
                        P.copy(Sb, Sf, eng="pool")
                    drive([gA(k_ + 2), gB(k_ + 1), suffix(s_, d, c, bufsets[k_ % 2], first_touch, k_)])
                    if k_ % nch == nch - 1 and grp == 0:
                        P.dma(nsg[s_, d].rearrange("h k v -> k h v"), Sf, q="sp", is_output=True)
                for h in range(4):
                    for t_ in range(0, L, 512):
                        w = min(512, L - t_)
                        b = nb()
                        srcs = []
                        for half in range(2):
                            fc = h * 2 + half
                            if grp == 0:
                                src = oaccA[:, fc, t_:t_ + w]
                            else:
                                src = (oacc_lo if t_ == 0 else oacc_hi)[:, fc, 0:512]
                            srcs.append(src)
                            sq = ar.view(ZB_ + half * 1024, [128, 512], BF16)
                            P.act(sq[:, 0:w], src, AF.Square)
                            P.mm(ps[:, b, 0:w], onesP, sq[:, 0:w], start=(half == 0), stop=(half == 1))
                        rs = rstd[:, 0, 0:w]
                        P.act(rs, ps[:, b, 0:w], AF.Ln, bias=epsT[:, 0:1], scale=1.0 / 256.0)
                        P.act(rs, rs, AF.Exp, scale=-0.5)
                        for half in range(2):
                            fc = h * 2 + half
                            t_f = ar.view(EG_, [128, 512], F32)
                            P.stt(t_f[:, 0:w], srcs[half], onorm2[:, half:half + 1], rs, ALU.mult, ALU.mult)
                            sl = slice(s_ * L + t_, s_ * L + t_ + w)
                            P.tt(mixT[:, fc, sl], t_f[:, 0:w], gS[:, fc, sl], ALU.mult)
            if "gla" in debug_names and grp == DBG_GRP:
                dump("d_mix", mixT, [128, 8, 1024], BF16)
            wo = ar.view(WPO, [128, KC, 1024], BF16)
            P.dma(wo[:, :, 0:512], w_out[:, :, 0:512], q="pool")
            P.dma(wo[:, :, 512:1024], w_out[:, :, 512:1024], q="pool")
            for n in range(KC):
                for tt in range(2):
                    b = nb()
                    for kc in range(KC):
                        P.mm(bank(b), wo[:, kc, n * 128:(n + 1) * 128], mixT[:, kc, tt * TT:(tt + 1) * TT],
                             start=(kc == 0), stop=(kc == KC - 1))
                    xs = xT[:, n, T0 + tt * TT:T0 + (tt + 1) * TT]
                    P.stt(xs, bank(b), modG[:, l, 1, n, grp:grp + 1], xs, ALU.mult, ALU.add)


    if stage != "all":
        ada_flush()
    if stage == "ffn0":
        ffn(0, 0)
        final_out()
    elif stage == "mix0":
        mixer_even()
        final_out()
    elif stage == "mix1":
        mixer_odd()
        final_out()
    else:
        def pre(l_, s_):
            def f(tt):
                modnorm_tile(l_, s_, tt)
                if tt == NTT - 1:
                    pre_normed[0] = (l_, s_)
            return f
        ffn(0, 0, after_tile=pre(0, 1))
        mixer_even()
        ffn(0, 1, after_tile=pre(1, 0))
        ffn(1, 0, after_tile=pre(1, 1))
        mixer_odd()
        ffn(1, 1, after_tile=final_tile)

    P.emit()
    stack.close()
    return nc, dbg


PT_ADAB = 0
PT_NORMG = PT_ADAB + 144
PT_FINALG = PT_NORMG + 48
PT_CONV = PT_FINALG + 8
PT_SINK = PT_CONV + 60
PT_ALOG = PT_SINK + 4
PT_DTB = PT_ALOG + 8
PT_ONORME = PT_DTB + 8
PT_ONORMO = PT_ONORME + 1
PT_COLS = PT_ONORMO + 2
DBG_GRP = 0
DBG_STEP = (0, 0)

C_IDENT = 0
C_UF, C_UB, C_SEL127, C_SEL0, C_ONES, C_OFFD, C_ML, C_MU, C_RM = [128 * i for i in range(1, 10)]
CST_COLS = 128 * 10
CSTB_COLS = 512 + 9 * 128


def _fm(v):
    v = np.asarray(v, np.float32)
    return np.ascontiguousarray(v.reshape(-1, 128).T)


def make_tables(inputs):
    pt = np.zeros((128, PT_COLS), np.float32)
    for l in range(2):
        pt[:, PT_ADAB + l * 72: PT_ADAB + (l + 1) * 72] = _fm(inputs["ada_b"][l])
        for s in range(3):
            o = PT_NORMG + (l * 3 + s) * 8
            pt[:, o:o + 8] = _fm(inputs["norm_g"][l, s])
    pt[:, PT_FINALG:PT_FINALG + 8] = _fm(inputs["final_g"])
    cv = np.asarray(inputs["even_conv"], np.float32)[0]
    pt[:, PT_CONV:PT_CONV + 60] = cv.T.reshape(12, 128, 5).transpose(1, 0, 2).reshape(128, 60)
    pt[:, PT_SINK:PT_SINK + 4] = np.broadcast_to(np.asarray(inputs["even_sink"], np.float32)[0][None, :], (128, 4))
    pt[:, PT_ALOG:PT_ALOG + 8] = np.broadcast_to(np.asarray(inputs["even_a_log"], np.float32)[0].reshape(1, 8), (128, 8))
    pt[:, PT_DTB:PT_DTB + 8] = np.broadcast_to(np.asarray(inputs["even_dt_bias"], np.float32)[0].reshape(1, 8), (128, 8))
    pt[:, PT_ONORME:PT_ONORME + 1] = np.asarray(inputs["even_onorm"], np.float32)[0].reshape(128, 1)
    pt[:, PT_ONORMO:PT_ONORMO + 2] = _fm(inputs["odd_onorm"][0])
    cst = np.zeros((128, CST_COLS), np.float32)
    cst[:, C_IDENT:C_IDENT + 128] = np.eye(128, dtype=np.float32)
    kk, ii = np.meshgrid(np.arange(128), np.arange(128), indexing="ij")
    NEG = -30000.0
    cst[:, C_UF:C_UF + 128] = (kk <= ii)
    cst[:, C_UB:C_UB + 128] = (kk >= ii)
    cst[127, C_SEL127:C_SEL127 + 128] = 1.0
    cst[0, C_SEL0:C_SEL0 + 128] = 1.0
    cst[:, C_ONES:C_ONES + 128] = 1.0
    cst[:, C_OFFD:C_OFFD + 128] = (kk != ii)
    cst[:, C_ML:C_ML + 128] = np.where(ii <= kk, 0.0, NEG)
    cst[:, C_MU:C_MU + 128] = np.where(ii >= kk, 0.0, NEG)
    rm = np.zeros((128, 128), np.float32)
    for dp in range(128):
        if (dp % 64) < 32:
            rm[dp + 32, dp] = -1.0
        else:
            rm[dp - 32, dp] = 1.0
    cst[:, C_RM:C_RM + 128] = rm
    return pt, cst


def make_bmask():
    i, j = np.meshgrid(np.arange(128), np.arange(128), indexing="ij")
    ms = [(i // 8 == j // 8) & (i != j)]
    for b in (8, 16, 32, 64):
        ms.append((i // (2 * b) == j // (2 * b)) & ((i // b) % 2 == 1) & ((j // b) % 2 == 0))
    for b in (8, 16, 32, 64):
        ms.append((i // (2 * b) == j // (2 * b)) & ((i // b) % 2 == 0) & ((j // b) % 2 == 1))
    return np.ascontiguousarray(np.concatenate([m.astype(np.float32) for m in ms], axis=1))


def make_rope():
    t = np.arange(1024)
    row = (t // 64).astype(np.float64)
    col = (t % 64).astype(np.float64)
    inv = 10000.0 ** (-np.arange(32, dtype=np.float64) / 32.0)
    ang = np.zeros((128, 1024))
    for d in range(128):
        pos = row if d < 64 else col
        ang[d] = pos * np.float32(inv[d % 32])
    ang32 = np.zeros((128, 1024), np.float32)
    inv32 = (np.float32(10000.0) ** (-np.arange(32, dtype=np.float32) / np.float32(32))).astype(np.float32)
    for d in range(128):
        pos = (row if d < 64 else col).astype(np.float32)
        ang32[d] = pos * inv32[d % 32]
    return np.ascontiguousarray(np.stack([np.cos(ang32), np.sin(ang32)], axis=1).astype(np.float32))


def make_in_maps(inputs, stage="all"):
    pt, cst = make_tables(inputs)
    rope_t = make_rope()
    bmask_t = make_bmask()
    wg = np.asarray(inputs["odd_w_gate"], np.float32)[0]
    wgpad_t = np.zeros((33, 2, 512), np.float32)
    wgpad_t[0:16, 0, :] = wg[0]
    wgpad_t[16:32, 1, :] = wg[1]
    wgpad_t[32, :, :] = np.asarray(inputs["odd_gate_bias"], np.float32)[0]
    maps = []
    xp = np.asarray(inputs["x_prompt"], np.float32)
    xs = np.asarray(inputs["x_sample"], np.float32)
    for c in range(8):
        xin = np.concatenate([xp[4 * c:4 * c + 4].reshape(1024, D), xs[c]], axis=0)
        cond = np.stack([np.asarray(inputs["c_ctx"], np.float32), np.asarray(inputs["c"], np.float32)[c]], axis=-1)
        condT = np.ascontiguousarray(cond.reshape(KC, 128, 2).transpose(1, 0, 2))
        m = {"xin": np.ascontiguousarray(xin), "condT": condT, "ptab": pt, "cst": cst,
             "ada_w": np.asarray(inputs["ada_w"], np.float32),
             "even_w_in": np.asarray(inputs["even_w_in"], np.float32),
             "even_w_out": np.asarray(inputs["even_w_out"], np.float32),
             "rope": rope_t, "bmask": bmask_t, "wgpad": wgpad_t,
             "odd_w_in": np.asarray(inputs["odd_w_in"], np.float32),
             "odd_w_out": np.asarray(inputs["odd_w_out"], np.float32),
             "state_gla": np.ascontiguousarray(np.asarray(inputs["state_gla"], np.float32)[c, 0]),
             "cache_k": np.ascontiguousarray(np.asarray(inputs["cache_k"], np.float32)[c, 0]),
             "cache_v": np.ascontiguousarray(np.asarray(inputs["cache_v"], np.float32)[c, 0]),
             "state_delta": np.ascontiguousarray(np.asarray(inputs["state_delta"], np.float32)[c, 0])}
        if stage in ("all", "ffn0"):
            m["ffn_w_gu"] = np.asarray(inputs["ffn_w_gu"], np.float32)
            m["ffn_w_down"] = np.asarray(inputs["ffn_w_down"], np.float32)
        maps.append(m)
    return maps


_CACHE = {}


def kernel(**inputs):
    if "nc" not in _CACHE:
        _CACHE["nc"] = build_program("all")[0]
    nc = _CACHE["nc"]
    maps = make_in_maps(inputs)
    res = run_bass_kernel_spmd(nc, maps, core_ids=list(range(8)))
    rs = res.results
    ys = [np.asarray(r["yout"], np.float32) for r in rs]
    y_prompt = np.concatenate([y[:1024].reshape(4, 256, D) for y in ys], axis=0)
    y_sample = np.stack([y[1024:] for y in ys], axis=0)
    nsd = np.concatenate([np.asarray(r["nsd"], np.float32) for r in rs], axis=0)[:, None]
    nck = np.concatenate([np.asarray(r["nck"], np.float32).reshape(4, 256, 2, 128) for r in rs], axis=0)[:, None]
    ncv = np.concatenate([np.asarray(r["ncv"], np.float32).reshape(4, 256, 2, 128) for r in rs], axis=0)[:, None]
    nsg = np.concatenate([np.asarray(r["nsg"], np.float32) for r in rs], axis=0)[:, None]
    return (y_prompt, y_sample, np.ascontiguousarray(nsd), np.ascontiguousarray(nck),
            np.ascontiguousarray(ncv), np.ascontiguousarray(nsg))
```
